# Optimizing a Trainium2 kernel written in Bass

```python
import jax, jax.numpy as jnp
from jax import lax
import numpy as np

D_MODEL = 4096
BATCH = 1
SEQ = 16384
DEPTH = 2
DEC_BATCH = 16
DEC_SEQ = 16
PAST_LEN = 1024

CHUNK = 64
N_MIXERS = 2
N_HGRN_LAYERS = (DEPTH + N_MIXERS - 1) // N_MIXERS
N_CONV_LAYERS = DEPTH // N_MIXERS
HGRN_EXPAND = 128
HGRN_HEADS = D_MODEL // HGRN_EXPAND
HGRN_DK = HGRN_EXPAND
HGRN_DV = D_MODEL // HGRN_HEADS
HGRN_F_DIM = HGRN_HEADS * HGRN_DK
CONV_WIDTH = 3
D_FF = ((8 * D_MODEL + 3 * 256 - 1) // (3 * 256)) * 256
NORM_EPS = 1e-6

kernel_name = "hgrn2_shortconv_stream_step"


def _rmsnorm(x, gain):
    xf = x.astype(jnp.float32)
    y = xf * lax.rsqrt(jnp.mean(xf * xf, axis=-1, keepdims=True) + NORM_EPS)
    return (y * gain.astype(jnp.float32)).astype(x.dtype)


def _swiglu(h, w_in, w_out):
    gate, up = jnp.split(h @ w_in, 2, axis=-1)
    return (jax.nn.silu(gate) * up) @ w_out


def _hgrn2_chunk_step(S, inp):
    q, k, v, g = inp
    C = q.shape[2]
    b = jnp.cumsum(g, axis=2)
    o_inter = jnp.einsum('bhtk,bhkv->bhtv', q * jnp.exp(b), S)
    causal = jnp.tril(jnp.ones((C, C), dtype=bool))
    diff = b[:, :, :, None, :] - b[:, :, None, :, :]
    decay = jnp.exp(jnp.where(causal[None, None, :, :, None], diff, -jnp.inf))
    A = jnp.einsum('bhtk,bhsk,bhtsk->bhts', q, k, decay)
    o = o_inter + jnp.einsum('bhts,bhsv->bhtv', A, v)
    b_last = b[:, :, -1:, :]
    S_new = jnp.exp(b_last[:, :, 0, :])[..., None] * S + jnp.einsum(
        'bhsk,bhsv->bhkv', k * jnp.exp(b_last - b), v)
    return S_new, o


def _hgrn2_mixer(h, S0, lb, w_in, out_gain, w_out, chunk):
    Bn, L, _ = h.shape
    n_chunks = L // chunk
    proj = h @ w_in
    zq, zf, zi, zg = jnp.split(proj, [HGRN_F_DIM, 2 * HGRN_F_DIM, 2 * HGRN_F_DIM + D_MODEL], axis=-1)
    q = jax.nn.silu(zq.astype(jnp.float32))
    zf32 = zf.astype(jnp.float32)
    f = lb + (1.0 - lb) * jax.nn.sigmoid(zf32)
    logf = jnp.log(f)
    k = (1.0 - lb) * jax.nn.sigmoid(-zf32)
    v = zi.astype(jnp.float32)

    def to_chunks(t, d):
        return t.reshape(Bn, n_chunks, chunk, HGRN_HEADS, d).transpose(1, 0, 3, 2, 4)

    S_fin, o = lax.scan(_hgrn2_chunk_step, S0.astype(jnp.float32),
                        (to_chunks(q, HGRN_DK), to_chunks(k, HGRN_DK),
                         to_chunks(v, HGRN_DV), to_chunks(logf, HGRN_DK)))
    o = o.transpose(1, 0, 3, 2, 4).reshape(Bn, L, HGRN_HEADS, HGRN_DV)
    o = o * lax.rsqrt(jnp.mean(o * o, axis=-1, keepdims=True) + NORM_EPS) * out_gain.astype(jnp.float32)
    o = o * jax.nn.silu(zg.astype(jnp.float32)).reshape(Bn, L, HGRN_HEADS, HGRN_DV)
    y = o.reshape(Bn, L, D_MODEL).astype(h.dtype) @ w_out
    return y, S_fin.astype(S0.dtype)


def _short_conv_mixer(h, buf, w_in, w_conv, w_out):
    gb, gc, u = jnp.split(h @ w_in, 3, axis=-1)
    cu = gc * u
    full = jnp.concatenate([buf.astype(cu.dtype), cu], axis=1)
    conv = lax.conv_general_dilated(full, w_conv.astype(cu.dtype)[:, None, :], window_strides=(1,),
                                    padding='VALID', dimension_numbers=('NWC', 'WIO', 'NWC'),
                                    feature_group_count=D_MODEL)
    y = (gb * conv) @ w_out
    return y, full[:, -(CONV_WIDTH - 1):, :].astype(buf.dtype)


def _trunk(x, hgrn_state, conv_state, chunk, lb_all, hgrn_w_in, hgrn_out_gain, hgrn_w_out,
           conv_w_in, conv_w, conv_w_out, norm_mix, norm_ffn, ffn_w_in, ffn_w_out, norm_final):
    new_h, new_c = [], []
    for i in range(DEPTH):
        h = _rmsnorm(x, norm_mix[i])
        j = i // N_MIXERS
        if i % N_MIXERS == 0:
            y, s = _hgrn2_mixer(h, hgrn_state[j], lb_all[i], hgrn_w_in[j], hgrn_out_gain[j],
                                hgrn_w_out[j], chunk)
            new_h.append(s)
        else:
            y, s = _short_conv_mixer(h, conv_state[j], conv_w_in[j], conv_w[j], conv_w_out[j])
            new_c.append(s)
        x = x + y
        x = x + _swiglu(_rmsnorm(x, norm_ffn[i]), ffn_w_in[i], ffn_w_out[i])
    return _rmsnorm(x, norm_final), jnp.stack(new_h), jnp.stack(new_c)


def setup_inputs(seed: int = 0) -> dict:
    key = jax.random.key(seed)
    ks = jax.random.split(key, 16)
    f32 = jnp.float32
    nrm = lambda k, shape, s: jax.random.normal(k, shape, f32) * s
    return {
        "x_prompt": nrm(ks[0], (BATCH, SEQ, D_MODEL), 1.0),
        "x_sample": nrm(ks[1], (DEC_BATCH, DEC_SEQ, D_MODEL), 1.0),
        "state_hgrn": nrm(ks[2], (N_HGRN_LAYERS, DEC_BATCH, HGRN_HEADS, HGRN_DK, HGRN_DV), 0.5),
        "state_conv": nrm(ks[3], (N_CONV_LAYERS, DEC_BATCH, CONV_WIDTH - 1, D_MODEL), 1.0),
        "hgrn_lb_logits": nrm(ks[4], (DEPTH + 1, HGRN_F_DIM), 0.1),
        "hgrn_w_in": nrm(ks[5], (N_HGRN_LAYERS, D_MODEL, 2 * HGRN_F_DIM + 2 * D_MODEL), D_MODEL ** -0.5),
        "hgrn_out_gain": 1.0 + nrm(ks[6], (N_HGRN_LAYERS, HGRN_DV), 0.02),
        "hgrn_w_out": nrm(ks[7], (N_HGRN_LAYERS, D_MODEL, D_MODEL), D_MODEL ** -0.5),
        "conv_w_in": nrm(ks[8], (N_CONV_LAYERS, D_MODEL, 3 * D_MODEL), D_MODEL ** -0.5),
        "conv_w": nrm(ks[9], (N_CONV_LAYERS, CONV_WIDTH, D_MODEL), CONV_WIDTH ** -0.5),
        "conv_w_out": nrm(ks[10], (N_CONV_LAYERS, D_MODEL, D_MODEL), D_MODEL ** -0.5),
        "norm_mix": 1.0 + nrm(ks[11], (DEPTH, D_MODEL), 0.02),
        "norm_ffn": 1.0 + nrm(ks[12], (DEPTH, D_MODEL), 0.02),
        "ffn_w_in": nrm(ks[13], (DEPTH, D_MODEL, 2 * D_FF), D_MODEL ** -0.5),
        "ffn_w_out": nrm(ks[14], (DEPTH, D_FF, D_MODEL), D_FF ** -0.5),
        "norm_final": 1.0 + nrm(ks[15], (D_MODEL,), 0.02),
    }


def reference(x_prompt, x_sample, state_hgrn, state_conv, hgrn_lb_logits, hgrn_w_in, hgrn_out_gain,
              hgrn_w_out, conv_w_in, conv_w, conv_w_out, norm_mix, norm_ffn, ffn_w_in, ffn_w_out,
              norm_final):
    lb_all = jnp.cumsum(jax.nn.softmax(hgrn_lb_logits.astype(jnp.float32), axis=0), axis=0)
    weights = (lb_all, hgrn_w_in, hgrn_out_gain, hgrn_w_out, conv_w_in, conv_w, conv_w_out,
               norm_mix, norm_ffn, ffn_w_in, ffn_w_out, norm_final)
    bp = x_prompt.shape[0]
    h0 = jnp.zeros((N_HGRN_LAYERS, bp, HGRN_HEADS, HGRN_DK, HGRN_DV), state_hgrn.dtype)
    c0 = jnp.zeros((N_CONV_LAYERS, bp, CONV_WIDTH - 1, D_MODEL), state_conv.dtype)
    y_prompt, hgrn_p, conv_p = _trunk(x_prompt, h0, c0, CHUNK, *weights)
    y_sample, hgrn_s, conv_s = _trunk(x_sample, state_hgrn, state_conv, x_sample.shape[1], *weights)
    return (y_prompt, y_sample, hgrn_p, hgrn_s, conv_p, conv_s)
```

```python
import os
import numpy as np
import concourse.bass as bass
import concourse.mybir as mybir
from concourse.bass_utils import run_bass_kernel_spmd
from contextlib import ExitStack

F32 = mybir.dt.float32
BF16 = mybir.dt.bfloat16
AF = mybir.ActivationFunctionType
ALU = mybir.AluOpType
NCORES = 8
EPS = 1e-6


class _Stop(Exception):
    pass


class Cfg:
    stop = None

    def __init__(self, D=4096, NH=32, DFF=11008, SEQ=16384, T=512, DB=16, DS=16, G=4, NQ=4):
        self.D, self.NH, self.DFF, self.SEQ, self.T, self.DB, self.DS, self.G = D, NH, DFF, SEQ, T, DB, DS, G
        self.FC = D // 128
        self.KH = self.FC // 2
        self.NS = DFF // 128
        self.ROUNDS = SEQ // (NCORES * T)
        self.NPB = T // 128
        self.NCH = T // 64
        self.SPC = DB // NCORES
        self.EX = self.SPC * DS
        self.NX = T + self.EX
        self.NQ = NQ
        base = self.NS // NQ
        rem = self.NS % NQ
        self.QS = [base + (1 if i < rem else 0) for i in range(NQ)]
        self.QMAX = max(self.QS)
        self.NG = NH // G
        assert DS == 16 and self.SPC == 2 and D == NH * 128


FULL = Cfg()


def tile_w(W, kh):
    K, NO = W.shape
    kc = K // 128
    nh = kc // kh
    t = W.reshape(nh, kh, 128, NO // 128, 128)
    t = t.transpose(3, 0, 2, 1, 4)
    return np.ascontiguousarray(t).reshape(NO // 128 * nh, 128, kh * 128)


def prep_weights(cfg, inp):
    D, NH, NS = cfg.D, cfg.NH, cfg.NS
    out = {}
    W = inp["hgrn_w_in"][0]
    cols = np.concatenate([np.arange(s * D + h * 128, s * D + h * 128 + 128) for h in range(NH) for s in range(4)])
    out["w_hin"] = tile_w(W[:, cols], cfg.KH)
    out["w_hout"] = tile_w(inp["hgrn_w_out"][0], cfg.KH)
    W = inp["conv_w_in"][0]
    cols = np.concatenate([np.arange(s * D + f * 128, s * D + f * 128 + 128) for f in range(cfg.FC) for s in (1, 2)]
                          + [np.arange(0, D)])
    out["w_cin"] = tile_w(W[:, cols], cfg.KH)
    out["w_cout"] = tile_w(inp["conv_w_out"][0], cfg.KH)
    for l in range(2):
        W = inp["ffn_w_in"][l]
        cols = np.concatenate([np.arange(s * cfg.DFF + sl * 128, s * cfg.DFF + sl * 128 + 128)
                               for sl in range(NS) for s in range(2)])
        out[f"w_fin{l}"] = tile_w(W[:, cols], cfg.KH)
        W = inp["ffn_w_out"][l]
        tl = np.zeros((cfg.NQ, cfg.FC, 128, cfg.QMAX, 128), np.float32)
        s0 = 0
        for q in range(cfg.NQ):
            n = cfg.QS[q]
            blk = W[s0 * 128:(s0 + n) * 128, :].reshape(n, 128, cfg.FC, 128)
            tl[q, :, :, :n, :] = blk.transpose(2, 1, 0, 3)
            s0 += n
        out[f"w_fout{l}"] = tl.reshape(cfg.NQ * cfg.FC, 128, cfg.QMAX * 128)
    return out


WNAMES = ["w_hin", "w_hout", "w_fin0", "w_fout0", "w_cin", "w_cout", "w_fin1", "w_fout1"]


def wshape(cfg, name):
    D, NH, NS, FC, KH = cfg.D, cfg.NH, cfg.NS, cfg.FC, cfg.KH
    c = KH * 128
    if name == "w_hin":
        return (NH * 4 * 2, c)
    if name in ("w_hout", "w_cout"):
        return (FC * 2, c)
    if name == "w_cin":
        return (FC * 3 * 2, c)
    if name.startswith("w_fin"):
        return (NS * 2 * 2, c)
    return (cfg.NQ * FC, cfg.QMAX * 128)


def wpieces(cfg, name, pmax=int(os.environ.get("PMAX", "128"))):
    nt, c = wshape(cfg, name)
    out = []
    t0 = 0
    while t0 < nt:
        n = min(pmax, nt - t0)
        assert n % NCORES == 0
        out.append((t0, n))
        t0 += n
    return out


class Sem:
    def __init__(self, h):
        self.h = h
        self.v = 0


class Buf:
    __slots__ = ("name", "w", "r", "dsem")

    def __init__(self, name=""):
        self.name = name
        self.w = None
        self.r = []
        self.dsem = None


class Eng:
    def __init__(self, name, handle, sem, skip_self=False):
        self.name, self.e, self.sem, self.skip_self = name, handle, sem, skip_self
        self.seen = {}


class Prog:
    def __init__(self, nc, es):
        self.nc = nc
        self.es = es
        self.nsem = 0
        mk = self.new_sem
        self.pe = Eng("pe", nc.tensor, mk("pe"), skip_self=True)
        self.act = Eng("act", nc.scalar, mk("act"))
        self.dve = Eng("dve", nc.vector, mk("dve"))
        self.pool = Eng("pool", nc.gpsimd, mk("pool"))
        self.sp = Eng("sp", nc.sync, mk("sp"))
        self.engs = [self.pe, self.act, self.dve, self.pool, self.sp]
        self.nins = 0

    def new_sem(self, name):
        self.nsem += 1
        sm = Sem(self.es.enter_context(self.nc.semaphore(f"s_{name}_{self.nsem}")))
        if not hasattr(self, "sems"):
            self.sems = []
        self.sems.append(sm)
        return sm

    def _waits(self, eng, reads, writes):
        need = {}

        def req(ev):
            if ev is None:
                return
            s, v = ev
            if eng.skip_self and s is eng.sem:
                return
            if eng.seen.get(s, 0) >= v:
                return
            if need.get(s, 0) < v:
                need[s] = v
        for b in reads:
            req(b.w)
        for b in writes:
            req(b.w)
            for ev in b.r:
                req(ev)
        for s, v in need.items():
            eng.e.wait_ge(s.h, v)
            eng.seen[s] = v
            self.nins += 1

    def op(self, eng, fns, reads=(), writes=(), sem=None, inc=1):
        if callable(fns):
            fns = [fns]
        self._waits(eng, reads, writes)
        ins = None
        for f in fns:
            ins = f(eng.e)
            self.nins += 1
        s = eng.sem if sem is None else sem
        s.v += inc
        ins.then_inc(s.h, inc)
        ev = (s, s.v)
        for b in reads:
            b.r.append(ev)
        for b in writes:
            b.w = ev
            b.r = []
        return ev

    def dma(self, eng, out, in_, sembuf, reads=(), writes=(), **kw):
        if isinstance(sembuf, Sem):
            sem = sembuf
        else:
            if sembuf.dsem is None:
                sembuf.dsem = self.new_sem("d" + sembuf.name)
            sem = sembuf.dsem
        return self.op(eng, lambda e: e.dma_start(out=out, in_=in_, **kw), reads, writes, sem=sem, inc=16)

    def barrier(self):
        for e in self.engs:
            for sm in self.sems:
                if sm.v == 0 or (sm is e.sem):
                    continue
                if e.seen.get(sm, 0) < sm.v:
                    e.e.wait_ge(sm.h, sm.v)
                    e.seen[sm] = sm.v


class Arena:
    def __init__(self, t, words):
        self.t, self.words, self.off = t, words, 0

    def mark(self):
        return self.off

    def reset(self, m):
        self.off = m

    def f32(self, shape):
        n = int(np.prod(shape[1:]))
        self.off = (self.off + 15) // 16 * 16
        ap = self.t[0:shape[0], self.off:self.off + n]
        self.off += n
        assert self.off <= self.words, ("arena overflow", self.off, self.words)
        return _reshape(ap, shape)

    def bf16(self, shape):
        n = int(np.prod(shape[1:]))
        nw = (n + 1) // 2
        self.off = (self.off + 15) // 16 * 16
        ap = self.t[0:shape[0], self.off:self.off + nw].bitcast(BF16)[:, 0:n]
        self.off += nw
        assert self.off <= self.words, ("arena overflow", self.off, self.words)
        return _reshape(ap, shape)


def _reshape(ap, shape):
    if len(shape) == 2:
        return ap
    if len(shape) == 3:
        return ap.rearrange("p (a b) -> p a b", a=shape[1], b=shape[2])
    if len(shape) == 4:
        return ap.rearrange("p (a b c) -> p a b c", a=shape[1], b=shape[2], c=shape[3])
    raise ValueError


def const_layout(cfg):
    NX = cfg.NX
    o = {}
    c = 0
    for name, n in (("identf", 128), ("maskA", 128), ("maskS", 32), ("smask", NX), ("sm01", 2), ("pm64", 2)):
        o[name] = (c, n)
        c += n
    return o, c


def make_consts(cfg):
    lay, cwid = const_layout(cfg)
    T, NX = cfg.T, cfg.NX
    C = np.zeros((128, cwid), np.float32)
    o, n = lay["identf"]; C[:, o:o + n] = np.eye(128, dtype=np.float32)
    s_ = np.arange(128)[:, None]; t_ = np.arange(128)[None, :]
    o, n = lay["maskA"]; C[:, o:o + n] = ((s_ <= t_) & (s_ // 64 == t_ // 64)).astype(np.float32)
    s2 = np.arange(32)[:, None]; t2 = np.arange(32)[None, :]
    o, n = lay["maskS"]; C[0:32, o:o + n] = ((s2 <= t2) & (s2 // 16 == t2 // 16)).astype(np.float32)
    sm = np.ones(NX, np.float32); sm[0:T:64] = 0.0; sm[T:NX:16] = 0.0
    o, n = lay["smask"]; C[:, o:o + n] = sm[None, :]
    o, n = lay["sm01"]; C[0:16, o] = 1.0; C[16:32, o + 1] = 1.0
    o, n = lay["pm64"]; C[0:64, o] = 1.0; C[64:128, o + 1] = 1.0
    return C


def build(cfg):
    nc = bass.Bass("TRN2", target_bir_lowering=False, num_devices=NCORES)
    D, NH, FC, KH, T, EX, NX, G = cfg.D, cfg.NH, cfg.FC, cfg.KH, cfg.T, cfg.EX, cfg.NX, cfg.G
    NCH, NPB, R, NS = cfg.NCH, cfg.NPB, cfg.ROUNDS, cfg.NS
    WC = KH * 128
    WCMAX = max(WC, cfg.QMAX * 128)
    clay, CWID = const_layout(cfg)

    def din(name, shape, dt=F32):
        return nc.dram_tensor(name, list(shape), dt, kind="ExternalInput").ap()

    def dout(name, shape):
        return nc.dram_tensor(name, list(shape), F32, kind="ExternalOutput").ap()

    def dint(name, shape):
        return nc.dram_tensor(name, list(shape), F32, kind="Internal").ap()

    xp = din("xp", [R * T, D])
    xs = din("xs", [EX, D])
    sh = din("sh", [cfg.SPC, NH, 128, 128])
    sc = din("sc", [cfg.SPC, 2, D])
    lbl = din("lbl", [128, 3, NH])
    ogain = din("ogain", [128, 1])
    norms = din("norms", [128, 5, FC])
    cw = din("cw", [128, 3, FC])
    rmask = din("rmask", [128, 17])
    consts = din("consts", [128, CWID])
    wsh, wag_in, wag = {}, {}, {}
    WP = []
    for n in WNAMES:
        _, c = wshape(cfg, n)
        PT = 2 if n.startswith("w_fout") else 4
        for pi, (t0, nt) in enumerate(wpieces(cfg, n)):
            key = f"{n}_{pi}"
            assert nt % PT == 0
            WP.append((key, n, t0, nt, PT))
            wsh[key] = din(key, [nt // NCORES * 128, c])
            wag_in[key] = dint(key + "_i", [nt // NCORES * 128, c])
            wag[key] = dint(key + "_g", [nt * 128, c])
    NMID = 4
    midA = [dint(f"midA{i}", [4 * 64, WC]) for i in range(NMID)]
    midB = [dint(f"midB{i}", [4 * 32, cfg.QMAX * 128]) for i in range(NMID)]
    yp = dout("yp", [R * T, D])
    ys = dout("ys", [EX, D])
    hp = dout("hp", [NH, 128, 128])
    hs = dout("hs", [cfg.SPC, NH, 128, 128])
    cp = dout("cp", [R, 2, D])
    cs = dout("cs", [cfg.SPC, 2, D])
    agh_in = dint("agh_in", [G * 128, 129])
    agh = dint("agh", [NCORES * G * 128, 129])
    agh_mid = dint("agh_mid", [NCORES // 2 * G * 128, 129])
    agc_in = dint("agc_in", [128, FC * 2])
    agc = dint("agc", [NCORES * 128, FC * 2])
    agc_mid = dint("agc_mid", [NCORES // 2 * 128, FC * 2])
    sround = dint("sround", [128, NH, 128])

    es = ExitStack()
    with es:
        AW = 52000
        arena_t = es.enter_context(nc.sbuf_tensor("arena", [128, AW], F32))
        ar = Arena(arena_t, AW)
        banks = [es.enter_context(nc.psum_tensor(f"bank{i}", [128, 512], F32)) for i in range(6)]
        bank6b = es.enter_context(nc.psum_tensor("bank6b", [128, 1024], BF16))
        banks.append(None)
        banks.append(es.enter_context(nc.psum_tensor("bank7", [128, 512], F32)))
        P = Prog(nc, es)
        pe, act, dve, pool, sp = P.pe, P.act, P.dve, P.pool, P.sp
        B = Buf

        def chk(name):
            if cfg.stop == name:
                raise _Stop()

        const_t = ar.f32([128, CWID])
        cv = lambda k: const_t[:, clay[k][0]:clay[k][0] + clay[k][1]]
        identf = cv("identf"); smask = cv("smask"); sm01 = cv("sm01")[0:32, :]; pm64 = cv("pm64")
        identb = ar.bf16([128, 128]); onesb = ar.bf16([128, 128]); zerob = ar.bf16([128, 128])
        maskA = ar.bf16([128, 128]); maskS = ar.bf16([32, 32]); ones_f = ar.f32([128, 16])
        norms_t = ar.f32([128, 5, FC]); cw_t = ar.f32([128, 3, FC]); rm_t = ar.f32([128, 17])
        lb_t = ar.f32([128, NH]); oml_t = ar.f32([128, NH]); noml_t = ar.f32([128, NH])
        lbl_t = ar.f32([128, 3, NH]); og_t = ar.f32([128, 1])
        prev7 = ar.f32([128, 2 * FC])
        NSLOT = 6
        wslots = [ar.bf16([128, WCMAX]) for _ in range(NSLOT)]
        h_t = ar.bf16([128, FC, NX])
        aux = ar.bf16([128, FC, NX + 8])
        rstd_t = ar.f32([128, NX])
        gtmp = [ar.f32([128, NX]) for _ in range(2)]
        sqt = [ar.bf16([128, NX]) for _ in range(2)]
        ostage = [ar.f32([128, 1024]) for _ in range(2)]
        hal_t = ar.f32([128, 2, FC]); halg = ar.f32([128, NCORES, 2 * FC]); halo = ar.f32([128, 2, FC])
        XR0 = ar.mark()
        x_sb = ar.f32([128, FC, NX])
        XRx = ar.mark()
        XR1 = AW
        ar.reset(XR0)
        xst = [ar.f32([128, D // 2]) for _ in range(2)]
        junk = ar.f32([128, D // 2])
        ssq = ar.f32([128, 8])
        XRa = ar.mark()
        ar.reset(XR0)
        gs_ol = ar.bf16([128, G, NX]); gs_qc = ar.bf16([128, G, NX]); gs_sg = ar.bf16([128, G, NX])
        tq = ar.f32([128, NX]); tsig = ar.f32([128, NX]); tg = ar.f32([128, NX]); tk = ar.f32([128, NX])
        tb = ar.f32([128, NX]); teb = ar.f32([128, NX]); tenb = ar.f32([128, NX])
        tqt = ar.bf16([128, NX]); tkt = ar.bf16([128, NX]); tv = ar.bf16([128, NX])
        ktok = ar.bf16([128, NPB, 128]); vtok = ar.bf16([128, NPB, 128])
        ktokm = ar.bf16([128, NPB, 2, 128])
        vtoks = ar.bf16([32, 128]); kt01 = ar.bf16([32, 2, 128]); ktoks = ar.bf16([32, 128])
        atm = ar.bf16([128, NPB, 128]); atms = ar.bf16([32, 32])
        td = ar.f32([128, NCH + 2]); tci = ar.f32([128, NCH]); tE = ar.f32([128, NCH])
        Mf = [ar.f32([128, 128]) for _ in range(2)]; Mtmp = ar.f32([128, 128]); Mb = ar.bf16([128, NCH, 128])
        Mbs = ar.bf16([128, 2, 128])
        LD = ar.f32([128, G, 129]); LDj = [ar.f32([128, G, 129]) for _ in range(2)]
        S_t = ar.f32([128, G, 128]); Sin = ar.f32([128, G, 128]); Stmp = ar.f32([128, G, 128]); M0b = ar.bf16([128, G, 128])
        S0s = ar.f32([128, 2, G, 128]); NSs = ar.f32([128, 2, G, 128])
        to32 = ar.f32([128, NX]); osq = ar.bf16([128, NX]); t1 = ar.f32([128, NX]); rs2 = ar.f32([128, NX])
        XRb = ar.mark()
        print("arena words: xsb_end", XRx, "p0_end", XRa, "p1_end", XRb, "of", AW)
        ar.reset(max(XRx, XRa, XRb))

        b_const = B("const"); b_prev7 = B("prev7")
        b_wslot = [B(f"ws{i}") for i in range(NSLOT)]
        b_h = [B(f"h{i}") for i in range(FC)]
        b_aux = [B(f"aux{i}") for i in range(max(FC, cfg.QMAX))]
        b_x = [B(f"x{i}") for i in range(FC)]
        b_bank = [B(f"bank{i}") for i in range(8)]
        b_rstd = B("rstd"); b_gtmp = [B("g0"), B("g1")]; b_sq = [B("q0"), B("q1")]
        b_ost = [B("o0"), B("o1")]
        s_cc = P.new_sem("cc")
        b_wag = {}
        RG1 = [[0, 1, 2, 3], [4, 5, 6, 7]]
        RG2 = [[0, 4], [1, 5], [2, 6], [3, 7]]

        def allgather(in_ap, mid_ap, out_ap, reads, outbuf, midbuf):
            P.op(pool, lambda e: e.collective_compute("AllGather", ALU.bypass, replica_groups=RG1, ins=[in_ap], outs=[mid_ap]),
                 reads=reads, writes=[midbuf], sem=s_cc, inc=1)
            P.op(pool, lambda e: e.collective_compute("AllGather", ALU.bypass, replica_groups=RG2, ins=[mid_ap], outs=[out_ap]),
                 reads=[midbuf], writes=[outbuf], sem=s_cc, inc=1)
        b_xst = [B("xst0"), B("xst1")]; b_ssq = B("ssq"); b_junk = B("junk")
        b_gs = [B(f"gs{i}") for i in range(G)]
        bt = {k: B(k) for k in ("tq", "tsig", "tg", "tk", "tb", "teb", "tenb", "tqt", "tkt", "tv", "ktok", "vtok", "kts",
                                "atm", "td", "tci", "tE", "ktokm", "M0", "M1", "Mtmp", "Mb", "Mbs", "LD", "S", "Sin", "Stmp", "M0b",
                                "S0s", "NSs", "to32", "osq", "t1", "rs2")}
        b_ld = [B("ld0"), B("ld1")]
        b_sr = [B(f"sr{g}") for g in range(cfg.NG)]
        b_agh_in = B("agh_in"); b_agh = B("agh"); b_agc_in = B("agc_in"); b_agc = B("agc")
        b_agh_mid = B("agh_mid"); b_agc_mid = B("agc_mid")
        b_hal = B("hal"); b_halg = B("halg"); b_halo = B("halo")
        out_bufs = []

        P.dma(sp, const_t, consts, b_const, writes=[b_const])
        for dst, src in ((norms_t, norms), (cw_t, cw), (rm_t, rmask), (lbl_t, lbl), (og_t, ogain)):
            P.dma(sp, dst, src, b_const, writes=[b_const])
        for ap_, val in ((onesb, 1.0), (zerob, 0.0), (prev7, 0.0), (ones_f, 1.0)):
            P.op(dve, lambda e, ap_=ap_, val=val: e.memset(ap_, val), writes=[b_prev7])
        P.op(dve, lambda e: e.tensor_copy(out=identb, in_=identf), reads=[b_const], writes=[b_prev7])
        P.op(dve, lambda e: e.tensor_copy(out=maskA, in_=cv("maskA")), reads=[b_const], writes=[b_prev7])
        P.op(dve, lambda e: e.tensor_copy(out=maskS, in_=cv("maskS")[0:32, :]), reads=[b_const], writes=[b_prev7])
        e3 = gtmp[0][:, 0:3 * NH].rearrange("p (a b) -> p a b", a=3)
        P.op(act, lambda e: e.activation(out=e3, in_=lbl_t, func=AF.Exp), reads=[b_const], writes=[b_gtmp[0]])
        s1 = gtmp[1][:, 0:NH]
        P.op(dve, lambda e: e.tensor_tensor(out=s1, in0=e3[:, 0, :], in1=e3[:, 1, :], op=ALU.add), reads=[b_gtmp[0]], writes=[b_gtmp[1]])
        P.op(dve, lambda e: e.tensor_tensor(out=s1, in0=s1, in1=e3[:, 2, :], op=ALU.add), reads=[b_gtmp[0], b_gtmp[1]], writes=[b_gtmp[1]])
        P.op(dve, lambda e: e.reciprocal(out=s1, in_=s1), reads=[b_gtmp[1]], writes=[b_gtmp[1]])
        P.op(dve, lambda e: e.tensor_tensor(out=lb_t, in0=e3[:, 0, :], in1=s1, op=ALU.mult), reads=[b_gtmp[0], b_gtmp[1]], writes=[b_prev7])
        P.op(dve, lambda e: e.tensor_scalar(out=oml_t, in0=lb_t, scalar1=-1.0, scalar2=1.0, op0=ALU.mult, op1=ALU.add),
             reads=[b_prev7], writes=[b_prev7])
        P.op(dve, lambda e: e.tensor_scalar(out=noml_t, in0=lb_t, scalar1=1.0, scalar2=-1.0, op0=ALU.mult, op1=ALU.add),
             reads=[b_prev7], writes=[b_prev7])
        P.barrier()
        try:
            chk("consts")
            _go = True
        except _Stop:
            _go = False
        b_const = B("const_ro")

        RG = [list(range(NCORES))]
        b_win = {}
        wp_by_name = {}
        gstate = {}
        b_midA = [B(f"midA{i}") for i in range(NMID)]
        b_midB = [B(f"midB{i}") for i in range(NMID)]
        midctr = {"A": 0, "B": 0}
        LOOKAHEAD = 6
        for (key, n, t0, nt, PT) in (WP if _go else []):
            wp_by_name.setdefault(n, []).append((key, t0, nt, PT))
            rows, c = wsh[key].shape
            b_win[key] = B(key + "_i")
            for r0 in range(0, rows, 512):
                r1 = min(rows, r0 + 512)
                P.dma(act, wag_in[key][r0:r1, :], wsh[key][r0:r1, :], b_win[key], writes=[b_win[key]])
            npc = nt // PT
            gstate[key] = {"s1": 0, "s2": 0, "np": npc, "PT": PT, "mids": {}}
            b_wag[key] = [B(f"{key}_p{i}") for i in range(npc)]

        def gather_step(key, upto):
            st = gstate[key]
            PT = st["PT"]
            RPR = PT * 16
            kind = "B" if PT == 2 else "A"
            mids, bmids = (midB, b_midB) if kind == "B" else (midA, b_midA)
            upto = min(upto, st["np"] - 1)

            def s2(p):
                k = st["mids"][p]
                P.op(pool, lambda e: e.collective_compute("AllGather", ALU.bypass, replica_groups=RG2, ins=[mids[k]],
                                                          outs=[wag[key][p * PT * 128:(p + 1) * PT * 128, :]]),
                     reads=[bmids[k]], writes=[b_wag[key][p]], sem=s_cc, inc=1)
            while st["s1"] <= upto:
                p = st["s1"]
                k = midctr[kind] % NMID
                midctr[kind] += 1
                st["mids"][p] = k
                P.op(pool, lambda e: e.collective_compute("AllGather", ALU.bypass, replica_groups=RG1,
                                                          ins=[wag_in[key][p * RPR:(p + 1) * RPR, :]], outs=[mids[k]]),
                     reads=[b_win[key]], writes=[bmids[k]], sem=s_cc, inc=1)
                st["s1"] += 1
                while st["s2"] < st["s1"] - 1:
                    s2(st["s2"])
                    st["s2"] += 1
            if st["s1"] == st["np"]:
                while st["s2"] < st["np"]:
                    s2(st["s2"])
                    st["s2"] += 1

        def ensure_piece(key, p):
            st = gstate[key]
            gather_step(key, p + LOOKAHEAD)
            while st["s2"] <= p:
                k = st["s2"]
                gather_flush_one(key)

        def gather_flush_one(key):
            st = gstate[key]
            PT = st["PT"]
            kind = "B" if PT == 2 else "A"
            mids, bmids = (midB, b_midB) if kind == "B" else (midA, b_midA)
            p = st["s2"]
            k = st["mids"][p]
            P.op(pool, lambda e: e.collective_compute("AllGather", ALU.bypass, replica_groups=RG2, ins=[mids[k]],
                                                      outs=[wag[key][p * PT * 128:(p + 1) * PT * 128, :]]),
                 reads=[bmids[k]], writes=[b_wag[key][p]], sem=s_cc, inc=1)
            st["s2"] += 1

        wctr = [0]

        def wload(name, tile_idx, ncols):
            i = wctr[0] % NSLOT
            wctr[0] += 1
            for (key, t0, nt, PT) in wp_by_name[name]:
                if t0 <= tile_idx < t0 + nt:
                    break
            li = tile_idx - t0
            pidx = li // PT
            ensure_piece(key, pidx)
            src = wag[key][li * 128:(li + 1) * 128, 0:ncols]
            P.dma(pool, wslots[i][:, 0:ncols], src, b_wslot[i], reads=[b_wag[key][pidx]], writes=[b_wslot[i]])
            return wslots[i], b_wslot[i]

        NACC = 6
        accctr = [0]

        def acc_get():
            i = accctr[0] % NACC
            accctr[0] += 1
            return i

        def acc_seg(i, c0, n):
            if c0 < T:
                return banks[i][:, c0:c0 + n]
            return banks[7][:, 32 * i + (c0 - T):32 * i + (c0 - T) + n]

        def acc_bufs(i, N):
            return [b_bank[i]] + ([b_bank[7]] if N > T else [])

        def segs(N):
            return [(0, T)] + ([(T, N - T)] if N > T else [])

        def mm_acc(i, N, steps, reads, first, last):
            fl = []
            for (c0, n) in segs(N):
                for k, (lh, rf) in enumerate(steps):
                    fl.append(lambda e, c0=c0, n=n, lh=lh, rf=rf, k=k: e.matmul(
                        acc_seg(i, c0, n), lh, rf(c0, n), start=(first and k == 0), stop=(last and k == len(steps) - 1)))
            P.op(pe, fl, reads=reads, writes=acc_bufs(i, N))

        def proj(name, tile0, N, rhs_bufs, rhs_fn, i):
            for half in range(2):
                wt, wb = wload(name, tile0 + half, WC)
                w3 = wt[:, 0:WC].rearrange("p (k c) -> p k c", c=128)
                steps = [(w3[:, k, :], (lambda c0, n, kc=half * KH + k: rhs_fn(kc, c0, n))) for k in range(KH)]
                mm_acc(i, N, steps, [wb] + list(rhs_bufs), first=(half == 0), last=(half == 1))

        def rsqrt_inplace(ap, buf):
            P.op(act, lambda e: e.activation(out=ap, in_=ap, func=AF.Sqrt), reads=[buf], writes=[buf])
            P.op(dve, lambda e: e.reciprocal(out=ap, in_=ap), reads=[buf], writes=[buf])

        def fm_norm(N):
            i = acc_get()
            for fc in range(FC):
                j = fc % 2
                P.op(act, lambda e, fc=fc, j=j: e.activation(out=sqt[j][:, 0:N], in_=x_sb[:, fc, 0:N], func=AF.Square),
                     reads=[b_x[fc]], writes=[b_sq[j]])
                mm_acc(i, N, [(onesb, lambda c0, n, j=j: sqt[j][:, c0:c0 + n])], [b_sq[j]], first=(fc == 0), last=(fc == FC - 1))
            for (c0, n) in segs(N):
                P.op(dve, lambda e, c0=c0, n=n: e.tensor_scalar(out=rstd_t[:, c0:c0 + n], in0=acc_seg(i, c0, n), scalar1=1.0 / D,
                                                                 scalar2=EPS, op0=ALU.mult, op1=ALU.add),
                     reads=acc_bufs(i, N), writes=[b_rstd])
            rsqrt_inplace(rstd_t[:, 0:N], b_rstd)

        def fm_norm_apply(N, gain_idx):
            for fc in range(FC):
                P.op(dve, lambda e, fc=fc: e.scalar_tensor_tensor(out=h_t[:, fc, 0:N], in0=x_sb[:, fc, 0:N],
                                                                  scalar=norms_t[:, gain_idx, fc:fc + 1], in1=rstd_t[:, 0:N],
                                                                  op0=ALU.mult, op1=ALU.mult),
                     reads=[b_x[fc], b_rstd], writes=[b_h[fc]])

        actv = aux.rearrange("p a b -> p (a b)")[:, 0:cfg.QMAX * NX].rearrange("p (a b) -> p a b", b=NX)
        hrd = lambda kc, c0, n: h_t[:, kc, c0:c0 + n]

        def ffn(l, N):
            fm_norm(N)
            fm_norm_apply(N, 1 + 2 * l)
            s0 = 0
            for q in range(cfg.NQ):
                nq = cfg.QS[q]
                for sl in range(nq):
                    s = s0 + sl
                    ig = acc_get()
                    proj(f"w_fin{l}", (s * 2 + 0) * 2, N, b_h, hrd, ig)
                    iu = acc_get()
                    proj(f"w_fin{l}", (s * 2 + 1) * 2, N, b_h, hrd, iu)
                    j = s % 2
                    for (c0, n) in segs(N):
                        P.op(act, lambda e, c0=c0, n=n, j=j, ig=ig: e.activation(out=gtmp[j][:, c0:c0 + n], in_=acc_seg(ig, c0, n), func=AF.Silu),
                             reads=acc_bufs(ig, N), writes=[b_gtmp[j]])
                    for (c0, n) in segs(N):
                        P.op(dve, lambda e, c0=c0, n=n, j=j, iu=iu, sl=sl: e.tensor_tensor(out=actv[:, sl, c0:c0 + n], in0=gtmp[j][:, c0:c0 + n],
                                                                                        in1=acc_seg(iu, c0, n), op=ALU.mult),
                             reads=acc_bufs(iu, N) + [b_gtmp[j]], writes=[b_aux[sl]])
                for fc in range(FC):
                    wt, wb = wload(f"w_fout{l}", q * FC + fc, cfg.QMAX * 128)
                    w3 = wt[:, 0:cfg.QMAX * 128].rearrange("p (k c) -> p k c", c=128)
                    i = acc_get()
                    steps = [(w3[:, k, :], (lambda c0, n, k=k: actv[:, k, c0:c0 + n])) for k in range(nq)]
                    mm_acc(i, N, steps, [wb] + b_aux[0:nq], first=True, last=True)
                    for (c0, n) in segs(N):
                        P.op(dve, lambda e, c0=c0, n=n, fc=fc, i=i: e.tensor_tensor(out=x_sb[:, fc, c0:c0 + n], in0=x_sb[:, fc, c0:c0 + n],
                                                                                 in1=acc_seg(i, c0, n), op=ALU.add),
                             reads=acc_bufs(i, N) + [b_x[fc]], writes=[b_x[fc]])
                s0 += nq

        bk6 = bank6b[:, :]
        psTk = bk6[:, 0:NPB * 128].rearrange("p (a b) -> p a b", b=128)
        psTv = bk6[:, 512:512 + NPB * 128].rearrange("p (a b) -> p a b", b=128)
        psTks = bk6[:, 0:128]; psTvs = bk6[:, 512:640]
        psATs = banks[7][:, 320:352]
        HW_ = D // 2

        def round_body(r):
            last_round = (r == R - 1)
            N = NX if last_round else T
            P.barrier()
            tok_blocks = [(xp[r * T + tbk * 128: r * T + tbk * 128 + 128, :], 128, tbk * 128) for tbk in range(NPB)]
            if last_round:
                tok_blocks.append((xs[0:EX, :], EX, T))
            for (src, nt, c0) in tok_blocks:
                for hf in range(2):
                    P.dma(sp, xst[hf][0:nt, :], src[:, hf * HW_:(hf + 1) * HW_], b_xst[hf], writes=[b_xst[hf]])
                    P.op(act, lambda e, hf=hf, nt=nt: e.activation(out=junk[0:nt, :], in_=xst[hf][0:nt, :], func=AF.Square,
                                                                 accum_out=ssq[0:nt, hf:hf + 1]),
                         reads=[b_xst[hf]], writes=[b_ssq, b_junk])
                P.op(dve, lambda e, nt=nt: e.tensor_tensor(out=ssq[0:nt, 2:3], in0=ssq[0:nt, 0:1], in1=ssq[0:nt, 1:2], op=ALU.add),
                     reads=[b_ssq], writes=[b_ssq])
                P.op(dve, lambda e, nt=nt: e.tensor_scalar(out=ssq[0:nt, 2:3], in0=ssq[0:nt, 2:3], scalar1=1.0 / D, scalar2=EPS,
                                                          op0=ALU.mult, op1=ALU.add), reads=[b_ssq], writes=[b_ssq])
                P.op(act, lambda e, nt=nt: e.activation(out=ssq[0:nt, 2:3], in_=ssq[0:nt, 2:3], func=AF.Sqrt), reads=[b_ssq], writes=[b_ssq])
                P.op(dve, lambda e, nt=nt: e.reciprocal(out=ssq[0:nt, 3:4], in_=ssq[0:nt, 2:3]), reads=[b_ssq], writes=[b_ssq])
                for hf in range(2):
                    P.op(dve, lambda e, hf=hf, nt=nt: e.tensor_scalar(out=xst[hf][0:nt, :], in0=xst[hf][0:nt, :], scalar1=ssq[0:nt, 3:4],
                                                                    scalar2=None, op0=ALU.mult), reads=[b_ssq, b_xst[hf]], writes=[b_xst[hf]])
                    for f4 in range(0, FC // 2, 4):
                        nf = min(4, FC // 2 - f4)
                        i = acc_get()
                        P.op(pe, [lambda e, hf=hf, f=f, f4=f4, nt=nt, i=i: e.transpose(banks[i][:, (f - f4) * 128:(f - f4) * 128 + nt],
                                                                                         xst[hf][0:nt, f * 128:(f + 1) * 128], identf[0:nt, 0:nt])
                                  for f in range(f4, f4 + nf)], reads=[b_xst[hf]], writes=[b_bank[i]])
                        for f in range(f4, f4 + nf):
                            fcg = hf * (FC // 2) + f
                            P.op(dve, lambda e, f=f, f4=f4, fcg=fcg, nt=nt, c0=c0, i=i: e.tensor_scalar(
                                out=h_t[:, fcg, c0:c0 + nt], in0=banks[i][:, (f - f4) * 128:(f - f4) * 128 + nt],
                                scalar1=norms_t[:, 0, fcg:fcg + 1], scalar2=None, op0=ALU.mult),
                                reads=[b_bank[i]], writes=[b_h[fcg]])
            P.barrier()
            chk("p0")

            for grp in range(cfg.NG):
                if last_round:
                    for q in range(2):
                        P.dma(sp, S0s[:, q, :, :], sh[q, grp * G:(grp + 1) * G, :, :].rearrange("h k v -> k h v"), bt["S0s"], writes=[bt["S0s"]])
                for hg in range(G):
                    hd = grp * G + hg
                    iq = acc_get(); proj("w_hin", (hd * 4 + 0) * 2, N, b_h, hrd, iq)
                    if_ = acc_get(); proj("w_hin", (hd * 4 + 1) * 2, N, b_h, hrd, if_)
                    ii = acc_get(); proj("w_hin", (hd * 4 + 2) * 2, N, b_h, hrd, ii)
                    ig = acc_get(); proj("w_hin", (hd * 4 + 3) * 2, N, b_h, hrd, ig)
                    for (c0, n) in segs(N):
                        P.op(act, lambda e, c0=c0, n=n: e.activation(out=tq[:, c0:c0 + n], in_=acc_seg(iq, c0, n), func=AF.Silu),
                             reads=acc_bufs(iq, N), writes=[bt["tq"]])
                        P.op(act, lambda e, c0=c0, n=n: e.activation(out=gs_sg[:, hg, c0:c0 + n], in_=acc_seg(ig, c0, n), func=AF.Silu),
                             reads=acc_bufs(ig, N), writes=[b_gs[hg]])
                        P.op(act, lambda e, c0=c0, n=n: e.activation(out=tsig[:, c0:c0 + n], in_=acc_seg(if_, c0, n), func=AF.Sigmoid),
                             reads=acc_bufs(if_, N), writes=[bt["tsig"]])
                        P.op(dve, lambda e, c0=c0, n=n: e.tensor_copy(out=tv[:, c0:c0 + n], in_=acc_seg(ii, c0, n)),
                             reads=acc_bufs(ii, N), writes=[bt["tv"]])
                    chk("h_proj")
                    P.op(act, lambda e: e.activation(out=tg[:, 0:N], in_=tsig[:, 0:N], func=AF.Ln, scale=oml_t[:, hd:hd + 1], bias=lb_t[:, hd:hd + 1]),
                         reads=[bt["tsig"]], writes=[bt["tg"]])
                    P.op(dve, lambda e: e.tensor_scalar(out=tk[:, 0:N], in0=tsig[:, 0:N], scalar1=noml_t[:, hd:hd + 1], scalar2=oml_t[:, hd:hd + 1],
                                                        op0=ALU.mult, op1=ALU.add), reads=[bt["tsig"]], writes=[bt["tk"]])
                    P.op(dve, lambda e: e.tensor_tensor_scan(out=tb[:, 0:N], data0=smask[:, 0:N], data1=tg[:, 0:N], initial=0.0,
                                                             op0=ALU.mult, op1=ALU.add), reads=[bt["tg"]], writes=[bt["tb"]])
                    P.op(act, lambda e: e.activation(out=teb[:, 0:N], in_=tb[:, 0:N], func=AF.Exp), reads=[bt["tb"]], writes=[bt["teb"]])
                    P.op(act, lambda e: e.activation(out=tenb[:, 0:N], in_=tb[:, 0:N], func=AF.Exp, scale=-1.0), reads=[bt["tb"]], writes=[bt["tenb"]])
                    P.op(dve, lambda e: e.tensor_tensor(out=tqt[:, 0:N], in0=tq[:, 0:N], in1=teb[:, 0:N], op=ALU.mult),
                         reads=[bt["tq"], bt["teb"]], writes=[bt["tqt"]])
                    P.op(dve, lambda e: e.tensor_tensor(out=tkt[:, 0:N], in0=tk[:, 0:N], in1=tenb[:, 0:N], op=ALU.mult),
                         reads=[bt["tk"], bt["tenb"]], writes=[bt["tkt"]])
                    chk("h_act")
                    bl = tb[:, 0:T].rearrange("p (c t) -> p c t", t=64)[:, :, 63]
                    P.op(act, lambda e: e.activation(out=td[:, 0:NCH], in_=bl, func=AF.Exp), reads=[bt["tb"]], writes=[bt["td"]])
                    if last_round:
                        bls = tb[:, T:NX].rearrange("p (c t) -> p c t", t=16)[:, :, 15]
                        P.op(act, lambda e: e.activation(out=td[:, NCH:NCH + 2], in_=bls, func=AF.Exp), reads=[bt["tb"]], writes=[bt["td"]])
                    P.op(dve, lambda e: e.tensor_tensor_scan(out=tci, data0=ones_f[:, 0:NCH], data1=bl, initial=0.0, op0=ALU.mult, op1=ALU.add),
                         reads=[bt["tb"]], writes=[bt["tci"]])
                    P.op(act, lambda e: e.activation(out=tE, in_=tci, func=AF.Exp), reads=[bt["tci"]], writes=[bt["tE"]])
                    P.op(dve, lambda e: e.tensor_copy(out=gs_qc[:, hg, 0:64], in_=tqt[:, 0:64]), reads=[bt["tqt"]], writes=[b_gs[hg]])
                    if NCH > 1:
                        P.op(dve, lambda e: e.tensor_tensor(out=gs_qc[:, hg, 64:T].rearrange("p (c t) -> p c t", t=64),
                                                            in0=tqt[:, 64:T].rearrange("p (c t) -> p c t", t=64),
                                                            in1=tE[:, 0:NCH - 1].unsqueeze(2).to_broadcast([128, NCH - 1, 64]), op=ALU.mult),
                             reads=[bt["tqt"], bt["tE"]], writes=[b_gs[hg]])
                    chk("h_dec")
                    P.op(pe, [lambda e, pb=pb: e.transpose(psTk[:, pb, :], tkt[:, pb * 128:(pb + 1) * 128], identb) for pb in range(NPB)]
                         + [lambda e, pb=pb: e.transpose(psTv[:, pb, :], tv[:, pb * 128:(pb + 1) * 128], identb) for pb in range(NPB)],
                         reads=[bt["tkt"], bt["tv"]], writes=[b_bank[6]])
                    chk("h_tr0")
                    P.op(act, lambda e: e.activation(out=ktok, in_=psTk, func=AF.Copy), reads=[b_bank[6]], writes=[bt["ktok"]])
                    chk("h_tr1")
                    P.op(act, lambda e: e.activation(out=vtok, in_=psTv, func=AF.Copy), reads=[b_bank[6]], writes=[bt["vtok"]])
                    if last_round:
                        P.op(pe, [lambda e: e.transpose(psTks[0:32, :], tkt[:, T:NX], identb), lambda e: e.transpose(psTvs[0:32, :], tv[:, T:NX], identb)],
                             reads=[bt["tkt"], bt["tv"]], writes=[b_bank[6]])
                        P.op(act, lambda e: e.activation(out=vtoks, in_=psTvs[0:32, :], func=AF.Copy), reads=[b_bank[6]], writes=[bt["kts"]])
                        P.op(act, lambda e: e.activation(out=ktoks, in_=psTks[0:32, :], func=AF.Copy), reads=[b_bank[6]], writes=[bt["kts"]])
                        for q in range(2):
                            P.op(dve, lambda e, q=q: e.tensor_scalar(out=kt01[:, q, :], in0=ktoks, scalar1=sm01[:, q:q + 1], scalar2=None, op0=ALU.mult),
                                 reads=[bt["kts"]], writes=[bt["kts"]])
                    chk("h_tr")
                    for j in range(2):
                        P.op(dve, lambda e, j=j: e.tensor_scalar(out=ktokm[:, :, j, :], in0=ktok, scalar1=pm64[:, j:j + 1], scalar2=None, op0=ALU.mult),
                             reads=[bt["ktok"]], writes=[bt["ktokm"]])
                    iU = []
                    for half in range(0, NCH, 4):
                        i = acc_get()
                        iU.append(i)
                        P.op(pe, [lambda e, c=c, i=i, half=half: e.matmul(banks[i][:, (c - half) * 128:(c - half + 1) * 128],
                                                                           ktokm[:, c // 2, c % 2, :], vtok[:, c // 2, :], start=True, stop=True)
                                  for c in range(half, min(NCH, half + 4))], reads=[bt["ktokm"], bt["vtok"]], writes=[b_bank[i]])
                    if last_round:
                        iUs = acc_get()
                        P.op(pe, [lambda e, q=q: e.matmul(banks[iUs][:, q * 128:(q + 1) * 128], kt01[:, q, :], vtoks, start=True, stop=True) for q in range(2)],
                             reads=[bt["kts"]], writes=[b_bank[iUs]])
                    chk("h_U")
                    Ubank = lambda c: banks[iU[c // 4]][:, (c % 4) * 128:(c % 4 + 1) * 128]
                    bU = lambda c: b_bank[iU[c // 4]]
                    cur = 0
                    for c in range(NCH):
                        fin = (c == NCH - 1)
                        dst = LD[:, hg, 0:128] if fin else Mf[1 - cur]
                        dbuf = bt["LD"] if fin else bt[f"M{1 - cur}"]
                        if c == 0:
                            P.op(dve, lambda e, dst=dst: e.tensor_scalar(out=dst, in0=Ubank(0), scalar1=td[:, 0:1], scalar2=None, op0=ALU.mult),
                                 reads=[bU(0), bt["td"]], writes=[dbuf])
                        else:
                            P.op(dve, lambda e, c=c, cur=cur: e.tensor_scalar(out=Mtmp, in0=Mf[cur], scalar1=td[:, c:c + 1], scalar2=None, op0=ALU.mult),
                                 reads=[bt[f"M{cur}"], bt["td"]], writes=[bt["Mtmp"]])
                            P.op(dve, lambda e, c=c, dst=dst: e.scalar_tensor_tensor(out=dst, in0=Ubank(c), scalar=td[:, c:c + 1], in1=Mtmp,
                                                                                   op0=ALU.mult, op1=ALU.add),
                                 reads=[bU(c), bt["td"], bt["Mtmp"]], writes=[dbuf])
                        if not fin:
                            P.op(act, lambda e, c=c, cur=cur: e.activation(out=Mb[:, c + 1, :], in_=Mf[1 - cur], func=AF.Copy),
                                 reads=[bt[f"M{1 - cur}"]], writes=[bt["Mb"]])
                        cur = 1 - cur
                    P.op(act, lambda e: e.activation(out=LD[:, hg, 128:129], in_=tE[:, NCH - 1:NCH], func=AF.Copy), reads=[bt["tE"]], writes=[bt["LD"]])
                    if last_round:
                        for q in range(2):
                            P.op(act, lambda e, q=q: e.activation(out=Mbs[:, q, :], in_=S0s[:, q, hg, :], func=AF.Copy), reads=[bt["S0s"]], writes=[bt["Mbs"]])
                            P.op(dve, lambda e, q=q: e.tensor_scalar(out=Mtmp, in0=S0s[:, q, hg, :], scalar1=td[:, NCH + q:NCH + q + 1], scalar2=None, op0=ALU.mult),
                                 reads=[bt["S0s"], bt["td"]], writes=[bt["Mtmp"]])
                            P.op(dve, lambda e, q=q: e.scalar_tensor_tensor(out=NSs[:, q, hg, :], in0=banks[iUs][:, q * 128:(q + 1) * 128],
                                                                          scalar=td[:, NCH + q:NCH + q + 1], in1=Mtmp, op0=ALU.mult, op1=ALU.add),
                                 reads=[b_bank[iUs], bt["td"], bt["Mtmp"]], writes=[bt["NSs"]])
                    chk("h_chain")
                    iA = acc_get()
                    P.op(pe, [lambda e, pb=pb: e.matmul(banks[iA][:, pb * 128:(pb + 1) * 128], tkt[:, pb * 128:(pb + 1) * 128],
                                                        tqt[:, pb * 128:(pb + 1) * 128], start=True, stop=True) for pb in range(NPB)]
                         + ([lambda e: e.matmul(psATs[0:32, :], tkt[:, T:NX], tqt[:, T:NX], start=True, stop=True)] if last_round else []),
                         reads=[bt["tkt"], bt["tqt"]], writes=[b_bank[iA]] + ([b_bank[7]] if last_round else []))
                    P.op(dve, lambda e: e.tensor_tensor(out=atm, in0=banks[iA][:, 0:NPB * 128].rearrange("p (a b) -> p a b", b=128),
                                                        in1=maskA.unsqueeze(1).to_broadcast([128, NPB, 128]), op=ALU.mult),
                         reads=[b_bank[iA]], writes=[bt["atm"]])
                    if last_round:
                        P.op(dve, lambda e: e.tensor_tensor(out=atms, in0=psATs[0:32, :], in1=maskS, op=ALU.mult),
                             reads=[b_bank[7]], writes=[bt["atm"]])
                    chk("h_A")
                    iO = acc_get()
                    fl = []
                    for c in range(NCH):
                        pb, j = c // 2, c % 2
                        fl.append(lambda e, c=c, pb=pb, j=j: e.matmul(banks[iO][:, c * 64:(c + 1) * 64], vtok[:, pb, :], atm[:, pb, j * 64:(j + 1) * 64],
                                                                     start=True, stop=False))
                        fl.append(lambda e, c=c: e.matmul(banks[iO][:, c * 64:(c + 1) * 64], (zerob if c == 0 else Mb[:, c, :]), tqt[:, c * 64:(c + 1) * 64],
                                                         start=False, stop=True))
                    if last_round:
                        for q in range(2):
                            fl.append(lambda e, q=q: e.matmul(acc_seg(iO, T + 16 * q, 16), vtoks, atms[:, q * 16:(q + 1) * 16], start=True, stop=False))
                            fl.append(lambda e, q=q: e.matmul(acc_seg(iO, T + 16 * q, 16), Mbs[:, q, :], tqt[:, T + 16 * q:T + 16 * q + 16], start=False, stop=True))
                    P.op(pe, fl, reads=[bt["vtok"], bt["atm"], bt["Mb"], bt["tqt"], bt["kts"], bt["Mbs"]], writes=acc_bufs(iO, N))
                    for (c0, n) in segs(N):
                        P.op(act, lambda e, c0=c0, n=n: e.activation(out=gs_ol[:, hg, c0:c0 + n], in_=acc_seg(iO, c0, n), func=AF.Copy),
                             reads=acc_bufs(iO, N), writes=[b_gs[hg]])
                chk("p1h")
                if last_round:
                    for q in range(2):
                        P.dma(sp, hs[q, grp * G:(grp + 1) * G, :, :].rearrange("h k v -> k h v"), NSs[:, q, :, :], bt["NSs"], reads=[bt["NSs"]], writes=[bt["NSs"]])
                    out_bufs.append(bt["NSs"])
                P.dma(sp, agh_in.rearrange("(g k) c -> k g c", k=128), LD, b_agh_in, reads=[bt["LD"]], writes=[b_agh_in])
                allgather(agh_in, agh_mid, agh, [b_agh_in], b_agh, b_agh_mid)
                if r == 0:
                    P.op(dve, lambda e: e.memset(S_t, 0.0), writes=[bt["S"]])
                else:
                    P.dma(sp, S_t, sround[:, grp * G:(grp + 1) * G, :], bt["S"], reads=[b_sr[grp]], writes=[bt["S"]])
                P.op(dve, lambda e: e.memset(Sin, 0.0), writes=[bt["Sin"]])
                for j in range(NCORES):
                    jj = j % 2
                    P.dma(sp, LDj[jj], agh[j * G * 128:(j + 1) * G * 128, :].rearrange("(g k) c -> k g c", k=128), b_ld[jj], reads=[b_agh], writes=[b_ld[jj]])
                    P.op(dve, lambda e, j=j: e.scalar_tensor_tensor(out=Sin, in0=S_t, scalar=rm_t[:, j:j + 1], in1=Sin, op0=ALU.mult, op1=ALU.add),
                         reads=[bt["S"], bt["Sin"]], writes=[bt["Sin"]])
                    P.op(dve, lambda e, jj=jj: e.tensor_tensor(out=Stmp, in0=S_t, in1=LDj[jj][:, :, 128:129].to_broadcast([128, G, 128]), op=ALU.mult),
                         reads=[bt["S"], b_ld[jj]], writes=[bt["Stmp"]])
                    P.op(dve, lambda e, jj=jj: e.tensor_tensor(out=S_t, in0=Stmp, in1=LDj[jj][:, :, 0:128], op=ALU.add),
                         reads=[bt["Stmp"], b_ld[jj]], writes=[bt["S"]])
                if last_round:
                    P.dma(sp, hp[grp * G:(grp + 1) * G, :, :].rearrange("h k v -> k h v"), S_t, b_sr[grp], reads=[bt["S"]], writes=[b_sr[grp]])
                    out_bufs.append(b_sr[grp])
                else:
                    P.dma(sp, sround[:, grp * G:(grp + 1) * G, :], S_t, b_sr[grp], reads=[bt["S"]], writes=[b_sr[grp]])
                P.op(act, lambda e: e.activation(out=M0b, in_=Sin, func=AF.Copy), reads=[bt["Sin"]], writes=[bt["M0b"]])
                for hg in range(G):
                    hd = grp * G + hg
                    iC = acc_get()
                    P.op(pe, lambda e: e.matmul(banks[iC][:, 0:T], M0b[:, hg, :], gs_qc[:, hg, 0:T], start=True, stop=True),
                         reads=[bt["M0b"], b_gs[hg]], writes=[b_bank[iC]])
                    P.op(dve, lambda e: e.tensor_tensor(out=to32[:, 0:T], in0=banks[iC][:, 0:T], in1=gs_ol[:, hg, 0:T], op=ALU.add),
                         reads=[b_bank[iC], b_gs[hg]], writes=[bt["to32"]])
                    if last_round:
                        P.op(dve, lambda e: e.tensor_copy(out=to32[:, T:NX], in_=gs_ol[:, hg, T:NX]), reads=[b_gs[hg]], writes=[bt["to32"]])
                    P.op(act, lambda e: e.activation(out=osq[:, 0:N], in_=to32[:, 0:N], func=AF.Square), reads=[bt["to32"]], writes=[bt["osq"]])
                    iS = acc_get()
                    mm_acc(iS, N, [(onesb, lambda c0, n: osq[:, c0:c0 + n])], [bt["osq"]], first=True, last=True)
                    for (c0, n) in segs(N):
                        P.op(dve, lambda e, c0=c0, n=n: e.tensor_scalar(out=rs2[:, c0:c0 + n], in0=acc_seg(iS, c0, n), scalar1=1.0 / 128, scalar2=EPS,
                                                                         op0=ALU.mult, op1=ALU.add), reads=acc_bufs(iS, N), writes=[bt["rs2"]])
                    rsqrt_inplace(rs2[:, 0:N], bt["rs2"])
                    P.op(dve, lambda e: e.scalar_tensor_tensor(out=t1[:, 0:N], in0=to32[:, 0:N], scalar=og_t[:, 0:1], in1=rs2[:, 0:N], op0=ALU.mult, op1=ALU.mult),
                         reads=[bt["to32"], bt["rs2"]], writes=[bt["t1"]])
                    P.op(dve, lambda e: e.tensor_tensor(out=aux[:, hd, 0:N], in0=t1[:, 0:N], in1=gs_sg[:, hg, 0:N], op=ALU.mult),
                         reads=[bt["t1"], b_gs[hg]], writes=[b_aux[hd]])

            chk("p1")
            P.barrier()
            for fc in range(FC):
                j = fc % 2
                xr = ostage[j][:, 0:(NPB + 1) * 128].rearrange("p (a b) -> p a b", b=128)
                for tbk in range(NPB):
                    P.dma(sp, xr[:, tbk, :], xp[r * T + tbk * 128:r * T + tbk * 128 + 128, fc * 128:(fc + 1) * 128], b_ost[j], writes=[b_ost[j]])
                if last_round:
                    P.dma(sp, xr[0:EX, NPB, :], xs[:, fc * 128:(fc + 1) * 128], b_ost[j], writes=[b_ost[j]])
                i = acc_get()
                for half in range(2):
                    wt, wb = wload("w_hout", fc * 2 + half, WC)
                    w3 = wt[:, 0:WC].rearrange("p (k c) -> p k c", c=128)
                    fl = []
                    for (c0, n) in segs(N):
                        for k in range(KH):
                            kc = half * KH + k
                            fl.append(lambda e, c0=c0, n=n, k=k, kc=kc, w3=w3, half=half: e.matmul(acc_seg(i, c0, n), w3[:, k, :], aux[:, kc, c0:c0 + n],
                                                                                                    start=(half == 0 and k == 0), stop=False))
                    rd = [wb] + b_aux[0:FC]
                    if half == 1:
                        for tbk in range(NPB):
                            fl.append(lambda e, tbk=tbk, xr=xr: e.matmul(banks[i][:, tbk * 128:(tbk + 1) * 128], xr[:, tbk, :], identf, start=False, stop=True))
                        if last_round:
                            fl.append(lambda e, xr=xr: e.matmul(acc_seg(i, T, EX), xr[0:EX, NPB, :], identf[0:EX, 0:EX], start=False, stop=True))
                        rd = rd + [b_ost[j]]
                    P.op(pe, fl, reads=rd, writes=acc_bufs(i, N))
                for (c0, n) in segs(N):
                    P.op(act, lambda e, c0=c0, n=n, fc=fc, i=i: e.activation(out=x_sb[:, fc, c0:c0 + n], in_=acc_seg(i, c0, n), func=AF.Copy),
                         reads=acc_bufs(i, N), writes=[b_x[fc]])
            chk("p2")
            ffn(0, N)
            chk("ffn0")
            fm_norm(N)
            fm_norm_apply(N, 2)
            cuh = aux
            offs = [(2, 0, T)] + ([(T + 4, T, 16), (T + 22, T + 16, 16)] if last_round else [])
            cst = ostage[0][:, 0:4 * FC].rearrange("p (q t f) -> p q t f", q=2, t=2)
            for fc in range(FC):
                igc = acc_get(); proj("w_cin", (fc * 2 + 0) * 2, N, b_h, hrd, igc)
                iu = acc_get(); proj("w_cin", (fc * 2 + 1) * 2, N, b_h, hrd, iu)
                j = fc % 2
                for (c0, n) in segs(N):
                    P.op(act, lambda e, c0=c0, n=n, j=j, igc=igc: e.activation(out=gtmp[j][:, c0:c0 + n], in_=acc_seg(igc, c0, n), func=AF.Copy),
                         reads=acc_bufs(igc, N), writes=[b_gtmp[j]])
                    P.op(dve, lambda e, c0=c0, n=n, j=j, iu=iu: e.tensor_tensor(out=gtmp[j][:, c0:c0 + n], in0=gtmp[j][:, c0:c0 + n], in1=acc_seg(iu, c0, n), op=ALU.mult),
                         reads=acc_bufs(iu, N) + [b_gtmp[j]], writes=[b_gtmp[j]])
                for (co, ct, n) in offs:
                    P.op(act, lambda e, co=co, ct=ct, n=n, fc=fc, j=j: e.activation(out=cuh[:, fc, co:co + n], in_=gtmp[j][:, ct:ct + n], func=AF.Copy),
                         reads=[b_gtmp[j]], writes=[b_aux[fc]])
                P.op(dve, lambda e, fc=fc, j=j: e.tensor_copy(out=hal_t[:, :, fc], in_=gtmp[j][:, T - 2:T]), reads=[b_gtmp[j]], writes=[b_hal])
                if last_round:
                    for q in range(2):
                        P.op(dve, lambda e, fc=fc, j=j, q=q: e.tensor_copy(out=cst[:, q, :, fc], in_=gtmp[j][:, T + 16 * q + 14:T + 16 * q + 16]),
                             reads=[b_gtmp[j]], writes=[b_ost[0]])
            hal2 = hal_t.rearrange("p a b -> p (a b)")
            P.dma(sp, agc_in, hal2, b_agc_in, reads=[b_hal], writes=[b_agc_in])
            allgather(agc_in, agc_mid, agc, [b_agc_in], b_agc, b_agc_mid)
            P.dma(sp, halg, agc.rearrange("(r p) c -> p r c", p=128), b_halg, reads=[b_agc], writes=[b_halg])
            halo2 = halo.rearrange("p a b -> p (a b)")
            P.op(dve, lambda e: e.tensor_scalar(out=halo2, in0=prev7, scalar1=rm_t[:, 16:17], scalar2=None, op0=ALU.mult),
                 reads=[b_prev7, b_halg], writes=[b_halo])
            for j in range(NCORES):
                P.op(dve, lambda e, j=j: e.scalar_tensor_tensor(out=halo2, in0=halg[:, j, :], scalar=rm_t[:, 8 + j:9 + j], in1=halo2, op0=ALU.mult, op1=ALU.add),
                     reads=[b_halg, b_halo], writes=[b_halo])
            P.op(dve, lambda e: e.tensor_copy(out=prev7, in_=halg[:, NCORES - 1, :]), reads=[b_halg, b_halo], writes=[b_prev7])
            iT = acc_get()
            P.op(pe, lambda e: e.transpose(banks[iT][0:FC * 2, 0:128], hal2, identf), reads=[b_hal], writes=[b_bank[iT]])
            P.op(act, lambda e: e.activation(out=ostage[1][0:FC * 2, 0:128], in_=banks[iT][0:FC * 2, 0:128], func=AF.Copy), reads=[b_bank[iT]], writes=[b_ost[1]])
            P.dma(sp, cp[r, :, :].rearrange("t (f p) -> (t f) p", p=128), ostage[1][0:FC * 2, 0:128], b_ost[1], reads=[b_ost[1]], writes=[b_ost[1]])
            if last_round:
                for q in range(2):
                    iT = acc_get()
                    P.op(pe, lambda e, q=q, iT=iT: e.transpose(banks[iT][0:FC * 2, 0:128], ostage[0][:, q * FC * 2:(q + 1) * FC * 2], identf),
                         reads=[b_ost[0]], writes=[b_bank[iT]])
                    P.op(act, lambda e, q=q, iT=iT: e.activation(out=ostage[1][0:FC * 2, 128 * (q + 1):128 * (q + 2)], in_=banks[iT][0:FC * 2, 0:128], func=AF.Copy),
                         reads=[b_bank[iT]], writes=[b_ost[1]])
                    P.dma(sp, cs[q, :, :].rearrange("t (f p) -> (t f) p", p=128), ostage[1][0:FC * 2, 128 * (q + 1):128 * (q + 2)], b_ost[1],
                          reads=[b_ost[1]], writes=[b_ost[1]])
            out_bufs.append(b_ost[1])
            for fc in range(FC):
                P.op(act, lambda e, fc=fc: e.activation(out=cuh[:, fc, 0:2], in_=halo[:, :, fc], func=AF.Copy), reads=[b_halo], writes=[b_aux[fc]])
            if last_round:
                P.dma(sp, ostage[0][0:4 * FC, 0:128], sc.rearrange("q t (f p) -> (q t f) p", p=128), b_ost[0], writes=[b_ost[0]])
                iT = acc_get()
                P.op(pe, lambda e: e.transpose(banks[iT][:, 0:4 * FC], ostage[0][0:4 * FC, 0:128], identf[0:4 * FC, 0:4 * FC]), reads=[b_ost[0]], writes=[b_bank[iT]])
                scv = banks[iT][:, 0:4 * FC].rearrange("p (q t f) -> p q t f", q=2, t=2)
                for q in range(2):
                    for t_ in range(2):
                        co = (T + 2 if q == 0 else T + 20) + t_
                        P.op(act, lambda e, q=q, t_=t_, co=co: e.activation(out=cuh[:, :, co], in_=scv[:, q, t_, :], func=AF.Copy),
                             reads=[b_bank[iT]], writes=b_aux[0:FC])
            for fc in range(FC):
                igb = acc_get(); proj("w_cin", FC * 4 + fc * 2, N, b_h, hrd, igb)
                j = fc % 2
                for (co, ct, n) in offs:
                    P.op(dve, lambda e, co=co, ct=ct, n=n, fc=fc, j=j: e.tensor_scalar(out=gtmp[j][:, ct:ct + n], in0=cuh[:, fc, co:co + n],
                                                                                   scalar1=cw_t[:, 2, fc:fc + 1], scalar2=None, op0=ALU.mult),
                         reads=[b_aux[fc]], writes=[b_gtmp[j]])
                    P.op(dve, lambda e, co=co, ct=ct, n=n, fc=fc, j=j: e.scalar_tensor_tensor(out=gtmp[j][:, ct:ct + n], in0=cuh[:, fc, co - 1:co - 1 + n],
                                                                                          scalar=cw_t[:, 1, fc:fc + 1], in1=gtmp[j][:, ct:ct + n], op0=ALU.mult, op1=ALU.add),
                         reads=[b_aux[fc], b_gtmp[j]], writes=[b_gtmp[j]])
                    P.op(dve, lambda e, co=co, ct=ct, n=n, fc=fc, j=j: e.scalar_tensor_tensor(out=gtmp[j][:, ct:ct + n], in0=cuh[:, fc, co - 2:co - 2 + n],
                                                                                          scalar=cw_t[:, 0, fc:fc + 1], in1=gtmp[j][:, ct:ct + n], op0=ALU.mult, op1=ALU.add),
                         reads=[b_aux[fc], b_gtmp[j]], writes=[b_gtmp[j]])
                for (c0, n) in segs(N):
                    P.op(dve, lambda e, c0=c0, n=n, fc=fc, j=j, igb=igb: e.tensor_tensor(out=cuh[:, fc, c0:c0 + n], in0=gtmp[j][:, c0:c0 + n], in1=acc_seg(igb, c0, n), op=ALU.mult),
                         reads=acc_bufs(igb, N) + [b_gtmp[j], b_aux[fc]], writes=[b_aux[fc]])
            for fc in range(FC):
                i = acc_get()
                proj("w_cout", fc * 2, N, b_aux[0:FC], lambda kc, c0, n: cuh[:, kc, c0:c0 + n], i)
                for (c0, n) in segs(N):
                    P.op(dve, lambda e, c0=c0, n=n, fc=fc, i=i: e.tensor_tensor(out=x_sb[:, fc, c0:c0 + n], in0=x_sb[:, fc, c0:c0 + n], in1=acc_seg(i, c0, n), op=ALU.add),
                         reads=acc_bufs(i, N) + [b_x[fc]], writes=[b_x[fc]])
            chk("conv")
            ffn(1, N)
            chk("ffn1")
            fm_norm(N)
            out_blocks = [(yp[r * T + tbk * 128:r * T + tbk * 128 + 128, :], 128, tbk * 128) for tbk in range(NPB)]
            if last_round:
                out_blocks.append((ys[0:EX, :], EX, T))
            FB = 1024 // 128
            oc_ = 0
            for (dst, nt, c0) in out_blocks:
                for f8 in range(0, FC, FB):
                    nf8 = min(FB, FC - f8)
                    so = oc_ % 2
                    oc_ += 1
                    for f4 in range(f8, f8 + nf8, 4):
                        nf = min(4, f8 + nf8 - f4)
                        i = acc_get()
                        for f in range(f4, f4 + nf):
                            j = f % 2
                            P.op(dve, lambda e, f=f, j=j, c0=c0, nt=nt: e.scalar_tensor_tensor(out=gtmp[j][:, 0:nt], in0=x_sb[:, f, c0:c0 + nt], scalar=norms_t[:, 4, f:f + 1],
                                                                                             in1=rstd_t[:, c0:c0 + nt], op0=ALU.mult, op1=ALU.mult),
                                 reads=[b_x[f], b_rstd], writes=[b_gtmp[j]])
                            P.op(pe, lambda e, f=f, f4=f4, j=j, nt=nt, i=i: e.transpose(banks[i][0:nt, (f - f4) * 128:(f - f4 + 1) * 128], gtmp[j][:, 0:nt], identf),
                                 reads=[b_gtmp[j]], writes=[b_bank[i]])
                        P.op(act, lambda e, f4=f4, f8=f8, nf=nf, nt=nt, i=i, so=so: e.activation(out=ostage[so][0:nt, (f4 - f8) * 128:(f4 - f8 + nf) * 128],
                                                                                                in_=banks[i][0:nt, 0:nf * 128], func=AF.Copy),
                             reads=[b_bank[i]], writes=[b_ost[so]])
                    P.dma(sp, dst[:, f8 * 128:(f8 + nf8) * 128], ostage[so][0:nt, 0:nf8 * 128], b_ost[so], reads=[b_ost[so]], writes=[b_ost[so]])
            out_bufs.extend(b_ost)

        try:
            if not _go:
                raise _Stop()
            chk("wag")
            for r in range(R):
                round_body(r)
        except _Stop:
            pass
        P.barrier()
        seen = set()
        for bf in out_bufs:
            sm = bf.dsem
            if sm is not None and id(sm) not in seen and sm.v:
                seen.add(id(sm))
                nc.sync.wait_ge(sm.h, sm.v)
        print("instructions:", P.nins, "sems:", P.nsem)
    return nc


def make_in_maps(cfg, inp):
    D, NH, FC, T, R = cfg.D, cfg.NH, cfg.FC, cfg.T, cfg.ROUNDS
    W = prep_weights(cfg, inp)
    lbl = np.ascontiguousarray(inp["hgrn_lb_logits"].reshape(3, NH, 128).transpose(2, 0, 1)).astype(np.float32)
    nv = np.stack([inp["norm_mix"][0], inp["norm_ffn"][0], inp["norm_mix"][1], inp["norm_ffn"][1], inp["norm_final"]])
    norms = np.ascontiguousarray(nv.reshape(5, FC, 128).transpose(2, 0, 1)).astype(np.float32)
    cw = np.ascontiguousarray(inp["conv_w"][0].reshape(3, FC, 128).transpose(2, 0, 1)).astype(np.float32)
    ogain = np.ascontiguousarray(inp["hgrn_out_gain"][0].reshape(128, 1)).astype(np.float32)
    xp_full = inp["x_prompt"][0]
    consts = make_consts(cfg)
    maps = []
    for c in range(NCORES):
        m = {}
        m["xp"] = np.ascontiguousarray(np.concatenate([xp_full[(r * NCORES + c) * T:(r * NCORES + c + 1) * T] for r in range(R)], axis=0))
        m["xs"] = np.ascontiguousarray(inp["x_sample"][cfg.SPC * c:cfg.SPC * (c + 1)].reshape(cfg.EX, D))
        m["sh"] = np.ascontiguousarray(inp["state_hgrn"][0, cfg.SPC * c:cfg.SPC * (c + 1)])
        m["sc"] = np.ascontiguousarray(inp["state_conv"][0, cfg.SPC * c:cfg.SPC * (c + 1)])
        m["lbl"] = lbl; m["ogain"] = ogain; m["norms"] = norms; m["cw"] = cw
        m["consts"] = consts
        rm = np.zeros((128, 17), np.float32)
        rm[:, c] = 1.0
        if c >= 1:
            rm[:, 8 + c - 1] = 1.0
        else:
            rm[:, 16] = 1.0
        m["rmask"] = rm
        for n in WNAMES:
            PT = 2 if n.startswith("w_fout") else 4
            RPR = PT * 16
            for pi, (t0, nt) in enumerate(wpieces(cfg, n)):
                ch = W[n][t0:t0 + nt].reshape(nt // PT, PT * 128, -1)
                m[f"{n}_{pi}"] = np.ascontiguousarray(ch[:, c * RPR:(c + 1) * RPR, :].reshape(nt // PT * RPR, -1))
        maps.append(m)
    return maps


def assemble(cfg, res):
    D, NH, T, R = cfg.D, cfg.NH, cfg.T, cfg.ROUNDS
    yp = np.zeros((1, cfg.SEQ, D), np.float32)
    for c in range(NCORES):
        for r in range(R):
            j = r * NCORES + c
            yp[0, j * T:(j + 1) * T] = res[c]["yp"][r * T:(r + 1) * T]
    ys = np.concatenate([res[c]["ys"].reshape(cfg.SPC, cfg.DS, D) for c in range(NCORES)], axis=0)
    hp = res[0]["hp"].reshape(1, 1, NH, 128, 128)
    hs = np.concatenate([res[c]["hs"] for c in range(NCORES)], axis=0).reshape(1, cfg.DB, NH, 128, 128)
    cp = res[NCORES - 1]["cp"][R - 1].reshape(1, 1, 2, D)
    cs = np.concatenate([res[c]["cs"] for c in range(NCORES)], axis=0).reshape(1, cfg.DB, 2, D)
    return (yp, ys, hp, hs, cp, cs)


_NC_CACHE = {}


def run(cfg, inp):
    key = (cfg.D, cfg.NH, cfg.DFF, cfg.SEQ, cfg.T)
    if key not in _NC_CACHE:
        _NC_CACHE[key] = build(cfg)
    nc = _NC_CACHE[key]
    maps = make_in_maps(cfg, inp)
    res = run_bass_kernel_spmd(nc, maps, core_ids=list(range(NCORES)))
    return assemble(cfg, res.results)


def kernel(**inputs):
    inp = {k: np.asarray(v) for k, v in inputs.items()}
    return run(FULL, inp)
```

```python
import os
import numpy as np
import concourse.bass as bass
import concourse.mybir as mybir
from concourse.bass_utils import run_bass_kernel_spmd
from contextlib import ExitStack

F32 = mybir.dt.float32
BF16 = mybir.dt.bfloat16
AF = mybir.ActivationFunctionType
ALU = mybir.AluOpType
NCORES = 8
EPS = 1e-6


class _Stop(Exception):
    pass


class Cfg:
    stop = None

    def __init__(self, D=4096, NH=32, DFF=11008, SEQ=16384, T=512, DB=16, DS=16, G=4, NQ=4):
        self.D, self.NH, self.DFF, self.SEQ, self.T, self.DB, self.DS, self.G = D, NH, DFF, SEQ, T, DB, DS, G
        self.FC = D // 128
        self.KH = self.FC // 2
        self.NS = DFF // 128
        self.ROUNDS = SEQ // (NCORES * T)
        self.NPB = T // 128
        self.NCH = T // 64
        self.SPC = DB // NCORES
        self.EX = self.SPC * DS
        self.NX = T + self.EX
        self.NQ = NQ
        base = self.NS // NQ
        rem = self.NS % NQ
        self.QS = [base + (1 if i < rem else 0) for i in range(NQ)]
        self.QMAX = max(self.QS)
        self.NG = NH // G
        assert DS == 16 and self.SPC == 2 and D == NH * 128


FULL = Cfg()


def tile_w(W, kh):
    K, NO = W.shape
    kc = K // 128
    nh = kc // kh
    t = W.reshape(nh, kh, 128, NO // 128, 128)
    t = t.transpose(3, 0, 2, 1, 4)
    return np.ascontiguousarray(t).reshape(NO // 128 * nh, 128, kh * 128)


def prep_weights(cfg, inp):
    D, NH, NS = cfg.D, cfg.NH, cfg.NS
    out = {}
    W = inp["hgrn_w_in"][0]
    cols = np.concatenate([np.arange(s * D + h * 128, s * D + h * 128 + 128) for h in range(NH) for s in range(4)])
    out["w_hin"] = tile_w(W[:, cols], cfg.KH)
    out["w_hout"] = tile_w(inp["hgrn_w_out"][0], cfg.KH)
    W = inp["conv_w_in"][0]
    cols = np.concatenate([np.arange(s * D + f * 128, s * D + f * 128 + 128) for f in range(cfg.FC) for s in (1, 2)]
                          + [np.arange(0, D)])
    out["w_cin"] = tile_w(W[:, cols], cfg.KH)
    out["w_cout"] = tile_w(inp["conv_w_out"][0], cfg.KH)
    for l in range(2):
        W = inp["ffn_w_in"][l]
        cols = np.concatenate([np.arange(s * cfg.DFF + sl * 128, s * cfg.DFF + sl * 128 + 128)
                               for sl in range(NS) for s in range(2)])
        out[f"w_fin{l}"] = tile_w(W[:, cols], cfg.KH)
        W = inp["ffn_w_out"][l]
        tl = np.zeros((cfg.NQ, cfg.FC, 128, cfg.QMAX, 128), np.float32)
        s0 = 0
        for q in range(cfg.NQ):
            n = cfg.QS[q]
            blk = W[s0 * 128:(s0 + n) * 128, :].reshape(n, 128, cfg.FC, 128)
            tl[q, :, :, :n, :] = blk.transpose(2, 1, 0, 3)
            s0 += n
        out[f"w_fout{l}"] = tl.reshape(cfg.NQ * cfg.FC, 128, cfg.QMAX * 128)
    return out


WNAMES = ["w_hin", "w_hout", "w_fin0", "w_fout0", "w_cin", "w_cout", "w_fin1", "w_fout1"]


def wshape(cfg, name):
    D, NH, NS, FC, KH = cfg.D, cfg.NH, cfg.NS, cfg.FC, cfg.KH
    c = KH * 128
    if name == "w_hin":
        return (NH * 4 * 2, c)
    if name in ("w_hout", "w_cout"):
        return (FC * 2, c)
    if name == "w_cin":
        return (FC * 3 * 2, c)
    if name.startswith("w_fin"):
        return (NS * 2 * 2, c)
    return (cfg.NQ * FC, cfg.QMAX * 128)


def PT_OF(name):
    return 4 if name.startswith("w_fout") else 8


def wpieces(cfg, name, pmax=int(os.environ.get("PMAX", "128"))):
    nt, c = wshape(cfg, name)
    out = []
    t0 = 0
    while t0 < nt:
        n = min(pmax, nt - t0)
        assert n % NCORES == 0
        out.append((t0, n))
        t0 += n
    return out


class Sem:
    def __init__(self, h):
        self.h = h
        self.v = 0


class Buf:
    __slots__ = ("name", "w", "r", "dsem")

    def __init__(self, name=""):
        self.name = name
        self.w = None
        self.r = []
        self.dsem = None


class Eng:
    def __init__(self, name, handle, sem, skip_self=False):
        self.name, self.e, self.sem, self.skip_self = name, handle, sem, skip_self
        self.seen = {}


class Prog:
    def __init__(self, nc, es):
        self.nc = nc
        self.es = es
        self.nsem = 0
        mk = self.new_sem
        self.pe = Eng("pe", nc.tensor, mk("pe"), skip_self=True)
        self.act = Eng("act", nc.scalar, mk("act"))
        self.dve = Eng("dve", nc.vector, mk("dve"))
        self.pool = Eng("pool", nc.gpsimd, mk("pool"))
        self.sp = Eng("sp", nc.sync, mk("sp"))
        self.engs = [self.pe, self.act, self.dve, self.pool, self.sp]
        self.nins = 0

    def new_sem(self, name):
        self.nsem += 1
        sm = Sem(self.es.enter_context(self.nc.semaphore(f"s_{name}_{self.nsem}")))
        if not hasattr(self, "sems"):
            self.sems = []
        self.sems.append(sm)
        return sm

    def _waits(self, eng, reads, writes):
        need = {}

        def req(ev):
            if ev is None:
                return
            s, v = ev
            if eng.skip_self and s is eng.sem:
                return
            if eng.seen.get(s, 0) >= v:
                return
            if need.get(s, 0) < v:
                need[s] = v
        for b in reads:
            req(b.w)
        for b in writes:
            req(b.w)
            for ev in b.r:
                req(ev)
        for s, v in need.items():
            eng.e.wait_ge(s.h, v)
            eng.seen[s] = v
            self.nins += 1

    def op(self, eng, fns, reads=(), writes=(), sem=None, inc=1):
        if callable(fns):
            fns = [fns]
        self._waits(eng, reads, writes)
        ins = None
        for f in fns:
            ins = f(eng.e)
            self.nins += 1
        s = eng.sem if sem is None else sem
        s.v += inc
        ins.then_inc(s.h, inc)
        ev = (s, s.v)
        for b in reads:
            b.r.append(ev)
        for b in writes:
            b.w = ev
            b.r = []
        return ev

    def dma(self, eng, out, in_, sembuf, reads=(), writes=(), **kw):
        if isinstance(sembuf, Sem):
            sem = sembuf
        else:
            if sembuf.dsem is None:
                sembuf.dsem = self.new_sem("d" + sembuf.name)
            sem = sembuf.dsem
        return self.op(eng, lambda e: e.dma_start(out=out, in_=in_, **kw), reads, writes, sem=sem, inc=16)

    def barrier(self):
        for e in self.engs:
            for sm in self.sems:
                if sm.v == 0 or (sm is e.sem):
                    continue
                if e.seen.get(sm, 0) < sm.v:
                    e.e.wait_ge(sm.h, sm.v)
                    e.seen[sm] = sm.v


class Arena:
    def __init__(self, t, words):
        self.t, self.words, self.off = t, words, 0

    def mark(self):
        return self.off

    def reset(self, m):
        self.off = m

    def f32(self, shape):
        n = int(np.prod(shape[1:]))
        self.off = (self.off + 15) // 16 * 16
        ap = self.t[0:shape[0], self.off:self.off + n]
        self.off += n
        assert self.off <= self.words, ("arena overflow", self.off, self.words)
        return _reshape(ap, shape)

    def bf16(self, shape):
        n = int(np.prod(shape[1:]))
        nw = (n + 1) // 2
        self.off = (self.off + 15) // 16 * 16
        ap = self.t[0:shape[0], self.off:self.off + nw].bitcast(BF16)[:, 0:n]
        self.off += nw
        assert self.off <= self.words, ("arena overflow", self.off, self.words)
        return _reshape(ap, shape)


def _reshape(ap, shape):
    if len(shape) == 2:
        return ap
    if len(shape) == 3:
        return ap.rearrange("p (a b) -> p a b", a=shape[1], b=shape[2])
    if len(shape) == 4:
        return ap.rearrange("p (a b c) -> p a b c", a=shape[1], b=shape[2], c=shape[3])
    raise ValueError


def const_layout(cfg):
    NX = cfg.NX
    o = {}
    c = 0
    for name, n in (("identf", 128), ("maskA", 128), ("maskS", 32), ("smask", NX), ("sm01", 2), ("pm64", 2)):
        o[name] = (c, n)
        c += n
    return o, c


def make_consts(cfg):
    lay, cwid = const_layout(cfg)
    T, NX = cfg.T, cfg.NX
    C = np.zeros((128, cwid), np.float32)
    o, n = lay["identf"]; C[:, o:o + n] = np.eye(128, dtype=np.float32)
    s_ = np.arange(128)[:, None]; t_ = np.arange(128)[None, :]
    o, n = lay["maskA"]; C[:, o:o + n] = ((s_ <= t_) & (s_ // 64 == t_ // 64)).astype(np.float32)
    s2 = np.arange(32)[:, None]; t2 = np.arange(32)[None, :]
    o, n = lay["maskS"]; C[0:32, o:o + n] = ((s2 <= t2) & (s2 // 16 == t2 // 16)).astype(np.float32)
    sm = np.ones(NX, np.float32); sm[0:T:64] = 0.0; sm[T:NX:16] = 0.0
    o, n = lay["smask"]; C[:, o:o + n] = sm[None, :]
    o, n = lay["sm01"]; C[0:16, o] = 1.0; C[16:32, o + 1] = 1.0
    o, n = lay["pm64"]; C[0:64, o] = 1.0; C[64:128, o + 1] = 1.0
    return C


def build(cfg):
    nc = bass.Bass("TRN2", target_bir_lowering=False, num_devices=NCORES)
    D, NH, FC, KH, T, EX, NX, G = cfg.D, cfg.NH, cfg.FC, cfg.KH, cfg.T, cfg.EX, cfg.NX, cfg.G
    NCH, NPB, R, NS = cfg.NCH, cfg.NPB, cfg.ROUNDS, cfg.NS
    WC = KH * 128
    WCMAX = max(WC, cfg.QMAX * 128)
    clay, CWID = const_layout(cfg)

    def din(name, shape, dt=F32):
        return nc.dram_tensor(name, list(shape), dt, kind="ExternalInput").ap()

    def dout(name, shape):
        return nc.dram_tensor(name, list(shape), F32, kind="ExternalOutput").ap()

    def dint(name, shape):
        return nc.dram_tensor(name, list(shape), F32, kind="Internal").ap()

    xp = din("xp", [R * T, D])
    xs = din("xs", [EX, D])
    sh = din("sh", [cfg.SPC, NH, 128, 128])
    sc = din("sc", [cfg.SPC, 2, D])
    lbl = din("lbl", [128, 3, NH])
    ogain = din("ogain", [128, 1])
    norms = din("norms", [128, 5, FC])
    cw = din("cw", [128, 3, FC])
    rmask = din("rmask", [128, 17])
    consts = din("consts", [128, CWID])
    wsh, wag_in, wag = {}, {}, {}
    WP = []
    for n in WNAMES:
        _, c = wshape(cfg, n)
        PT = PT_OF(n)
        for pi, (t0, nt) in enumerate(wpieces(cfg, n)):
            key = f"{n}_{pi}"
            assert nt % PT == 0
            WP.append((key, n, t0, nt, PT))
            wsh[key] = din(key, [nt // NCORES * 128, c])
            wag_in[key] = nc.dram_tensor(key + "_i", [nt // NCORES * 128, c], BF16, kind="Internal").ap()
            wag[key] = nc.dram_tensor(key + "_g", [nt * 128, c], BF16, kind="Internal").ap()
    NMID = 4
    midA = [nc.dram_tensor(f"midA{i}", [4 * 16 * PT_OF("w_hin"), WC], BF16, kind="Internal").ap() for i in range(NMID)]
    midB = [nc.dram_tensor(f"midB{i}", [4 * 16 * PT_OF("w_fout0"), cfg.QMAX * 128], BF16, kind="Internal").ap() for i in range(NMID)]
    yp = dout("yp", [R * T, D])
    ys = dout("ys", [EX, D])
    hp = dout("hp", [NH, 128, 128])
    hs = dout("hs", [cfg.SPC, NH, 128, 128])
    cp = dout("cp", [R, 2, D])
    cs = dout("cs", [cfg.SPC, 2, D])
    agh_in = dint("agh_in", [G * 128, 129])
    agh = dint("agh", [NCORES * G * 128, 129])
    agh_mid = dint("agh_mid", [NCORES // 2 * G * 128, 129])
    agc_in = dint("agc_in", [128, FC * 2])
    agc = dint("agc", [NCORES * 128, FC * 2])
    agc_mid = dint("agc_mid", [NCORES // 2 * 128, FC * 2])
    sround = dint("sround", [128, NH, 128])

    es = ExitStack()
    with es:
        AW = 52000
        arena_t = es.enter_context(nc.sbuf_tensor("arena", [128, AW], F32))
        ar = Arena(arena_t, AW)
        banks = [es.enter_context(nc.psum_tensor(f"bank{i}", [128, 512], F32)) for i in range(6)]
        bank6b = es.enter_context(nc.psum_tensor("bank6b", [128, 1024], BF16))
        banks.append(None)
        banks.append(es.enter_context(nc.psum_tensor("bank7", [128, 512], F32)))
        P = Prog(nc, es)
        pe, act, dve, pool, sp = P.pe, P.act, P.dve, P.pool, P.sp
        B = Buf

        def chk(name):
            if cfg.stop == name:
                raise _Stop()

        const_t = ar.f32([128, CWID])
        cv = lambda k: const_t[:, clay[k][0]:clay[k][0] + clay[k][1]]
        identf = cv("identf"); smask = cv("smask"); sm01 = cv("sm01")[0:32, :]; pm64 = cv("pm64")
        identb = ar.bf16([128, 128]); onesb = ar.bf16([128, 128]); zerob = ar.bf16([128, 128])
        maskA = ar.bf16([128, 128]); maskS = ar.bf16([32, 32]); ones_f = ar.f32([128, 16])
        norms_t = ar.f32([128, 5, FC]); cw_t = ar.f32([128, 3, FC]); rm_t = ar.f32([128, 17])
        lb_t = ar.f32([128, NH]); oml_t = ar.f32([128, NH]); noml_t = ar.f32([128, NH])
        lbl_t = ar.f32([128, 3, NH]); og_t = ar.f32([128, 1])
        prev7 = ar.f32([128, 2 * FC])
        NSLOT = 6
        wslots = [ar.bf16([128, WCMAX]) for _ in range(NSLOT)]
        h_t = ar.bf16([128, FC, NX])
        aux = ar.bf16([128, FC, NX + 8])
        rstd_t = ar.f32([128, NX])
        gtmp = [ar.f32([128, NX]) for _ in range(2)]
        sqt = [ar.bf16([128, NX]) for _ in range(2)]
        ostage = [ar.f32([128, 1024]) for _ in range(2)]
        hal_t = ar.f32([128, 2, FC]); halg = ar.f32([128, NCORES, 2 * FC]); halo = ar.f32([128, 2, FC])
        XR0 = ar.mark()
        x_sb = ar.f32([128, FC, NX])
        XRx = ar.mark()
        XR1 = AW
        ar.reset(XR0)
        xst = [ar.f32([128, D // 2]) for _ in range(2)]
        junk = ar.f32([128, D // 2])
        ssq = ar.f32([128, 8])
        XRa = ar.mark()
        ar.reset(XR0)
        gs_ol = ar.bf16([128, G, NX]); gs_qc = ar.bf16([128, G, NX]); gs_sg = ar.bf16([128, G, NX])
        tq = ar.f32([128, NX]); tsig = ar.f32([128, NX]); tg = ar.f32([128, NX]); tk = ar.f32([128, NX])
        tb = ar.f32([128, NX]); teb = ar.f32([128, NX]); tenb = ar.f32([128, NX])
        tqt = ar.bf16([128, NX]); tkt = ar.bf16([128, NX]); tv = ar.bf16([128, NX])
        ktok = ar.bf16([128, NPB, 128]); vtok = ar.bf16([128, NPB, 128])
        ktokm = ar.bf16([128, NPB, 2, 128])
        vtoks = ar.bf16([32, 128]); kt01 = ar.bf16([32, 2, 128]); ktoks = ar.bf16([32, 128])
        atm = ar.bf16([128, NPB, 128]); atms = ar.bf16([32, 32])
        td = ar.f32([128, NCH + 2]); tci = ar.f32([128, NCH]); tE = ar.f32([128, NCH])
        Mf = [ar.f32([128, 128]) for _ in range(2)]; Mtmp = ar.f32([128, 128]); Mb = ar.bf16([128, NCH, 128])
        Mbs = ar.bf16([128, 2, 128])
        LD = ar.f32([128, G, 129]); LDj = [ar.f32([128, G, 129]) for _ in range(2)]
        S_t = ar.f32([128, G, 128]); Sin = ar.f32([128, G, 128]); Stmp = ar.f32([128, G, 128]); M0b = ar.bf16([128, G, 128])
        S0s = ar.f32([128, 2, G, 128]); NSs = ar.f32([128, 2, G, 128])
        to32 = ar.f32([128, NX]); osq = ar.bf16([128, NX]); t1 = ar.f32([128, NX]); rs2 = ar.f32([128, NX])
        XRb = ar.mark()
        print("arena words: xsb_end", XRx, "p0_end", XRa, "p1_end", XRb, "of", AW)
        ar.reset(max(XRx, XRa, XRb))

        b_const = B("const"); b_prev7 = B("prev7")
        b_wslot = [B(f"ws{i}") for i in range(NSLOT)]
        b_h = [B(f"h{i}") for i in range(FC)]
        b_aux = [B(f"aux{i}") for i in range(max(FC, cfg.QMAX))]
        b_x = [B(f"x{i}") for i in range(FC)]
        b_bank = [B(f"bank{i}") for i in range(8)]
        b_rstd = B("rstd"); b_gtmp = [B("g0"), B("g1")]; b_sq = [B("q0"), B("q1")]
        b_ost = [B("o0"), B("o1")]
        s_cc = P.new_sem("cc")
        b_wag = {}
        RG1 = [[0, 1, 2, 3], [4, 5, 6, 7]]
        RG2 = [[0, 4], [1, 5], [2, 6], [3, 7]]

        def allgather(in_ap, mid_ap, out_ap, reads, outbuf, midbuf):
            P.op(pool, lambda e: e.collective_compute("AllGather", ALU.bypass, replica_groups=RG1, ins=[in_ap], outs=[mid_ap]),
                 reads=reads, writes=[midbuf], sem=s_cc, inc=1)
            P.op(pool, lambda e: e.collective_compute("AllGather", ALU.bypass, replica_groups=RG2, ins=[mid_ap], outs=[out_ap]),
                 reads=[midbuf], writes=[outbuf], sem=s_cc, inc=1)
        b_xst = [B("xst0"), B("xst1")]; b_ssq = B("ssq"); b_junk = B("junk")
        b_gs = [B(f"gs{i}") for i in range(G)]
        bt = {k: B(k) for k in ("tq", "tsig", "tg", "tk", "tb", "teb", "tenb", "tqt", "tkt", "tv", "ktok", "vtok", "kts",
                                "atm", "td", "tci", "tE", "ktokm", "M0", "M1", "Mtmp", "Mb", "Mbs", "LD", "S", "Sin", "Stmp", "M0b",
                                "S0s", "NSs", "to32", "osq", "t1", "rs2")}
        b_ld = [B("ld0"), B("ld1")]
        b_sr = [B(f"sr{g}") for g in range(cfg.NG)]
        b_agh_in = B("agh_in"); b_agh = B("agh"); b_agc_in = B("agc_in"); b_agc = B("agc")
        b_agh_mid = B("agh_mid"); b_agc_mid = B("agc_mid")
        b_hal = B("hal"); b_halg = B("halg"); b_halo = B("halo")
        out_bufs = []

        P.dma(sp, const_t, consts, b_const, writes=[b_const])
        for dst, src in ((norms_t, norms), (cw_t, cw), (rm_t, rmask), (lbl_t, lbl), (og_t, ogain)):
            P.dma(sp, dst, src, b_const, writes=[b_const])
        for ap_, val in ((onesb, 1.0), (zerob, 0.0), (prev7, 0.0), (ones_f, 1.0)):
            P.op(dve, lambda e, ap_=ap_, val=val: e.memset(ap_, val), writes=[b_prev7])
        P.op(dve, lambda e: e.tensor_copy(out=identb, in_=identf), reads=[b_const], writes=[b_prev7])
        P.op(dve, lambda e: e.tensor_copy(out=maskA, in_=cv("maskA")), reads=[b_const], writes=[b_prev7])
        P.op(dve, lambda e: e.tensor_copy(out=maskS, in_=cv("maskS")[0:32, :]), reads=[b_const], writes=[b_prev7])
        e3 = gtmp[0][:, 0:3 * NH].rearrange("p (a b) -> p a b", a=3)
        P.op(act, lambda e: e.activation(out=e3, in_=lbl_t, func=AF.Exp), reads=[b_const], writes=[b_gtmp[0]])
        s1 = gtmp[1][:, 0:NH]
        P.op(dve, lambda e: e.tensor_tensor(out=s1, in0=e3[:, 0, :], in1=e3[:, 1, :], op=ALU.add), reads=[b_gtmp[0]], writes=[b_gtmp[1]])
        P.op(dve, lambda e: e.tensor_tensor(out=s1, in0=s1, in1=e3[:, 2, :], op=ALU.add), reads=[b_gtmp[0], b_gtmp[1]], writes=[b_gtmp[1]])
        P.op(dve, lambda e: e.reciprocal(out=s1, in_=s1), reads=[b_gtmp[1]], writes=[b_gtmp[1]])
        P.op(dve, lambda e: e.tensor_tensor(out=lb_t, in0=e3[:, 0, :], in1=s1, op=ALU.mult), reads=[b_gtmp[0], b_gtmp[1]], writes=[b_prev7])
        P.op(dve, lambda e: e.tensor_scalar(out=oml_t, in0=lb_t, scalar1=-1.0, scalar2=1.0, op0=ALU.mult, op1=ALU.add),
             reads=[b_prev7], writes=[b_prev7])
        P.op(dve, lambda e: e.tensor_scalar(out=noml_t, in0=lb_t, scalar1=1.0, scalar2=-1.0, op0=ALU.mult, op1=ALU.add),
             reads=[b_prev7], writes=[b_prev7])
        P.barrier()
        try:
            chk("consts")
            _go = True
        except _Stop:
            _go = False
        b_const = B("const_ro")

        RG = [list(range(NCORES))]
        b_win = {}
        wp_by_name = {}
        gstate = {}
        b_midA = [B(f"midA{i}") for i in range(NMID)]
        b_midB = [B(f"midB{i}") for i in range(NMID)]
        midctr = {"A": 0, "B": 0}
        LOOKAHEAD = 6
        for (key, n, t0, nt, PT) in (WP if _go else []):
            wp_by_name.setdefault(n, []).append((key, t0, nt, PT))
            rows, c = wsh[key].shape
            b_win[key] = B(key + "_i")
            for r0 in range(0, rows, 1024):
                r1 = min(rows, r0 + 1024)
                P.dma(pool, wag_in[key][r0:r1, :], wsh[key][r0:r1, :], b_win[key], writes=[b_win[key]])
            npc = nt // PT
            gstate[key] = {"s1": 0, "s2": 0, "np": npc, "PT": PT, "mids": {}}
            b_wag[key] = [B(f"{key}_p{i}") for i in range(npc)]

        def gather_step(key, upto):
            st = gstate[key]
            PT = st["PT"]
            RPR = PT * 16
            kind = "B" if PT == PT_OF("w_fout0") else "A"
            mids, bmids = (midB, b_midB) if kind == "B" else (midA, b_midA)
            upto = min(upto, st["np"] - 1)

            def s2(p):
                k = st["mids"][p]
                P.op(pool, lambda e: e.collective_compute("AllGather", ALU.bypass, replica_groups=RG2, ins=[mids[k]],
                                                          outs=[wag[key][p * PT * 128:(p + 1) * PT * 128, :]]),
                     reads=[bmids[k]], writes=[b_wag[key][p]], sem=s_cc, inc=1)
            while st["s1"] <= upto:
                p = st["s1"]
                k = midctr[kind] % NMID
                midctr[kind] += 1
                st["mids"][p] = k
                P.op(pool, lambda e: e.collective_compute("AllGather", ALU.bypass, replica_groups=RG1,
                                                          ins=[wag_in[key][p * RPR:(p + 1) * RPR, :]], outs=[mids[k]]),
                     reads=[b_win[key]], writes=[bmids[k]], sem=s_cc, inc=1)
                st["s1"] += 1
                while st["s2"] < st["s1"] - 1:
                    s2(st["s2"])
                    st["s2"] += 1
            if st["s1"] == st["np"]:
                while st["s2"] < st["np"]:
                    s2(st["s2"])
                    st["s2"] += 1

        def ensure_piece(key, p):
            st = gstate[key]
            gather_step(key, p + LOOKAHEAD)
            while st["s2"] <= p:
                k = st["s2"]
                gather_flush_one(key)

        def gather_flush_one(key):
            st = gstate[key]
            PT = st["PT"]
            kind = "B" if PT == PT_OF("w_fout0") else "A"
            mids, bmids = (midB, b_midB) if kind == "B" else (midA, b_midA)
            p = st["s2"]
            k = st["mids"][p]
            P.op(pool, lambda e: e.collective_compute("AllGather", ALU.bypass, replica_groups=RG2, ins=[mids[k]],
                                                      outs=[wag[key][p * PT * 128:(p + 1) * PT * 128, :]]),
                 reads=[bmids[k]], writes=[b_wag[key][p]], sem=s_cc, inc=1)
            st["s2"] += 1

        wctr = [0]

        def wload(name, tile_idx, ncols):
            i = wctr[0] % NSLOT
            wctr[0] += 1
            for (key, t0, nt, PT) in wp_by_name[name]:
                if t0 <= tile_idx < t0 + nt:
                    break
            li = tile_idx - t0
            pidx = li // PT
            ensure_piece(key, pidx)
            src = wag[key][li * 128:(li + 1) * 128, 0:ncols]
            P.dma(pool, wslots[i][:, 0:ncols], src, b_wslot[i], reads=[b_wag[key][pidx]], writes=[b_wslot[i]])
            return wslots[i], b_wslot[i]

        NACC = 6
        accctr = [0]

        def acc_get():
            i = accctr[0] % NACC
            accctr[0] += 1
            return i

        def acc_seg(i, c0, n):
            if c0 < T:
                return banks[i][:, c0:c0 + n]
            return banks[7][:, 32 * i + (c0 - T):32 * i + (c0 - T) + n]

        def acc_bufs(i, N):
            return [b_bank[i]] + ([b_bank[7]] if N > T else [])

        def segs(N):
            return [(0, T)] + ([(T, N - T)] if N > T else [])

        def mm_acc(i, N, steps, reads, first, last):
            fl = []
            for (c0, n) in segs(N):
                for k, (lh, rf) in enumerate(steps):
                    fl.append(lambda e, c0=c0, n=n, lh=lh, rf=rf, k=k: e.matmul(
                        acc_seg(i, c0, n), lh, rf(c0, n), start=(first and k == 0), stop=(last and k == len(steps) - 1)))
            P.op(pe, fl, reads=reads, writes=acc_bufs(i, N))

        def proj(name, tile0, N, rhs_bufs, rhs_fn, i):
            for half in range(2):
                wt, wb = wload(name, tile0 + half, WC)
                w3 = wt[:, 0:WC].rearrange("p (k c) -> p k c", c=128)
                steps = [(w3[:, k, :], (lambda c0, n, kc=half * KH + k: rhs_fn(kc, c0, n))) for k in range(KH)]
                mm_acc(i, N, steps, [wb] + list(rhs_bufs), first=(half == 0), last=(half == 1))

        def rsqrt_inplace(ap, buf):
            P.op(act, lambda e: e.activation(out=ap, in_=ap, func=AF.Sqrt), reads=[buf], writes=[buf])
            P.op(dve, lambda e: e.reciprocal(out=ap, in_=ap), reads=[buf], writes=[buf])

        def fm_norm(N):
            i = acc_get()
            for fc in range(FC):
                j = fc % 2
                P.op(act, lambda e, fc=fc, j=j: e.activation(out=sqt[j][:, 0:N], in_=x_sb[:, fc, 0:N], func=AF.Square),
                     reads=[b_x[fc]], writes=[b_sq[j]])
                mm_acc(i, N, [(onesb, lambda c0, n, j=j: sqt[j][:, c0:c0 + n])], [b_sq[j]], first=(fc == 0), last=(fc == FC - 1))
            for (c0, n) in segs(N):
                P.op(dve, lambda e, c0=c0, n=n: e.tensor_scalar(out=rstd_t[:, c0:c0 + n], in0=acc_seg(i, c0, n), scalar1=1.0 / D,
                                                                 scalar2=EPS, op0=ALU.mult, op1=ALU.add),
                     reads=acc_bufs(i, N), writes=[b_rstd])
            rsqrt_inplace(rstd_t[:, 0:N], b_rstd)

        def fm_norm_apply(N, gain_idx):
            for fc in range(FC):
                P.op(dve, lambda e, fc=fc: e.scalar_tensor_tensor(out=h_t[:, fc, 0:N], in0=x_sb[:, fc, 0:N],
                                                                  scalar=norms_t[:, gain_idx, fc:fc + 1], in1=rstd_t[:, 0:N],
                                                                  op0=ALU.mult, op1=ALU.mult),
                     reads=[b_x[fc], b_rstd], writes=[b_h[fc]])

        actv = aux.rearrange("p a b -> p (a b)")[:, 0:cfg.QMAX * NX].rearrange("p (a b) -> p a b", b=NX)
        hrd = lambda kc, c0, n: h_t[:, kc, c0:c0 + n]

        def ffn(l, N):
            fm_norm(N)
            fm_norm_apply(N, 1 + 2 * l)
            s0 = 0
            for q in range(cfg.NQ):
                nq = cfg.QS[q]
                for sl in range(nq):
                    s = s0 + sl
                    ig = acc_get()
                    proj(f"w_fin{l}", (s * 2 + 0) * 2, N, b_h, hrd, ig)
                    iu = acc_get()
                    proj(f"w_fin{l}", (s * 2 + 1) * 2, N, b_h, hrd, iu)
                    j = s % 2
                    for (c0, n) in segs(N):
                        P.op(act, lambda e, c0=c0, n=n, j=j, ig=ig: e.activation(out=gtmp[j][:, c0:c0 + n], in_=acc_seg(ig, c0, n), func=AF.Silu),
                             reads=acc_bufs(ig, N), writes=[b_gtmp[j]])
                    for (c0, n) in segs(N):
                        P.op(dve, lambda e, c0=c0, n=n, j=j, iu=iu, sl=sl: e.tensor_tensor(out=actv[:, sl, c0:c0 + n], in0=gtmp[j][:, c0:c0 + n],
                                                                                        in1=acc_seg(iu, c0, n), op=ALU.mult),
                             reads=acc_bufs(iu, N) + [b_gtmp[j]], writes=[b_aux[sl]])
                for fc in range(FC):
                    wt, wb = wload(f"w_fout{l}", q * FC + fc, cfg.QMAX * 128)
                    w3 = wt[:, 0:cfg.QMAX * 128].rearrange("p (k c) -> p k c", c=128)
                    i = acc_get()
                    steps = [(w3[:, k, :], (lambda c0, n, k=k: actv[:, k, c0:c0 + n])) for k in range(nq)]
                    mm_acc(i, N, steps, [wb] + b_aux[0:nq], first=True, last=True)
                    for (c0, n) in segs(N):
                        P.op(dve, lambda e, c0=c0, n=n, fc=fc, i=i: e.tensor_tensor(out=x_sb[:, fc, c0:c0 + n], in0=x_sb[:, fc, c0:c0 + n],
                                                                                 in1=acc_seg(i, c0, n), op=ALU.add),
                             reads=acc_bufs(i, N) + [b_x[fc]], writes=[b_x[fc]])
                s0 += nq

        bk6 = bank6b[:, :]
        psTk = bk6[:, 0:NPB * 128].rearrange("p (a b) -> p a b", b=128)
        psTv = bk6[:, 512:512 + NPB * 128].rearrange("p (a b) -> p a b", b=128)
        psTks = bk6[:, 0:128]; psTvs = bk6[:, 512:640]
        psATs = banks[7][:, 320:352]
        HW_ = D // 2

        def round_body(r):
            last_round = (r == R - 1)
            N = NX if last_round else T
            P.barrier()
            tok_blocks = [(xp[r * T + tbk * 128: r * T + tbk * 128 + 128, :], 128, tbk * 128) for tbk in range(NPB)]
            if last_round:
                tok_blocks.append((xs[0:EX, :], EX, T))
            for (src, nt, c0) in tok_blocks:
                for hf in range(2):
                    P.dma(sp, xst[hf][0:nt, :], src[:, hf * HW_:(hf + 1) * HW_], b_xst[hf], writes=[b_xst[hf]])
                    P.op(act, lambda e, hf=hf, nt=nt: e.activation(out=junk[0:nt, :], in_=xst[hf][0:nt, :], func=AF.Square,
                                                                 accum_out=ssq[0:nt, hf:hf + 1]),
                         reads=[b_xst[hf]], writes=[b_ssq, b_junk])
                P.op(dve, lambda e, nt=nt: e.tensor_tensor(out=ssq[0:nt, 2:3], in0=ssq[0:nt, 0:1], in1=ssq[0:nt, 1:2], op=ALU.add),
                     reads=[b_ssq], writes=[b_ssq])
                P.op(dve, lambda e, nt=nt: e.tensor_scalar(out=ssq[0:nt, 2:3], in0=ssq[0:nt, 2:3], scalar1=1.0 / D, scalar2=EPS,
                                                          op0=ALU.mult, op1=ALU.add), reads=[b_ssq], writes=[b_ssq])
                P.op(act, lambda e, nt=nt: e.activation(out=ssq[0:nt, 2:3], in_=ssq[0:nt, 2:3], func=AF.Sqrt), reads=[b_ssq], writes=[b_ssq])
                P.op(dve, lambda e, nt=nt: e.reciprocal(out=ssq[0:nt, 3:4], in_=ssq[0:nt, 2:3]), reads=[b_ssq], writes=[b_ssq])
                for hf in range(2):
                    P.op(dve, lambda e, hf=hf, nt=nt: e.tensor_scalar(out=xst[hf][0:nt, :], in0=xst[hf][0:nt, :], scalar1=ssq[0:nt, 3:4],
                                                                    scalar2=None, op0=ALU.mult), reads=[b_ssq, b_xst[hf]], writes=[b_xst[hf]])
                    for f4 in range(0, FC // 2, 4):
                        nf = min(4, FC // 2 - f4)
                        i = acc_get()
                        P.op(pe, [lambda e, hf=hf, f=f, f4=f4, nt=nt, i=i: e.transpose(banks[i][:, (f - f4) * 128:(f - f4) * 128 + nt],
                                                                                         xst[hf][0:nt, f * 128:(f + 1) * 128], identf[0:nt, 0:nt])
                                  for f in range(f4, f4 + nf)], reads=[b_xst[hf]], writes=[b_bank[i]])
                        for f in range(f4, f4 + nf):
                            fcg = hf * (FC // 2) + f
                            P.op(dve, lambda e, f=f, f4=f4, fcg=fcg, nt=nt, c0=c0, i=i: e.tensor_scalar(
                                out=h_t[:, fcg, c0:c0 + nt], in0=banks[i][:, (f - f4) * 128:(f - f4) * 128 + nt],
                                scalar1=norms_t[:, 0, fcg:fcg + 1], scalar2=None, op0=ALU.mult),
                                reads=[b_bank[i]], writes=[b_h[fcg]])
            P.barrier()
            chk("p0")

            for grp in range(cfg.NG):
                if last_round:
                    for q in range(2):
                        P.dma(sp, S0s[:, q, :, :], sh[q, grp * G:(grp + 1) * G, :, :].rearrange("h k v -> k h v"), bt["S0s"], writes=[bt["S0s"]])
                for hg in range(G):
                    hd = grp * G + hg
                    iq = acc_get(); proj("w_hin", (hd * 4 + 0) * 2, N, b_h, hrd, iq)
                    if_ = acc_get(); proj("w_hin", (hd * 4 + 1) * 2, N, b_h, hrd, if_)
                    ii = acc_get(); proj("w_hin", (hd * 4 + 2) * 2, N, b_h, hrd, ii)
                    ig = acc_get(); proj("w_hin", (hd * 4 + 3) * 2, N, b_h, hrd, ig)
                    for (c0, n) in segs(N):
                        P.op(act, lambda e, c0=c0, n=n: e.activation(out=tq[:, c0:c0 + n], in_=acc_seg(iq, c0, n), func=AF.Silu),
                             reads=acc_bufs(iq, N), writes=[bt["tq"]])
                        P.op(act, lambda e, c0=c0, n=n: e.activation(out=gs_sg[:, hg, c0:c0 + n], in_=acc_seg(ig, c0, n), func=AF.Silu),
                             reads=acc_bufs(ig, N), writes=[b_gs[hg]])
                        P.op(act, lambda e, c0=c0, n=n: e.activation(out=tsig[:, c0:c0 + n], in_=acc_seg(if_, c0, n), func=AF.Sigmoid),
                             reads=acc_bufs(if_, N), writes=[bt["tsig"]])
                        P.op(dve, lambda e, c0=c0, n=n: e.tensor_copy(out=tv[:, c0:c0 + n], in_=acc_seg(ii, c0, n)),
                             reads=acc_bufs(ii, N), writes=[bt["tv"]])
                    chk("h_proj")
                    P.op(act, lambda e: e.activation(out=tg[:, 0:N], in_=tsig[:, 0:N], func=AF.Ln, scale=oml_t[:, hd:hd + 1], bias=lb_t[:, hd:hd + 1]),
                         reads=[bt["tsig"]], writes=[bt["tg"]])
                    P.op(dve, lambda e: e.tensor_scalar(out=tk[:, 0:N], in0=tsig[:, 0:N], scalar1=noml_t[:, hd:hd + 1], scalar2=oml_t[:, hd:hd + 1],
                                                        op0=ALU.mult, op1=ALU.add), reads=[bt["tsig"]], writes=[bt["tk"]])
                    P.op(dve, lambda e: e.tensor_tensor_scan(out=tb[:, 0:N], data0=smask[:, 0:N], data1=tg[:, 0:N], initial=0.0,
                                                             op0=ALU.mult, op1=ALU.add), reads=[bt["tg"]], writes=[bt["tb"]])
                    P.op(act, lambda e: e.activation(out=teb[:, 0:N], in_=tb[:, 0:N], func=AF.Exp), reads=[bt["tb"]], writes=[bt["teb"]])
                    P.op(act, lambda e: e.activation(out=tenb[:, 0:N], in_=tb[:, 0:N], func=AF.Exp, scale=-1.0), reads=[bt["tb"]], writes=[bt["tenb"]])
                    P.op(dve, lambda e: e.tensor_tensor(out=tqt[:, 0:N], in0=tq[:, 0:N], in1=teb[:, 0:N], op=ALU.mult),
                         reads=[bt["tq"], bt["teb"]], writes=[bt["tqt"]])
                    P.op(dve, lambda e: e.tensor_tensor(out=tkt[:, 0:N], in0=tk[:, 0:N], in1=tenb[:, 0:N], op=ALU.mult),
                         reads=[bt["tk"], bt["tenb"]], writes=[bt["tkt"]])
                    chk("h_act")
                    bl = tb[:, 0:T].rearrange("p (c t) -> p c t", t=64)[:, :, 63]
                    P.op(act, lambda e: e.activation(out=td[:, 0:NCH], in_=bl, func=AF.Exp), reads=[bt["tb"]], writes=[bt["td"]])
                    if last_round:
                        bls = tb[:, T:NX].rearrange("p (c t) -> p c t", t=16)[:, :, 15]
                        P.op(act, lambda e: e.activation(out=td[:, NCH:NCH + 2], in_=bls, func=AF.Exp), reads=[bt["tb"]], writes=[bt["td"]])
                    P.op(dve, lambda e: e.tensor_tensor_scan(out=tci, data0=ones_f[:, 0:NCH], data1=bl, initial=0.0, op0=ALU.mult, op1=ALU.add),
                         reads=[bt["tb"]], writes=[bt["tci"]])
                    P.op(act, lambda e: e.activation(out=tE, in_=tci, func=AF.Exp), reads=[bt["tci"]], writes=[bt["tE"]])
                    P.op(dve, lambda e: e.tensor_copy(out=gs_qc[:, hg, 0:64], in_=tqt[:, 0:64]), reads=[bt["tqt"]], writes=[b_gs[hg]])
                    if NCH > 1:
                        P.op(dve, lambda e: e.tensor_tensor(out=gs_qc[:, hg, 64:T].rearrange("p (c t) -> p c t", t=64),
                                                            in0=tqt[:, 64:T].rearrange("p (c t) -> p c t", t=64),
                                                            in1=tE[:, 0:NCH - 1].unsqueeze(2).to_broadcast([128, NCH - 1, 64]), op=ALU.mult),
                             reads=[bt["tqt"], bt["tE"]], writes=[b_gs[hg]])
                    chk("h_dec")
                    P.op(pe, [lambda e, pb=pb: e.transpose(psTk[:, pb, :], tkt[:, pb * 128:(pb + 1) * 128], identb) for pb in range(NPB)]
                         + [lambda e, pb=pb: e.transpose(psTv[:, pb, :], tv[:, pb * 128:(pb + 1) * 128], identb) for pb in range(NPB)],
                         reads=[bt["tkt"], bt["tv"]], writes=[b_bank[6]])
                    chk("h_tr0")
                    P.op(act, lambda e: e.activation(out=ktok, in_=psTk, func=AF.Copy), reads=[b_bank[6]], writes=[bt["ktok"]])
                    chk("h_tr1")
                    P.op(act, lambda e: e.activation(out=vtok, in_=psTv, func=AF.Copy), reads=[b_bank[6]], writes=[bt["vtok"]])
                    if last_round:
                        P.op(pe, [lambda e: e.transpose(psTks[0:32, :], tkt[:, T:NX], identb), lambda e: e.transpose(psTvs[0:32, :], tv[:, T:NX], identb)],
                             reads=[bt["tkt"], bt["tv"]], writes=[b_bank[6]])
                        P.op(act, lambda e: e.activation(out=vtoks, in_=psTvs[0:32, :], func=AF.Copy), reads=[b_bank[6]], writes=[bt["kts"]])
                        P.op(act, lambda e: e.activation(out=ktoks, in_=psTks[0:32, :], func=AF.Copy), reads=[b_bank[6]], writes=[bt["kts"]])
                        for q in range(2):
                            P.op(dve, lambda e, q=q: e.tensor_scalar(out=kt01[:, q, :], in0=ktoks, scalar1=sm01[:, q:q + 1], scalar2=None, op0=ALU.mult),
                                 reads=[bt["kts"]], writes=[bt["kts"]])
                    chk("h_tr")
                    for j in range(2):
                        P.op(dve, lambda e, j=j: e.tensor_scalar(out=ktokm[:, :, j, :], in0=ktok, scalar1=pm64[:, j:j + 1], scalar2=None, op0=ALU.mult),
                             reads=[bt["ktok"]], writes=[bt["ktokm"]])
                    iU = []
                    for half in range(0, NCH, 4):
                        i = acc_get()
                        iU.append(i)
                        P.op(pe, [lambda e, c=c, i=i, half=half: e.matmul(banks[i][:, (c - half) * 128:(c - half + 1) * 128],
                                                                           ktokm[:, c // 2, c % 2, :], vtok[:, c // 2, :], start=True, stop=True)
                                  for c in range(half, min(NCH, half + 4))], reads=[bt["ktokm"], bt["vtok"]], writes=[b_bank[i]])
                    if last_round:
                        iUs = acc_get()
                        P.op(pe, [lambda e, q=q: e.matmul(banks[iUs][:, q * 128:(q + 1) * 128], kt01[:, q, :], vtoks, start=True, stop=True) for q in range(2)],
                             reads=[bt["kts"]], writes=[b_bank[iUs]])
                    chk("h_U")
                    Ubank = lambda c: banks[iU[c // 4]][:, (c % 4) * 128:(c % 4 + 1) * 128]
                    bU = lambda c: b_bank[iU[c // 4]]
                    cur = 0
                    for c in range(NCH):
                        fin = (c == NCH - 1)
                        dst = LD[:, hg, 0:128] if fin else Mf[1 - cur]
                        dbuf = bt["LD"] if fin else bt[f"M{1 - cur}"]
                        if c == 0:
                            P.op(dve, lambda e, dst=dst: e.tensor_scalar(out=dst, in0=Ubank(0), scalar1=td[:, 0:1], scalar2=None, op0=ALU.mult),
                                 reads=[bU(0), bt["td"]], writes=[dbuf])
                        else:
                            P.op(dve, lambda e, c=c, cur=cur: e.tensor_scalar(out=Mtmp, in0=Mf[cur], scalar1=td[:, c:c + 1], scalar2=None, op0=ALU.mult),
                                 reads=[bt[f"M{cur}"], bt["td"]], writes=[bt["Mtmp"]])
                            P.op(dve, lambda e, c=c, dst=dst: e.scalar_tensor_tensor(out=dst, in0=Ubank(c), scalar=td[:, c:c + 1], in1=Mtmp,
                                                                                   op0=ALU.mult, op1=ALU.add),
                                 reads=[bU(c), bt["td"], bt["Mtmp"]], writes=[dbuf])
                        if not fin:
                            P.op(act, lambda e, c=c, cur=cur: e.activation(out=Mb[:, c + 1, :], in_=Mf[1 - cur], func=AF.Copy),
                                 reads=[bt[f"M{1 - cur}"]], writes=[bt["Mb"]])
                        cur = 1 - cur
                    P.op(act, lambda e: e.activation(out=LD[:, hg, 128:129], in_=tE[:, NCH - 1:NCH], func=AF.Copy), reads=[bt["tE"]], writes=[bt["LD"]])
                    if last_round:
                        for q in range(2):
                            P.op(act, lambda e, q=q: e.activation(out=Mbs[:, q, :], in_=S0s[:, q, hg, :], func=AF.Copy), reads=[bt["S0s"]], writes=[bt["Mbs"]])
                            P.op(dve, lambda e, q=q: e.tensor_scalar(out=Mtmp, in0=S0s[:, q, hg, :], scalar1=td[:, NCH + q:NCH + q + 1], scalar2=None, op0=ALU.mult),
                                 reads=[bt["S0s"], bt["td"]], writes=[bt["Mtmp"]])
                            P.op(dve, lambda e, q=q: e.scalar_tensor_tensor(out=NSs[:, q, hg, :], in0=banks[iUs][:, q * 128:(q + 1) * 128],
                                                                          scalar=td[:, NCH + q:NCH + q + 1], in1=Mtmp, op0=ALU.mult, op1=ALU.add),
                                 reads=[b_bank[iUs], bt["td"], bt["Mtmp"]], writes=[bt["NSs"]])
                    chk("h_chain")
                    iA = acc_get()
                    P.op(pe, [lambda e, pb=pb: e.matmul(banks[iA][:, pb * 128:(pb + 1) * 128], tkt[:, pb * 128:(pb + 1) * 128],
                                                        tqt[:, pb * 128:(pb + 1) * 128], start=True, stop=True) for pb in range(NPB)]
                         + ([lambda e: e.matmul(psATs[0:32, :], tkt[:, T:NX], tqt[:, T:NX], start=True, stop=True)] if last_round else []),
                         reads=[bt["tkt"], bt["tqt"]], writes=[b_bank[iA]] + ([b_bank[7]] if last_round else []))
                    P.op(dve, lambda e: e.tensor_tensor(out=atm, in0=banks[iA][:, 0:NPB * 128].rearrange("p (a b) -> p a b", b=128),
                                                        in1=maskA.unsqueeze(1).to_broadcast([128, NPB, 128]), op=ALU.mult),
                         reads=[b_bank[iA]], writes=[bt["atm"]])
                    if last_round:
                        P.op(dve, lambda e: e.tensor_tensor(out=atms, in0=psATs[0:32, :], in1=maskS, op=ALU.mult),
                             reads=[b_bank[7]], writes=[bt["atm"]])
                    chk("h_A")
                    iO = acc_get()
                    fl = []
                    for c in range(NCH):
                        pb, j = c // 2, c % 2
                        fl.append(lambda e, c=c, pb=pb, j=j: e.matmul(banks[iO][:, c * 64:(c + 1) * 64], vtok[:, pb, :], atm[:, pb, j * 64:(j + 1) * 64],
                                                                     start=True, stop=False))
                        fl.append(lambda e, c=c: e.matmul(banks[iO][:, c * 64:(c + 1) * 64], (zerob if c == 0 else Mb[:, c, :]), tqt[:, c * 64:(c + 1) * 64],
                                                         start=False, stop=True))
                    if last_round:
                        for q in range(2):
                            fl.append(lambda e, q=q: e.matmul(acc_seg(iO, T + 16 * q, 16), vtoks, atms[:, q * 16:(q + 1) * 16], start=True, stop=False))
                            fl.append(lambda e, q=q: e.matmul(acc_seg(iO, T + 16 * q, 16), Mbs[:, q, :], tqt[:, T + 16 * q:T + 16 * q + 16], start=False, stop=True))
                    P.op(pe, fl, reads=[bt["vtok"], bt["atm"], bt["Mb"], bt["tqt"], bt["kts"], bt["Mbs"]], writes=acc_bufs(iO, N))
                    for (c0, n) in segs(N):
                        P.op(act, lambda e, c0=c0, n=n: e.activation(out=gs_ol[:, hg, c0:c0 + n], in_=acc_seg(iO, c0, n), func=AF.Copy),
                             reads=acc_bufs(iO, N), writes=[b_gs[hg]])
                chk("p1h")
                if last_round:
                    for q in range(2):
                        P.dma(sp, hs[q, grp * G:(grp + 1) * G, :, :].rearrange("h k v -> k h v"), NSs[:, q, :, :], bt["NSs"], reads=[bt["NSs"]], writes=[bt["NSs"]])
                    out_bufs.append(bt["NSs"])
                P.dma(sp, agh_in.rearrange("(g k) c -> k g c", k=128), LD, b_agh_in, reads=[bt["LD"]], writes=[b_agh_in])
                allgather(agh_in, agh_mid, agh, [b_agh_in], b_agh, b_agh_mid)
                if r == 0:
                    P.op(dve, lambda e: e.memset(S_t, 0.0), writes=[bt["S"]])
                else:
                    P.dma(sp, S_t, sround[:, grp * G:(grp + 1) * G, :], bt["S"], reads=[b_sr[grp]], writes=[bt["S"]])
                P.op(dve, lambda e: e.memset(Sin, 0.0), writes=[bt["Sin"]])
                for j in range(NCORES):
                    jj = j % 2
                    P.dma(sp, LDj[jj], agh[j * G * 128:(j + 1) * G * 128, :].rearrange("(g k) c -> k g c", k=128), b_ld[jj], reads=[b_agh], writes=[b_ld[jj]])
                    P.op(dve, lambda e, j=j: e.scalar_tensor_tensor(out=Sin, in0=S_t, scalar=rm_t[:, j:j + 1], in1=Sin, op0=ALU.mult, op1=ALU.add),
                         reads=[bt["S"], bt["Sin"]], writes=[bt["Sin"]])
                    P.op(dve, lambda e, jj=jj: e.tensor_tensor(out=Stmp, in0=S_t, in1=LDj[jj][:, :, 128:129].to_broadcast([128, G, 128]), op=ALU.mult),
                         reads=[bt["S"], b_ld[jj]], writes=[bt["Stmp"]])
                    P.op(dve, lambda e, jj=jj: e.tensor_tensor(out=S_t, in0=Stmp, in1=LDj[jj][:, :, 0:128], op=ALU.add),
                         reads=[bt["Stmp"], b_ld[jj]], writes=[bt["S"]])
                if last_round:
                    P.dma(sp, hp[grp * G:(grp + 1) * G, :, :].rearrange("h k v -> k h v"), S_t, b_sr[grp], reads=[bt["S"]], writes=[b_sr[grp]])
                    out_bufs.append(b_sr[grp])
                else:
                    P.dma(sp, sround[:, grp * G:(grp + 1) * G, :], S_t, b_sr[grp], reads=[bt["S"]], writes=[b_sr[grp]])
                P.op(act, lambda e: e.activation(out=M0b, in_=Sin, func=AF.Copy), reads=[bt["Sin"]], writes=[bt["M0b"]])
                for hg in range(G):
                    hd = grp * G + hg
                    iC = acc_get()
                    P.op(pe, lambda e: e.matmul(banks[iC][:, 0:T], M0b[:, hg, :], gs_qc[:, hg, 0:T], start=True, stop=True),
                         reads=[bt["M0b"], b_gs[hg]], writes=[b_bank[iC]])
                    P.op(dve, lambda e: e.tensor_tensor(out=to32[:, 0:T], in0=banks[iC][:, 0:T], in1=gs_ol[:, hg, 0:T], op=ALU.add),
                         reads=[b_bank[iC], b_gs[hg]], writes=[bt["to32"]])
                    if last_round:
                        P.op(dve, lambda e: e.tensor_copy(out=to32[:, T:NX], in_=gs_ol[:, hg, T:NX]), reads=[b_gs[hg]], writes=[bt["to32"]])
                    P.op(act, lambda e: e.activation(out=osq[:, 0:N], in_=to32[:, 0:N], func=AF.Square), reads=[bt["to32"]], writes=[bt["osq"]])
                    iS = acc_get()
                    mm_acc(iS, N, [(onesb, lambda c0, n: osq[:, c0:c0 + n])], [bt["osq"]], first=True, last=True)
                    for (c0, n) in segs(N):
                        P.op(dve, lambda e, c0=c0, n=n: e.tensor_scalar(out=rs2[:, c0:c0 + n], in0=acc_seg(iS, c0, n), scalar1=1.0 / 128, scalar2=EPS,
                                                                         op0=ALU.mult, op1=ALU.add), reads=acc_bufs(iS, N), writes=[bt["rs2"]])
                    rsqrt_inplace(rs2[:, 0:N], bt["rs2"])
                    P.op(dve, lambda e: e.scalar_tensor_tensor(out=t1[:, 0:N], in0=to32[:, 0:N], scalar=og_t[:, 0:1], in1=rs2[:, 0:N], op0=ALU.mult, op1=ALU.mult),
                         reads=[bt["to32"], bt["rs2"]], writes=[bt["t1"]])
                    P.op(dve, lambda e: e.tensor_tensor(out=aux[:, hd, 0:N], in0=t1[:, 0:N], in1=gs_sg[:, hg, 0:N], op=ALU.mult),
                         reads=[bt["t1"], b_gs[hg]], writes=[b_aux[hd]])

            chk("p1")
            P.barrier()
            for fc in range(FC):
                j = fc % 2
                xr = ostage[j][:, 0:(NPB + 1) * 128].rearrange("p (a b) -> p a b", b=128)
                for tbk in range(NPB):
                    P.dma(sp, xr[:, tbk, :], xp[r * T + tbk * 128:r * T + tbk * 128 + 128, fc * 128:(fc + 1) * 128], b_ost[j], writes=[b_ost[j]])
                if last_round:
                    P.dma(sp, xr[0:EX, NPB, :], xs[:, fc * 128:(fc + 1) * 128], b_ost[j], writes=[b_ost[j]])
                i = acc_get()
                for half in range(2):
                    wt, wb = wload("w_hout", fc * 2 + half, WC)
                    w3 = wt[:, 0:WC].rearrange("p (k c) -> p k c", c=128)
                    fl = []
                    for (c0, n) in segs(N):
                        for k in range(KH):
                            kc = half * KH + k
                            fl.append(lambda e, c0=c0, n=n, k=k, kc=kc, w3=w3, half=half: e.matmul(acc_seg(i, c0, n), w3[:, k, :], aux[:, kc, c0:c0 + n],
                                                                                                    start=(half == 0 and k == 0), stop=False))
                    rd = [wb] + b_aux[0:FC]
                    if half == 1:
                        for tbk in range(NPB):
                            fl.append(lambda e, tbk=tbk, xr=xr: e.matmul(banks[i][:, tbk * 128:(tbk + 1) * 128], xr[:, tbk, :], identf, start=False, stop=True))
                        if last_round:
                            fl.append(lambda e, xr=xr: e.matmul(acc_seg(i, T, EX), xr[0:EX, NPB, :], identf[0:EX, 0:EX], start=False, stop=True))
                        rd = rd + [b_ost[j]]
                    P.op(pe, fl, reads=rd, writes=acc_bufs(i, N))
                for (c0, n) in segs(N):
                    P.op(act, lambda e, c0=c0, n=n, fc=fc, i=i: e.activation(out=x_sb[:, fc, c0:c0 + n], in_=acc_seg(i, c0, n), func=AF.Copy),
                         reads=acc_bufs(i, N), writes=[b_x[fc]])
            chk("p2")
            ffn(0, N)
            chk("ffn0")
            fm_norm(N)
            fm_norm_apply(N, 2)
            cuh = aux
            offs = [(2, 0, T)] + ([(T + 4, T, 16), (T + 22, T + 16, 16)] if last_round else [])
            cst = ostage[0][:, 0:4 * FC].rearrange("p (q t f) -> p q t f", q=2, t=2)
            for fc in range(FC):
                igc = acc_get(); proj("w_cin", (fc * 2 + 0) * 2, N, b_h, hrd, igc)
                iu = acc_get(); proj("w_cin", (fc * 2 + 1) * 2, N, b_h, hrd, iu)
                j = fc % 2
                for (c0, n) in segs(N):
                    P.op(act, lambda e, c0=c0, n=n, j=j, igc=igc: e.activation(out=gtmp[j][:, c0:c0 + n], in_=acc_seg(igc, c0, n), func=AF.Copy),
                         reads=acc_bufs(igc, N), writes=[b_gtmp[j]])
                    P.op(dve, lambda e, c0=c0, n=n, j=j, iu=iu: e.tensor_tensor(out=gtmp[j][:, c0:c0 + n], in0=gtmp[j][:, c0:c0 + n], in1=acc_seg(iu, c0, n), op=ALU.mult),
                         reads=acc_bufs(iu, N) + [b_gtmp[j]], writes=[b_gtmp[j]])
                for (co, ct, n) in offs:
                    P.op(act, lambda e, co=co, ct=ct, n=n, fc=fc, j=j: e.activation(out=cuh[:, fc, co:co + n], in_=gtmp[j][:, ct:ct + n], func=AF.Copy),
                         reads=[b_gtmp[j]], writes=[b_aux[fc]])
                P.op(dve, lambda e, fc=fc, j=j: e.tensor_copy(out=hal_t[:, :, fc], in_=gtmp[j][:, T - 2:T]), reads=[b_gtmp[j]], writes=[b_hal])
                if last_round:
                    for q in range(2):
                        P.op(dve, lambda e, fc=fc, j=j, q=q: e.tensor_copy(out=cst[:, q, :, fc], in_=gtmp[j][:, T + 16 * q + 14:T + 16 * q + 16]),
                             reads=[b_gtmp[j]], writes=[b_ost[0]])
            hal2 = hal_t.rearrange("p a b -> p (a b)")
            P.dma(sp, agc_in, hal2, b_agc_in, reads=[b_hal], writes=[b_agc_in])
            allgather(agc_in, agc_mid, agc, [b_agc_in], b_agc, b_agc_mid)
            P.dma(sp, halg, agc.rearrange("(r p) c -> p r c", p=128), b_halg, reads=[b_agc], writes=[b_halg])
            halo2 = halo.rearrange("p a b -> p (a b)")
            P.op(dve, lambda e: e.tensor_scalar(out=halo2, in0=prev7, scalar1=rm_t[:, 16:17], scalar2=None, op0=ALU.mult),
                 reads=[b_prev7, b_halg], writes=[b_halo])
            for j in range(NCORES):
                P.op(dve, lambda e, j=j: e.scalar_tensor_tensor(out=halo2, in0=halg[:, j, :], scalar=rm_t[:, 8 + j:9 + j], in1=halo2, op0=ALU.mult, op1=ALU.add),
                     reads=[b_halg, b_halo], writes=[b_halo])
            P.op(dve, lambda e: e.tensor_copy(out=prev7, in_=halg[:, NCORES - 1, :]), reads=[b_halg, b_halo], writes=[b_prev7])
            iT = acc_get()
            P.op(pe, lambda e: e.transpose(banks[iT][0:FC * 2, 0:128], hal2, identf), reads=[b_hal], writes=[b_bank[iT]])
            P.op(act, lambda e: e.activation(out=ostage[1][0:FC * 2, 0:128], in_=banks[iT][0:FC * 2, 0:128], func=AF.Copy), reads=[b_bank[iT]], writes=[b_ost[1]])
            P.dma(sp, cp[r, :, :].rearrange("t (f p) -> (t f) p", p=128), ostage[1][0:FC * 2, 0:128], b_ost[1], reads=[b_ost[1]], writes=[b_ost[1]])
            if last_round:
                for q in range(2):
                    iT = acc_get()
                    P.op(pe, lambda e, q=q, iT=iT: e.transpose(banks[iT][0:FC * 2, 0:128], ostage[0][:, q * FC * 2:(q + 1) * FC * 2], identf),
                         reads=[b_ost[0]], writes=[b_bank[iT]])
                    P.op(act, lambda e, q=q, iT=iT: e.activation(out=ostage[1][0:FC * 2, 128 * (q + 1):128 * (q + 2)], in_=banks[iT][0:FC * 2, 0:128], func=AF.Copy),
                         reads=[b_bank[iT]], writes=[b_ost[1]])
                    P.dma(sp, cs[q, :, :].rearrange("t (f p) -> (t f) p", p=128), ostage[1][0:FC * 2, 128 * (q + 1):128 * (q + 2)], b_ost[1],
                          reads=[b_ost[1]], writes=[b_ost[1]])
            out_bufs.append(b_ost[1])
            for fc in range(FC):
                P.op(act, lambda e, fc=fc: e.activation(out=cuh[:, fc, 0:2], in_=halo[:, :, fc], func=AF.Copy), reads=[b_halo], writes=[b_aux[fc]])
            if last_round:
                P.dma(sp, ostage[0][0:4 * FC, 0:128], sc.rearrange("q t (f p) -> (q t f) p", p=128), b_ost[0], writes=[b_ost[0]])
                iT = acc_get()
                P.op(pe, lambda e: e.transpose(banks[iT][:, 0:4 * FC], ostage[0][0:4 * FC, 0:128], identf[0:4 * FC, 0:4 * FC]), reads=[b_ost[0]], writes=[b_bank[iT]])
                scv = banks[iT][:, 0:4 * FC].rearrange("p (q t f) -> p q t f", q=2, t=2)
                for q in range(2):
                    for t_ in range(2):
                        co = (T + 2 if q == 0 else T + 20) + t_
                        P.op(act, lambda e, q=q, t_=t_, co=co: e.activation(out=cuh[:, :, co], in_=scv[:, q, t_, :], func=AF.Copy),
                             reads=[b_bank[iT]], writes=b_aux[0:FC])
            for fc in range(FC):
                igb = acc_get(); proj("w_cin", FC * 4 + fc * 2, N, b_h, hrd, igb)
                j = fc % 2
                for (co, ct, n) in offs:
                    P.op(dve, lambda e, co=co, ct=ct, n=n, fc=fc, j=j: e.tensor_scalar(out=gtmp[j][:, ct:ct + n], in0=cuh[:, fc, co:co + n],
                                                                                   scalar1=cw_t[:, 2, fc:fc + 1], scalar2=None, op0=ALU.mult),
                         reads=[b_aux[fc]], writes=[b_gtmp[j]])
                    P.op(dve, lambda e, co=co, ct=ct, n=n, fc=fc, j=j: e.scalar_tensor_tensor(out=gtmp[j][:, ct:ct + n], in0=cuh[:, fc, co - 1:co - 1 + n],
                                                                                          scalar=cw_t[:, 1, fc:fc + 1], in1=gtmp[j][:, ct:ct + n], op0=ALU.mult, op1=ALU.add),
                         reads=[b_aux[fc], b_gtmp[j]], writes=[b_gtmp[j]])
                    P.op(dve, lambda e, co=co, ct=ct, n=n, fc=fc, j=j: e.scalar_tensor_tensor(out=gtmp[j][:, ct:ct + n], in0=cuh[:, fc, co - 2:co - 2 + n],
                                                                                          scalar=cw_t[:, 0, fc:fc + 1], in1=gtmp[j][:, ct:ct + n], op0=ALU.mult, op1=ALU.add),
                         reads=[b_aux[fc], b_gtmp[j]], writes=[b_gtmp[j]])
                for (c0, n) in segs(N):
                    P.op(dve, lambda e, c0=c0, n=n, fc=fc, j=j, igb=igb: e.tensor_tensor(out=cuh[:, fc, c0:c0 + n], in0=gtmp[j][:, c0:c0 + n], in1=acc_seg(igb, c0, n), op=ALU.mult),
                         reads=acc_bufs(igb, N) + [b_gtmp[j], b_aux[fc]], writes=[b_aux[fc]])
            for fc in range(FC):
                i = acc_get()
                proj("w_cout", fc * 2, N, b_aux[0:FC], lambda kc, c0, n: cuh[:, kc, c0:c0 + n], i)
                for (c0, n) in segs(N):
                    P.op(dve, lambda e, c0=c0, n=n, fc=fc, i=i: e.tensor_tensor(out=x_sb[:, fc, c0:c0 + n], in0=x_sb[:, fc, c0:c0 + n], in1=acc_seg(i, c0, n), op=ALU.add),
                         reads=acc_bufs(i, N) + [b_x[fc]], writes=[b_x[fc]])
            chk("conv")
            ffn(1, N)
            chk("ffn1")
            fm_norm(N)
            out_blocks = [(yp[r * T + tbk * 128:r * T + tbk * 128 + 128, :], 128, tbk * 128) for tbk in range(NPB)]
            if last_round:
                out_blocks.append((ys[0:EX, :], EX, T))
            FB = 1024 // 128
            oc_ = 0
            for (dst, nt, c0) in out_blocks:
                for f8 in range(0, FC, FB):
                    nf8 = min(FB, FC - f8)
                    so = oc_ % 2
                    oc_ += 1
                    for f4 in range(f8, f8 + nf8, 4):
                        nf = min(4, f8 + nf8 - f4)
                        i = acc_get()
                        for f in range(f4, f4 + nf):
                            j = f % 2
                            P.op(dve, lambda e, f=f, j=j, c0=c0, nt=nt: e.scalar_tensor_tensor(out=gtmp[j][:, 0:nt], in0=x_sb[:, f, c0:c0 + nt], scalar=norms_t[:, 4, f:f + 1],
                                                                                             in1=rstd_t[:, c0:c0 + nt], op0=ALU.mult, op1=ALU.mult),
                                 reads=[b_x[f], b_rstd], writes=[b_gtmp[j]])
                            P.op(pe, lambda e, f=f, f4=f4, j=j, nt=nt, i=i: e.transpose(banks[i][0:nt, (f - f4) * 128:(f - f4 + 1) * 128], gtmp[j][:, 0:nt], identf),
                                 reads=[b_gtmp[j]], writes=[b_bank[i]])
                        P.op(act, lambda e, f4=f4, f8=f8, nf=nf, nt=nt, i=i, so=so: e.activation(out=ostage[so][0:nt, (f4 - f8) * 128:(f4 - f8 + nf) * 128],
                                                                                                in_=banks[i][0:nt, 0:nf * 128], func=AF.Copy),
                             reads=[b_bank[i]], writes=[b_ost[so]])
                    P.dma(sp, dst[:, f8 * 128:(f8 + nf8) * 128], ostage[so][0:nt, 0:nf8 * 128], b_ost[so], reads=[b_ost[so]], writes=[b_ost[so]])
            out_bufs.extend(b_ost)

        try:
            if not _go:
                raise _Stop()
            chk("wag")
            for r in range(R):
                round_body(r)
        except _Stop:
            pass
        P.barrier()
        seen = set()
        for bf in out_bufs:
            sm = bf.dsem
            if sm is not None and id(sm) not in seen and sm.v:
                seen.add(id(sm))
                nc.sync.wait_ge(sm.h, sm.v)
        print("instructions:", P.nins, "sems:", P.nsem)
    return nc


def make_in_maps(cfg, inp):
    D, NH, FC, T, R = cfg.D, cfg.NH, cfg.FC, cfg.T, cfg.ROUNDS
    W = prep_weights(cfg, inp)
    lbl = np.ascontiguousarray(inp["hgrn_lb_logits"].reshape(3, NH, 128).transpose(2, 0, 1)).astype(np.float32)
    nv = np.stack([inp["norm_mix"][0], inp["norm_ffn"][0], inp["norm_mix"][1], inp["norm_ffn"][1], inp["norm_final"]])
    norms = np.ascontiguousarray(nv.reshape(5, FC, 128).transpose(2, 0, 1)).astype(np.float32)
    cw = np.ascontiguousarray(inp["conv_w"][0].reshape(3, FC, 128).transpose(2, 0, 1)).astype(np.float32)
    ogain = np.ascontiguousarray(inp["hgrn_out_gain"][0].reshape(128, 1)).astype(np.float32)
    xp_full = inp["x_prompt"][0]
    consts = make_consts(cfg)
    maps = []
    for c in range(NCORES):
        m = {}
        m["xp"] = np.ascontiguousarray(np.concatenate([xp_full[(r * NCORES + c) * T:(r * NCORES + c + 1) * T] for r in range(R)], axis=0))
        m["xs"] = np.ascontiguousarray(inp["x_sample"][cfg.SPC * c:cfg.SPC * (c + 1)].reshape(cfg.EX, D))
        m["sh"] = np.ascontiguousarray(inp["state_hgrn"][0, cfg.SPC * c:cfg.SPC * (c + 1)])
        m["sc"] = np.ascontiguousarray(inp["state_conv"][0, cfg.SPC * c:cfg.SPC * (c + 1)])
        m["lbl"] = lbl; m["ogain"] = ogain; m["norms"] = norms; m["cw"] = cw
        m["consts"] = consts
        rm = np.zeros((128, 17), np.float32)
        rm[:, c] = 1.0
        if c >= 1:
            rm[:, 8 + c - 1] = 1.0
        else:
            rm[:, 16] = 1.0
        m["rmask"] = rm
        for n in WNAMES:
            PT = PT_OF(n)
            RPR = PT * 16
            for pi, (t0, nt) in enumerate(wpieces(cfg, n)):
                ch = W[n][t0:t0 + nt].reshape(nt // PT, PT * 128, -1)
                m[f"{n}_{pi}"] = np.ascontiguousarray(ch[:, c * RPR:(c + 1) * RPR, :].reshape(nt // PT * RPR, -1))
        maps.append(m)
    return maps


def assemble(cfg, res):
    D, NH, T, R = cfg.D, cfg.NH, cfg.T, cfg.ROUNDS
    yp = np.zeros((1, cfg.SEQ, D), np.float32)
    for c in range(NCORES):
        for r in range(R):
            j = r * NCORES + c
            yp[0, j * T:(j + 1) * T] = res[c]["yp"][r * T:(r + 1) * T]
    ys = np.concatenate([res[c]["ys"].reshape(cfg.SPC, cfg.DS, D) for c in range(NCORES)], axis=0)
    hp = res[0]["hp"].reshape(1, 1, NH, 128, 128)
    hs = np.concatenate([res[c]["hs"] for c in range(NCORES)], axis=0).reshape(1, cfg.DB, NH, 128, 128)
    cp = res[NCORES - 1]["cp"][R - 1].reshape(1, 1, 2, D)
    cs = np.concatenate([res[c]["cs"] for c in range(NCORES)], axis=0).reshape(1, cfg.DB, 2, D)
    return (yp, ys, hp, hs, cp, cs)


_NC_CACHE = {}


def run(cfg, inp):
    key = (cfg.D, cfg.NH, cfg.DFF, cfg.SEQ, cfg.T)
    if key not in _NC_CACHE:
        _NC_CACHE[key] = build(cfg)
    nc = _NC_CACHE[key]
    maps = make_in_maps(cfg, inp)
    res = run_bass_kernel_spmd(nc, maps, core_ids=list(range(NCORES)))
    return assemble(cfg, res.results)


def kernel(**inputs):
    inp = {k: np.asarray(v) for k, v in inputs.items()}
    return run(FULL, inp)
```

```python
import os
import numpy as np
import concourse.bass as bass
import concourse.mybir as mybir
from concourse.bass_utils import run_bass_kernel_spmd
from contextlib import ExitStack

F32 = mybir.dt.float32
BF16 = mybir.dt.bfloat16
AF = mybir.ActivationFunctionType
ALU = mybir.AluOpType
NCORES = 8
EPS = 1e-6


class _Stop(Exception):
    pass


class Cfg:
    stop = None

    def __init__(self, D=4096, NH=32, DFF=11008, SEQ=16384, T=512, DB=16, DS=16, G=4, NQ=4):
        self.D, self.NH, self.DFF, self.SEQ, self.T, self.DB, self.DS, self.G = D, NH, DFF, SEQ, T, DB, DS, G
        self.FC = D // 128
        self.KH = self.FC // 2
        self.NS = DFF // 128
        self.ROUNDS = SEQ // (NCORES * T)
        self.NPB = T // 128
        self.NCH = T // 64
        self.SPC = DB // NCORES
        self.EX = self.SPC * DS
        self.NX = T + self.EX
        self.NQ = NQ
        base = self.NS // NQ
        rem = self.NS % NQ
        self.QS = [base + (1 if i < rem else 0) for i in range(NQ)]
        self.QMAX = max(self.QS)
        self.NG = NH // G
        assert DS == 16 and self.SPC == 2 and D == NH * 128


FULL = Cfg()


def tile_w(W, kh):
    K, NO = W.shape
    kc = K // 128
    nh = kc // kh
    t = W.reshape(nh, kh, 128, NO // 128, 128)
    t = t.transpose(3, 0, 2, 1, 4)
    return np.ascontiguousarray(t).reshape(NO // 128 * nh, 128, kh * 128)


def prep_weights(cfg, inp):
    D, NH, NS = cfg.D, cfg.NH, cfg.NS
    out = {}
    W = inp["hgrn_w_in"][0]
    cols = np.concatenate([np.arange(s * D + h * 128, s * D + h * 128 + 128) for h in range(NH) for s in range(4)])
    out["w_hin"] = tile_w(W[:, cols], cfg.KH)
    out["w_hout"] = tile_w(inp["hgrn_w_out"][0], cfg.KH)
    W = inp["conv_w_in"][0]
    cols = np.concatenate([np.arange(s * D + f * 128, s * D + f * 128 + 128) for f in range(cfg.FC) for s in (1, 2)]
                          + [np.arange(0, D)])
    out["w_cin"] = tile_w(W[:, cols], cfg.KH)
    out["w_cout"] = tile_w(inp["conv_w_out"][0], cfg.KH)
    for l in range(2):
        W = inp["ffn_w_in"][l]
        cols = np.concatenate([np.arange(s * cfg.DFF + sl * 128, s * cfg.DFF + sl * 128 + 128)
                               for sl in range(NS) for s in range(2)])
        out[f"w_fin{l}"] = tile_w(W[:, cols], cfg.KH)
        W = inp["ffn_w_out"][l]
        tl = np.zeros((cfg.NQ, cfg.FC, 128, cfg.QMAX, 128), np.float32)
        s0 = 0
        for q in range(cfg.NQ):
            n = cfg.QS[q]
            blk = W[s0 * 128:(s0 + n) * 128, :].reshape(n, 128, cfg.FC, 128)
            tl[q, :, :, :n, :] = blk.transpose(2, 1, 0, 3)
            s0 += n
        out[f"w_fout{l}"] = tl.reshape(cfg.NQ * cfg.FC, 128, cfg.QMAX * 128)
    return out


WNAMES = ["w_hin", "w_hout", "w_fin0", "w_fout0", "w_cin", "w_cout", "w_fin1", "w_fout1"]


def wshape(cfg, name):
    D, NH, NS, FC, KH = cfg.D, cfg.NH, cfg.NS, cfg.FC, cfg.KH
    c = KH * 128
    if name == "w_hin":
        return (NH * 4 * 2, c)
    if name in ("w_hout", "w_cout"):
        return (FC * 2, c)
    if name == "w_cin":
        return (FC * 3 * 2, c)
    if name.startswith("w_fin"):
        return (NS * 2 * 2, c)
    return (cfg.NQ * FC, cfg.QMAX * 128)


def PT_OF(name):
    return 4 if name.startswith("w_fout") else 8


def wpieces(cfg, name, pmax=int(os.environ.get("PMAX", "128"))):
    nt, c = wshape(cfg, name)
    out = []
    t0 = 0
    while t0 < nt:
        n = min(pmax, nt - t0)
        assert n % NCORES == 0
        out.append((t0, n))
        t0 += n
    return out


class Sem:
    def __init__(self, h):
        self.h = h
        self.v = 0


class Buf:
    __slots__ = ("name", "w", "r", "dsem")

    def __init__(self, name=""):
        self.name = name
        self.w = None
        self.r = []
        self.dsem = None


class Eng:
    def __init__(self, name, handle, sem, skip_self=False):
        self.name, self.e, self.sem, self.skip_self = name, handle, sem, skip_self
        self.seen = {}


class Prog:
    def __init__(self, nc, es):
        self.nc = nc
        self.es = es
        self.nsem = 0
        mk = self.new_sem
        self.pe = Eng("pe", nc.tensor, mk("pe"), skip_self=True)
        self.act = Eng("act", nc.scalar, mk("act"))
        self.dve = Eng("dve", nc.vector, mk("dve"))
        self.pool = Eng("pool", nc.gpsimd, mk("pool"))
        self.sp = Eng("sp", nc.sync, mk("sp"))
        self.engs = [self.pe, self.act, self.dve, self.pool, self.sp]
        self.nins = 0

    def new_sem(self, name):
        self.nsem += 1
        sm = Sem(self.es.enter_context(self.nc.semaphore(f"s_{name}_{self.nsem}")))
        if not hasattr(self, "sems"):
            self.sems = []
        self.sems.append(sm)
        return sm

    def _waits(self, eng, reads, writes):
        need = {}

        def req(ev):
            if ev is None:
                return
            s, v = ev
            if eng.skip_self and s is eng.sem:
                return
            if eng.seen.get(s, 0) >= v:
                return
            if need.get(s, 0) < v:
                need[s] = v
        for b in reads:
            req(b.w)
        for b in writes:
            req(b.w)
            for ev in b.r:
                req(ev)
        for s, v in need.items():
            eng.e.wait_ge(s.h, v)
            eng.seen[s] = v
            self.nins += 1

    def op(self, eng, fns, reads=(), writes=(), sem=None, inc=1):
        if callable(fns):
            fns = [fns]
        self._waits(eng, reads, writes)
        ins = None
        for f in fns:
            ins = f(eng.e)
            self.nins += 1
        s = eng.sem if sem is None else sem
        s.v += inc
        ins.then_inc(s.h, inc)
        ev = (s, s.v)
        for b in reads:
            b.r.append(ev)
        for b in writes:
            b.w = ev
            b.r = []
        return ev

    def dma(self, eng, out, in_, sembuf, reads=(), writes=(), **kw):
        if isinstance(sembuf, Sem):
            sem = sembuf
        else:
            if sembuf.dsem is None:
                sembuf.dsem = self.new_sem("d" + sembuf.name)
            sem = sembuf.dsem
        return self.op(eng, lambda e: e.dma_start(out=out, in_=in_, **kw), reads, writes, sem=sem, inc=16)

    def barrier(self):
        for e in self.engs:
            for sm in self.sems:
                if sm.v == 0 or (sm is e.sem):
                    continue
                if e.seen.get(sm, 0) < sm.v:
                    e.e.wait_ge(sm.h, sm.v)
                    e.seen[sm] = sm.v


class Arena:
    def __init__(self, t, words):
        self.t, self.words, self.off = t, words, 0

    def mark(self):
        return self.off

    def reset(self, m):
        self.off = m

    def f32(self, shape):
        n = int(np.prod(shape[1:]))
        self.off = (self.off + 15) // 16 * 16
        ap = self.t[0:shape[0], self.off:self.off + n]
        self.off += n
        assert self.off <= self.words, ("arena overflow", self.off, self.words)
        return _reshape(ap, shape)

    def bf16(self, shape):
        n = int(np.prod(shape[1:]))
        nw = (n + 1) // 2
        self.off = (self.off + 15) // 16 * 16
        ap = self.t[0:shape[0], self.off:self.off + nw].bitcast(BF16)[:, 0:n]
        self.off += nw
        assert self.off <= self.words, ("arena overflow", self.off, self.words)
        return _reshape(ap, shape)


def _reshape(ap, shape):
    if len(shape) == 2:
        return ap
    if len(shape) == 3:
        return ap.rearrange("p (a b) -> p a b", a=shape[1], b=shape[2])
    if len(shape) == 4:
        return ap.rearrange("p (a b c) -> p a b c", a=shape[1], b=shape[2], c=shape[3])
    raise ValueError


def const_layout(cfg):
    NX = cfg.NX
    o = {}
    c = 0
    for name, n in (("identf", 128), ("maskA", 128), ("maskS", 32), ("smask", NX), ("sm01", 2), ("pm64", 2)):
        o[name] = (c, n)
        c += n
    return o, c


def make_consts(cfg):
    lay, cwid = const_layout(cfg)
    T, NX = cfg.T, cfg.NX
    C = np.zeros((128, cwid), np.float32)
    o, n = lay["identf"]; C[:, o:o + n] = np.eye(128, dtype=np.float32)
    s_ = np.arange(128)[:, None]; t_ = np.arange(128)[None, :]
    o, n = lay["maskA"]; C[:, o:o + n] = ((s_ <= t_) & (s_ // 64 == t_ // 64)).astype(np.float32)
    s2 = np.arange(32)[:, None]; t2 = np.arange(32)[None, :]
    o, n = lay["maskS"]; C[0:32, o:o + n] = ((s2 <= t2) & (s2 // 16 == t2 // 16)).astype(np.float32)
    sm = np.ones(NX, np.float32); sm[0:T:64] = 0.0; sm[T:NX:16] = 0.0
    o, n = lay["smask"]; C[:, o:o + n] = sm[None, :]
    o, n = lay["sm01"]; C[0:16, o] = 1.0; C[16:32, o + 1] = 1.0
    o, n = lay["pm64"]; C[0:64, o] = 1.0; C[64:128, o + 1] = 1.0
    return C


def build(cfg):
    nc = bass.Bass("TRN2", target_bir_lowering=False, num_devices=NCORES)
    D, NH, FC, KH, T, EX, NX, G = cfg.D, cfg.NH, cfg.FC, cfg.KH, cfg.T, cfg.EX, cfg.NX, cfg.G
    NCH, NPB, R, NS = cfg.NCH, cfg.NPB, cfg.ROUNDS, cfg.NS
    WC = KH * 128
    WCMAX = max(WC, cfg.QMAX * 128)
    clay, CWID = const_layout(cfg)

    def din(name, shape, dt=F32):
        return nc.dram_tensor(name, list(shape), dt, kind="ExternalInput").ap()

    def dout(name, shape):
        return nc.dram_tensor(name, list(shape), F32, kind="ExternalOutput").ap()

    def dint(name, shape):
        return nc.dram_tensor(name, list(shape), F32, kind="Internal").ap()

    xp = din("xp", [R * T, D])
    xs = din("xs", [EX, D])
    sh = din("sh", [cfg.SPC, NH, 128, 128])
    sc = din("sc", [cfg.SPC, 2, D])
    lbl = din("lbl", [128, 3, NH])
    ogain = din("ogain", [128, 1])
    norms = din("norms", [128, 5, FC])
    cw = din("cw", [128, 3, FC])
    rmask = din("rmask", [128, 17])
    consts = din("consts", [128, CWID])
    wsh, wag_in, wag = {}, {}, {}
    WP = []
    for n in WNAMES:
        _, c = wshape(cfg, n)
        PT = PT_OF(n)
        for pi, (t0, nt) in enumerate(wpieces(cfg, n)):
            key = f"{n}_{pi}"
            assert nt % PT == 0
            WP.append((key, n, t0, nt, PT))
            wsh[key] = din(key, [nt // NCORES * 128, c])
            wag_in[key] = nc.dram_tensor(key + "_i", [nt // NCORES * 128, c], BF16, kind="Internal").ap()
            wag[key] = nc.dram_tensor(key + "_g", [nt * 128, c], BF16, kind="Internal").ap()
    NMID = 6
    midA = [nc.dram_tensor(f"midA{i}", [4 * 16 * PT_OF("w_hin"), WC], BF16, kind="Internal").ap() for i in range(NMID)]
    midB = [nc.dram_tensor(f"midB{i}", [4 * 16 * PT_OF("w_fout0"), cfg.QMAX * 128], BF16, kind="Internal").ap() for i in range(NMID)]
    yp = dout("yp", [R * T, D])
    ys = dout("ys", [EX, D])
    hp = dout("hp", [NH, 128, 128])
    hs = dout("hs", [cfg.SPC, NH, 128, 128])
    cp = dout("cp", [R, 2, D])
    cs = dout("cs", [cfg.SPC, 2, D])
    agh_in = dint("agh_in", [G * 128, 129])
    agh = dint("agh", [NCORES * G * 128, 129])
    agh_mid = dint("agh_mid", [NCORES // 2 * G * 128, 129])
    agc_in = dint("agc_in", [128, FC * 2])
    agc = dint("agc", [NCORES * 128, FC * 2])
    agc_mid = dint("agc_mid", [NCORES // 2 * 128, FC * 2])
    sround = dint("sround", [128, NH, 128])

    es = ExitStack()
    with es:
        AW = 52000
        arena_t = es.enter_context(nc.sbuf_tensor("arena", [128, AW], F32))
        ar = Arena(arena_t, AW)
        banks = [es.enter_context(nc.psum_tensor(f"bank{i}", [128, 512], F32)) for i in range(6)]
        bank6b = es.enter_context(nc.psum_tensor("bank6b", [128, 1024], BF16))
        banks.append(None)
        banks.append(es.enter_context(nc.psum_tensor("bank7", [128, 512], F32)))
        P = Prog(nc, es)
        pe, act, dve, pool, sp = P.pe, P.act, P.dve, P.pool, P.sp
        B = Buf

        def chk(name):
            if cfg.stop == name:
                raise _Stop()

        const_t = ar.f32([128, CWID])
        cv = lambda k: const_t[:, clay[k][0]:clay[k][0] + clay[k][1]]
        identf = cv("identf"); smask = cv("smask"); sm01 = cv("sm01")[0:32, :]; pm64 = cv("pm64")
        identb = ar.bf16([128, 128]); onesb = ar.bf16([128, 128]); zerob = ar.bf16([128, 128])
        maskA = ar.bf16([128, 128]); maskS = ar.bf16([32, 32]); ones_f = ar.f32([128, 16])
        norms_t = ar.f32([128, 5, FC]); cw_t = ar.f32([128, 3, FC]); rm_t = ar.f32([128, 17])
        lb_t = ar.f32([128, NH]); oml_t = ar.f32([128, NH]); noml_t = ar.f32([128, NH])
        lbl_t = ar.f32([128, 3, NH]); og_t = ar.f32([128, 1])
        prev7 = ar.f32([128, 2 * FC])
        NSLOT = 6
        wslots = [ar.bf16([128, WCMAX]) for _ in range(NSLOT)]
        h_t = ar.bf16([128, FC, NX])
        aux = ar.bf16([128, FC, NX + 8])
        rstd_t = ar.f32([128, NX])
        gtmp = [ar.f32([128, NX]) for _ in range(2)]
        sqt = [ar.bf16([128, NX]) for _ in range(2)]
        ostage = [ar.f32([128, 1024]) for _ in range(2)]
        hal_t = ar.f32([128, 2, FC]); halg = ar.f32([128, NCORES, 2 * FC]); halo = ar.f32([128, 2, FC])
        XR0 = ar.mark()
        x_sb = ar.f32([128, FC, NX])
        XRx = ar.mark()
        XR1 = AW
        ar.reset(XR0)
        xst = [ar.f32([128, D // 2]) for _ in range(2)]
        junk = ar.f32([128, D // 2])
        ssq = ar.f32([128, 8])
        XRa = ar.mark()
        ar.reset(XR0)
        gs_ol = ar.bf16([128, G, NX]); gs_qc = ar.bf16([128, G, NX]); gs_sg = ar.bf16([128, G, NX])
        tq = ar.f32([128, NX]); tsig = ar.f32([128, NX]); tg = ar.f32([128, NX]); tk = ar.f32([128, NX])
        tb = ar.f32([128, NX]); teb = ar.f32([128, NX]); tenb = ar.f32([128, NX])
        tqt = ar.bf16([128, NX]); tkt = ar.bf16([128, NX]); tv = ar.bf16([128, NX])
        ktok = ar.bf16([128, NPB, 128]); vtok = ar.bf16([128, NPB, 128])
        ktokm = ar.bf16([128, NPB, 2, 128])
        vtoks = ar.bf16([32, 128]); kt01 = ar.bf16([32, 2, 128]); ktoks = ar.bf16([32, 128])
        atm = ar.bf16([128, NPB, 128]); atms = ar.bf16([32, 32])
        td = ar.f32([128, NCH + 2]); tci = ar.f32([128, NCH]); tE = ar.f32([128, NCH])
        Mf = [ar.f32([128, 128]) for _ in range(2)]; Mtmp = ar.f32([128, 128]); Mb = ar.bf16([128, NCH, 128])
        Mbs = ar.bf16([128, 2, 128])
        LD = ar.f32([128, G, 129]); LDj = [ar.f32([128, G, 129]) for _ in range(2)]
        S_t = ar.f32([128, G, 128]); Sin = ar.f32([128, G, 128]); Stmp = ar.f32([128, G, 128]); M0b = ar.bf16([128, G, 128])
        S0s = ar.f32([128, 2, G, 128]); NSs = ar.f32([128, 2, G, 128])
        to32 = ar.f32([128, NX]); osq = ar.bf16([128, NX]); t1 = ar.f32([128, NX]); rs2 = ar.f32([128, NX])
        XRb = ar.mark()
        print("arena words: xsb_end", XRx, "p0_end", XRa, "p1_end", XRb, "of", AW)
        ar.reset(max(XRx, XRa, XRb))

        b_const = B("const"); b_prev7 = B("prev7")
        b_wslot = [B(f"ws{i}") for i in range(NSLOT)]
        b_h = [B(f"h{i}") for i in range(FC)]
        b_aux = [B(f"aux{i}") for i in range(max(FC, cfg.QMAX))]
        b_x = [B(f"x{i}") for i in range(FC)]
        b_bank = [B(f"bank{i}") for i in range(8)]
        b_rstd = B("rstd"); b_gtmp = [B("g0"), B("g1")]; b_sq = [B("q0"), B("q1")]
        b_ost = [B("o0"), B("o1")]
        s_cc = P.new_sem("cc")
        b_wag = {}
        RG1 = [[0, 1, 2, 3], [4, 5, 6, 7]]
        RG2 = [[0, 4], [1, 5], [2, 6], [3, 7]]

        def allgather(in_ap, mid_ap, out_ap, reads, outbuf, midbuf):
            P.op(pool, lambda e: e.collective_compute("AllGather", ALU.bypass, replica_groups=RG1, ins=[in_ap], outs=[mid_ap]),
                 reads=reads, writes=[midbuf], sem=s_cc, inc=1)
            P.op(pool, lambda e: e.collective_compute("AllGather", ALU.bypass, replica_groups=RG2, ins=[mid_ap], outs=[out_ap]),
                 reads=[midbuf], writes=[outbuf], sem=s_cc, inc=1)
        b_xst = [B("xst0"), B("xst1")]; b_ssq = B("ssq"); b_junk = B("junk")
        b_gs = [B(f"gs{i}") for i in range(G)]
        bt = {k: B(k) for k in ("tq", "tsig", "tg", "tk", "tb", "teb", "tenb", "tqt", "tkt", "tv", "ktok", "vtok", "kts",
                                "atm", "td", "tci", "tE", "ktokm", "M0", "M1", "Mtmp", "Mb", "Mbs", "LD", "S", "Sin", "Stmp", "M0b",
                                "S0s", "NSs", "to32", "osq", "t1", "rs2")}
        b_ld = [B("ld0"), B("ld1")]
        b_sr = [B(f"sr{g}") for g in range(cfg.NG)]
        b_agh_in = B("agh_in"); b_agh = B("agh"); b_agc_in = B("agc_in"); b_agc = B("agc")
        b_agh_mid = B("agh_mid"); b_agc_mid = B("agc_mid")
        b_hal = B("hal"); b_halg = B("halg"); b_halo = B("halo")
        out_bufs = []

        P.dma(sp, const_t, consts, b_const, writes=[b_const])
        for dst, src in ((norms_t, norms), (cw_t, cw), (rm_t, rmask), (lbl_t, lbl), (og_t, ogain)):
            P.dma(sp, dst, src, b_const, writes=[b_const])
        for ap_, val in ((onesb, 1.0), (zerob, 0.0), (prev7, 0.0), (ones_f, 1.0)):
            P.op(dve, lambda e, ap_=ap_, val=val: e.memset(ap_, val), writes=[b_prev7])
        P.op(dve, lambda e: e.tensor_copy(out=identb, in_=identf), reads=[b_const], writes=[b_prev7])
        P.op(dve, lambda e: e.tensor_copy(out=maskA, in_=cv("maskA")), reads=[b_const], writes=[b_prev7])
        P.op(dve, lambda e: e.tensor_copy(out=maskS, in_=cv("maskS")[0:32, :]), reads=[b_const], writes=[b_prev7])
        e3 = gtmp[0][:, 0:3 * NH].rearrange("p (a b) -> p a b", a=3)
        P.op(act, lambda e: e.activation(out=e3, in_=lbl_t, func=AF.Exp), reads=[b_const], writes=[b_gtmp[0]])
        s1 = gtmp[1][:, 0:NH]
        P.op(dve, lambda e: e.tensor_tensor(out=s1, in0=e3[:, 0, :], in1=e3[:, 1, :], op=ALU.add), reads=[b_gtmp[0]], writes=[b_gtmp[1]])
        P.op(dve, lambda e: e.tensor_tensor(out=s1, in0=s1, in1=e3[:, 2, :], op=ALU.add), reads=[b_gtmp[0], b_gtmp[1]], writes=[b_gtmp[1]])
        P.op(dve, lambda e: e.reciprocal(out=s1, in_=s1), reads=[b_gtmp[1]], writes=[b_gtmp[1]])
        P.op(dve, lambda e: e.tensor_tensor(out=lb_t, in0=e3[:, 0, :], in1=s1, op=ALU.mult), reads=[b_gtmp[0], b_gtmp[1]], writes=[b_prev7])
        P.op(dve, lambda e: e.tensor_scalar(out=oml_t, in0=lb_t, scalar1=-1.0, scalar2=1.0, op0=ALU.mult, op1=ALU.add),
             reads=[b_prev7], writes=[b_prev7])
        P.op(dve, lambda e: e.tensor_scalar(out=noml_t, in0=lb_t, scalar1=1.0, scalar2=-1.0, op0=ALU.mult, op1=ALU.add),
             reads=[b_prev7], writes=[b_prev7])
        P.barrier()
        try:
            chk("consts")
            _go = True
        except _Stop:
            _go = False
        b_const = B("const_ro")

        RG = [list(range(NCORES))]
        b_win = {}
        wp_by_name = {}
        gstate = {}
        b_midA = [B(f"midA{i}") for i in range(NMID)]
        b_midB = [B(f"midB{i}") for i in range(NMID)]
        midctr = {"A": 0, "B": 0}
        LOOKAHEAD = 10
        for (key, n, t0, nt, PT) in (WP if _go else []):
            wp_by_name.setdefault(n, []).append((key, t0, nt, PT))
            rows, c = wsh[key].shape
            b_win[key] = B(key + "_i")
            for r0 in range(0, rows, 1024):
                r1 = min(rows, r0 + 1024)
                P.dma(pool, wag_in[key][r0:r1, :], wsh[key][r0:r1, :], b_win[key], writes=[b_win[key]])
            npc = nt // PT
            gstate[key] = {"s1": 0, "s2": 0, "np": npc, "PT": PT, "mids": {}}
            b_wag[key] = [B(f"{key}_p{i}") for i in range(npc)]

        def gather_step(key, upto):
            st = gstate[key]
            PT = st["PT"]
            RPR = PT * 16
            kind = "B" if PT == PT_OF("w_fout0") else "A"
            mids, bmids = (midB, b_midB) if kind == "B" else (midA, b_midA)
            upto = min(upto, st["np"] - 1)

            def s2(p):
                k = st["mids"][p]
                P.op(pool, lambda e: e.collective_compute("AllGather", ALU.bypass, replica_groups=RG2, ins=[mids[k]],
                                                          outs=[wag[key][p * PT * 128:(p + 1) * PT * 128, :]]),
                     reads=[bmids[k]], writes=[b_wag[key][p]], sem=s_cc, inc=1)
            while st["s1"] <= upto:
                p = st["s1"]
                k = midctr[kind] % NMID
                midctr[kind] += 1
                st["mids"][p] = k
                P.op(pool, lambda e: e.collective_compute("AllGather", ALU.bypass, replica_groups=RG1,
                                                          ins=[wag_in[key][p * RPR:(p + 1) * RPR, :]], outs=[mids[k]]),
                     reads=[b_win[key]], writes=[bmids[k]], sem=s_cc, inc=1)
                st["s1"] += 1
                while st["s2"] < st["s1"] - 1:
                    s2(st["s2"])
                    st["s2"] += 1
            if st["s1"] == st["np"]:
                while st["s2"] < st["np"]:
                    s2(st["s2"])
                    st["s2"] += 1

        def ensure_piece(key, p):
            st = gstate[key]
            gather_step(key, p + LOOKAHEAD)
            while st["s2"] <= p:
                k = st["s2"]
                gather_flush_one(key)

        def gather_flush_one(key):
            st = gstate[key]
            PT = st["PT"]
            kind = "B" if PT == PT_OF("w_fout0") else "A"
            mids, bmids = (midB, b_midB) if kind == "B" else (midA, b_midA)
            p = st["s2"]
            k = st["mids"][p]
            P.op(pool, lambda e: e.collective_compute("AllGather", ALU.bypass, replica_groups=RG2, ins=[mids[k]],
                                                      outs=[wag[key][p * PT * 128:(p + 1) * PT * 128, :]]),
                 reads=[bmids[k]], writes=[b_wag[key][p]], sem=s_cc, inc=1)
            st["s2"] += 1

        wctr = [0]

        def wload(name, tile_idx, ncols):
            i = wctr[0] % NSLOT
            wctr[0] += 1
            for (key, t0, nt, PT) in wp_by_name[name]:
                if t0 <= tile_idx < t0 + nt:
                    break
            li = tile_idx - t0
            pidx = li // PT
            ensure_piece(key, pidx)
            src = wag[key][li * 128:(li + 1) * 128, 0:ncols]
            P.dma(sp, wslots[i][:, 0:ncols], src, b_wslot[i], reads=[b_wag[key][pidx]], writes=[b_wslot[i]])
            return wslots[i], b_wslot[i]

        NACC = 6
        accctr = [0]

        def acc_get():
            i = accctr[0] % NACC
            accctr[0] += 1
            return i

        def acc_seg(i, c0, n):
            if c0 < T:
                return banks[i][:, c0:c0 + n]
            return banks[7][:, 32 * i + (c0 - T):32 * i + (c0 - T) + n]

        def acc_bufs(i, N):
            return [b_bank[i]] + ([b_bank[7]] if N > T else [])

        def segs(N):
            return [(0, T)] + ([(T, N - T)] if N > T else [])

        def mm_acc(i, N, steps, reads, first, last):
            fl = []
            for (c0, n) in segs(N):
                for k, (lh, rf) in enumerate(steps):
                    fl.append(lambda e, c0=c0, n=n, lh=lh, rf=rf, k=k: e.matmul(
                        acc_seg(i, c0, n), lh, rf(c0, n), start=(first and k == 0), stop=(last and k == len(steps) - 1)))
            P.op(pe, fl, reads=reads, writes=acc_bufs(i, N))

        def proj(name, tile0, N, rhs_bufs, rhs_fn, i):
            for half in range(2):
                wt, wb = wload(name, tile0 + half, WC)
                w3 = wt[:, 0:WC].rearrange("p (k c) -> p k c", c=128)
                steps = [(w3[:, k, :], (lambda c0, n, kc=half * KH + k: rhs_fn(kc, c0, n))) for k in range(KH)]
                mm_acc(i, N, steps, [wb] + list(rhs_bufs), first=(half == 0), last=(half == 1))

        def rsqrt_inplace(ap, buf):
            P.op(act, lambda e: e.activation(out=ap, in_=ap, func=AF.Sqrt), reads=[buf], writes=[buf])
            P.op(dve, lambda e: e.reciprocal(out=ap, in_=ap), reads=[buf], writes=[buf])

        def fm_norm(N):
            i = acc_get()
            for fc in range(FC):
                j = fc % 2
                P.op(act, lambda e, fc=fc, j=j: e.activation(out=sqt[j][:, 0:N], in_=x_sb[:, fc, 0:N], func=AF.Square),
                     reads=[b_x[fc]], writes=[b_sq[j]])
                mm_acc(i, N, [(onesb, lambda c0, n, j=j: sqt[j][:, c0:c0 + n])], [b_sq[j]], first=(fc == 0), last=(fc == FC - 1))
            for (c0, n) in segs(N):
                P.op(dve, lambda e, c0=c0, n=n: e.tensor_scalar(out=rstd_t[:, c0:c0 + n], in0=acc_seg(i, c0, n), scalar1=1.0 / D,
                                                                 scalar2=EPS, op0=ALU.mult, op1=ALU.add),
                     reads=acc_bufs(i, N), writes=[b_rstd])
            rsqrt_inplace(rstd_t[:, 0:N], b_rstd)

        def fm_norm_apply(N, gain_idx):
            for fc in range(FC):
                P.op(dve, lambda e, fc=fc: e.scalar_tensor_tensor(out=h_t[:, fc, 0:N], in0=x_sb[:, fc, 0:N],
                                                                  scalar=norms_t[:, gain_idx, fc:fc + 1], in1=rstd_t[:, 0:N],
                                                                  op0=ALU.mult, op1=ALU.mult),
                     reads=[b_x[fc], b_rstd], writes=[b_h[fc]])

        actv = aux.rearrange("p a b -> p (a b)")[:, 0:cfg.QMAX * NX].rearrange("p (a b) -> p a b", b=NX)
        hrd = lambda kc, c0, n: h_t[:, kc, c0:c0 + n]

        def ffn(l, N):
            fm_norm(N)
            fm_norm_apply(N, 1 + 2 * l)
            s0 = 0
            for q in range(cfg.NQ):
                nq = cfg.QS[q]
                for sl in range(nq):
                    s = s0 + sl
                    ig = acc_get()
                    proj(f"w_fin{l}", (s * 2 + 0) * 2, N, b_h, hrd, ig)
                    iu = acc_get()
                    proj(f"w_fin{l}", (s * 2 + 1) * 2, N, b_h, hrd, iu)
                    j = s % 2
                    for (c0, n) in segs(N):
                        P.op(act, lambda e, c0=c0, n=n, j=j, ig=ig: e.activation(out=gtmp[j][:, c0:c0 + n], in_=acc_seg(ig, c0, n), func=AF.Silu),
                             reads=acc_bufs(ig, N), writes=[b_gtmp[j]])
                    for (c0, n) in segs(N):
                        P.op(dve, lambda e, c0=c0, n=n, j=j, iu=iu, sl=sl: e.tensor_tensor(out=actv[:, sl, c0:c0 + n], in0=gtmp[j][:, c0:c0 + n],
                                                                                        in1=acc_seg(iu, c0, n), op=ALU.mult),
                             reads=acc_bufs(iu, N) + [b_gtmp[j]], writes=[b_aux[sl]])
                for fc in range(FC):
                    wt, wb = wload(f"w_fout{l}", q * FC + fc, cfg.QMAX * 128)
                    w3 = wt[:, 0:cfg.QMAX * 128].rearrange("p (k c) -> p k c", c=128)
                    i = acc_get()
                    steps = [(w3[:, k, :], (lambda c0, n, k=k: actv[:, k, c0:c0 + n])) for k in range(nq)]
                    mm_acc(i, N, steps, [wb] + b_aux[0:nq], first=True, last=True)
                    for (c0, n) in segs(N):
                        P.op(dve, lambda e, c0=c0, n=n, fc=fc, i=i: e.tensor_tensor(out=x_sb[:, fc, c0:c0 + n], in0=x_sb[:, fc, c0:c0 + n],
                                                                                 in1=acc_seg(i, c0, n), op=ALU.add),
                             reads=acc_bufs(i, N) + [b_x[fc]], writes=[b_x[fc]])
                s0 += nq

        bk6 = bank6b[:, :]
        psTk = bk6[:, 0:NPB * 128].rearrange("p (a b) -> p a b", b=128)
        psTv = bk6[:, 512:512 + NPB * 128].rearrange("p (a b) -> p a b", b=128)
        psTks = bk6[:, 0:128]; psTvs = bk6[:, 512:640]
        psATs = banks[7][:, 320:352]
        HW_ = D // 2

        def round_body(r):
            last_round = (r == R - 1)
            N = NX if last_round else T
            P.barrier()
            tok_blocks = [(xp[r * T + tbk * 128: r * T + tbk * 128 + 128, :], 128, tbk * 128) for tbk in range(NPB)]
            if last_round:
                tok_blocks.append((xs[0:EX, :], EX, T))
            for (src, nt, c0) in tok_blocks:
                for hf in range(2):
                    P.dma(sp, xst[hf][0:nt, :], src[:, hf * HW_:(hf + 1) * HW_], b_xst[hf], writes=[b_xst[hf]])
                    P.op(act, lambda e, hf=hf, nt=nt: e.activation(out=junk[0:nt, :], in_=xst[hf][0:nt, :], func=AF.Square,
                                                                 accum_out=ssq[0:nt, hf:hf + 1]),
                         reads=[b_xst[hf]], writes=[b_ssq, b_junk])
                P.op(dve, lambda e, nt=nt: e.tensor_tensor(out=ssq[0:nt, 2:3], in0=ssq[0:nt, 0:1], in1=ssq[0:nt, 1:2], op=ALU.add),
                     reads=[b_ssq], writes=[b_ssq])
                P.op(dve, lambda e, nt=nt: e.tensor_scalar(out=ssq[0:nt, 2:3], in0=ssq[0:nt, 2:3], scalar1=1.0 / D, scalar2=EPS,
                                                          op0=ALU.mult, op1=ALU.add), reads=[b_ssq], writes=[b_ssq])
                P.op(act, lambda e, nt=nt: e.activation(out=ssq[0:nt, 2:3], in_=ssq[0:nt, 2:3], func=AF.Sqrt), reads=[b_ssq], writes=[b_ssq])
                P.op(dve, lambda e, nt=nt: e.reciprocal(out=ssq[0:nt, 3:4], in_=ssq[0:nt, 2:3]), reads=[b_ssq], writes=[b_ssq])
                for hf in range(2):
                    P.op(dve, lambda e, hf=hf, nt=nt: e.tensor_scalar(out=xst[hf][0:nt, :], in0=xst[hf][0:nt, :], scalar1=ssq[0:nt, 3:4],
                                                                    scalar2=None, op0=ALU.mult), reads=[b_ssq, b_xst[hf]], writes=[b_xst[hf]])
                    for f4 in range(0, FC // 2, 4):
                        nf = min(4, FC // 2 - f4)
                        i = acc_get()
                        P.op(pe, [lambda e, hf=hf, f=f, f4=f4, nt=nt, i=i: e.transpose(banks[i][:, (f - f4) * 128:(f - f4) * 128 + nt],
                                                                                         xst[hf][0:nt, f * 128:(f + 1) * 128], identf[0:nt, 0:nt])
                                  for f in range(f4, f4 + nf)], reads=[b_xst[hf]], writes=[b_bank[i]])
                        for f in range(f4, f4 + nf):
                            fcg = hf * (FC // 2) + f
                            P.op(dve, lambda e, f=f, f4=f4, fcg=fcg, nt=nt, c0=c0, i=i: e.tensor_scalar(
                                out=h_t[:, fcg, c0:c0 + nt], in0=banks[i][:, (f - f4) * 128:(f - f4) * 128 + nt],
                                scalar1=norms_t[:, 0, fcg:fcg + 1], scalar2=None, op0=ALU.mult),
                                reads=[b_bank[i]], writes=[b_h[fcg]])
            P.barrier()
            chk("p0")

            for grp in range(cfg.NG):
                if last_round:
                    for q in range(2):
                        P.dma(sp, S0s[:, q, :, :], sh[q, grp * G:(grp + 1) * G, :, :].rearrange("h k v -> k h v"), bt["S0s"], writes=[bt["S0s"]])
                for hg in range(G):
                    hd = grp * G + hg
                    iq = acc_get(); proj("w_hin", (hd * 4 + 0) * 2, N, b_h, hrd, iq)
                    if_ = acc_get(); proj("w_hin", (hd * 4 + 1) * 2, N, b_h, hrd, if_)
                    ii = acc_get(); proj("w_hin", (hd * 4 + 2) * 2, N, b_h, hrd, ii)
                    ig = acc_get(); proj("w_hin", (hd * 4 + 3) * 2, N, b_h, hrd, ig)
                    for (c0, n) in segs(N):
                        P.op(act, lambda e, c0=c0, n=n: e.activation(out=tq[:, c0:c0 + n], in_=acc_seg(iq, c0, n), func=AF.Silu),
                             reads=acc_bufs(iq, N), writes=[bt["tq"]])
                        P.op(act, lambda e, c0=c0, n=n: e.activation(out=gs_sg[:, hg, c0:c0 + n], in_=acc_seg(ig, c0, n), func=AF.Silu),
                             reads=acc_bufs(ig, N), writes=[b_gs[hg]])
                        P.op(act, lambda e, c0=c0, n=n: e.activation(out=tsig[:, c0:c0 + n], in_=acc_seg(if_, c0, n), func=AF.Sigmoid),
                             reads=acc_bufs(if_, N), writes=[bt["tsig"]])
                        P.op(dve, lambda e, c0=c0, n=n: e.tensor_copy(out=tv[:, c0:c0 + n], in_=acc_seg(ii, c0, n)),
                             reads=acc_bufs(ii, N), writes=[bt["tv"]])
                    chk("h_proj")
                    P.op(act, lambda e: e.activation(out=tg[:, 0:N], in_=tsig[:, 0:N], func=AF.Ln, scale=oml_t[:, hd:hd + 1], bias=lb_t[:, hd:hd + 1]),
                         reads=[bt["tsig"]], writes=[bt["tg"]])
                    P.op(dve, lambda e: e.tensor_scalar(out=tk[:, 0:N], in0=tsig[:, 0:N], scalar1=noml_t[:, hd:hd + 1], scalar2=oml_t[:, hd:hd + 1],
                                                        op0=ALU.mult, op1=ALU.add), reads=[bt["tsig"]], writes=[bt["tk"]])
                    P.op(dve, lambda e: e.tensor_tensor_scan(out=tb[:, 0:N], data0=smask[:, 0:N], data1=tg[:, 0:N], initial=0.0,
                                                             op0=ALU.mult, op1=ALU.add), reads=[bt["tg"]], writes=[bt["tb"]])
                    P.op(act, lambda e: e.activation(out=teb[:, 0:N], in_=tb[:, 0:N], func=AF.Exp), reads=[bt["tb"]], writes=[bt["teb"]])
                    P.op(act, lambda e: e.activation(out=tenb[:, 0:N], in_=tb[:, 0:N], func=AF.Exp, scale=-1.0), reads=[bt["tb"]], writes=[bt["tenb"]])
                    P.op(dve, lambda e: e.tensor_tensor(out=tqt[:, 0:N], in0=tq[:, 0:N], in1=teb[:, 0:N], op=ALU.mult),
                         reads=[bt["tq"], bt["teb"]], writes=[bt["tqt"]])
                    P.op(dve, lambda e: e.tensor_tensor(out=tkt[:, 0:N], in0=tk[:, 0:N], in1=tenb[:, 0:N], op=ALU.mult),
                         reads=[bt["tk"], bt["tenb"]], writes=[bt["tkt"]])
                    chk("h_act")
                    bl = tb[:, 0:T].rearrange("p (c t) -> p c t", t=64)[:, :, 63]
                    P.op(act, lambda e: e.activation(out=td[:, 0:NCH], in_=bl, func=AF.Exp), reads=[bt["tb"]], writes=[bt["td"]])
                    if last_round:
                        bls = tb[:, T:NX].rearrange("p (c t) -> p c t", t=16)[:, :, 15]
                        P.op(act, lambda e: e.activation(out=td[:, NCH:NCH + 2], in_=bls, func=AF.Exp), reads=[bt["tb"]], writes=[bt["td"]])
                    P.op(dve, lambda e: e.tensor_tensor_scan(out=tci, data0=ones_f[:, 0:NCH], data1=bl, initial=0.0, op0=ALU.mult, op1=ALU.add),
                         reads=[bt["tb"]], writes=[bt["tci"]])
                    P.op(act, lambda e: e.activation(out=tE, in_=tci, func=AF.Exp), reads=[bt["tci"]], writes=[bt["tE"]])
                    P.op(dve, lambda e: e.tensor_copy(out=gs_qc[:, hg, 0:64], in_=tqt[:, 0:64]), reads=[bt["tqt"]], writes=[b_gs[hg]])
                    if NCH > 1:
                        P.op(dve, lambda e: e.tensor_tensor(out=gs_qc[:, hg, 64:T].rearrange("p (c t) -> p c t", t=64),
                                                            in0=tqt[:, 64:T].rearrange("p (c t) -> p c t", t=64),
                                                            in1=tE[:, 0:NCH - 1].unsqueeze(2).to_broadcast([128, NCH - 1, 64]), op=ALU.mult),
                             reads=[bt["tqt"], bt["tE"]], writes=[b_gs[hg]])
                    chk("h_dec")
                    P.op(pe, [lambda e, pb=pb: e.transpose(psTk[:, pb, :], tkt[:, pb * 128:(pb + 1) * 128], identb) for pb in range(NPB)]
                         + [lambda e, pb=pb: e.transpose(psTv[:, pb, :], tv[:, pb * 128:(pb + 1) * 128], identb) for pb in range(NPB)],
                         reads=[bt["tkt"], bt["tv"]], writes=[b_bank[6]])
                    chk("h_tr0")
                    P.op(act, lambda e: e.activation(out=ktok, in_=psTk, func=AF.Copy), reads=[b_bank[6]], writes=[bt["ktok"]])
                    chk("h_tr1")
                    P.op(act, lambda e: e.activation(out=vtok, in_=psTv, func=AF.Copy), reads=[b_bank[6]], writes=[bt["vtok"]])
                    if last_round:
                        P.op(pe, [lambda e: e.transpose(psTks[0:32, :], tkt[:, T:NX], identb), lambda e: e.transpose(psTvs[0:32, :], tv[:, T:NX], identb)],
                             reads=[bt["tkt"], bt["tv"]], writes=[b_bank[6]])
                        P.op(act, lambda e: e.activation(out=vtoks, in_=psTvs[0:32, :], func=AF.Copy), reads=[b_bank[6]], writes=[bt["kts"]])
                        P.op(act, lambda e: e.activation(out=ktoks, in_=psTks[0:32, :], func=AF.Copy), reads=[b_bank[6]], writes=[bt["kts"]])
                        for q in range(2):
                            P.op(dve, lambda e, q=q: e.tensor_scalar(out=kt01[:, q, :], in0=ktoks, scalar1=sm01[:, q:q + 1], scalar2=None, op0=ALU.mult),
                                 reads=[bt["kts"]], writes=[bt["kts"]])
                    chk("h_tr")
                    for j in range(2):
                        P.op(dve, lambda e, j=j: e.tensor_scalar(out=ktokm[:, :, j, :], in0=ktok, scalar1=pm64[:, j:j + 1], scalar2=None, op0=ALU.mult),
                             reads=[bt["ktok"]], writes=[bt["ktokm"]])
                    iU = []
                    for half in range(0, NCH, 4):
                        i = acc_get()
                        iU.append(i)
                        P.op(pe, [lambda e, c=c, i=i, half=half: e.matmul(banks[i][:, (c - half) * 128:(c - half + 1) * 128],
                                                                           ktokm[:, c // 2, c % 2, :], vtok[:, c // 2, :], start=True, stop=True)
                                  for c in range(half, min(NCH, half + 4))], reads=[bt["ktokm"], bt["vtok"]], writes=[b_bank[i]])
                    if last_round:
                        iUs = acc_get()
                        P.op(pe, [lambda e, q=q: e.matmul(banks[iUs][:, q * 128:(q + 1) * 128], kt01[:, q, :], vtoks, start=True, stop=True) for q in range(2)],
                             reads=[bt["kts"]], writes=[b_bank[iUs]])
                    chk("h_U")
                    Ubank = lambda c: banks[iU[c // 4]][:, (c % 4) * 128:(c % 4 + 1) * 128]
                    bU = lambda c: b_bank[iU[c // 4]]
                    cur = 0
                    for c in range(NCH):
                        fin = (c == NCH - 1)
                        dst = LD[:, hg, 0:128] if fin else Mf[1 - cur]
                        dbuf = bt["LD"] if fin else bt[f"M{1 - cur}"]
                        if c == 0:
                            P.op(dve, lambda e, dst=dst: e.tensor_scalar(out=dst, in0=Ubank(0), scalar1=td[:, 0:1], scalar2=None, op0=ALU.mult),
                                 reads=[bU(0), bt["td"]], writes=[dbuf])
                        else:
                            P.op(dve, lambda e, c=c, cur=cur: e.tensor_scalar(out=Mtmp, in0=Mf[cur], scalar1=td[:, c:c + 1], scalar2=None, op0=ALU.mult),
                                 reads=[bt[f"M{cur}"], bt["td"]], writes=[bt["Mtmp"]])
                            P.op(dve, lambda e, c=c, dst=dst: e.scalar_tensor_tensor(out=dst, in0=Ubank(c), scalar=td[:, c:c + 1], in1=Mtmp,
                                                                                   op0=ALU.mult, op1=ALU.add),
                                 reads=[bU(c), bt["td"], bt["Mtmp"]], writes=[dbuf])
                        if not fin:
                            P.op(act, lambda e, c=c, cur=cur: e.activation(out=Mb[:, c + 1, :], in_=Mf[1 - cur], func=AF.Copy),
                                 reads=[bt[f"M{1 - cur}"]], writes=[bt["Mb"]])
                        cur = 1 - cur
                    P.op(act, lambda e: e.activation(out=LD[:, hg, 128:129], in_=tE[:, NCH - 1:NCH], func=AF.Copy), reads=[bt["tE"]], writes=[bt["LD"]])
                    if last_round:
                        for q in range(2):
                            P.op(act, lambda e, q=q: e.activation(out=Mbs[:, q, :], in_=S0s[:, q, hg, :], func=AF.Copy), reads=[bt["S0s"]], writes=[bt["Mbs"]])
                            P.op(dve, lambda e, q=q: e.tensor_scalar(out=Mtmp, in0=S0s[:, q, hg, :], scalar1=td[:, NCH + q:NCH + q + 1], scalar2=None, op0=ALU.mult),
                                 reads=[bt["S0s"], bt["td"]], writes=[bt["Mtmp"]])
                            P.op(dve, lambda e, q=q: e.scalar_tensor_tensor(out=NSs[:, q, hg, :], in0=banks[iUs][:, q * 128:(q + 1) * 128],
                                                                          scalar=td[:, NCH + q:NCH + q + 1], in1=Mtmp, op0=ALU.mult, op1=ALU.add),
                                 reads=[b_bank[iUs], bt["td"], bt["Mtmp"]], writes=[bt["NSs"]])
                    chk("h_chain")
                    iA = acc_get()
                    P.op(pe, [lambda e, pb=pb: e.matmul(banks[iA][:, pb * 128:(pb + 1) * 128], tkt[:, pb * 128:(pb + 1) * 128],
                                                        tqt[:, pb * 128:(pb + 1) * 128], start=True, stop=True) for pb in range(NPB)]
                         + ([lambda e: e.matmul(psATs[0:32, :], tkt[:, T:NX], tqt[:, T:NX], start=True, stop=True)] if last_round else []),
                         reads=[bt["tkt"], bt["tqt"]], writes=[b_bank[iA]] + ([b_bank[7]] if last_round else []))
                    P.op(dve, lambda e: e.tensor_tensor(out=atm, in0=banks[iA][:, 0:NPB * 128].rearrange("p (a b) -> p a b", b=128),
                                                        in1=maskA.unsqueeze(1).to_broadcast([128, NPB, 128]), op=ALU.mult),
                         reads=[b_bank[iA]], writes=[bt["atm"]])
                    if last_round:
                        P.op(dve, lambda e: e.tensor_tensor(out=atms, in0=psATs[0:32, :], in1=maskS, op=ALU.mult),
                             reads=[b_bank[7]], writes=[bt["atm"]])
                    chk("h_A")
                    iO = acc_get()
                    fl = []
                    for c in range(NCH):
                        pb, j = c // 2, c % 2
                        fl.append(lambda e, c=c, pb=pb, j=j: e.matmul(banks[iO][:, c * 64:(c + 1) * 64], vtok[:, pb, :], atm[:, pb, j * 64:(j + 1) * 64],
                                                                     start=True, stop=False))
                        fl.append(lambda e, c=c: e.matmul(banks[iO][:, c * 64:(c + 1) * 64], (zerob if c == 0 else Mb[:, c, :]), tqt[:, c * 64:(c + 1) * 64],
                                                         start=False, stop=True))
                    if last_round:
                        for q in range(2):
                            fl.append(lambda e, q=q: e.matmul(acc_seg(iO, T + 16 * q, 16), vtoks, atms[:, q * 16:(q + 1) * 16], start=True, stop=False))
                            fl.append(lambda e, q=q: e.matmul(acc_seg(iO, T + 16 * q, 16), Mbs[:, q, :], tqt[:, T + 16 * q:T + 16 * q + 16], start=False, stop=True))
                    P.op(pe, fl, reads=[bt["vtok"], bt["atm"], bt["Mb"], bt["tqt"], bt["kts"], bt["Mbs"]], writes=acc_bufs(iO, N))
                    for (c0, n) in segs(N):
                        P.op(act, lambda e, c0=c0, n=n: e.activation(out=gs_ol[:, hg, c0:c0 + n], in_=acc_seg(iO, c0, n), func=AF.Copy),
                             reads=acc_bufs(iO, N), writes=[b_gs[hg]])
                chk("p1h")
                if last_round:
                    for q in range(2):
                        P.dma(sp, hs[q, grp * G:(grp + 1) * G, :, :].rearrange("h k v -> k h v"), NSs[:, q, :, :], bt["NSs"], reads=[bt["NSs"]], writes=[bt["NSs"]])
                    out_bufs.append(bt["NSs"])
                P.dma(sp, agh_in.rearrange("(g k) c -> k g c", k=128), LD, b_agh_in, reads=[bt["LD"]], writes=[b_agh_in])
                allgather(agh_in, agh_mid, agh, [b_agh_in], b_agh, b_agh_mid)
                if r == 0:
                    P.op(dve, lambda e: e.memset(S_t, 0.0), writes=[bt["S"]])
                else:
                    P.dma(sp, S_t, sround[:, grp * G:(grp + 1) * G, :], bt["S"], reads=[b_sr[grp]], writes=[bt["S"]])
                P.op(dve, lambda e: e.memset(Sin, 0.0), writes=[bt["Sin"]])
                for j in range(NCORES):
                    jj = j % 2
                    P.dma(sp, LDj[jj], agh[j * G * 128:(j + 1) * G * 128, :].rearrange("(g k) c -> k g c", k=128), b_ld[jj], reads=[b_agh], writes=[b_ld[jj]])
                    P.op(dve, lambda e, j=j: e.scalar_tensor_tensor(out=Sin, in0=S_t, scalar=rm_t[:, j:j + 1], in1=Sin, op0=ALU.mult, op1=ALU.add),
                         reads=[bt["S"], bt["Sin"]], writes=[bt["Sin"]])
                    P.op(dve, lambda e, jj=jj: e.tensor_tensor(out=Stmp, in0=S_t, in1=LDj[jj][:, :, 128:129].to_broadcast([128, G, 128]), op=ALU.mult),
                         reads=[bt["S"], b_ld[jj]], writes=[bt["Stmp"]])
                    P.op(dve, lambda e, jj=jj: e.tensor_tensor(out=S_t, in0=Stmp, in1=LDj[jj][:, :, 0:128], op=ALU.add),
                         reads=[bt["Stmp"], b_ld[jj]], writes=[bt["S"]])
                if last_round:
                    P.dma(sp, hp[grp * G:(grp + 1) * G, :, :].rearrange("h k v -> k h v"), S_t, b_sr[grp], reads=[bt["S"]], writes=[b_sr[grp]])
                    out_bufs.append(b_sr[grp])
                else:
                    P.dma(sp, sround[:, grp * G:(grp + 1) * G, :], S_t, b_sr[grp], reads=[bt["S"]], writes=[b_sr[grp]])
                P.op(act, lambda e: e.activation(out=M0b, in_=Sin, func=AF.Copy), reads=[bt["Sin"]], writes=[bt["M0b"]])
                for hg in range(G):
                    hd = grp * G + hg
                    iC = acc_get()
                    P.op(pe, lambda e: e.matmul(banks[iC][:, 0:T], M0b[:, hg, :], gs_qc[:, hg, 0:T], start=True, stop=True),
                         reads=[bt["M0b"], b_gs[hg]], writes=[b_bank[iC]])
                    P.op(dve, lambda e: e.tensor_tensor(out=to32[:, 0:T], in0=banks[iC][:, 0:T], in1=gs_ol[:, hg, 0:T], op=ALU.add),
                         reads=[b_bank[iC], b_gs[hg]], writes=[bt["to32"]])
                    if last_round:
                        P.op(dve, lambda e: e.tensor_copy(out=to32[:, T:NX], in_=gs_ol[:, hg, T:NX]), reads=[b_gs[hg]], writes=[bt["to32"]])
                    P.op(act, lambda e: e.activation(out=osq[:, 0:N], in_=to32[:, 0:N], func=AF.Square), reads=[bt["to32"]], writes=[bt["osq"]])
                    iS = acc_get()
                    mm_acc(iS, N, [(onesb, lambda c0, n: osq[:, c0:c0 + n])], [bt["osq"]], first=True, last=True)
                    for (c0, n) in segs(N):
                        P.op(dve, lambda e, c0=c0, n=n: e.tensor_scalar(out=rs2[:, c0:c0 + n], in0=acc_seg(iS, c0, n), scalar1=1.0 / 128, scalar2=EPS,
                                                                         op0=ALU.mult, op1=ALU.add), reads=acc_bufs(iS, N), writes=[bt["rs2"]])
                    rsqrt_inplace(rs2[:, 0:N], bt["rs2"])
                    P.op(dve, lambda e: e.scalar_tensor_tensor(out=t1[:, 0:N], in0=to32[:, 0:N], scalar=og_t[:, 0:1], in1=rs2[:, 0:N], op0=ALU.mult, op1=ALU.mult),
                         reads=[bt["to32"], bt["rs2"]], writes=[bt["t1"]])
                    P.op(dve, lambda e: e.tensor_tensor(out=aux[:, hd, 0:N], in0=t1[:, 0:N], in1=gs_sg[:, hg, 0:N], op=ALU.mult),
                         reads=[bt["t1"], b_gs[hg]], writes=[b_aux[hd]])

            chk("p1")
            P.barrier()
            for fc in range(FC):
                j = fc % 2
                xr = ostage[j][:, 0:(NPB + 1) * 128].rearrange("p (a b) -> p a b", b=128)
                for tbk in range(NPB):
                    P.dma(sp, xr[:, tbk, :], xp[r * T + tbk * 128:r * T + tbk * 128 + 128, fc * 128:(fc + 1) * 128], b_ost[j], writes=[b_ost[j]])
                if last_round:
                    P.dma(sp, xr[0:EX, NPB, :], xs[:, fc * 128:(fc + 1) * 128], b_ost[j], writes=[b_ost[j]])
                i = acc_get()
                for half in range(2):
                    wt, wb = wload("w_hout", fc * 2 + half, WC)
                    w3 = wt[:, 0:WC].rearrange("p (k c) -> p k c", c=128)
                    fl = []
                    for (c0, n) in segs(N):
                        for k in range(KH):
                            kc = half * KH + k
                            fl.append(lambda e, c0=c0, n=n, k=k, kc=kc, w3=w3, half=half: e.matmul(acc_seg(i, c0, n), w3[:, k, :], aux[:, kc, c0:c0 + n],
                                                                                                    start=(half == 0 and k == 0), stop=False))
                    rd = [wb] + b_aux[0:FC]
                    if half == 1:
                        for tbk in range(NPB):
                            fl.append(lambda e, tbk=tbk, xr=xr: e.matmul(banks[i][:, tbk * 128:(tbk + 1) * 128], xr[:, tbk, :], identf, start=False, stop=True))
                        if last_round:
                            fl.append(lambda e, xr=xr: e.matmul(acc_seg(i, T, EX), xr[0:EX, NPB, :], identf[0:EX, 0:EX], start=False, stop=True))
                        rd = rd + [b_ost[j]]
                    P.op(pe, fl, reads=rd, writes=acc_bufs(i, N))
                for (c0, n) in segs(N):
                    P.op(act, lambda e, c0=c0, n=n, fc=fc, i=i: e.activation(out=x_sb[:, fc, c0:c0 + n], in_=acc_seg(i, c0, n), func=AF.Copy),
                         reads=acc_bufs(i, N), writes=[b_x[fc]])
            chk("p2")
            ffn(0, N)
            chk("ffn0")
            fm_norm(N)
            fm_norm_apply(N, 2)
            cuh = aux
            offs = [(2, 0, T)] + ([(T + 4, T, 16), (T + 22, T + 16, 16)] if last_round else [])
            cst = ostage[0][:, 0:4 * FC].rearrange("p (q t f) -> p q t f", q=2, t=2)
            for fc in range(FC):
                igc = acc_get(); proj("w_cin", (fc * 2 + 0) * 2, N, b_h, hrd, igc)
                iu = acc_get(); proj("w_cin", (fc * 2 + 1) * 2, N, b_h, hrd, iu)
                j = fc % 2
                for (c0, n) in segs(N):
                    P.op(act, lambda e, c0=c0, n=n, j=j, igc=igc: e.activation(out=gtmp[j][:, c0:c0 + n], in_=acc_seg(igc, c0, n), func=AF.Copy),
                         reads=acc_bufs(igc, N), writes=[b_gtmp[j]])
                    P.op(dve, lambda e, c0=c0, n=n, j=j, iu=iu: e.tensor_tensor(out=gtmp[j][:, c0:c0 + n], in0=gtmp[j][:, c0:c0 + n], in1=acc_seg(iu, c0, n), op=ALU.mult),
                         reads=acc_bufs(iu, N) + [b_gtmp[j]], writes=[b_gtmp[j]])
                for (co, ct, n) in offs:
                    P.op(act, lambda e, co=co, ct=ct, n=n, fc=fc, j=j: e.activation(out=cuh[:, fc, co:co + n], in_=gtmp[j][:, ct:ct + n], func=AF.Copy),
                         reads=[b_gtmp[j]], writes=[b_aux[fc]])
                P.op(dve, lambda e, fc=fc, j=j: e.tensor_copy(out=hal_t[:, :, fc], in_=gtmp[j][:, T - 2:T]), reads=[b_gtmp[j]], writes=[b_hal])
                if last_round:
                    for q in range(2):
                        P.op(dve, lambda e, fc=fc, j=j, q=q: e.tensor_copy(out=cst[:, q, :, fc], in_=gtmp[j][:, T + 16 * q + 14:T + 16 * q + 16]),
                             reads=[b_gtmp[j]], writes=[b_ost[0]])
            hal2 = hal_t.rearrange("p a b -> p (a b)")
            P.dma(sp, agc_in, hal2, b_agc_in, reads=[b_hal], writes=[b_agc_in])
            allgather(agc_in, agc_mid, agc, [b_agc_in], b_agc, b_agc_mid)
            P.dma(sp, halg, agc.rearrange("(r p) c -> p r c", p=128), b_halg, reads=[b_agc], writes=[b_halg])
            halo2 = halo.rearrange("p a b -> p (a b)")
            P.op(dve, lambda e: e.tensor_scalar(out=halo2, in0=prev7, scalar1=rm_t[:, 16:17], scalar2=None, op0=ALU.mult),
                 reads=[b_prev7, b_halg], writes=[b_halo])
            for j in range(NCORES):
                P.op(dve, lambda e, j=j: e.scalar_tensor_tensor(out=halo2, in0=halg[:, j, :], scalar=rm_t[:, 8 + j:9 + j], in1=halo2, op0=ALU.mult, op1=ALU.add),
                     reads=[b_halg, b_halo], writes=[b_halo])
            P.op(dve, lambda e: e.tensor_copy(out=prev7, in_=halg[:, NCORES - 1, :]), reads=[b_halg, b_halo], writes=[b_prev7])
            iT = acc_get()
            P.op(pe, lambda e: e.transpose(banks[iT][0:FC * 2, 0:128], hal2, identf), reads=[b_hal], writes=[b_bank[iT]])
            P.op(act, lambda e: e.activation(out=ostage[1][0:FC * 2, 0:128], in_=banks[iT][0:FC * 2, 0:128], func=AF.Copy), reads=[b_bank[iT]], writes=[b_ost[1]])
            P.dma(sp, cp[r, :, :].rearrange("t (f p) -> (t f) p", p=128), ostage[1][0:FC * 2, 0:128], b_ost[1], reads=[b_ost[1]], writes=[b_ost[1]])
            if last_round:
                for q in range(2):
                    iT = acc_get()
                    P.op(pe, lambda e, q=q, iT=iT: e.transpose(banks[iT][0:FC * 2, 0:128], ostage[0][:, q * FC * 2:(q + 1) * FC * 2], identf),
                         reads=[b_ost[0]], writes=[b_bank[iT]])
                    P.op(act, lambda e, q=q, iT=iT: e.activation(out=ostage[1][0:FC * 2, 128 * (q + 1):128 * (q + 2)], in_=banks[iT][0:FC * 2, 0:128], func=AF.Copy),
                         reads=[b_bank[iT]], writes=[b_ost[1]])
                    P.dma(sp, cs[q, :, :].rearrange("t (f p) -> (t f) p", p=128), ostage[1][0:FC * 2, 128 * (q + 1):128 * (q + 2)], b_ost[1],
                          reads=[b_ost[1]], writes=[b_ost[1]])
            out_bufs.append(b_ost[1])
            for fc in range(FC):
                P.op(act, lambda e, fc=fc: e.activation(out=cuh[:, fc, 0:2], in_=halo[:, :, fc], func=AF.Copy), reads=[b_halo], writes=[b_aux[fc]])
            if last_round:
                P.dma(sp, ostage[0][0:4 * FC, 0:128], sc.rearrange("q t (f p) -> (q t f) p", p=128), b_ost[0], writes=[b_ost[0]])
                iT = acc_get()
                P.op(pe, lambda e: e.transpose(banks[iT][:, 0:4 * FC], ostage[0][0:4 * FC, 0:128], identf[0:4 * FC, 0:4 * FC]), reads=[b_ost[0]], writes=[b_bank[iT]])
                scv = banks[iT][:, 0:4 * FC].rearrange("p (q t f) -> p q t f", q=2, t=2)
                for q in range(2):
                    for t_ in range(2):
                        co = (T + 2 if q == 0 else T + 20) + t_
                        P.op(act, lambda e, q=q, t_=t_, co=co: e.activation(out=cuh[:, :, co], in_=scv[:, q, t_, :], func=AF.Copy),
                             reads=[b_bank[iT]], writes=b_aux[0:FC])
            for fc in range(FC):
                igb = acc_get(); proj("w_cin", FC * 4 + fc * 2, N, b_h, hrd, igb)
                j = fc % 2
                for (co, ct, n) in offs:
                    P.op(dve, lambda e, co=co, ct=ct, n=n, fc=fc, j=j: e.tensor_scalar(out=gtmp[j][:, ct:ct + n], in0=cuh[:, fc, co:co + n],
                                                                                   scalar1=cw_t[:, 2, fc:fc + 1], scalar2=None, op0=ALU.mult),
                         reads=[b_aux[fc]], writes=[b_gtmp[j]])
                    P.op(dve, lambda e, co=co, ct=ct, n=n, fc=fc, j=j: e.scalar_tensor_tensor(out=gtmp[j][:, ct:ct + n], in0=cuh[:, fc, co - 1:co - 1 + n],
                                                                                          scalar=cw_t[:, 1, fc:fc + 1], in1=gtmp[j][:, ct:ct + n], op0=ALU.mult, op1=ALU.add),
                         reads=[b_aux[fc], b_gtmp[j]], writes=[b_gtmp[j]])
                    P.op(dve, lambda e, co=co, ct=ct, n=n, fc=fc, j=j: e.scalar_tensor_tensor(out=gtmp[j][:, ct:ct + n], in0=cuh[:, fc, co - 2:co - 2 + n],
                                                                                          scalar=cw_t[:, 0, fc:fc + 1], in1=gtmp[j][:, ct:ct + n], op0=ALU.mult, op1=ALU.add),
                         reads=[b_aux[fc], b_gtmp[j]], writes=[b_gtmp[j]])
                for (c0, n) in segs(N):
                    P.op(dve, lambda e, c0=c0, n=n, fc=fc, j=j, igb=igb: e.tensor_tensor(out=cuh[:, fc, c0:c0 + n], in0=gtmp[j][:, c0:c0 + n], in1=acc_seg(igb, c0, n), op=ALU.mult),
                         reads=acc_bufs(igb, N) + [b_gtmp[j], b_aux[fc]], writes=[b_aux[fc]])
            for fc in range(FC):
                i = acc_get()
                proj("w_cout", fc * 2, N, b_aux[0:FC], lambda kc, c0, n: cuh[:, kc, c0:c0 + n], i)
                for (c0, n) in segs(N):
                    P.op(dve, lambda e, c0=c0, n=n, fc=fc, i=i: e.tensor_tensor(out=x_sb[:, fc, c0:c0 + n], in0=x_sb[:, fc, c0:c0 + n], in1=acc_seg(i, c0, n), op=ALU.add),
                         reads=acc_bufs(i, N) + [b_x[fc]], writes=[b_x[fc]])
            chk("conv")
            ffn(1, N)
            chk("ffn1")
            fm_norm(N)
            out_blocks = [(yp[r * T + tbk * 128:r * T + tbk * 128 + 128, :], 128, tbk * 128) for tbk in range(NPB)]
            if last_round:
                out_blocks.append((ys[0:EX, :], EX, T))
            FB = 1024 // 128
            oc_ = 0
            for (dst, nt, c0) in out_blocks:
                for f8 in range(0, FC, FB):
                    nf8 = min(FB, FC - f8)
                    so = oc_ % 2
                    oc_ += 1
                    for f4 in range(f8, f8 + nf8, 4):
                        nf = min(4, f8 + nf8 - f4)
                        i = acc_get()
                        for f in range(f4, f4 + nf):
                            j = f % 2
                            P.op(dve, lambda e, f=f, j=j, c0=c0, nt=nt: e.scalar_tensor_tensor(out=gtmp[j][:, 0:nt], in0=x_sb[:, f, c0:c0 + nt], scalar=norms_t[:, 4, f:f + 1],
                                                                                             in1=rstd_t[:, c0:c0 + nt], op0=ALU.mult, op1=ALU.mult),
                                 reads=[b_x[f], b_rstd], writes=[b_gtmp[j]])
                            P.op(pe, lambda e, f=f, f4=f4, j=j, nt=nt, i=i: e.transpose(banks[i][0:nt, (f - f4) * 128:(f - f4 + 1) * 128], gtmp[j][:, 0:nt], identf),
                                 reads=[b_gtmp[j]], writes=[b_bank[i]])
                        P.op(act, lambda e, f4=f4, f8=f8, nf=nf, nt=nt, i=i, so=so: e.activation(out=ostage[so][0:nt, (f4 - f8) * 128:(f4 - f8 + nf) * 128],
                                                                                                in_=banks[i][0:nt, 0:nf * 128], func=AF.Copy),
                             reads=[b_bank[i]], writes=[b_ost[so]])
                    P.dma(sp, dst[:, f8 * 128:(f8 + nf8) * 128], ostage[so][0:nt, 0:nf8 * 128], b_ost[so], reads=[b_ost[so]], writes=[b_ost[so]])
            out_bufs.extend(b_ost)

        try:
            if not _go:
                raise _Stop()
            chk("wag")
            for r in range(R):
                round_body(r)
        except _Stop:
            pass
        P.barrier()
        seen = set()
        for bf in out_bufs:
            sm = bf.dsem
            if sm is not None and id(sm) not in seen and sm.v:
                seen.add(id(sm))
                nc.sync.wait_ge(sm.h, sm.v)
        print("instructions:", P.nins, "sems:", P.nsem)
    return nc


def make_in_maps(cfg, inp):
    D, NH, FC, T, R = cfg.D, cfg.NH, cfg.FC, cfg.T, cfg.ROUNDS
    W = prep_weights(cfg, inp)
    lbl = np.ascontiguousarray(inp["hgrn_lb_logits"].reshape(3, NH, 128).transpose(2, 0, 1)).astype(np.float32)
    nv = np.stack([inp["norm_mix"][0], inp["norm_ffn"][0], inp["norm_mix"][1], inp["norm_ffn"][1], inp["norm_final"]])
    norms = np.ascontiguousarray(nv.reshape(5, FC, 128).transpose(2, 0, 1)).astype(np.float32)
    cw = np.ascontiguousarray(inp["conv_w"][0].reshape(3, FC, 128).transpose(2, 0, 1)).astype(np.float32)
    ogain = np.ascontiguousarray(inp["hgrn_out_gain"][0].reshape(128, 1)).astype(np.float32)
    xp_full = inp["x_prompt"][0]
    consts = make_consts(cfg)
    maps = []
    for c in range(NCORES):
        m = {}
        m["xp"] = np.ascontiguousarray(np.concatenate([xp_full[(r * NCORES + c) * T:(r * NCORES + c + 1) * T] for r in range(R)], axis=0))
        m["xs"] = np.ascontiguousarray(inp["x_sample"][cfg.SPC * c:cfg.SPC * (c + 1)].reshape(cfg.EX, D))
        m["sh"] = np.ascontiguousarray(inp["state_hgrn"][0, cfg.SPC * c:cfg.SPC * (c + 1)])
        m["sc"] = np.ascontiguousarray(inp["state_conv"][0, cfg.SPC * c:cfg.SPC * (c + 1)])
        m["lbl"] = lbl; m["ogain"] = ogain; m["norms"] = norms; m["cw"] = cw
        m["consts"] = consts
        rm = np.zeros((128, 17), np.float32)
        rm[:, c] = 1.0
        if c >= 1:
            rm[:, 8 + c - 1] = 1.0
        else:
            rm[:, 16] = 1.0
        m["rmask"] = rm
        for n in WNAMES:
            PT = PT_OF(n)
            RPR = PT * 16
            for pi, (t0, nt) in enumerate(wpieces(cfg, n)):
                ch = W[n][t0:t0 + nt].reshape(nt // PT, PT * 128, -1)
                m[f"{n}_{pi}"] = np.ascontiguousarray(ch[:, c * RPR:(c + 1) * RPR, :].reshape(nt // PT * RPR, -1))
        maps.append(m)
    return maps


def assemble(cfg, res):
    D, NH, T, R = cfg.D, cfg.NH, cfg.T, cfg.ROUNDS
    yp = np.zeros((1, cfg.SEQ, D), np.float32)
    for c in range(NCORES):
        for r in range(R):
            j = r * NCORES + c
            yp[0, j * T:(j + 1) * T] = res[c]["yp"][r * T:(r + 1) * T]
    ys = np.concatenate([res[c]["ys"].reshape(cfg.SPC, cfg.DS, D) for c in range(NCORES)], axis=0)
    hp = res[0]["hp"].reshape(1, 1, NH, 128, 128)
    hs = np.concatenate([res[c]["hs"] for c in range(NCORES)], axis=0).reshape(1, cfg.DB, NH, 128, 128)
    cp = res[NCORES - 1]["cp"][R - 1].reshape(1, 1, 2, D)
    cs = np.concatenate([res[c]["cs"] for c in range(NCORES)], axis=0).reshape(1, cfg.DB, 2, D)
    return (yp, ys, hp, hs, cp, cs)


_NC_CACHE = {}


def run(cfg, inp):
    key = (cfg.D, cfg.NH, cfg.DFF, cfg.SEQ, cfg.T)
    if key not in _NC_CACHE:
        _NC_CACHE[key] = build(cfg)
    nc = _NC_CACHE[key]
    maps = make_in_maps(cfg, inp)
    res = run_bass_kernel_spmd(nc, maps, core_ids=list(range(NCORES)))
    return assemble(cfg, res.results)


def kernel(**inputs):
    inp = {k: np.asarray(v) for k, v in inputs.items()}
    return run(FULL, inp)
```

```python
import os
import numpy as np
import concourse.bass as bass
import concourse.mybir as mybir
from concourse.bass_utils import run_bass_kernel_spmd
from contextlib import ExitStack

F32 = mybir.dt.float32
BF16 = mybir.dt.bfloat16
AF = mybir.ActivationFunctionType
ALU = mybir.AluOpType
NCORES = 8
EPS = 1e-6


class _Stop(Exception):
    pass


class Cfg:
    stop = None

    def __init__(self, D=4096, NH=32, DFF=11008, SEQ=16384, T=512, DB=16, DS=16, G=4, NQ=4):
        self.D, self.NH, self.DFF, self.SEQ, self.T, self.DB, self.DS, self.G = D, NH, DFF, SEQ, T, DB, DS, G
        self.FC = D // 128
        self.KH = self.FC // 2
        self.NS = DFF // 128
        self.ROUNDS = SEQ // (NCORES * T)
        self.NPB = T // 128
        self.NCH = T // 64
        self.SPC = DB // NCORES
        self.EX = self.SPC * DS
        self.NX = T + self.EX
        self.NQ = NQ
        base = self.NS // NQ
        rem = self.NS % NQ
        self.QS = [base + (1 if i < rem else 0) for i in range(NQ)]
        self.QMAX = max(self.QS)
        self.NG = NH // G
        assert DS == 16 and self.SPC == 2 and D == NH * 128


FULL = Cfg()


def tile_w(W, kh):
    K, NO = W.shape
    kc = K // 128
    nh = kc // kh
    t = W.reshape(nh, kh, 128, NO // 128, 128)
    t = t.transpose(3, 0, 2, 1, 4)
    return np.ascontiguousarray(t).reshape(NO // 128 * nh, 128, kh * 128)


def prep_weights(cfg, inp):
    D, NH, NS = cfg.D, cfg.NH, cfg.NS
    out = {}
    W = inp["hgrn_w_in"][0]
    cols = np.concatenate([np.arange(s * D + h * 128, s * D + h * 128 + 128) for h in range(NH) for s in range(4)])
    out["w_hin"] = tile_w(W[:, cols], cfg.KH)
    out["w_hout"] = tile_w(inp["hgrn_w_out"][0], cfg.KH)
    W = inp["conv_w_in"][0]
    cols = np.concatenate([np.arange(s * D + f * 128, s * D + f * 128 + 128) for f in range(cfg.FC) for s in (1, 2)]
                          + [np.arange(0, D)])
    out["w_cin"] = tile_w(W[:, cols], cfg.KH)
    out["w_cout"] = tile_w(inp["conv_w_out"][0], cfg.KH)
    for l in range(2):
        W = inp["ffn_w_in"][l]
        cols = np.concatenate([np.arange(s * cfg.DFF + sl * 128, s * cfg.DFF + sl * 128 + 128)
                               for sl in range(NS) for s in range(2)])
        out[f"w_fin{l}"] = tile_w(W[:, cols], cfg.KH)
        W = inp["ffn_w_out"][l]
        tl = np.zeros((cfg.NQ, cfg.FC, 128, cfg.QMAX, 128), np.float32)
        s0 = 0
        for q in range(cfg.NQ):
            n = cfg.QS[q]
            blk = W[s0 * 128:(s0 + n) * 128, :].reshape(n, 128, cfg.FC, 128)
            tl[q, :, :, :n, :] = blk.transpose(2, 1, 0, 3)
            s0 += n
        out[f"w_fout{l}"] = tl.reshape(cfg.NQ * cfg.FC, 128, cfg.QMAX * 128)
    return out


WNAMES = ["w_hin", "w_hout", "w_fin0", "w_fout0", "w_cin", "w_cout", "w_fin1", "w_fout1"]


def wshape(cfg, name):
    D, NH, NS, FC, KH = cfg.D, cfg.NH, cfg.NS, cfg.FC, cfg.KH
    c = KH * 128
    if name == "w_hin":
        return (NH * 4 * 2, c)
    if name in ("w_hout", "w_cout"):
        return (FC * 2, c)
    if name == "w_cin":
        return (FC * 3 * 2, c)
    if name.startswith("w_fin"):
        return (NS * 2 * 2, c)
    return (cfg.NQ * FC, cfg.QMAX * 128)


def PT_OF(name):
    return 4 if name.startswith("w_fout") else 8


def wpieces(cfg, name, pmax=int(os.environ.get("PMAX", "128"))):
    nt, c = wshape(cfg, name)
    out = []
    t0 = 0
    while t0 < nt:
        n = min(pmax, nt - t0)
        assert n % NCORES == 0
        out.append((t0, n))
        t0 += n
    return out


class Sem:
    def __init__(self, h):
        self.h = h
        self.v = 0


class Buf:
    __slots__ = ("name", "w", "r", "dsem")

    def __init__(self, name=""):
        self.name = name
        self.w = None
        self.r = []
        self.dsem = None


class Eng:
    def __init__(self, name, handle, sem, skip_self=False):
        self.name, self.e, self.sem, self.skip_self = name, handle, sem, skip_self
        self.seen = {}


class Prog:
    def __init__(self, nc, es):
        self.nc = nc
        self.es = es
        self.nsem = 0
        mk = self.new_sem
        self.pe = Eng("pe", nc.tensor, mk("pe"), skip_self=True)
        self.act = Eng("act", nc.scalar, mk("act"))
        self.dve = Eng("dve", nc.vector, mk("dve"))
        self.pool = Eng("pool", nc.gpsimd, mk("pool"))
        self.sp = Eng("sp", nc.sync, mk("sp"))
        self.engs = [self.pe, self.act, self.dve, self.pool, self.sp]
        self.nins = 0

    def new_sem(self, name):
        self.nsem += 1
        sm = Sem(self.es.enter_context(self.nc.semaphore(f"s_{name}_{self.nsem}")))
        if not hasattr(self, "sems"):
            self.sems = []
        self.sems.append(sm)
        return sm

    def _waits(self, eng, reads, writes):
        need = {}

        def req(ev):
            if ev is None:
                return
            s, v = ev
            if eng.skip_self and s is eng.sem:
                return
            if eng.seen.get(s, 0) >= v:
                return
            if need.get(s, 0) < v:
                need[s] = v
        for b in reads:
            req(b.w)
        for b in writes:
            req(b.w)
            for ev in b.r:
                req(ev)
        for s, v in need.items():
            eng.e.wait_ge(s.h, v)
            eng.seen[s] = v
            self.nins += 1

    def op(self, eng, fns, reads=(), writes=(), sem=None, inc=1):
        if callable(fns):
            fns = [fns]
        self._waits(eng, reads, writes)
        ins = None
        for f in fns:
            ins = f(eng.e)
            self.nins += 1
        s = eng.sem if sem is None else sem
        s.v += inc
        ins.then_inc(s.h, inc)
        ev = (s, s.v)
        for b in reads:
            b.r.append(ev)
        for b in writes:
            b.w = ev
            b.r = []
        return ev

    def dma(self, eng, out, in_, sembuf, reads=(), writes=(), **kw):
        if isinstance(sembuf, Sem):
            sem = sembuf
        else:
            if sembuf.dsem is None:
                sembuf.dsem = self.new_sem("d" + sembuf.name)
            sem = sembuf.dsem
        return self.op(eng, lambda e: e.dma_start(out=out, in_=in_, **kw), reads, writes, sem=sem, inc=16)

    def barrier(self):
        for e in self.engs:
            for sm in self.sems:
                if sm.v == 0 or (sm is e.sem):
                    continue
                if e.seen.get(sm, 0) < sm.v:
                    e.e.wait_ge(sm.h, sm.v)
                    e.seen[sm] = sm.v


class Arena:
    def __init__(self, t, words):
        self.t, self.words, self.off = t, words, 0

    def mark(self):
        return self.off

    def reset(self, m):
        self.off = m

    def f32(self, shape):
        n = int(np.prod(shape[1:]))
        self.off = (self.off + 15) // 16 * 16
        ap = self.t[0:shape[0], self.off:self.off + n]
        self.off += n
        assert self.off <= self.words, ("arena overflow", self.off, self.words)
        return _reshape(ap, shape)

    def bf16(self, shape):
        n = int(np.prod(shape[1:]))
        nw = (n + 1) // 2
        self.off = (self.off + 15) // 16 * 16
        ap = self.t[0:shape[0], self.off:self.off + nw].bitcast(BF16)[:, 0:n]
        self.off += nw
        assert self.off <= self.words, ("arena overflow", self.off, self.words)
        return _reshape(ap, shape)


def _reshape(ap, shape):
    if len(shape) == 2:
        return ap
    if len(shape) == 3:
        return ap.rearrange("p (a b) -> p a b", a=shape[1], b=shape[2])
    if len(shape) == 4:
        return ap.rearrange("p (a b c) -> p a b c", a=shape[1], b=shape[2], c=shape[3])
    raise ValueError


def const_layout(cfg):
    NX = cfg.NX
    o = {}
    c = 0
    for name, n in (("identf", 128), ("maskA", 128), ("maskS", 32), ("smask", NX), ("sm01", 2), ("pm64", 2)):
        o[name] = (c, n)
        c += n
    return o, c


def make_consts(cfg):
    lay, cwid = const_layout(cfg)
    T, NX = cfg.T, cfg.NX
    C = np.zeros((128, cwid), np.float32)
    o, n = lay["identf"]; C[:, o:o + n] = np.eye(128, dtype=np.float32)
    s_ = np.arange(128)[:, None]; t_ = np.arange(128)[None, :]
    o, n = lay["maskA"]; C[:, o:o + n] = ((s_ <= t_) & (s_ // 64 == t_ // 64)).astype(np.float32)
    s2 = np.arange(32)[:, None]; t2 = np.arange(32)[None, :]
    o, n = lay["maskS"]; C[0:32, o:o + n] = ((s2 <= t2) & (s2 // 16 == t2 // 16)).astype(np.float32)
    sm = np.ones(NX, np.float32); sm[0:T:64] = 0.0; sm[T:NX:16] = 0.0
    o, n = lay["smask"]; C[:, o:o + n] = sm[None, :]
    o, n = lay["sm01"]; C[0:16, o] = 1.0; C[16:32, o + 1] = 1.0
    o, n = lay["pm64"]; C[0:64, o] = 1.0; C[64:128, o + 1] = 1.0
    return C


def build(cfg):
    nc = bass.Bass("TRN2", target_bir_lowering=False, num_devices=NCORES)
    D, NH, FC, KH, T, EX, NX, G = cfg.D, cfg.NH, cfg.FC, cfg.KH, cfg.T, cfg.EX, cfg.NX, cfg.G
    NCH, NPB, R, NS = cfg.NCH, cfg.NPB, cfg.ROUNDS, cfg.NS
    WC = KH * 128
    WCMAX = max(WC, cfg.QMAX * 128)
    clay, CWID = const_layout(cfg)

    def din(name, shape, dt=F32):
        return nc.dram_tensor(name, list(shape), dt, kind="ExternalInput").ap()

    def dout(name, shape):
        return nc.dram_tensor(name, list(shape), F32, kind="ExternalOutput").ap()

    def dint(name, shape):
        return nc.dram_tensor(name, list(shape), F32, kind="Internal").ap()

    xp = din("xp", [R * T, D])
    xs = din("xs", [EX, D])
    sh = din("sh", [cfg.SPC, NH, 128, 128])
    sc = din("sc", [cfg.SPC, 2, D])
    lbl = din("lbl", [128, 3, NH])
    ogain = din("ogain", [128, 1])
    norms = din("norms", [128, 5, FC])
    cw = din("cw", [128, 3, FC])
    rmask = din("rmask", [128, 17])
    consts = din("consts", [128, CWID])
    wsh, wag_in, wag = {}, {}, {}
    WP = []
    for n in WNAMES:
        _, c = wshape(cfg, n)
        PT = PT_OF(n)
        for pi, (t0, nt) in enumerate(wpieces(cfg, n)):
            key = f"{n}_{pi}"
            assert nt % PT == 0
            WP.append((key, n, t0, nt, PT))
            wsh[key] = din(key, [nt // NCORES * 128, c])
            wag_in[key] = nc.dram_tensor(key + "_i", [nt // NCORES * 128, c], BF16, kind="Internal").ap()
            wag[key] = nc.dram_tensor(key + "_g", [nt * 128, c], BF16, kind="Internal").ap()
    NMID = 6
    midA = [nc.dram_tensor(f"midA{i}", [4 * 16 * PT_OF("w_hin"), WC], BF16, kind="Internal").ap() for i in range(NMID)]
    midB = [nc.dram_tensor(f"midB{i}", [4 * 16 * PT_OF("w_fout0"), cfg.QMAX * 128], BF16, kind="Internal").ap() for i in range(NMID)]
    yp = dout("yp", [R * T, D])
    ys = dout("ys", [EX, D])
    hp = dout("hp", [NH, 128, 128])
    hs = dout("hs", [cfg.SPC, NH, 128, 128])
    cp = dout("cp", [R, 2, D])
    cs = dout("cs", [cfg.SPC, 2, D])
    agh_in = dint("agh_in", [G * 128, 129])
    agh = dint("agh", [NCORES * G * 128, 129])
    agh_mid = dint("agh_mid", [NCORES // 2 * G * 128, 129])
    agc_in = dint("agc_in", [128, FC * 2])
    agc = dint("agc", [NCORES * 128, FC * 2])
    agc_mid = dint("agc_mid", [NCORES // 2 * 128, FC * 2])
    sround = dint("sround", [128, NH, 128])

    es = ExitStack()
    with es:
        AW = 52000
        arena_t = es.enter_context(nc.sbuf_tensor("arena", [128, AW], F32))
        ar = Arena(arena_t, AW)
        banks = [es.enter_context(nc.psum_tensor(f"bank{i}", [128, 512], F32)) for i in range(6)]
        bank6b = es.enter_context(nc.psum_tensor("bank6b", [128, 1024], BF16))
        banks.append(None)
        banks.append(es.enter_context(nc.psum_tensor("bank7", [128, 512], F32)))
        P = Prog(nc, es)
        pe, act, dve, pool, sp = P.pe, P.act, P.dve, P.pool, P.sp
        B = Buf

        def chk(name):
            if cfg.stop == name:
                raise _Stop()

        const_t = ar.f32([128, CWID])
        cv = lambda k: const_t[:, clay[k][0]:clay[k][0] + clay[k][1]]
        identf = cv("identf"); smask = cv("smask"); sm01 = cv("sm01")[0:32, :]; pm64 = cv("pm64")
        identb = ar.bf16([128, 128]); onesb = ar.bf16([128, 128]); zerob = ar.bf16([128, 128])
        maskA = ar.bf16([128, 128]); maskS = ar.bf16([32, 32]); ones_f = ar.f32([128, 16])
        norms_t = ar.f32([128, 5, FC]); cw_t = ar.f32([128, 3, FC]); rm_t = ar.f32([128, 17])
        lb_t = ar.f32([128, NH]); oml_t = ar.f32([128, NH]); noml_t = ar.f32([128, NH])
        lbl_t = ar.f32([128, 3, NH]); og_t = ar.f32([128, 1])
        prev7 = ar.f32([128, 2 * FC])
        NSLOT = 6
        wslots = [ar.bf16([128, WCMAX]) for _ in range(NSLOT)]
        h_t = ar.bf16([128, FC, NX])
        aux = ar.bf16([128, FC, NX + 8])
        rstd_t = ar.f32([128, NX])
        gtmp = [ar.f32([128, NX]) for _ in range(2)]
        sqt = [ar.bf16([128, NX]) for _ in range(2)]
        ostage = [ar.f32([128, 1024]) for _ in range(2)]
        hal_t = ar.f32([128, 2, FC]); halg = ar.f32([128, NCORES, 2 * FC]); halo = ar.f32([128, 2, FC])
        XR0 = ar.mark()
        x_sb = ar.f32([128, FC, NX])
        XRx = ar.mark()
        XR1 = AW
        ar.reset(XR0)
        xst = [ar.f32([128, D // 2]) for _ in range(2)]
        junk = ar.f32([128, D // 2])
        ssq = ar.f32([128, 8])
        XRa = ar.mark()
        ar.reset(XR0)
        gs_ol = ar.bf16([128, G, NX]); gs_qc = ar.bf16([128, G, NX]); gs_sg = ar.bf16([128, G, NX])
        tq = ar.f32([128, NX]); tsig = ar.f32([128, NX]); tg = ar.f32([128, NX]); tk = ar.f32([128, NX])
        tb = ar.f32([128, NX]); teb = ar.f32([128, NX]); tenb = ar.f32([128, NX])
        tqt = ar.bf16([128, NX]); tkt = ar.bf16([128, NX]); tvs = [ar.bf16([128, NX]), ar.bf16([128, NX])]
        ktok = ar.bf16([128, NPB, 128]); vtok = ar.bf16([128, NPB, 128])
        ktokm = ar.bf16([128, NPB, 2, 128])
        vtoks = ar.bf16([32, 128]); kt01 = ar.bf16([32, 2, 128]); ktoks = ar.bf16([32, 128])
        atm = ar.bf16([128, NPB, 128]); atms = ar.bf16([32, 32])
        td = ar.f32([128, NCH + 2]); tci = ar.f32([128, NCH]); tE = ar.f32([128, NCH])
        Mf = [ar.f32([128, 128]) for _ in range(2)]; Mtmp = ar.f32([128, 128]); Mb = ar.bf16([128, NCH, 128])
        Mbs = ar.bf16([128, 2, 128])
        LD = ar.f32([128, G, 129]); LDj = [ar.f32([128, G, 129]) for _ in range(2)]
        S_t = ar.f32([128, G, 128]); Sin = ar.f32([128, G, 128]); Stmp = ar.f32([128, G, 128]); M0b = ar.bf16([128, G, 128])
        S0s = ar.f32([128, 2, G, 128]); NSs = ar.f32([128, 2, G, 128])
        to32 = ar.f32([128, NX]); osq = ar.bf16([128, NX]); t1 = ar.f32([128, NX]); rs2 = ar.f32([128, NX])
        XRb = ar.mark()
        print("arena words: xsb_end", XRx, "p0_end", XRa, "p1_end", XRb, "of", AW)
        ar.reset(max(XRx, XRa, XRb))

        b_const = B("const"); b_prev7 = B("prev7")
        b_wslot = [B(f"ws{i}") for i in range(NSLOT)]
        b_h = [B(f"h{i}") for i in range(FC)]
        b_aux = [B(f"aux{i}") for i in range(max(FC, cfg.QMAX))]
        b_x = [B(f"x{i}") for i in range(FC)]
        b_bank = [B(f"bank{i}") for i in range(8)]
        b_rstd = B("rstd"); b_gtmp = [B("g0"), B("g1")]; b_sq = [B("q0"), B("q1")]
        b_ost = [B("o0"), B("o1")]
        s_cc = P.new_sem("cc")
        b_wag = {}
        RG1 = [[0, 1, 2, 3], [4, 5, 6, 7]]
        RG2 = [[0, 4], [1, 5], [2, 6], [3, 7]]

        def allgather(in_ap, mid_ap, out_ap, reads, outbuf, midbuf):
            P.op(pool, lambda e: e.collective_compute("AllGather", ALU.bypass, replica_groups=RG1, ins=[in_ap], outs=[mid_ap]),
                 reads=reads, writes=[midbuf], sem=s_cc, inc=1)
            P.op(pool, lambda e: e.collective_compute("AllGather", ALU.bypass, replica_groups=RG2, ins=[mid_ap], outs=[out_ap]),
                 reads=[midbuf], writes=[outbuf], sem=s_cc, inc=1)
        b_xst = [B("xst0"), B("xst1")]; b_ssq = B("ssq"); b_junk = B("junk")
        b_gs = [B(f"gs{i}") for i in range(G)]
        bt = {k: B(k) for k in ("tq", "tsig", "tg", "tk", "tb", "teb", "tenb", "tqt", "tkt", "tv0", "tv1", "ktok", "vtok", "kts",
                                "atm", "td", "tci", "tE", "ktokm", "M0", "M1", "Mtmp", "Mb", "Mbs", "LD", "S", "Sin", "Stmp", "M0b",
                                "S0s", "NSs", "to32", "osq", "t1", "rs2")}
        b_ld = [B("ld0"), B("ld1")]
        b_sr = [B(f"sr{g}") for g in range(cfg.NG)]
        b_agh_in = B("agh_in"); b_agh = B("agh"); b_agc_in = B("agc_in"); b_agc = B("agc")
        b_agh_mid = B("agh_mid"); b_agc_mid = B("agc_mid")
        b_hal = B("hal"); b_halg = B("halg"); b_halo = B("halo")
        out_bufs = []

        P.dma(sp, const_t, consts, b_const, writes=[b_const])
        for dst, src in ((norms_t, norms), (cw_t, cw), (rm_t, rmask), (lbl_t, lbl), (og_t, ogain)):
            P.dma(sp, dst, src, b_const, writes=[b_const])
        for ap_, val in ((onesb, 1.0), (zerob, 0.0), (prev7, 0.0), (ones_f, 1.0)):
            P.op(dve, lambda e, ap_=ap_, val=val: e.memset(ap_, val), writes=[b_prev7])
        P.op(dve, lambda e: e.tensor_copy(out=identb, in_=identf), reads=[b_const], writes=[b_prev7])
        P.op(dve, lambda e: e.tensor_copy(out=maskA, in_=cv("maskA")), reads=[b_const], writes=[b_prev7])
        P.op(dve, lambda e: e.tensor_copy(out=maskS, in_=cv("maskS")[0:32, :]), reads=[b_const], writes=[b_prev7])
        e3 = gtmp[0][:, 0:3 * NH].rearrange("p (a b) -> p a b", a=3)
        P.op(act, lambda e: e.activation(out=e3, in_=lbl_t, func=AF.Exp), reads=[b_const], writes=[b_gtmp[0]])
        s1 = gtmp[1][:, 0:NH]
        P.op(dve, lambda e: e.tensor_tensor(out=s1, in0=e3[:, 0, :], in1=e3[:, 1, :], op=ALU.add), reads=[b_gtmp[0]], writes=[b_gtmp[1]])
        P.op(dve, lambda e: e.tensor_tensor(out=s1, in0=s1, in1=e3[:, 2, :], op=ALU.add), reads=[b_gtmp[0], b_gtmp[1]], writes=[b_gtmp[1]])
        P.op(dve, lambda e: e.reciprocal(out=s1, in_=s1), reads=[b_gtmp[1]], writes=[b_gtmp[1]])
        P.op(dve, lambda e: e.tensor_tensor(out=lb_t, in0=e3[:, 0, :], in1=s1, op=ALU.mult), reads=[b_gtmp[0], b_gtmp[1]], writes=[b_prev7])
        P.op(dve, lambda e: e.tensor_scalar(out=oml_t, in0=lb_t, scalar1=-1.0, scalar2=1.0, op0=ALU.mult, op1=ALU.add),
             reads=[b_prev7], writes=[b_prev7])
        P.op(dve, lambda e: e.tensor_scalar(out=noml_t, in0=lb_t, scalar1=1.0, scalar2=-1.0, op0=ALU.mult, op1=ALU.add),
             reads=[b_prev7], writes=[b_prev7])
        P.barrier()
        try:
            chk("consts")
            _go = True
        except _Stop:
            _go = False
        b_const = B("const_ro")

        RG = [list(range(NCORES))]
        b_win = {}
        wp_by_name = {}
        gstate = {}
        b_midA = [B(f"midA{i}") for i in range(NMID)]
        b_midB = [B(f"midB{i}") for i in range(NMID)]
        midctr = {"A": 0, "B": 0}
        LOOKAHEAD = 10
        for (key, n, t0, nt, PT) in (WP if _go else []):
            wp_by_name.setdefault(n, []).append((key, t0, nt, PT))
            rows, c = wsh[key].shape
            b_win[key] = B(key + "_i")
            for r0 in range(0, rows, 1024):
                r1 = min(rows, r0 + 1024)
                P.dma(pool, wag_in[key][r0:r1, :], wsh[key][r0:r1, :], b_win[key], writes=[b_win[key]])
            npc = nt // PT
            gstate[key] = {"s1": 0, "s2": 0, "np": npc, "PT": PT, "mids": {}}
            b_wag[key] = [B(f"{key}_p{i}") for i in range(npc)]

        def gather_step(key, upto):
            st = gstate[key]
            PT = st["PT"]
            RPR = PT * 16
            kind = "B" if PT == PT_OF("w_fout0") else "A"
            mids, bmids = (midB, b_midB) if kind == "B" else (midA, b_midA)
            upto = min(upto, st["np"] - 1)

            def s2(p):
                k = st["mids"][p]
                P.op(pool, lambda e: e.collective_compute("AllGather", ALU.bypass, replica_groups=RG2, ins=[mids[k]],
                                                          outs=[wag[key][p * PT * 128:(p + 1) * PT * 128, :]]),
                     reads=[bmids[k]], writes=[b_wag[key][p]], sem=s_cc, inc=1)
            while st["s1"] <= upto:
                p = st["s1"]
                k = midctr[kind] % NMID
                midctr[kind] += 1
                st["mids"][p] = k
                P.op(pool, lambda e: e.collective_compute("AllGather", ALU.bypass, replica_groups=RG1,
                                                          ins=[wag_in[key][p * RPR:(p + 1) * RPR, :]], outs=[mids[k]]),
                     reads=[b_win[key]], writes=[bmids[k]], sem=s_cc, inc=1)
                st["s1"] += 1
                while st["s2"] < st["s1"] - 1:
                    s2(st["s2"])
                    st["s2"] += 1
            if st["s1"] == st["np"]:
                while st["s2"] < st["np"]:
                    s2(st["s2"])
                    st["s2"] += 1

        def ensure_piece(key, p):
            st = gstate[key]
            gather_step(key, p + LOOKAHEAD)
            while st["s2"] <= p:
                k = st["s2"]
                gather_flush_one(key)

        def gather_flush_one(key):
            st = gstate[key]
            PT = st["PT"]
            kind = "B" if PT == PT_OF("w_fout0") else "A"
            mids, bmids = (midB, b_midB) if kind == "B" else (midA, b_midA)
            p = st["s2"]
            k = st["mids"][p]
            P.op(pool, lambda e: e.collective_compute("AllGather", ALU.bypass, replica_groups=RG2, ins=[mids[k]],
                                                      outs=[wag[key][p * PT * 128:(p + 1) * PT * 128, :]]),
                 reads=[bmids[k]], writes=[b_wag[key][p]], sem=s_cc, inc=1)
            st["s2"] += 1

        wctr = [0]

        def wload(name, tile_idx, ncols):
            i = wctr[0] % NSLOT
            wctr[0] += 1
            for (key, t0, nt, PT) in wp_by_name[name]:
                if t0 <= tile_idx < t0 + nt:
                    break
            li = tile_idx - t0
            pidx = li // PT
            ensure_piece(key, pidx)
            src = wag[key][li * 128:(li + 1) * 128, 0:ncols]
            P.dma(sp, wslots[i][:, 0:ncols], src, b_wslot[i], reads=[b_wag[key][pidx]], writes=[b_wslot[i]])
            return wslots[i], b_wslot[i]

        NACC = 6
        accctr = [0]

        def acc_get():
            i = accctr[0] % NACC
            accctr[0] += 1
            return i

        def acc_seg(i, c0, n):
            if c0 < T:
                return banks[i][:, c0:c0 + n]
            return banks[7][:, 32 * i + (c0 - T):32 * i + (c0 - T) + n]

        def acc_bufs(i, N):
            return [b_bank[i]] + ([b_bank[7]] if N > T else [])

        def segs(N):
            return [(0, T)] + ([(T, N - T)] if N > T else [])

        def mm_acc(i, N, steps, reads, first, last):
            fl = []
            for (c0, n) in segs(N):
                for k, (lh, rf) in enumerate(steps):
                    fl.append(lambda e, c0=c0, n=n, lh=lh, rf=rf, k=k: e.matmul(
                        acc_seg(i, c0, n), lh, rf(c0, n), start=(first and k == 0), stop=(last and k == len(steps) - 1)))
            P.op(pe, fl, reads=reads, writes=acc_bufs(i, N))

        def proj(name, tile0, N, rhs_bufs, rhs_fn, i):
            for half in range(2):
                wt, wb = wload(name, tile0 + half, WC)
                w3 = wt[:, 0:WC].rearrange("p (k c) -> p k c", c=128)
                steps = [(w3[:, k, :], (lambda c0, n, kc=half * KH + k: rhs_fn(kc, c0, n))) for k in range(KH)]
                mm_acc(i, N, steps, [wb] + list(rhs_bufs), first=(half == 0), last=(half == 1))

        def rsqrt_inplace(ap, buf):
            P.op(act, lambda e: e.activation(out=ap, in_=ap, func=AF.Sqrt), reads=[buf], writes=[buf])
            P.op(dve, lambda e: e.reciprocal(out=ap, in_=ap), reads=[buf], writes=[buf])

        def fm_norm(N):
            i = acc_get()
            for fc in range(FC):
                j = fc % 2
                P.op(act, lambda e, fc=fc, j=j: e.activation(out=sqt[j][:, 0:N], in_=x_sb[:, fc, 0:N], func=AF.Square),
                     reads=[b_x[fc]], writes=[b_sq[j]])
                mm_acc(i, N, [(onesb, lambda c0, n, j=j: sqt[j][:, c0:c0 + n])], [b_sq[j]], first=(fc == 0), last=(fc == FC - 1))
            for (c0, n) in segs(N):
                P.op(dve, lambda e, c0=c0, n=n: e.tensor_scalar(out=rstd_t[:, c0:c0 + n], in0=acc_seg(i, c0, n), scalar1=1.0 / D,
                                                                 scalar2=EPS, op0=ALU.mult, op1=ALU.add),
                     reads=acc_bufs(i, N), writes=[b_rstd])
            rsqrt_inplace(rstd_t[:, 0:N], b_rstd)

        def fm_norm_apply(N, gain_idx):
            for fc in range(FC):
                P.op(dve, lambda e, fc=fc: e.scalar_tensor_tensor(out=h_t[:, fc, 0:N], in0=x_sb[:, fc, 0:N],
                                                                  scalar=norms_t[:, gain_idx, fc:fc + 1], in1=rstd_t[:, 0:N],
                                                                  op0=ALU.mult, op1=ALU.mult),
                     reads=[b_x[fc], b_rstd], writes=[b_h[fc]])

        actv = aux.rearrange("p a b -> p (a b)")[:, 0:cfg.QMAX * NX].rearrange("p (a b) -> p a b", b=NX)
        hrd = lambda kc, c0, n: h_t[:, kc, c0:c0 + n]

        def ffn(l, N):
            fm_norm(N)
            fm_norm_apply(N, 1 + 2 * l)
            s0 = 0
            for q in range(cfg.NQ):
                nq = cfg.QS[q]
                for sl in range(nq):
                    s = s0 + sl
                    ig = acc_get()
                    proj(f"w_fin{l}", (s * 2 + 0) * 2, N, b_h, hrd, ig)
                    iu = acc_get()
                    proj(f"w_fin{l}", (s * 2 + 1) * 2, N, b_h, hrd, iu)
                    j = s % 2
                    for (c0, n) in segs(N):
                        P.op(act, lambda e, c0=c0, n=n, j=j, ig=ig: e.activation(out=gtmp[j][:, c0:c0 + n], in_=acc_seg(ig, c0, n), func=AF.Silu),
                             reads=acc_bufs(ig, N), writes=[b_gtmp[j]])
                    for (c0, n) in segs(N):
                        P.op(dve, lambda e, c0=c0, n=n, j=j, iu=iu, sl=sl: e.tensor_tensor(out=actv[:, sl, c0:c0 + n], in0=gtmp[j][:, c0:c0 + n],
                                                                                        in1=acc_seg(iu, c0, n), op=ALU.mult),
                             reads=acc_bufs(iu, N) + [b_gtmp[j]], writes=[b_aux[sl]])
                for fc in range(FC):
                    wt, wb = wload(f"w_fout{l}", q * FC + fc, cfg.QMAX * 128)
                    w3 = wt[:, 0:cfg.QMAX * 128].rearrange("p (k c) -> p k c", c=128)
                    i = acc_get()
                    steps = [(w3[:, k, :], (lambda c0, n, k=k: actv[:, k, c0:c0 + n])) for k in range(nq)]
                    mm_acc(i, N, steps, [wb] + b_aux[0:nq], first=True, last=True)
                    for (c0, n) in segs(N):
                        P.op(dve, lambda e, c0=c0, n=n, fc=fc, i=i: e.tensor_tensor(out=x_sb[:, fc, c0:c0 + n], in0=x_sb[:, fc, c0:c0 + n],
                                                                                 in1=acc_seg(i, c0, n), op=ALU.add),
                             reads=acc_bufs(i, N) + [b_x[fc]], writes=[b_x[fc]])
                s0 += nq

        bk6 = bank6b[:, :]
        psTk = bk6[:, 0:NPB * 128].rearrange("p (a b) -> p a b", b=128)
        psTv = bk6[:, 512:512 + NPB * 128].rearrange("p (a b) -> p a b", b=128)
        psTks = bk6[:, 0:128]; psTvs = bk6[:, 512:640]
        psATs = banks[7][:, 320:352]
        HW_ = D // 2

        def round_body(r):
            last_round = (r == R - 1)
            N = NX if last_round else T
            P.barrier()
            tok_blocks = [(xp[r * T + tbk * 128: r * T + tbk * 128 + 128, :], 128, tbk * 128) for tbk in range(NPB)]
            if last_round:
                tok_blocks.append((xs[0:EX, :], EX, T))
            for (src, nt, c0) in tok_blocks:
                for hf in range(2):
                    P.dma(sp, xst[hf][0:nt, :], src[:, hf * HW_:(hf + 1) * HW_], b_xst[hf], writes=[b_xst[hf]])
                    P.op(act, lambda e, hf=hf, nt=nt: e.activation(out=junk[0:nt, :], in_=xst[hf][0:nt, :], func=AF.Square,
                                                                 accum_out=ssq[0:nt, hf:hf + 1]),
                         reads=[b_xst[hf]], writes=[b_ssq, b_junk])
                P.op(dve, lambda e, nt=nt: e.tensor_tensor(out=ssq[0:nt, 2:3], in0=ssq[0:nt, 0:1], in1=ssq[0:nt, 1:2], op=ALU.add),
                     reads=[b_ssq], writes=[b_ssq])
                P.op(dve, lambda e, nt=nt: e.tensor_scalar(out=ssq[0:nt, 2:3], in0=ssq[0:nt, 2:3], scalar1=1.0 / D, scalar2=EPS,
                                                          op0=ALU.mult, op1=ALU.add), reads=[b_ssq], writes=[b_ssq])
                P.op(act, lambda e, nt=nt: e.activation(out=ssq[0:nt, 2:3], in_=ssq[0:nt, 2:3], func=AF.Sqrt), reads=[b_ssq], writes=[b_ssq])
                P.op(dve, lambda e, nt=nt: e.reciprocal(out=ssq[0:nt, 3:4], in_=ssq[0:nt, 2:3]), reads=[b_ssq], writes=[b_ssq])
                for hf in range(2):
                    P.op(dve, lambda e, hf=hf, nt=nt: e.tensor_scalar(out=xst[hf][0:nt, :], in0=xst[hf][0:nt, :], scalar1=ssq[0:nt, 3:4],
                                                                    scalar2=None, op0=ALU.mult), reads=[b_ssq, b_xst[hf]], writes=[b_xst[hf]])
                    for f4 in range(0, FC // 2, 4):
                        nf = min(4, FC // 2 - f4)
                        i = acc_get()
                        P.op(pe, [lambda e, hf=hf, f=f, f4=f4, nt=nt, i=i: e.transpose(banks[i][:, (f - f4) * 128:(f - f4) * 128 + nt],
                                                                                         xst[hf][0:nt, f * 128:(f + 1) * 128], identf[0:nt, 0:nt])
                                  for f in range(f4, f4 + nf)], reads=[b_xst[hf]], writes=[b_bank[i]])
                        for f in range(f4, f4 + nf):
                            fcg = hf * (FC // 2) + f
                            P.op(dve, lambda e, f=f, f4=f4, fcg=fcg, nt=nt, c0=c0, i=i: e.tensor_scalar(
                                out=h_t[:, fcg, c0:c0 + nt], in0=banks[i][:, (f - f4) * 128:(f - f4) * 128 + nt],
                                scalar1=norms_t[:, 0, fcg:fcg + 1], scalar2=None, op0=ALU.mult),
                                reads=[b_bank[i]], writes=[b_h[fcg]])
            P.barrier()
            chk("p0")

            for grp in range(cfg.NG):
                if last_round:
                    for q in range(2):
                        P.dma(sp, S0s[:, q, :, :], sh[q, grp * G:(grp + 1) * G, :, :].rearrange("h k v -> k h v"), bt["S0s"], writes=[bt["S0s"]])
                def emit_proj(hg):
                    hd = grp * G + hg
                    tv = tvs[hg % 2]; btv = bt[f"tv{hg % 2}"]
                    iq = acc_get(); proj("w_hin", (hd * 4 + 0) * 2, N, b_h, hrd, iq)
                    if_ = acc_get(); proj("w_hin", (hd * 4 + 1) * 2, N, b_h, hrd, if_)
                    ii = acc_get(); proj("w_hin", (hd * 4 + 2) * 2, N, b_h, hrd, ii)
                    ig = acc_get(); proj("w_hin", (hd * 4 + 3) * 2, N, b_h, hrd, ig)
                    for (c0, n) in segs(N):
                        P.op(act, lambda e, c0=c0, n=n: e.activation(out=tq[:, c0:c0 + n], in_=acc_seg(iq, c0, n), func=AF.Silu),
                             reads=acc_bufs(iq, N), writes=[bt["tq"]])
                        P.op(act, lambda e, c0=c0, n=n: e.activation(out=gs_sg[:, hg, c0:c0 + n], in_=acc_seg(ig, c0, n), func=AF.Silu),
                             reads=acc_bufs(ig, N), writes=[b_gs[hg]])
                        P.op(act, lambda e, c0=c0, n=n: e.activation(out=tsig[:, c0:c0 + n], in_=acc_seg(if_, c0, n), func=AF.Sigmoid),
                             reads=acc_bufs(if_, N), writes=[bt["tsig"]])
                        P.op(dve, lambda e, c0=c0, n=n: e.tensor_copy(out=tv[:, c0:c0 + n], in_=acc_seg(ii, c0, n)),
                             reads=acc_bufs(ii, N), writes=[btv])
                    hstate[hg] = (iq, if_, ii, ig)

                def emit_chain(hg):
                    hd = grp * G + hg
                    P.op(act, lambda e: e.activation(out=tg[:, 0:N], in_=tsig[:, 0:N], func=AF.Ln, scale=oml_t[:, hd:hd + 1], bias=lb_t[:, hd:hd + 1]),
                         reads=[bt["tsig"]], writes=[bt["tg"]])
                    P.op(dve, lambda e: e.tensor_scalar(out=tk[:, 0:N], in0=tsig[:, 0:N], scalar1=noml_t[:, hd:hd + 1], scalar2=oml_t[:, hd:hd + 1],
                                                        op0=ALU.mult, op1=ALU.add), reads=[bt["tsig"]], writes=[bt["tk"]])
                    P.op(dve, lambda e: e.tensor_tensor_scan(out=tb[:, 0:N], data0=smask[:, 0:N], data1=tg[:, 0:N], initial=0.0,
                                                             op0=ALU.mult, op1=ALU.add), reads=[bt["tg"]], writes=[bt["tb"]])
                    P.op(act, lambda e: e.activation(out=teb[:, 0:N], in_=tb[:, 0:N], func=AF.Exp), reads=[bt["tb"]], writes=[bt["teb"]])
                    P.op(act, lambda e: e.activation(out=tenb[:, 0:N], in_=tb[:, 0:N], func=AF.Exp, scale=-1.0), reads=[bt["tb"]], writes=[bt["tenb"]])
                    P.op(dve, lambda e: e.tensor_tensor(out=tqt[:, 0:N], in0=tq[:, 0:N], in1=teb[:, 0:N], op=ALU.mult),
                         reads=[bt["tq"], bt["teb"]], writes=[bt["tqt"]])
                    P.op(dve, lambda e: e.tensor_tensor(out=tkt[:, 0:N], in0=tk[:, 0:N], in1=tenb[:, 0:N], op=ALU.mult),
                         reads=[bt["tk"], bt["tenb"]], writes=[bt["tkt"]])
                    chk("h_act")
                    bl = tb[:, 0:T].rearrange("p (c t) -> p c t", t=64)[:, :, 63]
                    P.op(act, lambda e: e.activation(out=td[:, 0:NCH], in_=bl, func=AF.Exp), reads=[bt["tb"]], writes=[bt["td"]])
                    if last_round:
                        bls = tb[:, T:NX].rearrange("p (c t) -> p c t", t=16)[:, :, 15]
                        P.op(act, lambda e: e.activation(out=td[:, NCH:NCH + 2], in_=bls, func=AF.Exp), reads=[bt["tb"]], writes=[bt["td"]])
                    P.op(dve, lambda e: e.tensor_tensor_scan(out=tci, data0=ones_f[:, 0:NCH], data1=bl, initial=0.0, op0=ALU.mult, op1=ALU.add),
                         reads=[bt["tb"]], writes=[bt["tci"]])
                    P.op(act, lambda e: e.activation(out=tE, in_=tci, func=AF.Exp), reads=[bt["tci"]], writes=[bt["tE"]])
                    P.op(dve, lambda e: e.tensor_copy(out=gs_qc[:, hg, 0:64], in_=tqt[:, 0:64]), reads=[bt["tqt"]], writes=[b_gs[hg]])
                    if NCH > 1:
                        P.op(dve, lambda e: e.tensor_tensor(out=gs_qc[:, hg, 64:T].rearrange("p (c t) -> p c t", t=64),
                                                            in0=tqt[:, 64:T].rearrange("p (c t) -> p c t", t=64),
                                                            in1=tE[:, 0:NCH - 1].unsqueeze(2).to_broadcast([128, NCH - 1, 64]), op=ALU.mult),
                             reads=[bt["tqt"], bt["tE"]], writes=[b_gs[hg]])

                def emit_rec(hg):
                    hd = grp * G + hg
                    tv = tvs[hg % 2]; btv = bt[f"tv{hg % 2}"]
                    chk("h_dec")
                    P.op(pe, [lambda e, pb=pb: e.transpose(psTk[:, pb, :], tkt[:, pb * 128:(pb + 1) * 128], identb) for pb in range(NPB)]
                         + [lambda e, pb=pb: e.transpose(psTv[:, pb, :], tv[:, pb * 128:(pb + 1) * 128], identb) for pb in range(NPB)],
                         reads=[bt["tkt"], btv], writes=[b_bank[6]])
                    chk("h_tr0")
                    P.op(act, lambda e: e.activation(out=ktok, in_=psTk, func=AF.Copy), reads=[b_bank[6]], writes=[bt["ktok"]])
                    chk("h_tr1")
                    P.op(act, lambda e: e.activation(out=vtok, in_=psTv, func=AF.Copy), reads=[b_bank[6]], writes=[bt["vtok"]])
                    if last_round:
                        P.op(pe, [lambda e: e.transpose(psTks[0:32, :], tkt[:, T:NX], identb), lambda e: e.transpose(psTvs[0:32, :], tv[:, T:NX], identb)],
                             reads=[bt["tkt"], btv], writes=[b_bank[6]])
                        P.op(act, lambda e: e.activation(out=vtoks, in_=psTvs[0:32, :], func=AF.Copy), reads=[b_bank[6]], writes=[bt["kts"]])
                        P.op(act, lambda e: e.activation(out=ktoks, in_=psTks[0:32, :], func=AF.Copy), reads=[b_bank[6]], writes=[bt["kts"]])
                        for q in range(2):
                            P.op(dve, lambda e, q=q: e.tensor_scalar(out=kt01[:, q, :], in0=ktoks, scalar1=sm01[:, q:q + 1], scalar2=None, op0=ALU.mult),
                                 reads=[bt["kts"]], writes=[bt["kts"]])
                    chk("h_tr")
                    for j in range(2):
                        P.op(dve, lambda e, j=j: e.tensor_scalar(out=ktokm[:, :, j, :], in0=ktok, scalar1=pm64[:, j:j + 1], scalar2=None, op0=ALU.mult),
                             reads=[bt["ktok"]], writes=[bt["ktokm"]])
                    iU = []
                    for half in range(0, NCH, 4):
                        i = acc_get()
                        iU.append(i)
                        P.op(pe, [lambda e, c=c, i=i, half=half: e.matmul(banks[i][:, (c - half) * 128:(c - half + 1) * 128],
                                                                           ktokm[:, c // 2, c % 2, :], vtok[:, c // 2, :], start=True, stop=True)
                                  for c in range(half, min(NCH, half + 4))], reads=[bt["ktokm"], bt["vtok"]], writes=[b_bank[i]])
                    if last_round:
                        iUs = acc_get()
                        P.op(pe, [lambda e, q=q: e.matmul(banks[iUs][:, q * 128:(q + 1) * 128], kt01[:, q, :], vtoks, start=True, stop=True) for q in range(2)],
                             reads=[bt["kts"]], writes=[b_bank[iUs]])
                    chk("h_U")
                    Ubank = lambda c: banks[iU[c // 4]][:, (c % 4) * 128:(c % 4 + 1) * 128]
                    bU = lambda c: b_bank[iU[c // 4]]
                    cur = 0
                    for c in range(NCH):
                        fin = (c == NCH - 1)
                        dst = LD[:, hg, 0:128] if fin else Mf[1 - cur]
                        dbuf = bt["LD"] if fin else bt[f"M{1 - cur}"]
                        if c == 0:
                            P.op(dve, lambda e, dst=dst: e.tensor_scalar(out=dst, in0=Ubank(0), scalar1=td[:, 0:1], scalar2=None, op0=ALU.mult),
                                 reads=[bU(0), bt["td"]], writes=[dbuf])
                        else:
                            P.op(dve, lambda e, c=c, cur=cur: e.tensor_scalar(out=Mtmp, in0=Mf[cur], scalar1=td[:, c:c + 1], scalar2=None, op0=ALU.mult),
                                 reads=[bt[f"M{cur}"], bt["td"]], writes=[bt["Mtmp"]])
                            P.op(dve, lambda e, c=c, dst=dst: e.scalar_tensor_tensor(out=dst, in0=Ubank(c), scalar=td[:, c:c + 1], in1=Mtmp,
                                                                                   op0=ALU.mult, op1=ALU.add),
                                 reads=[bU(c), bt["td"], bt["Mtmp"]], writes=[dbuf])
                        if not fin:
                            P.op(act, lambda e, c=c, cur=cur: e.activation(out=Mb[:, c + 1, :], in_=Mf[1 - cur], func=AF.Copy),
                                 reads=[bt[f"M{1 - cur}"]], writes=[bt["Mb"]])
                        cur = 1 - cur
                    P.op(act, lambda e: e.activation(out=LD[:, hg, 128:129], in_=tE[:, NCH - 1:NCH], func=AF.Copy), reads=[bt["tE"]], writes=[bt["LD"]])
                    if last_round:
                        for q in range(2):
                            P.op(act, lambda e, q=q: e.activation(out=Mbs[:, q, :], in_=S0s[:, q, hg, :], func=AF.Copy), reads=[bt["S0s"]], writes=[bt["Mbs"]])
                            P.op(dve, lambda e, q=q: e.tensor_scalar(out=Mtmp, in0=S0s[:, q, hg, :], scalar1=td[:, NCH + q:NCH + q + 1], scalar2=None, op0=ALU.mult),
                                 reads=[bt["S0s"], bt["td"]], writes=[bt["Mtmp"]])
                            P.op(dve, lambda e, q=q: e.scalar_tensor_tensor(out=NSs[:, q, hg, :], in0=banks[iUs][:, q * 128:(q + 1) * 128],
                                                                          scalar=td[:, NCH + q:NCH + q + 1], in1=Mtmp, op0=ALU.mult, op1=ALU.add),
                                 reads=[b_bank[iUs], bt["td"], bt["Mtmp"]], writes=[bt["NSs"]])
                    chk("h_chain")
                    iA = acc_get()
                    P.op(pe, [lambda e, pb=pb: e.matmul(banks[iA][:, pb * 128:(pb + 1) * 128], tkt[:, pb * 128:(pb + 1) * 128],
                                                        tqt[:, pb * 128:(pb + 1) * 128], start=True, stop=True) for pb in range(NPB)]
                         + ([lambda e: e.matmul(psATs[0:32, :], tkt[:, T:NX], tqt[:, T:NX], start=True, stop=True)] if last_round else []),
                         reads=[bt["tkt"], bt["tqt"]], writes=[b_bank[iA]] + ([b_bank[7]] if last_round else []))
                    P.op(dve, lambda e: e.tensor_tensor(out=atm, in0=banks[iA][:, 0:NPB * 128].rearrange("p (a b) -> p a b", b=128),
                                                        in1=maskA.unsqueeze(1).to_broadcast([128, NPB, 128]), op=ALU.mult),
                         reads=[b_bank[iA]], writes=[bt["atm"]])
                    if last_round:
                        P.op(dve, lambda e: e.tensor_tensor(out=atms, in0=psATs[0:32, :], in1=maskS, op=ALU.mult),
                             reads=[b_bank[7]], writes=[bt["atm"]])
                    chk("h_A")
                    iO = acc_get()
                    fl = []
                    for c in range(NCH):
                        pb, j = c // 2, c % 2
                        fl.append(lambda e, c=c, pb=pb, j=j: e.matmul(banks[iO][:, c * 64:(c + 1) * 64], vtok[:, pb, :], atm[:, pb, j * 64:(j + 1) * 64],
                                                                     start=True, stop=False))
                        fl.append(lambda e, c=c: e.matmul(banks[iO][:, c * 64:(c + 1) * 64], (zerob if c == 0 else Mb[:, c, :]), tqt[:, c * 64:(c + 1) * 64],
                                                         start=False, stop=True))
                    if last_round:
                        for q in range(2):
                            fl.append(lambda e, q=q: e.matmul(acc_seg(iO, T + 16 * q, 16), vtoks, atms[:, q * 16:(q + 1) * 16], start=True, stop=False))
                            fl.append(lambda e, q=q: e.matmul(acc_seg(iO, T + 16 * q, 16), Mbs[:, q, :], tqt[:, T + 16 * q:T + 16 * q + 16], start=False, stop=True))
                    P.op(pe, fl, reads=[bt["vtok"], bt["atm"], bt["Mb"], bt["tqt"], bt["kts"], bt["Mbs"]], writes=acc_bufs(iO, N))
                    for (c0, n) in segs(N):
                        P.op(act, lambda e, c0=c0, n=n: e.activation(out=gs_ol[:, hg, c0:c0 + n], in_=acc_seg(iO, c0, n), func=AF.Copy),
                             reads=acc_bufs(iO, N), writes=[b_gs[hg]])

                hstate = {}
                emit_proj(0)
                emit_chain(0)
                for hg in range(G):
                    if hg + 1 < G:
                        emit_proj(hg + 1)
                    emit_rec(hg)
                    if hg + 1 < G:
                        emit_chain(hg + 1)
                chk("p1h")
                if last_round:
                    for q in range(2):
                        P.dma(sp, hs[q, grp * G:(grp + 1) * G, :, :].rearrange("h k v -> k h v"), NSs[:, q, :, :], bt["NSs"], reads=[bt["NSs"]], writes=[bt["NSs"]])
                    out_bufs.append(bt["NSs"])
                P.dma(sp, agh_in.rearrange("(g k) c -> k g c", k=128), LD, b_agh_in, reads=[bt["LD"]], writes=[b_agh_in])
                allgather(agh_in, agh_mid, agh, [b_agh_in], b_agh, b_agh_mid)
                if r == 0:
                    P.op(dve, lambda e: e.memset(S_t, 0.0), writes=[bt["S"]])
                else:
                    P.dma(sp, S_t, sround[:, grp * G:(grp + 1) * G, :], bt["S"], reads=[b_sr[grp]], writes=[bt["S"]])
                P.op(dve, lambda e: e.memset(Sin, 0.0), writes=[bt["Sin"]])
                for j in range(NCORES):
                    jj = j % 2
                    P.dma(sp, LDj[jj], agh[j * G * 128:(j + 1) * G * 128, :].rearrange("(g k) c -> k g c", k=128), b_ld[jj], reads=[b_agh], writes=[b_ld[jj]])
                    P.op(dve, lambda e, j=j: e.scalar_tensor_tensor(out=Sin, in0=S_t, scalar=rm_t[:, j:j + 1], in1=Sin, op0=ALU.mult, op1=ALU.add),
                         reads=[bt["S"], bt["Sin"]], writes=[bt["Sin"]])
                    P.op(dve, lambda e, jj=jj: e.tensor_tensor(out=Stmp, in0=S_t, in1=LDj[jj][:, :, 128:129].to_broadcast([128, G, 128]), op=ALU.mult),
                         reads=[bt["S"], b_ld[jj]], writes=[bt["Stmp"]])
                    P.op(dve, lambda e, jj=jj: e.tensor_tensor(out=S_t, in0=Stmp, in1=LDj[jj][:, :, 0:128], op=ALU.add),
                         reads=[bt["Stmp"], b_ld[jj]], writes=[bt["S"]])
                if last_round:
                    P.dma(sp, hp[grp * G:(grp + 1) * G, :, :].rearrange("h k v -> k h v"), S_t, b_sr[grp], reads=[bt["S"]], writes=[b_sr[grp]])
                    out_bufs.append(b_sr[grp])
                else:
                    P.dma(sp, sround[:, grp * G:(grp + 1) * G, :], S_t, b_sr[grp], reads=[bt["S"]], writes=[b_sr[grp]])
                P.op(act, lambda e: e.activation(out=M0b, in_=Sin, func=AF.Copy), reads=[bt["Sin"]], writes=[bt["M0b"]])
                for hg in range(G):
                    hd = grp * G + hg
                    iC = acc_get()
                    P.op(pe, lambda e: e.matmul(banks[iC][:, 0:T], M0b[:, hg, :], gs_qc[:, hg, 0:T], start=True, stop=True),
                         reads=[bt["M0b"], b_gs[hg]], writes=[b_bank[iC]])
                    P.op(dve, lambda e: e.tensor_tensor(out=to32[:, 0:T], in0=banks[iC][:, 0:T], in1=gs_ol[:, hg, 0:T], op=ALU.add),
                         reads=[b_bank[iC], b_gs[hg]], writes=[bt["to32"]])
                    if last_round:
                        P.op(dve, lambda e: e.tensor_copy(out=to32[:, T:NX], in_=gs_ol[:, hg, T:NX]), reads=[b_gs[hg]], writes=[bt["to32"]])
                    P.op(act, lambda e: e.activation(out=osq[:, 0:N], in_=to32[:, 0:N], func=AF.Square), reads=[bt["to32"]], writes=[bt["osq"]])
                    iS = acc_get()
                    mm_acc(iS, N, [(onesb, lambda c0, n: osq[:, c0:c0 + n])], [bt["osq"]], first=True, last=True)
                    for (c0, n) in segs(N):
                        P.op(dve, lambda e, c0=c0, n=n: e.tensor_scalar(out=rs2[:, c0:c0 + n], in0=acc_seg(iS, c0, n), scalar1=1.0 / 128, scalar2=EPS,
                                                                         op0=ALU.mult, op1=ALU.add), reads=acc_bufs(iS, N), writes=[bt["rs2"]])
                    rsqrt_inplace(rs2[:, 0:N], bt["rs2"])
                    P.op(dve, lambda e: e.scalar_tensor_tensor(out=t1[:, 0:N], in0=to32[:, 0:N], scalar=og_t[:, 0:1], in1=rs2[:, 0:N], op0=ALU.mult, op1=ALU.mult),
                         reads=[bt["to32"], bt["rs2"]], writes=[bt["t1"]])
                    P.op(dve, lambda e: e.tensor_tensor(out=aux[:, hd, 0:N], in0=t1[:, 0:N], in1=gs_sg[:, hg, 0:N], op=ALU.mult),
                         reads=[bt["t1"], b_gs[hg]], writes=[b_aux[hd]])

            chk("p1")
            P.barrier()
            for fc in range(FC):
                j = fc % 2
                xr = ostage[j][:, 0:(NPB + 1) * 128].rearrange("p (a b) -> p a b", b=128)
                for tbk in range(NPB):
                    P.dma(sp, xr[:, tbk, :], xp[r * T + tbk * 128:r * T + tbk * 128 + 128, fc * 128:(fc + 1) * 128], b_ost[j], writes=[b_ost[j]])
                if last_round:
                    P.dma(sp, xr[0:EX, NPB, :], xs[:, fc * 128:(fc + 1) * 128], b_ost[j], writes=[b_ost[j]])
                i = acc_get()
                for half in range(2):
                    wt, wb = wload("w_hout", fc * 2 + half, WC)
                    w3 = wt[:, 0:WC].rearrange("p (k c) -> p k c", c=128)
                    fl = []
                    for (c0, n) in segs(N):
                        for k in range(KH):
                            kc = half * KH + k
                            fl.append(lambda e, c0=c0, n=n, k=k, kc=kc, w3=w3, half=half: e.matmul(acc_seg(i, c0, n), w3[:, k, :], aux[:, kc, c0:c0 + n],
                                                                                                    start=(half == 0 and k == 0), stop=False))
                    rd = [wb] + b_aux[0:FC]
                    if half == 1:
                        for tbk in range(NPB):
                            fl.append(lambda e, tbk=tbk, xr=xr: e.matmul(banks[i][:, tbk * 128:(tbk + 1) * 128], xr[:, tbk, :], identf, start=False, stop=True))
                        if last_round:
                            fl.append(lambda e, xr=xr: e.matmul(acc_seg(i, T, EX), xr[0:EX, NPB, :], identf[0:EX, 0:EX], start=False, stop=True))
                        rd = rd + [b_ost[j]]
                    P.op(pe, fl, reads=rd, writes=acc_bufs(i, N))
                for (c0, n) in segs(N):
                    P.op(act, lambda e, c0=c0, n=n, fc=fc, i=i: e.activation(out=x_sb[:, fc, c0:c0 + n], in_=acc_seg(i, c0, n), func=AF.Copy),
                         reads=acc_bufs(i, N), writes=[b_x[fc]])
            chk("p2")
            ffn(0, N)
            chk("ffn0")
            fm_norm(N)
            fm_norm_apply(N, 2)
            cuh = aux
            offs = [(2, 0, T)] + ([(T + 4, T, 16), (T + 22, T + 16, 16)] if last_round else [])
            cst = ostage[0][:, 0:4 * FC].rearrange("p (q t f) -> p q t f", q=2, t=2)
            for fc in range(FC):
                igc = acc_get(); proj("w_cin", (fc * 2 + 0) * 2, N, b_h, hrd, igc)
                iu = acc_get(); proj("w_cin", (fc * 2 + 1) * 2, N, b_h, hrd, iu)
                j = fc % 2
                for (c0, n) in segs(N):
                    P.op(act, lambda e, c0=c0, n=n, j=j, igc=igc: e.activation(out=gtmp[j][:, c0:c0 + n], in_=acc_seg(igc, c0, n), func=AF.Copy),
                         reads=acc_bufs(igc, N), writes=[b_gtmp[j]])
                    P.op(dve, lambda e, c0=c0, n=n, j=j, iu=iu: e.tensor_tensor(out=gtmp[j][:, c0:c0 + n], in0=gtmp[j][:, c0:c0 + n], in1=acc_seg(iu, c0, n), op=ALU.mult),
                         reads=acc_bufs(iu, N) + [b_gtmp[j]], writes=[b_gtmp[j]])
                for (co, ct, n) in offs:
                    P.op(act, lambda e, co=co, ct=ct, n=n, fc=fc, j=j: e.activation(out=cuh[:, fc, co:co + n], in_=gtmp[j][:, ct:ct + n], func=AF.Copy),
                         reads=[b_gtmp[j]], writes=[b_aux[fc]])
                P.op(dve, lambda e, fc=fc, j=j: e.tensor_copy(out=hal_t[:, :, fc], in_=gtmp[j][:, T - 2:T]), reads=[b_gtmp[j]], writes=[b_hal])
                if last_round:
                    for q in range(2):
                        P.op(dve, lambda e, fc=fc, j=j, q=q: e.tensor_copy(out=cst[:, q, :, fc], in_=gtmp[j][:, T + 16 * q + 14:T + 16 * q + 16]),
                             reads=[b_gtmp[j]], writes=[b_ost[0]])
            hal2 = hal_t.rearrange("p a b -> p (a b)")
            P.dma(sp, agc_in, hal2, b_agc_in, reads=[b_hal], writes=[b_agc_in])
            allgather(agc_in, agc_mid, agc, [b_agc_in], b_agc, b_agc_mid)
            P.dma(sp, halg, agc.rearrange("(r p) c -> p r c", p=128), b_halg, reads=[b_agc], writes=[b_halg])
            halo2 = halo.rearrange("p a b -> p (a b)")
            P.op(dve, lambda e: e.tensor_scalar(out=halo2, in0=prev7, scalar1=rm_t[:, 16:17], scalar2=None, op0=ALU.mult),
                 reads=[b_prev7, b_halg], writes=[b_halo])
            for j in range(NCORES):
                P.op(dve, lambda e, j=j: e.scalar_tensor_tensor(out=halo2, in0=halg[:, j, :], scalar=rm_t[:, 8 + j:9 + j], in1=halo2, op0=ALU.mult, op1=ALU.add),
                     reads=[b_halg, b_halo], writes=[b_halo])
            P.op(dve, lambda e: e.tensor_copy(out=prev7, in_=halg[:, NCORES - 1, :]), reads=[b_halg, b_halo], writes=[b_prev7])
            iT = acc_get()
            P.op(pe, lambda e: e.transpose(banks[iT][0:FC * 2, 0:128], hal2, identf), reads=[b_hal], writes=[b_bank[iT]])
            P.op(act, lambda e: e.activation(out=ostage[1][0:FC * 2, 0:128], in_=banks[iT][0:FC * 2, 0:128], func=AF.Copy), reads=[b_bank[iT]], writes=[b_ost[1]])
            P.dma(sp, cp[r, :, :].rearrange("t (f p) -> (t f) p", p=128), ostage[1][0:FC * 2, 0:128], b_ost[1], reads=[b_ost[1]], writes=[b_ost[1]])
            if last_round:
                for q in range(2):
                    iT = acc_get()
                    P.op(pe, lambda e, q=q, iT=iT: e.transpose(banks[iT][0:FC * 2, 0:128], ostage[0][:, q * FC * 2:(q + 1) * FC * 2], identf),
                         reads=[b_ost[0]], writes=[b_bank[iT]])
                    P.op(act, lambda e, q=q, iT=iT: e.activation(out=ostage[1][0:FC * 2, 128 * (q + 1):128 * (q + 2)], in_=banks[iT][0:FC * 2, 0:128], func=AF.Copy),
                         reads=[b_bank[iT]], writes=[b_ost[1]])
                    P.dma(sp, cs[q, :, :].rearrange("t (f p) -> (t f) p", p=128), ostage[1][0:FC * 2, 128 * (q + 1):128 * (q + 2)], b_ost[1],
                          reads=[b_ost[1]], writes=[b_ost[1]])
            out_bufs.append(b_ost[1])
            for fc in range(FC):
                P.op(act, lambda e, fc=fc: e.activation(out=cuh[:, fc, 0:2], in_=halo[:, :, fc], func=AF.Copy), reads=[b_halo], writes=[b_aux[fc]])
            if last_round:
                P.dma(sp, ostage[0][0:4 * FC, 0:128], sc.rearrange("q t (f p) -> (q t f) p", p=128), b_ost[0], writes=[b_ost[0]])
                iT = acc_get()
                P.op(pe, lambda e: e.transpose(banks[iT][:, 0:4 * FC], ostage[0][0:4 * FC, 0:128], identf[0:4 * FC, 0:4 * FC]), reads=[b_ost[0]], writes=[b_bank[iT]])
                scv = banks[iT][:, 0:4 * FC].rearrange("p (q t f) -> p q t f", q=2, t=2)
                for q in range(2):
                    for t_ in range(2):
                        co = (T + 2 if q == 0 else T + 20) + t_
                        P.op(act, lambda e, q=q, t_=t_, co=co: e.activation(out=cuh[:, :, co], in_=scv[:, q, t_, :], func=AF.Copy),
                             reads=[b_bank[iT]], writes=b_aux[0:FC])
            for fc in range(FC):
                igb = acc_get(); proj("w_cin", FC * 4 + fc * 2, N, b_h, hrd, igb)
                j = fc % 2
                for (co, ct, n) in offs:
                    P.op(dve, lambda e, co=co, ct=ct, n=n, fc=fc, j=j: e.tensor_scalar(out=gtmp[j][:, ct:ct + n], in0=cuh[:, fc, co:co + n],
                                                                                   scalar1=cw_t[:, 2, fc:fc + 1], scalar2=None, op0=ALU.mult),
                         reads=[b_aux[fc]], writes=[b_gtmp[j]])
                    P.op(dve, lambda e, co=co, ct=ct, n=n, fc=fc, j=j: e.scalar_tensor_tensor(out=gtmp[j][:, ct:ct + n], in0=cuh[:, fc, co - 1:co - 1 + n],
                                                                                          scalar=cw_t[:, 1, fc:fc + 1], in1=gtmp[j][:, ct:ct + n], op0=ALU.mult, op1=ALU.add),
                         reads=[b_aux[fc], b_gtmp[j]], writes=[b_gtmp[j]])
                    P.op(dve, lambda e, co=co, ct=ct, n=n, fc=fc, j=j: e.scalar_tensor_tensor(out=gtmp[j][:, ct:ct + n], in0=cuh[:, fc, co - 2:co - 2 + n],
                                                                                          scalar=cw_t[:, 0, fc:fc + 1], in1=gtmp[j][:, ct:ct + n], op0=ALU.mult, op1=ALU.add),
                         reads=[b_aux[fc], b_gtmp[j]], writes=[b_gtmp[j]])
                for (c0, n) in segs(N):
                    P.op(dve, lambda e, c0=c0, n=n, fc=fc, j=j, igb=igb: e.tensor_tensor(out=cuh[:, fc, c0:c0 + n], in0=gtmp[j][:, c0:c0 + n], in1=acc_seg(igb, c0, n), op=ALU.mult),
                         reads=acc_bufs(igb, N) + [b_gtmp[j], b_aux[fc]], writes=[b_aux[fc]])
            for fc in range(FC):
                i = acc_get()
                proj("w_cout", fc * 2, N, b_aux[0:FC], lambda kc, c0, n: cuh[:, kc, c0:c0 + n], i)
                for (c0, n) in segs(N):
                    P.op(dve, lambda e, c0=c0, n=n, fc=fc, i=i: e.tensor_tensor(out=x_sb[:, fc, c0:c0 + n], in0=x_sb[:, fc, c0:c0 + n], in1=acc_seg(i, c0, n), op=ALU.add),
                         reads=acc_bufs(i, N) + [b_x[fc]], writes=[b_x[fc]])
            chk("conv")
            ffn(1, N)
            chk("ffn1")
            fm_norm(N)
            out_blocks = [(yp[r * T + tbk * 128:r * T + tbk * 128 + 128, :], 128, tbk * 128) for tbk in range(NPB)]
            if last_round:
                out_blocks.append((ys[0:EX, :], EX, T))
            FB = 1024 // 128
            oc_ = 0
            for (dst, nt, c0) in out_blocks:
                for f8 in range(0, FC, FB):
                    nf8 = min(FB, FC - f8)
                    so = oc_ % 2
                    oc_ += 1
                    for f4 in range(f8, f8 + nf8, 4):
                        nf = min(4, f8 + nf8 - f4)
                        i = acc_get()
                        for f in range(f4, f4 + nf):
                            j = f % 2
                            P.op(dve, lambda e, f=f, j=j, c0=c0, nt=nt: e.scalar_tensor_tensor(out=gtmp[j][:, 0:nt], in0=x_sb[:, f, c0:c0 + nt], scalar=norms_t[:, 4, f:f + 1],
                                                                                             in1=rstd_t[:, c0:c0 + nt], op0=ALU.mult, op1=ALU.mult),
                                 reads=[b_x[f], b_rstd], writes=[b_gtmp[j]])
                            P.op(pe, lambda e, f=f, f4=f4, j=j, nt=nt, i=i: e.transpose(banks[i][0:nt, (f - f4) * 128:(f - f4 + 1) * 128], gtmp[j][:, 0:nt], identf),
                                 reads=[b_gtmp[j]], writes=[b_bank[i]])
                        P.op(act, lambda e, f4=f4, f8=f8, nf=nf, nt=nt, i=i, so=so: e.activation(out=ostage[so][0:nt, (f4 - f8) * 128:(f4 - f8 + nf) * 128],
                                                                                                in_=banks[i][0:nt, 0:nf * 128], func=AF.Copy),
                             reads=[b_bank[i]], writes=[b_ost[so]])
                    P.dma(sp, dst[:, f8 * 128:(f8 + nf8) * 128], ostage[so][0:nt, 0:nf8 * 128], b_ost[so], reads=[b_ost[so]], writes=[b_ost[so]])
            out_bufs.extend(b_ost)

        try:
            if not _go:
                raise _Stop()
            chk("wag")
            for r in range(R):
                round_body(r)
        except _Stop:
            pass
        P.barrier()
        seen = set()
        for bf in out_bufs:
            sm = bf.dsem
            if sm is not None and id(sm) not in seen and sm.v:
                seen.add(id(sm))
                nc.sync.wait_ge(sm.h, sm.v)
        print("instructions:", P.nins, "sems:", P.nsem)
    return nc


def make_in_maps(cfg, inp):
    D, NH, FC, T, R = cfg.D, cfg.NH, cfg.FC, cfg.T, cfg.ROUNDS
    W = prep_weights(cfg, inp)
    lbl = np.ascontiguousarray(inp["hgrn_lb_logits"].reshape(3, NH, 128).transpose(2, 0, 1)).astype(np.float32)
    nv = np.stack([inp["norm_mix"][0], inp["norm_ffn"][0], inp["norm_mix"][1], inp["norm_ffn"][1], inp["norm_final"]])
    norms = np.ascontiguousarray(nv.reshape(5, FC, 128).transpose(2, 0, 1)).astype(np.float32)
    cw = np.ascontiguousarray(inp["conv_w"][0].reshape(3, FC, 128).transpose(2, 0, 1)).astype(np.float32)
    ogain = np.ascontiguousarray(inp["hgrn_out_gain"][0].reshape(128, 1)).astype(np.float32)
    xp_full = inp["x_prompt"][0]
    consts = make_consts(cfg)
    maps = []
    for c in range(NCORES):
        m = {}
        m["xp"] = np.ascontiguousarray(np.concatenate([xp_full[(r * NCORES + c) * T:(r * NCORES + c + 1) * T] for r in range(R)], axis=0))
        m["xs"] = np.ascontiguousarray(inp["x_sample"][cfg.SPC * c:cfg.SPC * (c + 1)].reshape(cfg.EX, D))
        m["sh"] = np.ascontiguousarray(inp["state_hgrn"][0, cfg.SPC * c:cfg.SPC * (c + 1)])
        m["sc"] = np.ascontiguousarray(inp["state_conv"][0, cfg.SPC * c:cfg.SPC * (c + 1)])
        m["lbl"] = lbl; m["ogain"] = ogain; m["norms"] = norms; m["cw"] = cw
        m["consts"] = consts
        rm = np.zeros((128, 17), np.float32)
        rm[:, c] = 1.0
        if c >= 1:
            rm[:, 8 + c - 1] = 1.0
        else:
            rm[:, 16] = 1.0
        m["rmask"] = rm
        for n in WNAMES:
            PT = PT_OF(n)
            RPR = PT * 16
            for pi, (t0, nt) in enumerate(wpieces(cfg, n)):
                ch = W[n][t0:t0 + nt].reshape(nt // PT, PT * 128, -1)
                m[f"{n}_{pi}"] = np.ascontiguousarray(ch[:, c * RPR:(c + 1) * RPR, :].reshape(nt // PT * RPR, -1))
        maps.append(m)
    return maps


def assemble(cfg, res):
    D, NH, T, R = cfg.D, cfg.NH, cfg.T, cfg.ROUNDS
    yp = np.zeros((1, cfg.SEQ, D), np.float32)
    for c in range(NCORES):
        for r in range(R):
            j = r * NCORES + c
            yp[0, j * T:(j + 1) * T] = res[c]["yp"][r * T:(r + 1) * T]
    ys = np.concatenate([res[c]["ys"].reshape(cfg.SPC, cfg.DS, D) for c in range(NCORES)], axis=0)
    hp = res[0]["hp"].reshape(1, 1, NH, 128, 128)
    hs = np.concatenate([res[c]["hs"] for c in range(NCORES)], axis=0).reshape(1, cfg.DB, NH, 128, 128)
    cp = res[NCORES - 1]["cp"][R - 1].reshape(1, 1, 2, D)
    cs = np.concatenate([res[c]["cs"] for c in range(NCORES)], axis=0).reshape(1, cfg.DB, 2, D)
    return (yp, ys, hp, hs, cp, cs)


_NC_CACHE = {}


def run(cfg, inp):
    key = (cfg.D, cfg.NH, cfg.DFF, cfg.SEQ, cfg.T)
    if key not in _NC_CACHE:
        _NC_CACHE[key] = build(cfg)
    nc = _NC_CACHE[key]
    maps = make_in_maps(cfg, inp)
    res = run_bass_kernel_spmd(nc, maps, core_ids=list(range(NCORES)))
    return assemble(cfg, res.results)


def kernel(**inputs):
    inp = {k: np.asarray(v) for k, v in inputs.items()}
    return run(FULL, inp)
```

```python
import os
import numpy as np
import concourse.bass as bass
import concourse.mybir as mybir
from concourse.bass_utils import run_bass_kernel_spmd
from contextlib import ExitStack

F32 = mybir.dt.float32
BF16 = mybir.dt.bfloat16
AF = mybir.ActivationFunctionType
ALU = mybir.AluOpType
NCORES = 8
EPS = 1e-6


class _Stop(Exception):
    pass


class Cfg:
    stop = None

    def __init__(self, D=4096, NH=32, DFF=11008, SEQ=16384, T=512, DB=16, DS=16, G=4, NQ=4):
        self.D, self.NH, self.DFF, self.SEQ, self.T, self.DB, self.DS, self.G = D, NH, DFF, SEQ, T, DB, DS, G
        self.FC = D // 128
        self.KH = self.FC // 2
        self.NS = DFF // 128
        self.ROUNDS = SEQ // (NCORES * T)
        self.NPB = T // 128
        self.NCH = T // 64
        self.SPC = DB // NCORES
        self.EX = self.SPC * DS
        self.NX = T + self.EX
        self.NQ = NQ
        base = self.NS // NQ
        rem = self.NS % NQ
        self.QS = [base + (1 if i < rem else 0) for i in range(NQ)]
        self.QMAX = max(self.QS)
        self.NG = NH // G
        assert DS == 16 and self.SPC == 2 and D == NH * 128


FULL = Cfg()


def tile_w(W, kh):
    K, NO = W.shape
    kc = K // 128
    nh = kc // kh
    t = W.reshape(nh, kh, 128, NO // 128, 128)
    t = t.transpose(3, 0, 2, 1, 4)
    return np.ascontiguousarray(t).reshape(NO // 128 * nh, 128, kh * 128)


def prep_weights(cfg, inp):
    D, NH, NS = cfg.D, cfg.NH, cfg.NS
    out = {}
    W = inp["hgrn_w_in"][0]
    cols = np.concatenate([np.arange(s * D + h * 128, s * D + h * 128 + 128) for h in range(NH) for s in range(4)])
    out["w_hin"] = tile_w(W[:, cols], cfg.KH)
    out["w_hout"] = tile_w(inp["hgrn_w_out"][0], cfg.KH)
    W = inp["conv_w_in"][0]
    cols = np.concatenate([np.arange(s * D + f * 128, s * D + f * 128 + 128) for f in range(cfg.FC) for s in (1, 2)]
                          + [np.arange(0, D)])
    out["w_cin"] = tile_w(W[:, cols], cfg.KH)
    out["w_cout"] = tile_w(inp["conv_w_out"][0], cfg.KH)
    for l in range(2):
        W = inp["ffn_w_in"][l]
        cols = np.concatenate([np.arange(s * cfg.DFF + sl * 128, s * cfg.DFF + sl * 128 + 128)
                               for sl in range(NS) for s in range(2)])
        out[f"w_fin{l}"] = tile_w(W[:, cols], cfg.KH)
        W = inp["ffn_w_out"][l]
        tl = np.zeros((cfg.NQ, cfg.FC, 128, cfg.QMAX, 128), np.float32)
        s0 = 0
        for q in range(cfg.NQ):
            n = cfg.QS[q]
            blk = W[s0 * 128:(s0 + n) * 128, :].reshape(n, 128, cfg.FC, 128)
            tl[q, :, :, :n, :] = blk.transpose(2, 1, 0, 3)
            s0 += n
        out[f"w_fout{l}"] = tl.reshape(cfg.NQ * cfg.FC, 128, cfg.QMAX * 128)
    return out


WNAMES = ["w_hin", "w_hout", "w_fin0", "w_fout0", "w_cin", "w_cout", "w_fin1", "w_fout1"]


def wshape(cfg, name):
    D, NH, NS, FC, KH = cfg.D, cfg.NH, cfg.NS, cfg.FC, cfg.KH
    c = KH * 128
    if name == "w_hin":
        return (NH * 4 * 2, c)
    if name in ("w_hout", "w_cout"):
        return (FC * 2, c)
    if name == "w_cin":
        return (FC * 3 * 2, c)
    if name.startswith("w_fin"):
        return (NS * 2 * 2, c)
    return (cfg.NQ * FC, cfg.QMAX * 128)


def PT_OF(name):
    return 4 if name.startswith("w_fout") else 8


def wpieces(cfg, name, pmax=int(os.environ.get("PMAX", "128"))):
    nt, c = wshape(cfg, name)
    out = []
    t0 = 0
    if name == "w_hin" and nt >= 64:
        out.append((0, 16))
        t0 = 16
    while t0 < nt:
        n = min(pmax, nt - t0)
        assert n % NCORES == 0
        out.append((t0, n))
        t0 += n
    return out


class Sem:
    def __init__(self, h):
        self.h = h
        self.v = 0


class Buf:
    __slots__ = ("name", "w", "r", "dsem")

    def __init__(self, name=""):
        self.name = name
        self.w = None
        self.r = []
        self.dsem = None


class Eng:
    def __init__(self, name, handle, sem, skip_self=False):
        self.name, self.e, self.sem, self.skip_self = name, handle, sem, skip_self
        self.seen = {}


class Prog:
    def __init__(self, nc, es):
        self.nc = nc
        self.es = es
        self.nsem = 0
        mk = self.new_sem
        self.pe = Eng("pe", nc.tensor, mk("pe"), skip_self=True)
        self.act = Eng("act", nc.scalar, mk("act"))
        self.dve = Eng("dve", nc.vector, mk("dve"))
        self.pool = Eng("pool", nc.gpsimd, mk("pool"))
        self.sp = Eng("sp", nc.sync, mk("sp"))
        self.engs = [self.pe, self.act, self.dve, self.pool, self.sp]
        self.nins = 0

    def new_sem(self, name):
        self.nsem += 1
        sm = Sem(self.es.enter_context(self.nc.semaphore(f"s_{name}_{self.nsem}")))
        if not hasattr(self, "sems"):
            self.sems = []
        self.sems.append(sm)
        return sm

    def _waits(self, eng, reads, writes):
        need = {}

        def req(ev):
            if ev is None:
                return
            s, v = ev
            if eng.skip_self and s is eng.sem:
                return
            if eng.seen.get(s, 0) >= v:
                return
            if need.get(s, 0) < v:
                need[s] = v
        for b in reads:
            req(b.w)
        for b in writes:
            req(b.w)
            for ev in b.r:
                req(ev)
        for s, v in need.items():
            eng.e.wait_ge(s.h, v)
            eng.seen[s] = v
            self.nins += 1

    def op(self, eng, fns, reads=(), writes=(), sem=None, inc=1):
        if callable(fns):
            fns = [fns]
        self._waits(eng, reads, writes)
        ins = None
        for f in fns:
            ins = f(eng.e)
            self.nins += 1
        s = eng.sem if sem is None else sem
        s.v += inc
        ins.then_inc(s.h, inc)
        ev = (s, s.v)
        for b in reads:
            b.r.append(ev)
        for b in writes:
            b.w = ev
            b.r = []
        return ev

    def dma(self, eng, out, in_, sembuf, reads=(), writes=(), **kw):
        if isinstance(sembuf, Sem):
            sem = sembuf
        else:
            if sembuf.dsem is None:
                sembuf.dsem = self.new_sem("d" + sembuf.name)
            sem = sembuf.dsem
        return self.op(eng, lambda e: e.dma_start(out=out, in_=in_, **kw), reads, writes, sem=sem, inc=16)

    def barrier(self):
        for e in self.engs:
            for sm in self.sems:
                if sm.v == 0 or (sm is e.sem):
                    continue
                if e.seen.get(sm, 0) < sm.v:
                    e.e.wait_ge(sm.h, sm.v)
                    e.seen[sm] = sm.v


class Arena:
    def __init__(self, t, words):
        self.t, self.words, self.off = t, words, 0

    def mark(self):
        return self.off

    def reset(self, m):
        self.off = m

    def f32(self, shape):
        n = int(np.prod(shape[1:]))
        self.off = (self.off + 15) // 16 * 16
        ap = self.t[0:shape[0], self.off:self.off + n]
        self.off += n
        assert self.off <= self.words, ("arena overflow", self.off, self.words)
        return _reshape(ap, shape)

    def bf16(self, shape):
        n = int(np.prod(shape[1:]))
        nw = (n + 1) // 2
        self.off = (self.off + 15) // 16 * 16
        ap = self.t[0:shape[0], self.off:self.off + nw].bitcast(BF16)[:, 0:n]
        self.off += nw
        assert self.off <= self.words, ("arena overflow", self.off, self.words)
        return _reshape(ap, shape)


def _reshape(ap, shape):
    if len(shape) == 2:
        return ap
    if len(shape) == 3:
        return ap.rearrange("p (a b) -> p a b", a=shape[1], b=shape[2])
    if len(shape) == 4:
        return ap.rearrange("p (a b c) -> p a b c", a=shape[1], b=shape[2], c=shape[3])
    raise ValueError


def const_layout(cfg):
    NX = cfg.NX
    o = {}
    c = 0
    for name, n in (("identf", 128), ("maskA", 128), ("maskS", 32), ("smask", NX), ("sm01", 2), ("pm64", 2)):
        o[name] = (c, n)
        c += n
    return o, c


def make_consts(cfg):
    lay, cwid = const_layout(cfg)
    T, NX = cfg.T, cfg.NX
    C = np.zeros((128, cwid), np.float32)
    o, n = lay["identf"]; C[:, o:o + n] = np.eye(128, dtype=np.float32)
    s_ = np.arange(128)[:, None]; t_ = np.arange(128)[None, :]
    o, n = lay["maskA"]; C[:, o:o + n] = ((s_ <= t_) & (s_ // 64 == t_ // 64)).astype(np.float32)
    s2 = np.arange(32)[:, None]; t2 = np.arange(32)[None, :]
    o, n = lay["maskS"]; C[0:32, o:o + n] = ((s2 <= t2) & (s2 // 16 == t2 // 16)).astype(np.float32)
    sm = np.ones(NX, np.float32); sm[0:T:64] = 0.0; sm[T:NX:16] = 0.0
    o, n = lay["smask"]; C[:, o:o + n] = sm[None, :]
    o, n = lay["sm01"]; C[0:16, o] = 1.0; C[16:32, o + 1] = 1.0
    o, n = lay["pm64"]; C[0:64, o] = 1.0; C[64:128, o + 1] = 1.0
    return C


def build(cfg):
    nc = bass.Bass("TRN2", target_bir_lowering=False, num_devices=NCORES)
    D, NH, FC, KH, T, EX, NX, G = cfg.D, cfg.NH, cfg.FC, cfg.KH, cfg.T, cfg.EX, cfg.NX, cfg.G
    NCH, NPB, R, NS = cfg.NCH, cfg.NPB, cfg.ROUNDS, cfg.NS
    WC = KH * 128
    WCMAX = max(WC, cfg.QMAX * 128)
    clay, CWID = const_layout(cfg)

    def din(name, shape, dt=F32):
        return nc.dram_tensor(name, list(shape), dt, kind="ExternalInput").ap()

    def dout(name, shape):
        return nc.dram_tensor(name, list(shape), F32, kind="ExternalOutput").ap()

    def dint(name, shape):
        return nc.dram_tensor(name, list(shape), F32, kind="Internal").ap()

    xp = din("xp", [R * T, D])
    xs = din("xs", [EX, D])
    sh = din("sh", [cfg.SPC, NH, 128, 128])
    sc = din("sc", [cfg.SPC, 2, D])
    lbl = din("lbl", [128, 3, NH])
    ogain = din("ogain", [128, 1])
    norms = din("norms", [128, 5, FC])
    cw = din("cw", [128, 3, FC])
    rmask = din("rmask", [128, 17])
    consts = din("consts", [128, CWID])
    wsh, wag_in, wag = {}, {}, {}
    WP = []
    for n in WNAMES:
        _, c = wshape(cfg, n)
        PT = PT_OF(n)
        for pi, (t0, nt) in enumerate(wpieces(cfg, n)):
            key = f"{n}_{pi}"
            assert nt % PT == 0
            WP.append((key, n, t0, nt, PT))
            wsh[key] = din(key, [nt // NCORES * 128, c])
            wag_in[key] = nc.dram_tensor(key + "_i", [nt // NCORES * 128, c], BF16, kind="Internal").ap()
            wag[key] = nc.dram_tensor(key + "_g", [nt * 128, c], BF16, kind="Internal").ap()
    NMID = 6
    midA = [nc.dram_tensor(f"midA{i}", [4 * 16 * PT_OF("w_hin"), WC], BF16, kind="Internal").ap() for i in range(NMID)]
    midB = [nc.dram_tensor(f"midB{i}", [4 * 16 * PT_OF("w_fout0"), cfg.QMAX * 128], BF16, kind="Internal").ap() for i in range(NMID)]
    yp = dout("yp", [R * T, D])
    ys = dout("ys", [EX, D])
    hp = dout("hp", [NH, 128, 128])
    hs = dout("hs", [cfg.SPC, NH, 128, 128])
    cp = dout("cp", [R, 2, D])
    cs = dout("cs", [cfg.SPC, 2, D])
    agh_in = dint("agh_in", [G * 128, 129])
    agh = dint("agh", [NCORES * G * 128, 129])
    agh_mid = dint("agh_mid", [NCORES // 2 * G * 128, 129])
    agc_in = dint("agc_in", [128, FC * 2])
    agc = dint("agc", [NCORES * 128, FC * 2])
    agc_mid = dint("agc_mid", [NCORES // 2 * 128, FC * 2])
    sround = dint("sround", [128, NH, 128])

    es = ExitStack()
    with es:
        AW = 53000
        arena_t = es.enter_context(nc.sbuf_tensor("arena", [128, AW], F32))
        ar = Arena(arena_t, AW)
        banks = [es.enter_context(nc.psum_tensor(f"bank{i}", [128, 512], F32)) for i in range(6)]
        bank6b = es.enter_context(nc.psum_tensor("bank6b", [128, 1024], BF16))
        banks.append(None)
        banks.append(es.enter_context(nc.psum_tensor("bank7", [128, 512], F32)))
        P = Prog(nc, es)
        pe, act, dve, pool, sp = P.pe, P.act, P.dve, P.pool, P.sp
        B = Buf

        def chk(name):
            if cfg.stop == name:
                raise _Stop()

        const_t = ar.f32([128, CWID])
        cv = lambda k: const_t[:, clay[k][0]:clay[k][0] + clay[k][1]]
        identf = cv("identf"); smask = cv("smask"); sm01 = cv("sm01")[0:32, :]; pm64 = cv("pm64")
        identb = ar.bf16([128, 128]); onesb = ar.bf16([128, 128]); zerob = ar.bf16([128, 128])
        maskA = ar.bf16([128, 128]); maskS = ar.bf16([32, 32]); ones_f = ar.f32([128, 16])
        norms_t = ar.f32([128, 5, FC]); cw_t = ar.f32([128, 3, FC]); rm_t = ar.f32([128, 17])
        lb_t = ar.f32([128, NH]); oml_t = ar.f32([128, NH]); noml_t = ar.f32([128, NH])
        lbl_t = ar.f32([128, 3, NH]); og_t = ar.f32([128, 1])
        prev7 = ar.f32([128, 2 * FC])
        NSLOT = 6
        wslots = [ar.bf16([128, WCMAX]) for _ in range(NSLOT)]
        h_t = ar.bf16([128, FC, NX])
        aux = ar.bf16([128, FC, NX + 8])
        rstd_t = ar.f32([128, NX])
        gtmp = [ar.f32([128, NX]) for _ in range(2)]
        sqt = [ar.bf16([128, NX]) for _ in range(2)]
        ostage = [ar.f32([128, 1024]) for _ in range(2)]
        hal_t = ar.f32([128, 2, FC]); halg = ar.f32([128, NCORES, 2 * FC]); halo = ar.f32([128, 2, FC])
        XR0 = ar.mark()
        x_sb = ar.f32([128, FC, NX])
        XRx = ar.mark()
        XR1 = AW
        ar.reset(XR0)
        xst = [ar.f32([128, D // 2]) for _ in range(2)]
        junk = ar.f32([128, D // 2])
        ssq = ar.f32([128, 8])
        XRa = ar.mark()
        ar.reset(XR0)
        gs_ol2 = [ar.bf16([128, G, NX]) for _ in range(2)]; gs_qc2 = [ar.bf16([128, G, NX]) for _ in range(2)]
        gs_sg2 = [ar.bf16([128, G, NX]) for _ in range(2)]
        tq = ar.f32([128, NX]); tsig = ar.f32([128, NX]); tg = ar.f32([128, NX]); tk = ar.f32([128, NX])
        tb = ar.f32([128, NX]); teb = tsig; tenb = tg
        tqt = ar.bf16([128, NX]); tkt = ar.bf16([128, NX]); tvs = [ar.bf16([128, NX]), ar.bf16([128, NX])]
        ktok = ar.bf16([128, NPB, 128]); vtok = ar.bf16([128, NPB, 128])
        ktokm = ar.bf16([128, NPB, 2, 128])
        vtoks = ar.bf16([32, 128]); kt01 = ar.bf16([32, 2, 128]); ktoks = ar.bf16([32, 128])
        atm = ar.bf16([128, NPB, 128]); atms = ar.bf16([32, 32])
        td = ar.f32([128, NCH + 2]); tci = ar.f32([128, NCH]); tE = ar.f32([128, NCH])
        Mf = [ar.f32([128, 128]) for _ in range(2)]; Mtmp = ar.f32([128, 128]); Mb = ar.bf16([128, NCH, 128])
        Mbs = ar.bf16([128, 2, 128])
        LD2 = [ar.f32([128, G, 129]) for _ in range(2)]; LDj = [ar.f32([128, G, 129]) for _ in range(2)]
        S_t = ar.f32([128, G, 128]); Sin = ar.f32([128, G, 128]); Stmp = ar.f32([128, G, 128]); M0b = ar.bf16([128, G, 128])
        S0s = ar.f32([128, 2, G, 128]); NSs = ar.f32([128, 2, G, 128])
        to32 = ar.f32([128, NX]); osq = ar.bf16([128, NX]); t1 = to32; rs2 = ar.f32([128, NX])
        XRb = ar.mark()
        print("arena words: xsb_end", XRx, "p0_end", XRa, "p1_end", XRb, "of", AW)
        ar.reset(max(XRx, XRa, XRb))

        b_const = B("const"); b_prev7 = B("prev7")
        b_wslot = [B(f"ws{i}") for i in range(NSLOT)]
        b_h = [B(f"h{i}") for i in range(FC)]
        b_aux = [B(f"aux{i}") for i in range(max(FC, cfg.QMAX))]
        b_x = [B(f"x{i}") for i in range(FC)]
        b_bank = [B(f"bank{i}") for i in range(8)]
        b_rstd = B("rstd"); b_gtmp = [B("g0"), B("g1")]; b_sq = [B("q0"), B("q1")]
        b_ost = [B("o0"), B("o1")]
        s_cc = P.new_sem("cc")
        b_wag = {}
        RG1 = [[0, 1, 2, 3], [4, 5, 6, 7]]
        RG2 = [[0, 4], [1, 5], [2, 6], [3, 7]]

        def allgather(in_ap, mid_ap, out_ap, reads, outbuf, midbuf):
            P.op(pool, lambda e: e.collective_compute("AllGather", ALU.bypass, replica_groups=RG1, ins=[in_ap], outs=[mid_ap]),
                 reads=reads, writes=[midbuf], sem=s_cc, inc=1)
            P.op(pool, lambda e: e.collective_compute("AllGather", ALU.bypass, replica_groups=RG2, ins=[mid_ap], outs=[out_ap]),
                 reads=[midbuf], writes=[outbuf], sem=s_cc, inc=1)
        b_xst = [B("xst0"), B("xst1")]; b_ssq = B("ssq"); b_junk = B("junk")
        b_gs2 = [[B(f"gs{j}_{i}") for i in range(G)] for j in range(2)]
        bt = {k: B(k) for k in ("tq", "tsig", "tg", "tk", "tb", "teb", "tenb", "tqt", "tkt", "tv0", "tv1", "ktok", "vtok", "kts",
                                "atm", "td", "tci", "tE", "ktokm", "LD0", "LD1", "M0", "M1", "Mtmp", "Mb", "Mbs", "LD", "S", "Sin", "Stmp", "M0b",
                                "S0s", "NSs", "to32", "osq", "t1", "rs2")}
        b_ld = [B("ld0"), B("ld1")]
        b_sr = [B(f"sr{g}") for g in range(cfg.NG)]
        b_agh_in = B("agh_in"); b_agh = B("agh"); b_agc_in = B("agc_in"); b_agc = B("agc")
        b_agh_mid = B("agh_mid"); b_agc_mid = B("agc_mid")
        b_hal = B("hal"); b_halg = B("halg"); b_halo = B("halo")
        out_bufs = []

        P.dma(sp, const_t, consts, b_const, writes=[b_const])
        for dst, src in ((norms_t, norms), (cw_t, cw), (rm_t, rmask), (lbl_t, lbl), (og_t, ogain)):
            P.dma(sp, dst, src, b_const, writes=[b_const])
        for ap_, val in ((onesb, 1.0), (zerob, 0.0), (prev7, 0.0), (ones_f, 1.0)):
            P.op(dve, lambda e, ap_=ap_, val=val: e.memset(ap_, val), writes=[b_prev7])
        P.op(dve, lambda e: e.tensor_copy(out=identb, in_=identf), reads=[b_const], writes=[b_prev7])
        P.op(dve, lambda e: e.tensor_copy(out=maskA, in_=cv("maskA")), reads=[b_const], writes=[b_prev7])
        P.op(dve, lambda e: e.tensor_copy(out=maskS, in_=cv("maskS")[0:32, :]), reads=[b_const], writes=[b_prev7])
        e3 = gtmp[0][:, 0:3 * NH].rearrange("p (a b) -> p a b", a=3)
        P.op(act, lambda e: e.activation(out=e3, in_=lbl_t, func=AF.Exp), reads=[b_const], writes=[b_gtmp[0]])
        s1 = gtmp[1][:, 0:NH]
        P.op(dve, lambda e: e.tensor_tensor(out=s1, in0=e3[:, 0, :], in1=e3[:, 1, :], op=ALU.add), reads=[b_gtmp[0]], writes=[b_gtmp[1]])
        P.op(dve, lambda e: e.tensor_tensor(out=s1, in0=s1, in1=e3[:, 2, :], op=ALU.add), reads=[b_gtmp[0], b_gtmp[1]], writes=[b_gtmp[1]])
        P.op(dve, lambda e: e.reciprocal(out=s1, in_=s1), reads=[b_gtmp[1]], writes=[b_gtmp[1]])
        P.op(dve, lambda e: e.tensor_tensor(out=lb_t, in0=e3[:, 0, :], in1=s1, op=ALU.mult), reads=[b_gtmp[0], b_gtmp[1]], writes=[b_prev7])
        P.op(dve, lambda e: e.tensor_scalar(out=oml_t, in0=lb_t, scalar1=-1.0, scalar2=1.0, op0=ALU.mult, op1=ALU.add),
             reads=[b_prev7], writes=[b_prev7])
        P.op(dve, lambda e: e.tensor_scalar(out=noml_t, in0=lb_t, scalar1=1.0, scalar2=-1.0, op0=ALU.mult, op1=ALU.add),
             reads=[b_prev7], writes=[b_prev7])
        P.barrier()
        try:
            chk("consts")
            _go = True
        except _Stop:
            _go = False
        b_const = B("const_ro")

        RG = [list(range(NCORES))]
        b_win = {}
        wp_by_name = {}
        gstate = {}
        b_midA = [B(f"midA{i}") for i in range(NMID)]
        b_midB = [B(f"midB{i}") for i in range(NMID)]
        midctr = {"A": 0, "B": 0}
        LOOKAHEAD = 10
        for (key, n, t0, nt, PT) in (WP if _go else []):
            wp_by_name.setdefault(n, []).append((key, t0, nt, PT))
            rows, c = wsh[key].shape
            b_win[key] = B(key + "_i")
            for r0 in range(0, rows, 1024):
                r1 = min(rows, r0 + 1024)
                P.dma(pool, wag_in[key][r0:r1, :], wsh[key][r0:r1, :], b_win[key], writes=[b_win[key]])
            npc = nt // PT
            gstate[key] = {"s1": 0, "s2": 0, "np": npc, "PT": PT, "mids": {}}
            b_wag[key] = [B(f"{key}_p{i}") for i in range(npc)]

        def gather_step(key, upto):
            st = gstate[key]
            PT = st["PT"]
            RPR = PT * 16
            kind = "B" if PT == PT_OF("w_fout0") else "A"
            mids, bmids = (midB, b_midB) if kind == "B" else (midA, b_midA)
            upto = min(upto, st["np"] - 1)

            def s2(p):
                k = st["mids"][p]
                P.op(pool, lambda e: e.collective_compute("AllGather", ALU.bypass, replica_groups=RG2, ins=[mids[k]],
                                                          outs=[wag[key][p * PT * 128:(p + 1) * PT * 128, :]]),
                     reads=[bmids[k]], writes=[b_wag[key][p]], sem=s_cc, inc=1)
            while st["s1"] <= upto:
                p = st["s1"]
                k = midctr[kind] % NMID
                midctr[kind] += 1
                st["mids"][p] = k
                P.op(pool, lambda e: e.collective_compute("AllGather", ALU.bypass, replica_groups=RG1,
                                                          ins=[wag_in[key][p * RPR:(p + 1) * RPR, :]], outs=[mids[k]]),
                     reads=[b_win[key]], writes=[bmids[k]], sem=s_cc, inc=1)
                st["s1"] += 1
                while st["s2"] < st["s1"] - 1:
                    s2(st["s2"])
                    st["s2"] += 1
            if st["s1"] == st["np"]:
                while st["s2"] < st["np"]:
                    s2(st["s2"])
                    st["s2"] += 1

        def ensure_piece(key, p):
            st = gstate[key]
            gather_step(key, p + LOOKAHEAD)
            while st["s2"] <= p:
                k = st["s2"]
                gather_flush_one(key)

        def gather_flush_one(key):
            st = gstate[key]
            PT = st["PT"]
            kind = "B" if PT == PT_OF("w_fout0") else "A"
            mids, bmids = (midB, b_midB) if kind == "B" else (midA, b_midA)
            p = st["s2"]
            k = st["mids"][p]
            P.op(pool, lambda e: e.collective_compute("AllGather", ALU.bypass, replica_groups=RG2, ins=[mids[k]],
                                                      outs=[wag[key][p * PT * 128:(p + 1) * PT * 128, :]]),
                 reads=[bmids[k]], writes=[b_wag[key][p]], sem=s_cc, inc=1)
            st["s2"] += 1

        wctr = [0]

        def wload(name, tile_idx, ncols):
            i = wctr[0] % NSLOT
            wctr[0] += 1
            for (key, t0, nt, PT) in wp_by_name[name]:
                if t0 <= tile_idx < t0 + nt:
                    break
            li = tile_idx - t0
            pidx = li // PT
            ensure_piece(key, pidx)
            src = wag[key][li * 128:(li + 1) * 128, 0:ncols]
            P.dma(sp, wslots[i][:, 0:ncols], src, b_wslot[i], reads=[b_wag[key][pidx]], writes=[b_wslot[i]])
            return wslots[i], b_wslot[i]

        NACC = 6
        accctr = [0]

        def acc_get():
            i = accctr[0] % NACC
            accctr[0] += 1
            return i

        def acc_seg(i, c0, n):
            if c0 < T:
                return banks[i][:, c0:c0 + n]
            return banks[7][:, 32 * i + (c0 - T):32 * i + (c0 - T) + n]

        def acc_bufs(i, N):
            return [b_bank[i]] + ([b_bank[7]] if N > T else [])

        def segs(N):
            return [(0, T)] + ([(T, N - T)] if N > T else [])

        def mm_acc(i, N, steps, reads, first, last):
            fl = []
            for (c0, n) in segs(N):
                for k, (lh, rf) in enumerate(steps):
                    fl.append(lambda e, c0=c0, n=n, lh=lh, rf=rf, k=k: e.matmul(
                        acc_seg(i, c0, n), lh, rf(c0, n), start=(first and k == 0), stop=(last and k == len(steps) - 1)))
            P.op(pe, fl, reads=reads, writes=acc_bufs(i, N))

        def proj(name, tile0, N, rhs_bufs, rhs_fn, i):
            for half in range(2):
                wt, wb = wload(name, tile0 + half, WC)
                w3 = wt[:, 0:WC].rearrange("p (k c) -> p k c", c=128)
                steps = [(w3[:, k, :], (lambda c0, n, kc=half * KH + k: rhs_fn(kc, c0, n))) for k in range(KH)]
                mm_acc(i, N, steps, [wb] + list(rhs_bufs), first=(half == 0), last=(half == 1))

        def rsqrt_inplace(ap, buf):
            P.op(act, lambda e: e.activation(out=ap, in_=ap, func=AF.Sqrt), reads=[buf], writes=[buf])
            P.op(dve, lambda e: e.reciprocal(out=ap, in_=ap), reads=[buf], writes=[buf])

        def fm_norm(N):
            i = acc_get()
            for fc in range(FC):
                j = fc % 2
                P.op(act, lambda e, fc=fc, j=j: e.activation(out=sqt[j][:, 0:N], in_=x_sb[:, fc, 0:N], func=AF.Square),
                     reads=[b_x[fc]], writes=[b_sq[j]])
                mm_acc(i, N, [(onesb, lambda c0, n, j=j: sqt[j][:, c0:c0 + n])], [b_sq[j]], first=(fc == 0), last=(fc == FC - 1))
            for (c0, n) in segs(N):
                P.op(dve, lambda e, c0=c0, n=n: e.tensor_scalar(out=rstd_t[:, c0:c0 + n], in0=acc_seg(i, c0, n), scalar1=1.0 / D,
                                                                 scalar2=EPS, op0=ALU.mult, op1=ALU.add),
                     reads=acc_bufs(i, N), writes=[b_rstd])
            rsqrt_inplace(rstd_t[:, 0:N], b_rstd)

        def fm_norm_apply(N, gain_idx):
            for fc in range(FC):
                P.op(dve, lambda e, fc=fc: e.scalar_tensor_tensor(out=h_t[:, fc, 0:N], in0=x_sb[:, fc, 0:N],
                                                                  scalar=norms_t[:, gain_idx, fc:fc + 1], in1=rstd_t[:, 0:N],
                                                                  op0=ALU.mult, op1=ALU.mult),
                     reads=[b_x[fc], b_rstd], writes=[b_h[fc]])

        actv = aux.rearrange("p a b -> p (a b)")[:, 0:cfg.QMAX * NX].rearrange("p (a b) -> p a b", b=NX)
        hrd = lambda kc, c0, n: h_t[:, kc, c0:c0 + n]

        def ffn(l, N):
            fm_norm(N)
            fm_norm_apply(N, 1 + 2 * l)
            s0 = 0
            for q in range(cfg.NQ):
                nq = cfg.QS[q]
                for sl in range(nq):
                    s = s0 + sl
                    ig = acc_get()
                    proj(f"w_fin{l}", (s * 2 + 0) * 2, N, b_h, hrd, ig)
                    iu = acc_get()
                    proj(f"w_fin{l}", (s * 2 + 1) * 2, N, b_h, hrd, iu)
                    j = s % 2
                    for (c0, n) in segs(N):
                        P.op(act, lambda e, c0=c0, n=n, j=j, ig=ig: e.activation(out=gtmp[j][:, c0:c0 + n], in_=acc_seg(ig, c0, n), func=AF.Silu),
                             reads=acc_bufs(ig, N), writes=[b_gtmp[j]])
                    for (c0, n) in segs(N):
                        P.op(dve, lambda e, c0=c0, n=n, j=j, iu=iu, sl=sl: e.tensor_tensor(out=actv[:, sl, c0:c0 + n], in0=gtmp[j][:, c0:c0 + n],
                                                                                        in1=acc_seg(iu, c0, n), op=ALU.mult),
                             reads=acc_bufs(iu, N) + [b_gtmp[j]], writes=[b_aux[sl]])
                for fc in range(FC):
                    wt, wb = wload(f"w_fout{l}", q * FC + fc, cfg.QMAX * 128)
                    w3 = wt[:, 0:cfg.QMAX * 128].rearrange("p (k c) -> p k c", c=128)
                    i = acc_get()
                    steps = [(w3[:, k, :], (lambda c0, n, k=k: actv[:, k, c0:c0 + n])) for k in range(nq)]
                    mm_acc(i, N, steps, [wb] + b_aux[0:nq], first=True, last=True)
                    for (c0, n) in segs(N):
                        P.op(dve, lambda e, c0=c0, n=n, fc=fc, i=i: e.tensor_tensor(out=x_sb[:, fc, c0:c0 + n], in0=x_sb[:, fc, c0:c0 + n],
                                                                                 in1=acc_seg(i, c0, n), op=ALU.add),
                             reads=acc_bufs(i, N) + [b_x[fc]], writes=[b_x[fc]])
                s0 += nq

        bk6 = bank6b[:, :]
        psTk = bk6[:, 0:NPB * 128].rearrange("p (a b) -> p a b", b=128)
        psTv = bk6[:, 512:512 + NPB * 128].rearrange("p (a b) -> p a b", b=128)
        psTks = bk6[:, 0:128]; psTvs = bk6[:, 512:640]
        psATs = banks[7][:, 320:352]
        HW_ = D // 2

        def round_body(r):
            last_round = (r == R - 1)
            N = NX if last_round else T
            P.barrier()
            tok_blocks = [(xp[r * T + tbk * 128: r * T + tbk * 128 + 128, :], 128, tbk * 128) for tbk in range(NPB)]
            if last_round:
                tok_blocks.append((xs[0:EX, :], EX, T))
            for (src, nt, c0) in tok_blocks:
                for hf in range(2):
                    P.dma(sp, xst[hf][0:nt, :], src[:, hf * HW_:(hf + 1) * HW_], b_xst[hf], writes=[b_xst[hf]])
                    P.op(act, lambda e, hf=hf, nt=nt: e.activation(out=junk[0:nt, :], in_=xst[hf][0:nt, :], func=AF.Square,
                                                                 accum_out=ssq[0:nt, hf:hf + 1]),
                         reads=[b_xst[hf]], writes=[b_ssq, b_junk])
                P.op(dve, lambda e, nt=nt: e.tensor_tensor(out=ssq[0:nt, 2:3], in0=ssq[0:nt, 0:1], in1=ssq[0:nt, 1:2], op=ALU.add),
                     reads=[b_ssq], writes=[b_ssq])
                P.op(dve, lambda e, nt=nt: e.tensor_scalar(out=ssq[0:nt, 2:3], in0=ssq[0:nt, 2:3], scalar1=1.0 / D, scalar2=EPS,
                                                          op0=ALU.mult, op1=ALU.add), reads=[b_ssq], writes=[b_ssq])
                P.op(act, lambda e, nt=nt: e.activation(out=ssq[0:nt, 2:3], in_=ssq[0:nt, 2:3], func=AF.Sqrt), reads=[b_ssq], writes=[b_ssq])
                P.op(dve, lambda e, nt=nt: e.reciprocal(out=ssq[0:nt, 3:4], in_=ssq[0:nt, 2:3]), reads=[b_ssq], writes=[b_ssq])
                for hf in range(2):
                    P.op(dve, lambda e, hf=hf, nt=nt: e.tensor_scalar(out=xst[hf][0:nt, :], in0=xst[hf][0:nt, :], scalar1=ssq[0:nt, 3:4],
                                                                    scalar2=None, op0=ALU.mult), reads=[b_ssq, b_xst[hf]], writes=[b_xst[hf]])
                    for f4 in range(0, FC // 2, 4):
                        nf = min(4, FC // 2 - f4)
                        i = acc_get()
                        P.op(pe, [lambda e, hf=hf, f=f, f4=f4, nt=nt, i=i: e.transpose(banks[i][:, (f - f4) * 128:(f - f4) * 128 + nt],
                                                                                         xst[hf][0:nt, f * 128:(f + 1) * 128], identf[0:nt, 0:nt])
                                  for f in range(f4, f4 + nf)], reads=[b_xst[hf]], writes=[b_bank[i]])
                        for f in range(f4, f4 + nf):
                            fcg = hf * (FC // 2) + f
                            P.op(dve, lambda e, f=f, f4=f4, fcg=fcg, nt=nt, c0=c0, i=i: e.tensor_scalar(
                                out=h_t[:, fcg, c0:c0 + nt], in0=banks[i][:, (f - f4) * 128:(f - f4) * 128 + nt],
                                scalar1=norms_t[:, 0, fcg:fcg + 1], scalar2=None, op0=ALU.mult),
                                reads=[b_bank[i]], writes=[b_h[fcg]])
            P.barrier()
            chk("p0")

            def grp_local(grp):
                gs_ol = gs_ol2[grp % 2]; gs_qc = gs_qc2[grp % 2]; gs_sg = gs_sg2[grp % 2]; b_gs = b_gs2[grp % 2]
                LD = LD2[grp % 2]; bLD = bt[f"LD{grp % 2}"]
                if last_round:
                    for q in range(2):
                        P.dma(sp, S0s[:, q, :, :], sh[q, grp * G:(grp + 1) * G, :, :].rearrange("h k v -> k h v"), bt["S0s"], writes=[bt["S0s"]])
                def emit_proj(hg):
                    hd = grp * G + hg
                    tv = tvs[hg % 2]; btv = bt[f"tv{hg % 2}"]
                    iq = acc_get(); proj("w_hin", (hd * 4 + 0) * 2, N, b_h, hrd, iq)
                    if_ = acc_get(); proj("w_hin", (hd * 4 + 1) * 2, N, b_h, hrd, if_)
                    ii = acc_get(); proj("w_hin", (hd * 4 + 2) * 2, N, b_h, hrd, ii)
                    ig = acc_get(); proj("w_hin", (hd * 4 + 3) * 2, N, b_h, hrd, ig)
                    for (c0, n) in segs(N):
                        P.op(act, lambda e, c0=c0, n=n: e.activation(out=tq[:, c0:c0 + n], in_=acc_seg(iq, c0, n), func=AF.Silu),
                             reads=acc_bufs(iq, N), writes=[bt["tq"]])
                        P.op(act, lambda e, c0=c0, n=n: e.activation(out=gs_sg[:, hg, c0:c0 + n], in_=acc_seg(ig, c0, n), func=AF.Silu),
                             reads=acc_bufs(ig, N), writes=[b_gs[hg]])
                        P.op(act, lambda e, c0=c0, n=n: e.activation(out=tsig[:, c0:c0 + n], in_=acc_seg(if_, c0, n), func=AF.Sigmoid),
                             reads=acc_bufs(if_, N), writes=[bt["tsig"]])
                        P.op(dve, lambda e, c0=c0, n=n: e.tensor_copy(out=tv[:, c0:c0 + n], in_=acc_seg(ii, c0, n)),
                             reads=acc_bufs(ii, N), writes=[btv])
                    hstate[hg] = (iq, if_, ii, ig)

                def emit_chain(hg):
                    hd = grp * G + hg
                    P.op(act, lambda e: e.activation(out=tg[:, 0:N], in_=tsig[:, 0:N], func=AF.Ln, scale=oml_t[:, hd:hd + 1], bias=lb_t[:, hd:hd + 1]),
                         reads=[bt["tsig"]], writes=[bt["tg"]])
                    P.op(dve, lambda e: e.tensor_scalar(out=tk[:, 0:N], in0=tsig[:, 0:N], scalar1=noml_t[:, hd:hd + 1], scalar2=oml_t[:, hd:hd + 1],
                                                        op0=ALU.mult, op1=ALU.add), reads=[bt["tsig"]], writes=[bt["tk"]])
                    P.op(dve, lambda e: e.tensor_tensor_scan(out=tb[:, 0:N], data0=smask[:, 0:N], data1=tg[:, 0:N], initial=0.0,
                                                             op0=ALU.mult, op1=ALU.add), reads=[bt["tg"]], writes=[bt["tb"]])
                    P.op(act, lambda e: e.activation(out=teb[:, 0:N], in_=tb[:, 0:N], func=AF.Exp), reads=[bt["tb"]], writes=[bt["tsig"]])
                    P.op(act, lambda e: e.activation(out=tenb[:, 0:N], in_=tb[:, 0:N], func=AF.Exp, scale=-1.0), reads=[bt["tb"]], writes=[bt["tg"]])
                    P.op(dve, lambda e: e.tensor_tensor(out=tqt[:, 0:N], in0=tq[:, 0:N], in1=teb[:, 0:N], op=ALU.mult),
                         reads=[bt["tq"], bt["tsig"]], writes=[bt["tqt"]])
                    P.op(dve, lambda e: e.tensor_tensor(out=tkt[:, 0:N], in0=tk[:, 0:N], in1=tenb[:, 0:N], op=ALU.mult),
                         reads=[bt["tk"], bt["tg"]], writes=[bt["tkt"]])
                    chk("h_act")
                    bl = tb[:, 0:T].rearrange("p (c t) -> p c t", t=64)[:, :, 63]
                    P.op(act, lambda e: e.activation(out=td[:, 0:NCH], in_=bl, func=AF.Exp), reads=[bt["tb"]], writes=[bt["td"]])
                    if last_round:
                        bls = tb[:, T:NX].rearrange("p (c t) -> p c t", t=16)[:, :, 15]
                        P.op(act, lambda e: e.activation(out=td[:, NCH:NCH + 2], in_=bls, func=AF.Exp), reads=[bt["tb"]], writes=[bt["td"]])
                    P.op(dve, lambda e: e.tensor_tensor_scan(out=tci, data0=ones_f[:, 0:NCH], data1=bl, initial=0.0, op0=ALU.mult, op1=ALU.add),
                         reads=[bt["tb"]], writes=[bt["tci"]])
                    P.op(act, lambda e: e.activation(out=tE, in_=tci, func=AF.Exp), reads=[bt["tci"]], writes=[bt["tE"]])
                    P.op(dve, lambda e: e.tensor_copy(out=gs_qc[:, hg, 0:64], in_=tqt[:, 0:64]), reads=[bt["tqt"]], writes=[b_gs[hg]])
                    if NCH > 1:
                        P.op(dve, lambda e: e.tensor_tensor(out=gs_qc[:, hg, 64:T].rearrange("p (c t) -> p c t", t=64),
                                                            in0=tqt[:, 64:T].rearrange("p (c t) -> p c t", t=64),
                                                            in1=tE[:, 0:NCH - 1].unsqueeze(2).to_broadcast([128, NCH - 1, 64]), op=ALU.mult),
                             reads=[bt["tqt"], bt["tE"]], writes=[b_gs[hg]])

                def emit_rec(hg):
                    hd = grp * G + hg
                    tv = tvs[hg % 2]; btv = bt[f"tv{hg % 2}"]
                    chk("h_dec")
                    P.op(pe, [lambda e, pb=pb: e.transpose(psTk[:, pb, :], tkt[:, pb * 128:(pb + 1) * 128], identb) for pb in range(NPB)]
                         + [lambda e, pb=pb: e.transpose(psTv[:, pb, :], tv[:, pb * 128:(pb + 1) * 128], identb) for pb in range(NPB)],
                         reads=[bt["tkt"], btv], writes=[b_bank[6]])
                    chk("h_tr0")
                    P.op(act, lambda e: e.activation(out=ktok, in_=psTk, func=AF.Copy), reads=[b_bank[6]], writes=[bt["ktok"]])
                    chk("h_tr1")
                    P.op(act, lambda e: e.activation(out=vtok, in_=psTv, func=AF.Copy), reads=[b_bank[6]], writes=[bt["vtok"]])
                    if last_round:
                        P.op(pe, [lambda e: e.transpose(psTks[0:32, :], tkt[:, T:NX], identb), lambda e: e.transpose(psTvs[0:32, :], tv[:, T:NX], identb)],
                             reads=[bt["tkt"], btv], writes=[b_bank[6]])
                        P.op(act, lambda e: e.activation(out=vtoks, in_=psTvs[0:32, :], func=AF.Copy), reads=[b_bank[6]], writes=[bt["kts"]])
                        P.op(act, lambda e: e.activation(out=ktoks, in_=psTks[0:32, :], func=AF.Copy), reads=[b_bank[6]], writes=[bt["kts"]])
                        for q in range(2):
                            P.op(dve, lambda e, q=q: e.tensor_scalar(out=kt01[:, q, :], in0=ktoks, scalar1=sm01[:, q:q + 1], scalar2=None, op0=ALU.mult),
                                 reads=[bt["kts"]], writes=[bt["kts"]])
                    chk("h_tr")
                    for j in range(2):
                        P.op(dve, lambda e, j=j: e.tensor_scalar(out=ktokm[:, :, j, :], in0=ktok, scalar1=pm64[:, j:j + 1], scalar2=None, op0=ALU.mult),
                             reads=[bt["ktok"]], writes=[bt["ktokm"]])
                    iU = []
                    for half in range(0, NCH, 4):
                        i = acc_get()
                        iU.append(i)
                        P.op(pe, [lambda e, c=c, i=i, half=half: e.matmul(banks[i][:, (c - half) * 128:(c - half + 1) * 128],
                                                                           ktokm[:, c // 2, c % 2, :], vtok[:, c // 2, :], start=True, stop=True)
                                  for c in range(half, min(NCH, half + 4))], reads=[bt["ktokm"], bt["vtok"]], writes=[b_bank[i]])
                    if last_round:
                        iUs = acc_get()
                        P.op(pe, [lambda e, q=q: e.matmul(banks[iUs][:, q * 128:(q + 1) * 128], kt01[:, q, :], vtoks, start=True, stop=True) for q in range(2)],
                             reads=[bt["kts"]], writes=[b_bank[iUs]])
                    chk("h_U")
                    Ubank = lambda c: banks[iU[c // 4]][:, (c % 4) * 128:(c % 4 + 1) * 128]
                    bU = lambda c: b_bank[iU[c // 4]]
                    cur = 0
                    for c in range(NCH):
                        fin = (c == NCH - 1)
                        dst = LD[:, hg, 0:128] if fin else Mf[1 - cur]
                        dbuf = bLD if fin else bt[f"M{1 - cur}"]
                        if c == 0:
                            P.op(dve, lambda e, dst=dst: e.tensor_scalar(out=dst, in0=Ubank(0), scalar1=td[:, 0:1], scalar2=None, op0=ALU.mult),
                                 reads=[bU(0), bt["td"]], writes=[dbuf])
                        else:
                            P.op(dve, lambda e, c=c, cur=cur: e.tensor_scalar(out=Mtmp, in0=Mf[cur], scalar1=td[:, c:c + 1], scalar2=None, op0=ALU.mult),
                                 reads=[bt[f"M{cur}"], bt["td"]], writes=[bt["Mtmp"]])
                            P.op(dve, lambda e, c=c, dst=dst: e.scalar_tensor_tensor(out=dst, in0=Ubank(c), scalar=td[:, c:c + 1], in1=Mtmp,
                                                                                   op0=ALU.mult, op1=ALU.add),
                                 reads=[bU(c), bt["td"], bt["Mtmp"]], writes=[dbuf])
                        if not fin:
                            P.op(act, lambda e, c=c, cur=cur: e.activation(out=Mb[:, c + 1, :], in_=Mf[1 - cur], func=AF.Copy),
                                 reads=[bt[f"M{1 - cur}"]], writes=[bt["Mb"]])
                        cur = 1 - cur
                    P.op(act, lambda e: e.activation(out=LD[:, hg, 128:129], in_=tE[:, NCH - 1:NCH], func=AF.Copy), reads=[bt["tE"]], writes=[bLD])
                    if last_round:
                        for q in range(2):
                            P.op(act, lambda e, q=q: e.activation(out=Mbs[:, q, :], in_=S0s[:, q, hg, :], func=AF.Copy), reads=[bt["S0s"]], writes=[bt["Mbs"]])
                            P.op(dve, lambda e, q=q: e.tensor_scalar(out=Mtmp, in0=S0s[:, q, hg, :], scalar1=td[:, NCH + q:NCH + q + 1], scalar2=None, op0=ALU.mult),
                                 reads=[bt["S0s"], bt["td"]], writes=[bt["Mtmp"]])
                            P.op(dve, lambda e, q=q: e.scalar_tensor_tensor(out=NSs[:, q, hg, :], in0=banks[iUs][:, q * 128:(q + 1) * 128],
                                                                          scalar=td[:, NCH + q:NCH + q + 1], in1=Mtmp, op0=ALU.mult, op1=ALU.add),
                                 reads=[b_bank[iUs], bt["td"], bt["Mtmp"]], writes=[bt["NSs"]])
                    chk("h_chain")
                    iA = acc_get()
                    P.op(pe, [lambda e, pb=pb: e.matmul(banks[iA][:, pb * 128:(pb + 1) * 128], tkt[:, pb * 128:(pb + 1) * 128],
                                                        tqt[:, pb * 128:(pb + 1) * 128], start=True, stop=True) for pb in range(NPB)]
                         + ([lambda e: e.matmul(psATs[0:32, :], tkt[:, T:NX], tqt[:, T:NX], start=True, stop=True)] if last_round else []),
                         reads=[bt["tkt"], bt["tqt"]], writes=[b_bank[iA]] + ([b_bank[7]] if last_round else []))
                    P.op(dve, lambda e: e.tensor_tensor(out=atm, in0=banks[iA][:, 0:NPB * 128].rearrange("p (a b) -> p a b", b=128),
                                                        in1=maskA.unsqueeze(1).to_broadcast([128, NPB, 128]), op=ALU.mult),
                         reads=[b_bank[iA]], writes=[bt["atm"]])
                    if last_round:
                        P.op(dve, lambda e: e.tensor_tensor(out=atms, in0=psATs[0:32, :], in1=maskS, op=ALU.mult),
                             reads=[b_bank[7]], writes=[bt["atm"]])
                    chk("h_A")
                    iO = acc_get()
                    fl = []
                    for c in range(NCH):
                        pb, j = c // 2, c % 2
                        fl.append(lambda e, c=c, pb=pb, j=j: e.matmul(banks[iO][:, c * 64:(c + 1) * 64], vtok[:, pb, :], atm[:, pb, j * 64:(j + 1) * 64],
                                                                     start=True, stop=False))
                        fl.append(lambda e, c=c: e.matmul(banks[iO][:, c * 64:(c + 1) * 64], (zerob if c == 0 else Mb[:, c, :]), tqt[:, c * 64:(c + 1) * 64],
                                                         start=False, stop=True))
                    if last_round:
                        for q in range(2):
                            fl.append(lambda e, q=q: e.matmul(acc_seg(iO, T + 16 * q, 16), vtoks, atms[:, q * 16:(q + 1) * 16], start=True, stop=False))
                            fl.append(lambda e, q=q: e.matmul(acc_seg(iO, T + 16 * q, 16), Mbs[:, q, :], tqt[:, T + 16 * q:T + 16 * q + 16], start=False, stop=True))
                    P.op(pe, fl, reads=[bt["vtok"], bt["atm"], bt["Mb"], bt["tqt"], bt["kts"], bt["Mbs"]], writes=acc_bufs(iO, N))
                    for (c0, n) in segs(N):
                        P.op(act, lambda e, c0=c0, n=n: e.activation(out=gs_ol[:, hg, c0:c0 + n], in_=acc_seg(iO, c0, n), func=AF.Copy),
                             reads=acc_bufs(iO, N), writes=[b_gs[hg]])

                hstate = {}
                emit_proj(0)
                emit_chain(0)
                for hg in range(G):
                    if hg + 1 < G:
                        emit_proj(hg + 1)
                    emit_rec(hg)
                    if hg + 1 < G:
                        emit_chain(hg + 1)
                chk("p1h")
                if last_round:
                    for q in range(2):
                        P.dma(sp, hs[q, grp * G:(grp + 1) * G, :, :].rearrange("h k v -> k h v"), NSs[:, q, :, :], bt["NSs"], reads=[bt["NSs"]], writes=[bt["NSs"]])
                    out_bufs.append(bt["NSs"])

            def grp_start_ex(grp):
                gs_ol = gs_ol2[grp % 2]; gs_qc = gs_qc2[grp % 2]; gs_sg = gs_sg2[grp % 2]; b_gs = b_gs2[grp % 2]
                LD = LD2[grp % 2]; bLD = bt[f"LD{grp % 2}"]
                P.dma(sp, agh_in.rearrange("(g k) c -> k g c", k=128), LD, b_agh_in, reads=[bLD], writes=[b_agh_in])
                allgather(agh_in, agh_mid, agh, [b_agh_in], b_agh, b_agh_mid)

            def grp_finish(grp):
                gs_ol = gs_ol2[grp % 2]; gs_qc = gs_qc2[grp % 2]; gs_sg = gs_sg2[grp % 2]; b_gs = b_gs2[grp % 2]
                LD = LD2[grp % 2]; bLD = bt[f"LD{grp % 2}"]
                if r == 0:
                    P.op(dve, lambda e: e.memset(S_t, 0.0), writes=[bt["S"]])
                else:
                    P.dma(sp, S_t, sround[:, grp * G:(grp + 1) * G, :], bt["S"], reads=[b_sr[grp]], writes=[bt["S"]])
                P.op(dve, lambda e: e.memset(Sin, 0.0), writes=[bt["Sin"]])
                for j in range(NCORES):
                    jj = j % 2
                    P.dma(sp, LDj[jj], agh[j * G * 128:(j + 1) * G * 128, :].rearrange("(g k) c -> k g c", k=128), b_ld[jj], reads=[b_agh], writes=[b_ld[jj]])
                    P.op(dve, lambda e, j=j: e.scalar_tensor_tensor(out=Sin, in0=S_t, scalar=rm_t[:, j:j + 1], in1=Sin, op0=ALU.mult, op1=ALU.add),
                         reads=[bt["S"], bt["Sin"]], writes=[bt["Sin"]])
                    P.op(dve, lambda e, jj=jj: e.tensor_tensor(out=Stmp, in0=S_t, in1=LDj[jj][:, :, 128:129].to_broadcast([128, G, 128]), op=ALU.mult),
                         reads=[bt["S"], b_ld[jj]], writes=[bt["Stmp"]])
                    P.op(dve, lambda e, jj=jj: e.tensor_tensor(out=S_t, in0=Stmp, in1=LDj[jj][:, :, 0:128], op=ALU.add),
                         reads=[bt["Stmp"], b_ld[jj]], writes=[bt["S"]])
                if last_round:
                    P.dma(sp, hp[grp * G:(grp + 1) * G, :, :].rearrange("h k v -> k h v"), S_t, b_sr[grp], reads=[bt["S"]], writes=[b_sr[grp]])
                    out_bufs.append(b_sr[grp])
                else:
                    P.dma(sp, sround[:, grp * G:(grp + 1) * G, :], S_t, b_sr[grp], reads=[bt["S"]], writes=[b_sr[grp]])
                P.op(act, lambda e: e.activation(out=M0b, in_=Sin, func=AF.Copy), reads=[bt["Sin"]], writes=[bt["M0b"]])
                for hg in range(G):
                    hd = grp * G + hg
                    iC = acc_get()
                    P.op(pe, lambda e: e.matmul(banks[iC][:, 0:T], M0b[:, hg, :], gs_qc[:, hg, 0:T], start=True, stop=True),
                         reads=[bt["M0b"], b_gs[hg]], writes=[b_bank[iC]])
                    P.op(dve, lambda e: e.tensor_tensor(out=to32[:, 0:T], in0=banks[iC][:, 0:T], in1=gs_ol[:, hg, 0:T], op=ALU.add),
                         reads=[b_bank[iC], b_gs[hg]], writes=[bt["to32"]])
                    if last_round:
                        P.op(dve, lambda e: e.tensor_copy(out=to32[:, T:NX], in_=gs_ol[:, hg, T:NX]), reads=[b_gs[hg]], writes=[bt["to32"]])
                    P.op(act, lambda e: e.activation(out=osq[:, 0:N], in_=to32[:, 0:N], func=AF.Square), reads=[bt["to32"]], writes=[bt["osq"]])
                    iS = acc_get()
                    mm_acc(iS, N, [(onesb, lambda c0, n: osq[:, c0:c0 + n])], [bt["osq"]], first=True, last=True)
                    for (c0, n) in segs(N):
                        P.op(dve, lambda e, c0=c0, n=n: e.tensor_scalar(out=rs2[:, c0:c0 + n], in0=acc_seg(iS, c0, n), scalar1=1.0 / 128, scalar2=EPS,
                                                                         op0=ALU.mult, op1=ALU.add), reads=acc_bufs(iS, N), writes=[bt["rs2"]])
                    rsqrt_inplace(rs2[:, 0:N], bt["rs2"])
                    P.op(dve, lambda e: e.scalar_tensor_tensor(out=t1[:, 0:N], in0=to32[:, 0:N], scalar=og_t[:, 0:1], in1=rs2[:, 0:N], op0=ALU.mult, op1=ALU.mult),
                         reads=[bt["to32"], bt["rs2"]], writes=[bt["to32"]])
                    P.op(dve, lambda e: e.tensor_tensor(out=aux[:, hd, 0:N], in0=t1[:, 0:N], in1=gs_sg[:, hg, 0:N], op=ALU.mult),
                         reads=[bt["to32"], b_gs[hg]], writes=[b_aux[hd]])


            grp_local(0)
            grp_start_ex(0)
            for grp in range(1, cfg.NG):
                grp_local(grp)
                grp_finish(grp - 1)
                grp_start_ex(grp)
            grp_finish(cfg.NG - 1)
            chk("p1")
            P.barrier()
            for fc in range(FC):
                j = fc % 2
                xr = ostage[j][:, 0:(NPB + 1) * 128].rearrange("p (a b) -> p a b", b=128)
                for tbk in range(NPB):
                    P.dma(sp, xr[:, tbk, :], xp[r * T + tbk * 128:r * T + tbk * 128 + 128, fc * 128:(fc + 1) * 128], b_ost[j], writes=[b_ost[j]])
                if last_round:
                    P.dma(sp, xr[0:EX, NPB, :], xs[:, fc * 128:(fc + 1) * 128], b_ost[j], writes=[b_ost[j]])
                i = acc_get()
                for half in range(2):
                    wt, wb = wload("w_hout", fc * 2 + half, WC)
                    w3 = wt[:, 0:WC].rearrange("p (k c) -> p k c", c=128)
                    fl = []
                    for (c0, n) in segs(N):
                        for k in range(KH):
                            kc = half * KH + k
                            fl.append(lambda e, c0=c0, n=n, k=k, kc=kc, w3=w3, half=half: e.matmul(acc_seg(i, c0, n), w3[:, k, :], aux[:, kc, c0:c0 + n],
                                                                                                    start=(half == 0 and k == 0), stop=False))
                    rd = [wb] + b_aux[0:FC]
                    if half == 1:
                        for tbk in range(NPB):
                            fl.append(lambda e, tbk=tbk, xr=xr: e.matmul(banks[i][:, tbk * 128:(tbk + 1) * 128], xr[:, tbk, :], identf, start=False, stop=True))
                        if last_round:
                            fl.append(lambda e, xr=xr: e.matmul(acc_seg(i, T, EX), xr[0:EX, NPB, :], identf[0:EX, 0:EX], start=False, stop=True))
                        rd = rd + [b_ost[j]]
                    P.op(pe, fl, reads=rd, writes=acc_bufs(i, N))
                for (c0, n) in segs(N):
                    P.op(act, lambda e, c0=c0, n=n, fc=fc, i=i: e.activation(out=x_sb[:, fc, c0:c0 + n], in_=acc_seg(i, c0, n), func=AF.Copy),
                         reads=acc_bufs(i, N), writes=[b_x[fc]])
            chk("p2")
            ffn(0, N)
            chk("ffn0")
            fm_norm(N)
            fm_norm_apply(N, 2)
            cuh = aux
            offs = [(2, 0, T)] + ([(T + 4, T, 16), (T + 22, T + 16, 16)] if last_round else [])
            cst = ostage[0][:, 0:4 * FC].rearrange("p (q t f) -> p q t f", q=2, t=2)
            for fc in range(FC):
                igc = acc_get(); proj("w_cin", (fc * 2 + 0) * 2, N, b_h, hrd, igc)
                iu = acc_get(); proj("w_cin", (fc * 2 + 1) * 2, N, b_h, hrd, iu)
                j = fc % 2
                for (c0, n) in segs(N):
                    P.op(act, lambda e, c0=c0, n=n, j=j, igc=igc: e.activation(out=gtmp[j][:, c0:c0 + n], in_=acc_seg(igc, c0, n), func=AF.Copy),
                         reads=acc_bufs(igc, N), writes=[b_gtmp[j]])
                    P.op(dve, lambda e, c0=c0, n=n, j=j, iu=iu: e.tensor_tensor(out=gtmp[j][:, c0:c0 + n], in0=gtmp[j][:, c0:c0 + n], in1=acc_seg(iu, c0, n), op=ALU.mult),
                         reads=acc_bufs(iu, N) + [b_gtmp[j]], writes=[b_gtmp[j]])
                for (co, ct, n) in offs:
                    P.op(act, lambda e, co=co, ct=ct, n=n, fc=fc, j=j: e.activation(out=cuh[:, fc, co:co + n], in_=gtmp[j][:, ct:ct + n], func=AF.Copy),
                         reads=[b_gtmp[j]], writes=[b_aux[fc]])
                P.op(dve, lambda e, fc=fc, j=j: e.tensor_copy(out=hal_t[:, :, fc], in_=gtmp[j][:, T - 2:T]), reads=[b_gtmp[j]], writes=[b_hal])
                if last_round:
                    for q in range(2):
                        P.op(dve, lambda e, fc=fc, j=j, q=q: e.tensor_copy(out=cst[:, q, :, fc], in_=gtmp[j][:, T + 16 * q + 14:T + 16 * q + 16]),
                             reads=[b_gtmp[j]], writes=[b_ost[0]])
            hal2 = hal_t.rearrange("p a b -> p (a b)")
            P.dma(sp, agc_in, hal2, b_agc_in, reads=[b_hal], writes=[b_agc_in])
            allgather(agc_in, agc_mid, agc, [b_agc_in], b_agc, b_agc_mid)
            P.dma(sp, halg, agc.rearrange("(r p) c -> p r c", p=128), b_halg, reads=[b_agc], writes=[b_halg])
            halo2 = halo.rearrange("p a b -> p (a b)")
            P.op(dve, lambda e: e.tensor_scalar(out=halo2, in0=prev7, scalar1=rm_t[:, 16:17], scalar2=None, op0=ALU.mult),
                 reads=[b_prev7, b_halg], writes=[b_halo])
            for j in range(NCORES):
                P.op(dve, lambda e, j=j: e.scalar_tensor_tensor(out=halo2, in0=halg[:, j, :], scalar=rm_t[:, 8 + j:9 + j], in1=halo2, op0=ALU.mult, op1=ALU.add),
                     reads=[b_halg, b_halo], writes=[b_halo])
            P.op(dve, lambda e: e.tensor_copy(out=prev7, in_=halg[:, NCORES - 1, :]), reads=[b_halg, b_halo], writes=[b_prev7])
            iT = acc_get()
            P.op(pe, lambda e: e.transpose(banks[iT][0:FC * 2, 0:128], hal2, identf), reads=[b_hal], writes=[b_bank[iT]])
            P.op(act, lambda e: e.activation(out=ostage[1][0:FC * 2, 0:128], in_=banks[iT][0:FC * 2, 0:128], func=AF.Copy), reads=[b_bank[iT]], writes=[b_ost[1]])
            P.dma(sp, cp[r, :, :].rearrange("t (f p) -> (t f) p", p=128), ostage[1][0:FC * 2, 0:128], b_ost[1], reads=[b_ost[1]], writes=[b_ost[1]])
            if last_round:
                for q in range(2):
                    iT = acc_get()
                    P.op(pe, lambda e, q=q, iT=iT: e.transpose(banks[iT][0:FC * 2, 0:128], ostage[0][:, q * FC * 2:(q + 1) * FC * 2], identf),
                         reads=[b_ost[0]], writes=[b_bank[iT]])
                    P.op(act, lambda e, q=q, iT=iT: e.activation(out=ostage[1][0:FC * 2, 128 * (q + 1):128 * (q + 2)], in_=banks[iT][0:FC * 2, 0:128], func=AF.Copy),
                         reads=[b_bank[iT]], writes=[b_ost[1]])
                    P.dma(sp, cs[q, :, :].rearrange("t (f p) -> (t f) p", p=128), ostage[1][0:FC * 2, 128 * (q + 1):128 * (q + 2)], b_ost[1],
                          reads=[b_ost[1]], writes=[b_ost[1]])
            out_bufs.append(b_ost[1])
            for fc in range(FC):
                P.op(act, lambda e, fc=fc: e.activation(out=cuh[:, fc, 0:2], in_=halo[:, :, fc], func=AF.Copy), reads=[b_halo], writes=[b_aux[fc]])
            if last_round:
                P.dma(sp, ostage[0][0:4 * FC, 0:128], sc.rearrange("q t (f p) -> (q t f) p", p=128), b_ost[0], writes=[b_ost[0]])
                iT = acc_get()
                P.op(pe, lambda e: e.transpose(banks[iT][:, 0:4 * FC], ostage[0][0:4 * FC, 0:128], identf[0:4 * FC, 0:4 * FC]), reads=[b_ost[0]], writes=[b_bank[iT]])
                scv = banks[iT][:, 0:4 * FC].rearrange("p (q t f) -> p q t f", q=2, t=2)
                for q in range(2):
                    for t_ in range(2):
                        co = (T + 2 if q == 0 else T + 20) + t_
                        P.op(act, lambda e, q=q, t_=t_, co=co: e.activation(out=cuh[:, :, co], in_=scv[:, q, t_, :], func=AF.Copy),
                             reads=[b_bank[iT]], writes=b_aux[0:FC])
            for fc in range(FC):
                igb = acc_get(); proj("w_cin", FC * 4 + fc * 2, N, b_h, hrd, igb)
                j = fc % 2
                for (co, ct, n) in offs:
                    P.op(dve, lambda e, co=co, ct=ct, n=n, fc=fc, j=j: e.tensor_scalar(out=gtmp[j][:, ct:ct + n], in0=cuh[:, fc, co:co + n],
                                                                                   scalar1=cw_t[:, 2, fc:fc + 1], scalar2=None, op0=ALU.mult),
                         reads=[b_aux[fc]], writes=[b_gtmp[j]])
                    P.op(dve, lambda e, co=co, ct=ct, n=n, fc=fc, j=j: e.scalar_tensor_tensor(out=gtmp[j][:, ct:ct + n], in0=cuh[:, fc, co - 1:co - 1 + n],
                                                                                          scalar=cw_t[:, 1, fc:fc + 1], in1=gtmp[j][:, ct:ct + n], op0=ALU.mult, op1=ALU.add),
                         reads=[b_aux[fc], b_gtmp[j]], writes=[b_gtmp[j]])
                    P.op(dve, lambda e, co=co, ct=ct, n=n, fc=fc, j=j: e.scalar_tensor_tensor(out=gtmp[j][:, ct:ct + n], in0=cuh[:, fc, co - 2:co - 2 + n],
                                                                                          scalar=cw_t[:, 0, fc:fc + 1], in1=gtmp[j][:, ct:ct + n], op0=ALU.mult, op1=ALU.add),
                         reads=[b_aux[fc], b_gtmp[j]], writes=[b_gtmp[j]])
                for (c0, n) in segs(N):
                    P.op(dve, lambda e, c0=c0, n=n, fc=fc, j=j, igb=igb: e.tensor_tensor(out=cuh[:, fc, c0:c0 + n], in0=gtmp[j][:, c0:c0 + n], in1=acc_seg(igb, c0, n), op=ALU.mult),
                         reads=acc_bufs(igb, N) + [b_gtmp[j], b_aux[fc]], writes=[b_aux[fc]])
            for fc in range(FC):
                i = acc_get()
                proj("w_cout", fc * 2, N, b_aux[0:FC], lambda kc, c0, n: cuh[:, kc, c0:c0 + n], i)
                for (c0, n) in segs(N):
                    P.op(dve, lambda e, c0=c0, n=n, fc=fc, i=i: e.tensor_tensor(out=x_sb[:, fc, c0:c0 + n], in0=x_sb[:, fc, c0:c0 + n], in1=acc_seg(i, c0, n), op=ALU.add),
                         reads=acc_bufs(i, N) + [b_x[fc]], writes=[b_x[fc]])
            chk("conv")
            ffn(1, N)
            chk("ffn1")
            fm_norm(N)
            out_blocks = [(yp[r * T + tbk * 128:r * T + tbk * 128 + 128, :], 128, tbk * 128) for tbk in range(NPB)]
            if last_round:
                out_blocks.append((ys[0:EX, :], EX, T))
            FB = 1024 // 128
            oc_ = 0
            for (dst, nt, c0) in out_blocks:
                for f8 in range(0, FC, FB):
                    nf8 = min(FB, FC - f8)
                    so = oc_ % 2
                    oc_ += 1
                    for f4 in range(f8, f8 + nf8, 4):
                        nf = min(4, f8 + nf8 - f4)
                        i = acc_get()
                        for f in range(f4, f4 + nf):
                            j = f % 2
                            P.op(dve, lambda e, f=f, j=j, c0=c0, nt=nt: e.scalar_tensor_tensor(out=gtmp[j][:, 0:nt], in0=x_sb[:, f, c0:c0 + nt], scalar=norms_t[:, 4, f:f + 1],
                                                                                             in1=rstd_t[:, c0:c0 + nt], op0=ALU.mult, op1=ALU.mult),
                                 reads=[b_x[f], b_rstd], writes=[b_gtmp[j]])
                            P.op(pe, lambda e, f=f, f4=f4, j=j, nt=nt, i=i: e.transpose(banks[i][0:nt, (f - f4) * 128:(f - f4 + 1) * 128], gtmp[j][:, 0:nt], identf),
                                 reads=[b_gtmp[j]], writes=[b_bank[i]])
                        P.op(act, lambda e, f4=f4, f8=f8, nf=nf, nt=nt, i=i, so=so: e.activation(out=ostage[so][0:nt, (f4 - f8) * 128:(f4 - f8 + nf) * 128],
                                                                                                in_=banks[i][0:nt, 0:nf * 128], func=AF.Copy),
                             reads=[b_bank[i]], writes=[b_ost[so]])
                    P.dma(sp, dst[:, f8 * 128:(f8 + nf8) * 128], ostage[so][0:nt, 0:nf8 * 128], b_ost[so], reads=[b_ost[so]], writes=[b_ost[so]])
            out_bufs.extend(b_ost)

        try:
            if not _go:
                raise _Stop()
            chk("wag")
            for r in range(R):
                round_body(r)
        except _Stop:
            pass
        P.barrier()
        seen = set()
        for bf in out_bufs:
            sm = bf.dsem
            if sm is not None and id(sm) not in seen and sm.v:
                seen.add(id(sm))
                nc.sync.wait_ge(sm.h, sm.v)
        print("instructions:", P.nins, "sems:", P.nsem)
    return nc


def make_in_maps(cfg, inp):
    D, NH, FC, T, R = cfg.D, cfg.NH, cfg.FC, cfg.T, cfg.ROUNDS
    W = prep_weights(cfg, inp)
    lbl = np.ascontiguousarray(inp["hgrn_lb_logits"].reshape(3, NH, 128).transpose(2, 0, 1)).astype(np.float32)
    nv = np.stack([inp["norm_mix"][0], inp["norm_ffn"][0], inp["norm_mix"][1], inp["norm_ffn"][1], inp["norm_final"]])
    norms = np.ascontiguousarray(nv.reshape(5, FC, 128).transpose(2, 0, 1)).astype(np.float32)
    cw = np.ascontiguousarray(inp["conv_w"][0].reshape(3, FC, 128).transpose(2, 0, 1)).astype(np.float32)
    ogain = np.ascontiguousarray(inp["hgrn_out_gain"][0].reshape(128, 1)).astype(np.float32)
    xp_full = inp["x_prompt"][0]
    consts = make_consts(cfg)
    maps = []
    for c in range(NCORES):
        m = {}
        m["xp"] = np.ascontiguousarray(np.concatenate([xp_full[(r * NCORES + c) * T:(r * NCORES + c + 1) * T] for r in range(R)], axis=0))
        m["xs"] = np.ascontiguousarray(inp["x_sample"][cfg.SPC * c:cfg.SPC * (c + 1)].reshape(cfg.EX, D))
        m["sh"] = np.ascontiguousarray(inp["state_hgrn"][0, cfg.SPC * c:cfg.SPC * (c + 1)])
        m["sc"] = np.ascontiguousarray(inp["state_conv"][0, cfg.SPC * c:cfg.SPC * (c + 1)])
        m["lbl"] = lbl; m["ogain"] = ogain; m["norms"] = norms; m["cw"] = cw
        m["consts"] = consts
        rm = np.zeros((128, 17), np.float32)
        rm[:, c] = 1.0
        if c >= 1:
            rm[:, 8 + c - 1] = 1.0
        else:
            rm[:, 16] = 1.0
        m["rmask"] = rm
        for n in WNAMES:
            PT = PT_OF(n)
            RPR = PT * 16
            for pi, (t0, nt) in enumerate(wpieces(cfg, n)):
                ch = W[n][t0:t0 + nt].reshape(nt // PT, PT * 128, -1)
                m[f"{n}_{pi}"] = np.ascontiguousarray(ch[:, c * RPR:(c + 1) * RPR, :].reshape(nt // PT * RPR, -1))
        maps.append(m)
    return maps


def assemble(cfg, res):
    D, NH, T, R = cfg.D, cfg.NH, cfg.T, cfg.ROUNDS
    yp = np.zeros((1, cfg.SEQ, D), np.float32)
    for c in range(NCORES):
        for r in range(R):
            j = r * NCORES + c
            yp[0, j * T:(j + 1) * T] = res[c]["yp"][r * T:(r + 1) * T]
    ys = np.concatenate([res[c]["ys"].reshape(cfg.SPC, cfg.DS, D) for c in range(NCORES)], axis=0)
    hp = res[0]["hp"].reshape(1, 1, NH, 128, 128)
    hs = np.concatenate([res[c]["hs"] for c in range(NCORES)], axis=0).reshape(1, cfg.DB, NH, 128, 128)
    cp = res[NCORES - 1]["cp"][R - 1].reshape(1, 1, 2, D)
    cs = np.concatenate([res[c]["cs"] for c in range(NCORES)], axis=0).reshape(1, cfg.DB, 2, D)
    return (yp, ys, hp, hs, cp, cs)


_NC_CACHE = {}


def run(cfg, inp):
    key = (cfg.D, cfg.NH, cfg.DFF, cfg.SEQ, cfg.T)
    if key not in _NC_CACHE:
        _NC_CACHE[key] = build(cfg)
    nc = _NC_CACHE[key]
    maps = make_in_maps(cfg, inp)
    res = run_bass_kernel_spmd(nc, maps, core_ids=list(range(NCORES)))
    return assemble(cfg, res.results)


def kernel(**inputs):
    inp = {k: np.asarray(v) for k, v in inputs.items()}
    return run(FULL, inp)
```

```python
import os
import numpy as np
import concourse.bass as bass
import concourse.mybir as mybir
from concourse.bass_utils import run_bass_kernel_spmd
from contextlib import ExitStack

F32 = mybir.dt.float32
BF16 = mybir.dt.bfloat16
AF = mybir.ActivationFunctionType
ALU = mybir.AluOpType
NCORES = 8
EPS = 1e-6


class _Stop(Exception):
    pass


class Cfg:
    stop = None

    def __init__(self, D=4096, NH=32, DFF=11008, SEQ=16384, T=512, DB=16, DS=16, G=4, NQ=4):
        self.D, self.NH, self.DFF, self.SEQ, self.T, self.DB, self.DS, self.G = D, NH, DFF, SEQ, T, DB, DS, G
        self.FC = D // 128
        self.KH = self.FC // 2
        self.NS = DFF // 128
        self.ROUNDS = SEQ // (NCORES * T)
        self.NPB = T // 128
        self.NCH = T // 64
        self.SPC = DB // NCORES
        self.EX = self.SPC * DS
        self.NX = T + self.EX
        self.NQ = NQ
        base = self.NS // NQ
        rem = self.NS % NQ
        self.QS = [base + (1 if i < rem else 0) for i in range(NQ)]
        self.QMAX = max(self.QS)
        self.NG = NH // G
        assert DS == 16 and self.SPC == 2 and D == NH * 128


FULL = Cfg()


def tile_w(W, kh):
    K, NO = W.shape
    kc = K // 128
    nh = kc // kh
    t = W.reshape(nh, kh, 128, NO // 128, 128)
    t = t.transpose(3, 0, 2, 1, 4)
    return np.ascontiguousarray(t).reshape(NO // 128 * nh, 128, kh * 128)


def prep_weights(cfg, inp):
    D, NH, NS = cfg.D, cfg.NH, cfg.NS
    out = {}
    W = inp["hgrn_w_in"][0]
    cols = np.concatenate([np.arange(s * D + h * 128, s * D + h * 128 + 128) for h in range(NH) for s in range(4)])
    out["w_hin"] = tile_w(W[:, cols], cfg.KH)
    out["w_hout"] = tile_w(inp["hgrn_w_out"][0], cfg.KH)
    W = inp["conv_w_in"][0]
    cols = np.concatenate([np.arange(s * D + f * 128, s * D + f * 128 + 128) for f in range(cfg.FC) for s in (1, 2)]
                          + [np.arange(0, D)])
    out["w_cin"] = tile_w(W[:, cols], cfg.KH)
    out["w_cout"] = tile_w(inp["conv_w_out"][0], cfg.KH)
    for l in range(2):
        W = inp["ffn_w_in"][l]
        cols = np.concatenate([np.arange(s * cfg.DFF + sl * 128, s * cfg.DFF + sl * 128 + 128)
                               for sl in range(NS) for s in range(2)])
        out[f"w_fin{l}"] = tile_w(W[:, cols], cfg.KH)
        W = inp["ffn_w_out"][l]
        tl = np.zeros((cfg.NQ, cfg.FC, 128, cfg.QMAX, 128), np.float32)
        s0 = 0
        for q in range(cfg.NQ):
            n = cfg.QS[q]
            blk = W[s0 * 128:(s0 + n) * 128, :].reshape(n, 128, cfg.FC, 128)
            tl[q, :, :, :n, :] = blk.transpose(2, 1, 0, 3)
            s0 += n
        out[f"w_fout{l}"] = tl.reshape(cfg.NQ * cfg.FC, 128, cfg.QMAX * 128)
    return out


WINV = {0: 0, 1: 2, 2: 4, 3: 6, 4: 1, 5: 3, 6: 5, 7: 7}
WNAMES = ["w_hin", "w_hout", "w_fin0", "w_fout0", "w_cin", "w_cout", "w_fin1", "w_fout1"]


def wshape(cfg, name):
    D, NH, NS, FC, KH = cfg.D, cfg.NH, cfg.NS, cfg.FC, cfg.KH
    c = KH * 128
    if name == "w_hin":
        return (NH * 4 * 2, c)
    if name in ("w_hout", "w_cout"):
        return (FC * 2, c)
    if name == "w_cin":
        return (FC * 3 * 2, c)
    if name.startswith("w_fin"):
        return (NS * 2 * 2, c)
    return (cfg.NQ * FC, cfg.QMAX * 128)


def PT_OF(name):
    return 4 if name.startswith("w_fout") else 8


def wpieces(cfg, name, pmax=int(os.environ.get("PMAX", "128"))):
    nt, c = wshape(cfg, name)
    out = []
    t0 = 0
    if name == "w_hin" and nt >= 64:
        out.append((0, 16))
        t0 = 16
    while t0 < nt:
        n = min(pmax, nt - t0)
        assert n % NCORES == 0
        out.append((t0, n))
        t0 += n
    return out


class Sem:
    def __init__(self, h):
        self.h = h
        self.v = 0


class Buf:
    __slots__ = ("name", "w", "r", "dsem")

    def __init__(self, name=""):
        self.name = name
        self.w = None
        self.r = []
        self.dsem = None


class Eng:
    def __init__(self, name, handle, sem, skip_self=False):
        self.name, self.e, self.sem, self.skip_self = name, handle, sem, skip_self
        self.seen = {}


class Prog:
    def __init__(self, nc, es):
        self.nc = nc
        self.es = es
        self.nsem = 0
        mk = self.new_sem
        self.pe = Eng("pe", nc.tensor, mk("pe"), skip_self=True)
        self.act = Eng("act", nc.scalar, mk("act"))
        self.dve = Eng("dve", nc.vector, mk("dve"))
        self.pool = Eng("pool", nc.gpsimd, mk("pool"))
        self.sp = Eng("sp", nc.sync, mk("sp"))
        self.engs = [self.pe, self.act, self.dve, self.pool, self.sp]
        self.nins = 0

    def new_sem(self, name):
        self.nsem += 1
        sm = Sem(self.es.enter_context(self.nc.semaphore(f"s_{name}_{self.nsem}")))
        if not hasattr(self, "sems"):
            self.sems = []
        self.sems.append(sm)
        return sm

    def _waits(self, eng, reads, writes):
        need = {}

        def req(ev):
            if ev is None:
                return
            s, v = ev
            if eng.skip_self and s is eng.sem:
                return
            if eng.seen.get(s, 0) >= v:
                return
            if need.get(s, 0) < v:
                need[s] = v
        for b in reads:
            req(b.w)
        for b in writes:
            req(b.w)
            for ev in b.r:
                req(ev)
        for s, v in need.items():
            eng.e.wait_ge(s.h, v)
            eng.seen[s] = v
            self.nins += 1

    def op(self, eng, fns, reads=(), writes=(), sem=None, inc=1):
        if callable(fns):
            fns = [fns]
        self._waits(eng, reads, writes)
        ins = None
        for f in fns:
            ins = f(eng.e)
            self.nins += 1
        s = eng.sem if sem is None else sem
        s.v += inc
        ins.then_inc(s.h, inc)
        ev = (s, s.v)
        for b in reads:
            b.r.append(ev)
        for b in writes:
            b.w = ev
            b.r = []
        return ev

    def dma(self, eng, out, in_, sembuf, reads=(), writes=(), **kw):
        if isinstance(sembuf, Sem):
            sem = sembuf
        else:
            if sembuf.dsem is None:
                sembuf.dsem = self.new_sem("d" + sembuf.name)
            sem = sembuf.dsem
        return self.op(eng, lambda e: e.dma_start(out=out, in_=in_, **kw), reads, writes, sem=sem, inc=16)

    def barrier(self):
        for e in self.engs:
            for sm in self.sems:
                if sm.v == 0 or (sm is e.sem):
                    continue
                if e.seen.get(sm, 0) < sm.v:
                    e.e.wait_ge(sm.h, sm.v)
                    e.seen[sm] = sm.v


class Arena:
    def __init__(self, t, words):
        self.t, self.words, self.off = t, words, 0

    def mark(self):
        return self.off

    def reset(self, m):
        self.off = m

    def f32(self, shape):
        n = int(np.prod(shape[1:]))
        self.off = (self.off + 15) // 16 * 16
        ap = self.t[0:shape[0], self.off:self.off + n]
        self.off += n
        assert self.off <= self.words, ("arena overflow", self.off, self.words)
        return _reshape(ap, shape)

    def bf16(self, shape):
        n = int(np.prod(shape[1:]))
        nw = (n + 1) // 2
        self.off = (self.off + 15) // 16 * 16
        ap = self.t[0:shape[0], self.off:self.off + nw].bitcast(BF16)[:, 0:n]
        self.off += nw
        assert self.off <= self.words, ("arena overflow", self.off, self.words)
        return _reshape(ap, shape)


def _reshape(ap, shape):
    if len(shape) == 2:
        return ap
    if len(shape) == 3:
        return ap.rearrange("p (a b) -> p a b", a=shape[1], b=shape[2])
    if len(shape) == 4:
        return ap.rearrange("p (a b c) -> p a b c", a=shape[1], b=shape[2], c=shape[3])
    raise ValueError


def const_layout(cfg):
    NX = cfg.NX
    o = {}
    c = 0
    for name, n in (("identf", 128), ("maskA", 128), ("maskS", 32), ("smask", NX), ("sm01", 2), ("pm64", 2)):
        o[name] = (c, n)
        c += n
    return o, c


def make_consts(cfg):
    lay, cwid = const_layout(cfg)
    T, NX = cfg.T, cfg.NX
    C = np.zeros((128, cwid), np.float32)
    o, n = lay["identf"]; C[:, o:o + n] = np.eye(128, dtype=np.float32)
    s_ = np.arange(128)[:, None]; t_ = np.arange(128)[None, :]
    o, n = lay["maskA"]; C[:, o:o + n] = ((s_ <= t_) & (s_ // 64 == t_ // 64)).astype(np.float32)
    s2 = np.arange(32)[:, None]; t2 = np.arange(32)[None, :]
    o, n = lay["maskS"]; C[0:32, o:o + n] = ((s2 <= t2) & (s2 // 16 == t2 // 16)).astype(np.float32)
    sm = np.ones(NX, np.float32); sm[0:T:64] = 0.0; sm[T:NX:16] = 0.0
    o, n = lay["smask"]; C[:, o:o + n] = sm[None, :]
    o, n = lay["sm01"]; C[0:16, o] = 1.0; C[16:32, o + 1] = 1.0
    o, n = lay["pm64"]; C[0:64, o] = 1.0; C[64:128, o + 1] = 1.0
    return C


def build(cfg):
    nc = bass.Bass("TRN2", target_bir_lowering=False, num_devices=NCORES)
    D, NH, FC, KH, T, EX, NX, G = cfg.D, cfg.NH, cfg.FC, cfg.KH, cfg.T, cfg.EX, cfg.NX, cfg.G
    NCH, NPB, R, NS = cfg.NCH, cfg.NPB, cfg.ROUNDS, cfg.NS
    WC = KH * 128
    WCMAX = max(WC, cfg.QMAX * 128)
    clay, CWID = const_layout(cfg)

    def din(name, shape, dt=F32):
        return nc.dram_tensor(name, list(shape), dt, kind="ExternalInput").ap()

    def dout(name, shape):
        return nc.dram_tensor(name, list(shape), F32, kind="ExternalOutput").ap()

    def dint(name, shape):
        return nc.dram_tensor(name, list(shape), F32, kind="Internal").ap()

    xp = din("xp", [R * T, D])
    xs = din("xs", [EX, D])
    sh = din("sh", [cfg.SPC, NH, 128, 128])
    sc = din("sc", [cfg.SPC, 2, D])
    lbl = din("lbl", [128, 3, NH])
    ogain = din("ogain", [128, 1])
    norms = din("norms", [128, 5, FC])
    cw = din("cw", [128, 3, FC])
    rmask = din("rmask", [128, 17])
    consts = din("consts", [128, CWID])
    wsh, wag_in, wag = {}, {}, {}
    WP = []
    for n in WNAMES:
        _, c = wshape(cfg, n)
        PT = PT_OF(n)
        for pi, (t0, nt) in enumerate(wpieces(cfg, n)):
            key = f"{n}_{pi}"
            assert nt % PT == 0
            WP.append((key, n, t0, nt, PT))
            wsh[key] = din(key, [nt // NCORES * 128, c])
            wag_in[key] = nc.dram_tensor(key + "_i", [nt // NCORES * 128, c], BF16, kind="Internal").ap()
            wag[key] = nc.dram_tensor(key + "_g", [nt * 128, c], BF16, kind="Internal").ap()
    NMID = 6
    midA = [nc.dram_tensor(f"midA{i}", [4 * 16 * PT_OF("w_hin"), WC], BF16, kind="Internal").ap() for i in range(NMID)]
    midB = [nc.dram_tensor(f"midB{i}", [4 * 16 * PT_OF("w_fout0"), cfg.QMAX * 128], BF16, kind="Internal").ap() for i in range(NMID)]
    yp = dout("yp", [R * T, D])
    ys = dout("ys", [EX, D])
    hp = dout("hp", [NH, 128, 128])
    hs = dout("hs", [cfg.SPC, NH, 128, 128])
    cp = dout("cp", [R, 2, D])
    cs = dout("cs", [cfg.SPC, 2, D])
    agh_in = dint("agh_in", [G * 128, 129])
    agh = dint("agh", [NCORES * G * 128, 129])
    agh_mid = dint("agh_mid", [NCORES // 2 * G * 128, 129])
    agc_in = dint("agc_in", [128, FC * 2])
    agc = dint("agc", [NCORES * 128, FC * 2])
    agc_mid = dint("agc_mid", [NCORES // 2 * 128, FC * 2])
    sround = dint("sround", [128, NH, 128])

    es = ExitStack()
    with es:
        AW = 53000
        arena_t = es.enter_context(nc.sbuf_tensor("arena", [128, AW], F32))
        ar = Arena(arena_t, AW)
        banks = [es.enter_context(nc.psum_tensor(f"bank{i}", [128, 512], F32)) for i in range(6)]
        bank6b = es.enter_context(nc.psum_tensor("bank6b", [128, 1024], BF16))
        banks.append(None)
        banks.append(es.enter_context(nc.psum_tensor("bank7", [128, 512], F32)))
        P = Prog(nc, es)
        pe, act, dve, pool, sp = P.pe, P.act, P.dve, P.pool, P.sp
        B = Buf

        def chk(name):
            if cfg.stop == name:
                raise _Stop()

        const_t = ar.f32([128, CWID])
        cv = lambda k: const_t[:, clay[k][0]:clay[k][0] + clay[k][1]]
        identf = cv("identf"); smask = cv("smask"); sm01 = cv("sm01")[0:32, :]; pm64 = cv("pm64")
        identb = ar.bf16([128, 128]); onesb = ar.bf16([128, 128]); zerob = ar.bf16([128, 128])
        maskA = ar.bf16([128, 128]); maskS = ar.bf16([32, 32]); ones_f = ar.f32([128, 16])
        norms_t = ar.f32([128, 5, FC]); cw_t = ar.f32([128, 3, FC]); rm_t = ar.f32([128, 17])
        lb_t = ar.f32([128, NH]); oml_t = ar.f32([128, NH]); noml_t = ar.f32([128, NH])
        lbl_t = ar.f32([128, 3, NH]); og_t = ar.f32([128, 1])
        prev7 = ar.f32([128, 2 * FC])
        NSLOT = 6
        wslots = [ar.bf16([128, WCMAX]) for _ in range(NSLOT)]
        h_t = ar.bf16([128, FC, NX])
        aux = ar.bf16([128, FC, NX + 8])
        rstd_t = ar.f32([128, NX])
        gtmp = [ar.f32([128, NX]) for _ in range(2)]
        sqt = [ar.bf16([128, NX]) for _ in range(2)]
        ostage = [ar.f32([128, 1024]) for _ in range(2)]
        hal_t = ar.f32([128, 2, FC]); halg = ar.f32([128, NCORES, 2 * FC]); halo = ar.f32([128, 2, FC])
        XR0 = ar.mark()
        x_sb = ar.f32([128, FC, NX])
        XRx = ar.mark()
        XR1 = AW
        ar.reset(XR0)
        xst = [ar.f32([128, D // 2]) for _ in range(2)]
        junk = ar.f32([128, D // 2])
        ssq = ar.f32([128, 8])
        XRa = ar.mark()
        ar.reset(XR0)
        gs_ol2 = [ar.bf16([128, G, NX]) for _ in range(2)]; gs_qc2 = [ar.bf16([128, G, NX]) for _ in range(2)]
        gs_sg2 = [ar.bf16([128, G, NX]) for _ in range(2)]
        tq = ar.f32([128, NX]); tsig = ar.f32([128, NX]); tg = ar.f32([128, NX]); tk = ar.f32([128, NX])
        tb = ar.f32([128, NX]); teb = tsig; tenb = tg
        tqt = ar.bf16([128, NX]); tkt = ar.bf16([128, NX]); tvs = [ar.bf16([128, NX]), ar.bf16([128, NX])]
        ktok = ar.bf16([128, NPB, 128]); vtok = ar.bf16([128, NPB, 128])
        ktokm = ar.bf16([128, NPB, 2, 128])
        vtoks = ar.bf16([32, 128]); kt01 = ar.bf16([32, 2, 128]); ktoks = ar.bf16([32, 128])
        atm = ar.bf16([128, NPB, 128]); atms = ar.bf16([32, 32])
        td = ar.f32([128, NCH + 2]); tci = ar.f32([128, NCH]); tE = ar.f32([128, NCH])
        Mf = [ar.f32([128, 128]) for _ in range(2)]; Mtmp = ar.f32([128, 128]); Mb = ar.bf16([128, NCH, 128])
        Mbs = ar.bf16([128, 2, 128])
        LD2 = [ar.f32([128, G, 129]) for _ in range(2)]; LDj = [ar.f32([128, G, 129]) for _ in range(2)]
        S_t = ar.f32([128, G, 128]); Sin = ar.f32([128, G, 128]); Stmp = ar.f32([128, G, 128]); M0b = ar.bf16([128, G, 128])
        S0s = ar.f32([128, 2, G, 128]); NSs = ar.f32([128, 2, G, 128])
        to32 = ar.f32([128, NX]); osq = ar.bf16([128, NX]); t1 = to32; rs2 = ar.f32([128, NX])
        XRb = ar.mark()
        print("arena words: xsb_end", XRx, "p0_end", XRa, "p1_end", XRb, "of", AW)
        ar.reset(max(XRx, XRa, XRb))

        b_const = B("const"); b_prev7 = B("prev7")
        b_wslot = [B(f"ws{i}") for i in range(NSLOT)]
        b_h = [B(f"h{i}") for i in range(FC)]
        b_aux = [B(f"aux{i}") for i in range(max(FC, cfg.QMAX))]
        b_x = [B(f"x{i}") for i in range(FC)]
        b_bank = [B(f"bank{i}") for i in range(8)]
        b_rstd = B("rstd"); b_gtmp = [B("g0"), B("g1")]; b_sq = [B("q0"), B("q1")]
        b_ost = [B("o0"), B("o1")]
        s_cc = P.new_sem("cc")
        b_wag = {}
        RG1 = [[0, 1, 2, 3], [4, 5, 6, 7]]
        RG2 = [[0, 4], [1, 5], [2, 6], [3, 7]]

        def allgather(in_ap, mid_ap, out_ap, reads, outbuf, midbuf):
            P.op(pool, lambda e: e.collective_compute("AllGather", ALU.bypass, replica_groups=RG1, ins=[in_ap], outs=[mid_ap]),
                 reads=reads, writes=[midbuf], sem=s_cc, inc=1)
            P.op(pool, lambda e: e.collective_compute("AllGather", ALU.bypass, replica_groups=RG2, ins=[mid_ap], outs=[out_ap]),
                 reads=[midbuf], writes=[outbuf], sem=s_cc, inc=1)
        b_xst = [B("xst0"), B("xst1")]; b_ssq = B("ssq"); b_junk = B("junk")
        b_gs2 = [[B(f"gs{j}_{i}") for i in range(G)] for j in range(2)]
        bt = {k: B(k) for k in ("tq", "tsig", "tg", "tk", "tb", "teb", "tenb", "tqt", "tkt", "tv0", "tv1", "ktok", "vtok", "kts",
                                "atm", "td", "tci", "tE", "ktokm", "LD0", "LD1", "M0", "M1", "Mtmp", "Mb", "Mbs", "LD", "S", "Sin", "Stmp", "M0b",
                                "S0s", "NSs", "to32", "osq", "t1", "rs2")}
        b_ld = [B("ld0"), B("ld1")]
        b_sr = [B(f"sr{g}") for g in range(cfg.NG)]
        b_agh_in = B("agh_in"); b_agh = B("agh"); b_agc_in = B("agc_in"); b_agc = B("agc")
        b_agh_mid = B("agh_mid"); b_agc_mid = B("agc_mid")
        b_hal = B("hal"); b_halg = B("halg"); b_halo = B("halo")
        out_bufs = []

        P.dma(sp, const_t, consts, b_const, writes=[b_const])
        for dst, src in ((norms_t, norms), (cw_t, cw), (rm_t, rmask), (lbl_t, lbl), (og_t, ogain)):
            P.dma(sp, dst, src, b_const, writes=[b_const])
        for ap_, val in ((onesb, 1.0), (zerob, 0.0), (prev7, 0.0), (ones_f, 1.0)):
            P.op(dve, lambda e, ap_=ap_, val=val: e.memset(ap_, val), writes=[b_prev7])
        P.op(dve, lambda e: e.tensor_copy(out=identb, in_=identf), reads=[b_const], writes=[b_prev7])
        P.op(dve, lambda e: e.tensor_copy(out=maskA, in_=cv("maskA")), reads=[b_const], writes=[b_prev7])
        P.op(dve, lambda e: e.tensor_copy(out=maskS, in_=cv("maskS")[0:32, :]), reads=[b_const], writes=[b_prev7])
        e3 = gtmp[0][:, 0:3 * NH].rearrange("p (a b) -> p a b", a=3)
        P.op(act, lambda e: e.activation(out=e3, in_=lbl_t, func=AF.Exp), reads=[b_const], writes=[b_gtmp[0]])
        s1 = gtmp[1][:, 0:NH]
        P.op(dve, lambda e: e.tensor_tensor(out=s1, in0=e3[:, 0, :], in1=e3[:, 1, :], op=ALU.add), reads=[b_gtmp[0]], writes=[b_gtmp[1]])
        P.op(dve, lambda e: e.tensor_tensor(out=s1, in0=s1, in1=e3[:, 2, :], op=ALU.add), reads=[b_gtmp[0], b_gtmp[1]], writes=[b_gtmp[1]])
        P.op(dve, lambda e: e.reciprocal(out=s1, in_=s1), reads=[b_gtmp[1]], writes=[b_gtmp[1]])
        P.op(dve, lambda e: e.tensor_tensor(out=lb_t, in0=e3[:, 0, :], in1=s1, op=ALU.mult), reads=[b_gtmp[0], b_gtmp[1]], writes=[b_prev7])
        P.op(dve, lambda e: e.tensor_scalar(out=oml_t, in0=lb_t, scalar1=-1.0, scalar2=1.0, op0=ALU.mult, op1=ALU.add),
             reads=[b_prev7], writes=[b_prev7])
        P.op(dve, lambda e: e.tensor_scalar(out=noml_t, in0=lb_t, scalar1=1.0, scalar2=-1.0, op0=ALU.mult, op1=ALU.add),
             reads=[b_prev7], writes=[b_prev7])
        P.barrier()
        try:
            chk("consts")
            _go = True
        except _Stop:
            _go = False
        b_const = B("const_ro")

        RG = [list(range(NCORES))]
        b_win = {}
        wp_by_name = {}
        gstate = {}
        b_midA = [B(f"midA{i}") for i in range(NMID)]
        b_midB = [B(f"midB{i}") for i in range(NMID)]
        midctr = {"A": 0, "B": 0}
        LOOKAHEAD = 10
        for (key, n, t0, nt, PT) in (WP if _go else []):
            wp_by_name.setdefault(n, []).append((key, t0, nt, PT))
            rows, c = wsh[key].shape
            b_win[key] = B(key + "_i")
            for r0 in range(0, rows, 1024):
                r1 = min(rows, r0 + 1024)
                P.dma(pool, wag_in[key][r0:r1, :], wsh[key][r0:r1, :], b_win[key], writes=[b_win[key]])
            npc = nt // PT
            gstate[key] = {"s1": 0, "s2": 0, "np": npc, "PT": PT, "mids": {}}
            b_wag[key] = [B(f"{key}_p{i}") for i in range(npc)]

        def gather_step(key, upto):
            st = gstate[key]
            PT = st["PT"]
            RPR = PT * 16
            kind = "B" if PT == PT_OF("w_fout0") else "A"
            mids, bmids = (midB, b_midB) if kind == "B" else (midA, b_midA)
            upto = min(upto, st["np"] - 1)

            def s2(p):
                k = st["mids"][p]
                P.op(pool, lambda e: e.collective_compute("AllGather", ALU.bypass, replica_groups=RG1, ins=[mids[k][0:2 * RPR, :]],
                                                          outs=[wag[key][p * PT * 128:(p + 1) * PT * 128, :]]),
                     reads=[bmids[k]], writes=[b_wag[key][p]], sem=s_cc, inc=1)
            while st["s1"] <= upto:
                p = st["s1"]
                k = midctr[kind] % NMID
                midctr[kind] += 1
                st["mids"][p] = k
                P.op(pool, lambda e: e.collective_compute("AllGather", ALU.bypass, replica_groups=RG2,
                                                          ins=[wag_in[key][p * RPR:(p + 1) * RPR, :]], outs=[mids[k][0:2 * RPR, :]]),
                     reads=[b_win[key]], writes=[bmids[k]], sem=s_cc, inc=1)
                st["s1"] += 1
                while st["s2"] < st["s1"] - 1:
                    s2(st["s2"])
                    st["s2"] += 1
            if st["s1"] == st["np"]:
                while st["s2"] < st["np"]:
                    s2(st["s2"])
                    st["s2"] += 1

        def ensure_piece(key, p):
            st = gstate[key]
            gather_step(key, p + LOOKAHEAD)
            while st["s2"] <= p:
                k = st["s2"]
                gather_flush_one(key)

        def gather_flush_one(key):
            st = gstate[key]
            PT = st["PT"]
            kind = "B" if PT == PT_OF("w_fout0") else "A"
            mids, bmids = (midB, b_midB) if kind == "B" else (midA, b_midA)
            p = st["s2"]
            k = st["mids"][p]
            RPR = PT * 16
            P.op(pool, lambda e: e.collective_compute("AllGather", ALU.bypass, replica_groups=RG1, ins=[mids[k][0:2 * RPR, :]],
                                                      outs=[wag[key][p * PT * 128:(p + 1) * PT * 128, :]]),
                 reads=[bmids[k]], writes=[b_wag[key][p]], sem=s_cc, inc=1)
            st["s2"] += 1

        wctr = [0]

        def wload(name, tile_idx, ncols):
            i = wctr[0] % NSLOT
            wctr[0] += 1
            for (key, t0, nt, PT) in wp_by_name[name]:
                if t0 <= tile_idx < t0 + nt:
                    break
            li = tile_idx - t0
            pidx = li // PT
            ensure_piece(key, pidx)
            src = wag[key][li * 128:(li + 1) * 128, 0:ncols]
            P.dma(sp, wslots[i][:, 0:ncols], src, b_wslot[i], reads=[b_wag[key][pidx]], writes=[b_wslot[i]])
            return wslots[i], b_wslot[i]

        NACC = 6
        accctr = [0]

        def acc_get():
            i = accctr[0] % NACC
            accctr[0] += 1
            return i

        def acc_seg(i, c0, n):
            if c0 < T:
                return banks[i][:, c0:c0 + n]
            return banks[7][:, 32 * i + (c0 - T):32 * i + (c0 - T) + n]

        def acc_bufs(i, N):
            return [b_bank[i]] + ([b_bank[7]] if N > T else [])

        def segs(N):
            return [(0, T)] + ([(T, N - T)] if N > T else [])

        def mm_acc(i, N, steps, reads, first, last):
            fl = []
            for (c0, n) in segs(N):
                for k, (lh, rf) in enumerate(steps):
                    fl.append(lambda e, c0=c0, n=n, lh=lh, rf=rf, k=k: e.matmul(
                        acc_seg(i, c0, n), lh, rf(c0, n), start=(first and k == 0), stop=(last and k == len(steps) - 1)))
            P.op(pe, fl, reads=reads, writes=acc_bufs(i, N))

        def proj(name, tile0, N, rhs_bufs, rhs_fn, i):
            for half in range(2):
                wt, wb = wload(name, tile0 + half, WC)
                w3 = wt[:, 0:WC].rearrange("p (k c) -> p k c", c=128)
                steps = [(w3[:, k, :], (lambda c0, n, kc=half * KH + k: rhs_fn(kc, c0, n))) for k in range(KH)]
                mm_acc(i, N, steps, [wb] + list(rhs_bufs), first=(half == 0), last=(half == 1))

        def rsqrt_inplace(ap, buf):
            P.op(act, lambda e: e.activation(out=ap, in_=ap, func=AF.Sqrt), reads=[buf], writes=[buf])
            P.op(dve, lambda e: e.reciprocal(out=ap, in_=ap), reads=[buf], writes=[buf])

        def fm_norm(N):
            i = acc_get()
            for fc in range(FC):
                j = fc % 2
                P.op(act, lambda e, fc=fc, j=j: e.activation(out=sqt[j][:, 0:N], in_=x_sb[:, fc, 0:N], func=AF.Square),
                     reads=[b_x[fc]], writes=[b_sq[j]])
                mm_acc(i, N, [(onesb, lambda c0, n, j=j: sqt[j][:, c0:c0 + n])], [b_sq[j]], first=(fc == 0), last=(fc == FC - 1))
            for (c0, n) in segs(N):
                P.op(dve, lambda e, c0=c0, n=n: e.tensor_scalar(out=rstd_t[:, c0:c0 + n], in0=acc_seg(i, c0, n), scalar1=1.0 / D,
                                                                 scalar2=EPS, op0=ALU.mult, op1=ALU.add),
                     reads=acc_bufs(i, N), writes=[b_rstd])
            rsqrt_inplace(rstd_t[:, 0:N], b_rstd)

        def fm_norm_apply(N, gain_idx):
            for fc in range(FC):
                P.op(dve, lambda e, fc=fc: e.scalar_tensor_tensor(out=h_t[:, fc, 0:N], in0=x_sb[:, fc, 0:N],
                                                                  scalar=norms_t[:, gain_idx, fc:fc + 1], in1=rstd_t[:, 0:N],
                                                                  op0=ALU.mult, op1=ALU.mult),
                     reads=[b_x[fc], b_rstd], writes=[b_h[fc]])

        actv = aux.rearrange("p a b -> p (a b)")[:, 0:cfg.QMAX * NX].rearrange("p (a b) -> p a b", b=NX)
        hrd = lambda kc, c0, n: h_t[:, kc, c0:c0 + n]

        def ffn(l, N):
            fm_norm(N)
            fm_norm_apply(N, 1 + 2 * l)
            s0 = 0
            for q in range(cfg.NQ):
                nq = cfg.QS[q]
                for sl in range(nq):
                    s = s0 + sl
                    ig = acc_get()
                    proj(f"w_fin{l}", (s * 2 + 0) * 2, N, b_h, hrd, ig)
                    iu = acc_get()
                    proj(f"w_fin{l}", (s * 2 + 1) * 2, N, b_h, hrd, iu)
                    j = s % 2
                    for (c0, n) in segs(N):
                        P.op(act, lambda e, c0=c0, n=n, j=j, ig=ig: e.activation(out=gtmp[j][:, c0:c0 + n], in_=acc_seg(ig, c0, n), func=AF.Silu),
                             reads=acc_bufs(ig, N), writes=[b_gtmp[j]])
                    for (c0, n) in segs(N):
                        P.op(dve, lambda e, c0=c0, n=n, j=j, iu=iu, sl=sl: e.tensor_tensor(out=actv[:, sl, c0:c0 + n], in0=gtmp[j][:, c0:c0 + n],
                                                                                        in1=acc_seg(iu, c0, n), op=ALU.mult),
                             reads=acc_bufs(iu, N) + [b_gtmp[j]], writes=[b_aux[sl]])
                for fc in range(FC):
                    wt, wb = wload(f"w_fout{l}", q * FC + fc, cfg.QMAX * 128)
                    w3 = wt[:, 0:cfg.QMAX * 128].rearrange("p (k c) -> p k c", c=128)
                    i = acc_get()
                    steps = [(w3[:, k, :], (lambda c0, n, k=k: actv[:, k, c0:c0 + n])) for k in range(nq)]
                    mm_acc(i, N, steps, [wb] + b_aux[0:nq], first=True, last=True)
                    for (c0, n) in segs(N):
                        P.op(dve, lambda e, c0=c0, n=n, fc=fc, i=i: e.tensor_tensor(out=x_sb[:, fc, c0:c0 + n], in0=x_sb[:, fc, c0:c0 + n],
                                                                                 in1=acc_seg(i, c0, n), op=ALU.add),
                             reads=acc_bufs(i, N) + [b_x[fc]], writes=[b_x[fc]])
                s0 += nq

        bk6 = bank6b[:, :]
        psTk = bk6[:, 0:NPB * 128].rearrange("p (a b) -> p a b", b=128)
        psTv = bk6[:, 512:512 + NPB * 128].rearrange("p (a b) -> p a b", b=128)
        psTks = bk6[:, 0:128]; psTvs = bk6[:, 512:640]
        psATs = banks[7][:, 320:352]
        HW_ = D // 2

        def round_body(r):
            last_round = (r == R - 1)
            N = NX if last_round else T
            P.barrier()
            tok_blocks = [(xp[r * T + tbk * 128: r * T + tbk * 128 + 128, :], 128, tbk * 128) for tbk in range(NPB)]
            if last_round:
                tok_blocks.append((xs[0:EX, :], EX, T))
            for (src, nt, c0) in tok_blocks:
                for hf in range(2):
                    P.dma(sp, xst[hf][0:nt, :], src[:, hf * HW_:(hf + 1) * HW_], b_xst[hf], writes=[b_xst[hf]])
                    P.op(act, lambda e, hf=hf, nt=nt: e.activation(out=junk[0:nt, :], in_=xst[hf][0:nt, :], func=AF.Square,
                                                                 accum_out=ssq[0:nt, hf:hf + 1]),
                         reads=[b_xst[hf]], writes=[b_ssq, b_junk])
                P.op(dve, lambda e, nt=nt: e.tensor_tensor(out=ssq[0:nt, 2:3], in0=ssq[0:nt, 0:1], in1=ssq[0:nt, 1:2], op=ALU.add),
                     reads=[b_ssq], writes=[b_ssq])
                P.op(dve, lambda e, nt=nt: e.tensor_scalar(out=ssq[0:nt, 2:3], in0=ssq[0:nt, 2:3], scalar1=1.0 / D, scalar2=EPS,
                                                          op0=ALU.mult, op1=ALU.add), reads=[b_ssq], writes=[b_ssq])
                P.op(act, lambda e, nt=nt: e.activation(out=ssq[0:nt, 2:3], in_=ssq[0:nt, 2:3], func=AF.Sqrt), reads=[b_ssq], writes=[b_ssq])
                P.op(dve, lambda e, nt=nt: e.reciprocal(out=ssq[0:nt, 3:4], in_=ssq[0:nt, 2:3]), reads=[b_ssq], writes=[b_ssq])
                for hf in range(2):
                    P.op(dve, lambda e, hf=hf, nt=nt: e.tensor_scalar(out=xst[hf][0:nt, :], in0=xst[hf][0:nt, :], scalar1=ssq[0:nt, 3:4],
                                                                    scalar2=None, op0=ALU.mult), reads=[b_ssq, b_xst[hf]], writes=[b_xst[hf]])
                    for f4 in range(0, FC // 2, 4):
                        nf = min(4, FC // 2 - f4)
                        i = acc_get()
                        P.op(pe, [lambda e, hf=hf, f=f, f4=f4, nt=nt, i=i: e.transpose(banks[i][:, (f - f4) * 128:(f - f4) * 128 + nt],
                                                                                         xst[hf][0:nt, f * 128:(f + 1) * 128], identf[0:nt, 0:nt])
                                  for f in range(f4, f4 + nf)], reads=[b_xst[hf]], writes=[b_bank[i]])
                        for f in range(f4, f4 + nf):
                            fcg = hf * (FC // 2) + f
                            P.op(dve, lambda e, f=f, f4=f4, fcg=fcg, nt=nt, c0=c0, i=i: e.tensor_scalar(
                                out=h_t[:, fcg, c0:c0 + nt], in0=banks[i][:, (f - f4) * 128:(f - f4) * 128 + nt],
                                scalar1=norms_t[:, 0, fcg:fcg + 1], scalar2=None, op0=ALU.mult),
                                reads=[b_bank[i]], writes=[b_h[fcg]])
            P.barrier()
            chk("p0")

            def grp_local(grp):
                gs_ol = gs_ol2[grp % 2]; gs_qc = gs_qc2[grp % 2]; gs_sg = gs_sg2[grp % 2]; b_gs = b_gs2[grp % 2]
                LD = LD2[grp % 2]; bLD = bt[f"LD{grp % 2}"]
                if last_round:
                    for q in range(2):
                        P.dma(sp, S0s[:, q, :, :], sh[q, grp * G:(grp + 1) * G, :, :].rearrange("h k v -> k h v"), bt["S0s"], writes=[bt["S0s"]])
                def emit_proj(hg):
                    hd = grp * G + hg
                    tv = tvs[hg % 2]; btv = bt[f"tv{hg % 2}"]
                    iq = acc_get(); proj("w_hin", (hd * 4 + 0) * 2, N, b_h, hrd, iq)
                    if_ = acc_get(); proj("w_hin", (hd * 4 + 1) * 2, N, b_h, hrd, if_)
                    ii = acc_get(); proj("w_hin", (hd * 4 + 2) * 2, N, b_h, hrd, ii)
                    ig = acc_get(); proj("w_hin", (hd * 4 + 3) * 2, N, b_h, hrd, ig)
                    for (c0, n) in segs(N):
                        P.op(act, lambda e, c0=c0, n=n: e.activation(out=tq[:, c0:c0 + n], in_=acc_seg(iq, c0, n), func=AF.Silu),
                             reads=acc_bufs(iq, N), writes=[bt["tq"]])
                        P.op(act, lambda e, c0=c0, n=n: e.activation(out=gs_sg[:, hg, c0:c0 + n], in_=acc_seg(ig, c0, n), func=AF.Silu),
                             reads=acc_bufs(ig, N), writes=[b_gs[hg]])
                        P.op(act, lambda e, c0=c0, n=n: e.activation(out=tsig[:, c0:c0 + n], in_=acc_seg(if_, c0, n), func=AF.Sigmoid),
                             reads=acc_bufs(if_, N), writes=[bt["tsig"]])
                        P.op(dve, lambda e, c0=c0, n=n: e.tensor_copy(out=tv[:, c0:c0 + n], in_=acc_seg(ii, c0, n)),
                             reads=acc_bufs(ii, N), writes=[btv])
                    hstate[hg] = (iq, if_, ii, ig)

                def emit_chain(hg):
                    hd = grp * G + hg
                    P.op(act, lambda e: e.activation(out=tg[:, 0:N], in_=tsig[:, 0:N], func=AF.Ln, scale=oml_t[:, hd:hd + 1], bias=lb_t[:, hd:hd + 1]),
                         reads=[bt["tsig"]], writes=[bt["tg"]])
                    P.op(dve, lambda e: e.tensor_scalar(out=tk[:, 0:N], in0=tsig[:, 0:N], scalar1=noml_t[:, hd:hd + 1], scalar2=oml_t[:, hd:hd + 1],
                                                        op0=ALU.mult, op1=ALU.add), reads=[bt["tsig"]], writes=[bt["tk"]])
                    P.op(dve, lambda e: e.tensor_tensor_scan(out=tb[:, 0:N], data0=smask[:, 0:N], data1=tg[:, 0:N], initial=0.0,
                                                             op0=ALU.mult, op1=ALU.add), reads=[bt["tg"]], writes=[bt["tb"]])
                    P.op(act, lambda e: e.activation(out=teb[:, 0:N], in_=tb[:, 0:N], func=AF.Exp), reads=[bt["tb"]], writes=[bt["tsig"]])
                    P.op(act, lambda e: e.activation(out=tenb[:, 0:N], in_=tb[:, 0:N], func=AF.Exp, scale=-1.0), reads=[bt["tb"]], writes=[bt["tg"]])
                    P.op(dve, lambda e: e.tensor_tensor(out=tqt[:, 0:N], in0=tq[:, 0:N], in1=teb[:, 0:N], op=ALU.mult),
                         reads=[bt["tq"], bt["tsig"]], writes=[bt["tqt"]])
                    P.op(dve, lambda e: e.tensor_tensor(out=tkt[:, 0:N], in0=tk[:, 0:N], in1=tenb[:, 0:N], op=ALU.mult),
                         reads=[bt["tk"], bt["tg"]], writes=[bt["tkt"]])
                    chk("h_act")
                    bl = tb[:, 0:T].rearrange("p (c t) -> p c t", t=64)[:, :, 63]
                    P.op(act, lambda e: e.activation(out=td[:, 0:NCH], in_=bl, func=AF.Exp), reads=[bt["tb"]], writes=[bt["td"]])
                    if last_round:
                        bls = tb[:, T:NX].rearrange("p (c t) -> p c t", t=16)[:, :, 15]
                        P.op(act, lambda e: e.activation(out=td[:, NCH:NCH + 2], in_=bls, func=AF.Exp), reads=[bt["tb"]], writes=[bt["td"]])
                    P.op(dve, lambda e: e.tensor_tensor_scan(out=tci, data0=ones_f[:, 0:NCH], data1=bl, initial=0.0, op0=ALU.mult, op1=ALU.add),
                         reads=[bt["tb"]], writes=[bt["tci"]])
                    P.op(act, lambda e: e.activation(out=tE, in_=tci, func=AF.Exp), reads=[bt["tci"]], writes=[bt["tE"]])
                    P.op(dve, lambda e: e.tensor_copy(out=gs_qc[:, hg, 0:64], in_=tqt[:, 0:64]), reads=[bt["tqt"]], writes=[b_gs[hg]])
                    if NCH > 1:
                        P.op(dve, lambda e: e.tensor_tensor(out=gs_qc[:, hg, 64:T].rearrange("p (c t) -> p c t", t=64),
                                                            in0=tqt[:, 64:T].rearrange("p (c t) -> p c t", t=64),
                                                            in1=tE[:, 0:NCH - 1].unsqueeze(2).to_broadcast([128, NCH - 1, 64]), op=ALU.mult),
                             reads=[bt["tqt"], bt["tE"]], writes=[b_gs[hg]])

                def emit_rec(hg):
                    hd = grp * G + hg
                    tv = tvs[hg % 2]; btv = bt[f"tv{hg % 2}"]
                    chk("h_dec")
                    P.op(pe, [lambda e, pb=pb: e.transpose(psTk[:, pb, :], tkt[:, pb * 128:(pb + 1) * 128], identb) for pb in range(NPB)]
                         + [lambda e, pb=pb: e.transpose(psTv[:, pb, :], tv[:, pb * 128:(pb + 1) * 128], identb) for pb in range(NPB)],
                         reads=[bt["tkt"], btv], writes=[b_bank[6]])
                    chk("h_tr0")
                    P.op(act, lambda e: e.activation(out=ktok, in_=psTk, func=AF.Copy), reads=[b_bank[6]], writes=[bt["ktok"]])
                    chk("h_tr1")
                    P.op(act, lambda e: e.activation(out=vtok, in_=psTv, func=AF.Copy), reads=[b_bank[6]], writes=[bt["vtok"]])
                    if last_round:
                        P.op(pe, [lambda e: e.transpose(psTks[0:32, :], tkt[:, T:NX], identb), lambda e: e.transpose(psTvs[0:32, :], tv[:, T:NX], identb)],
                             reads=[bt["tkt"], btv], writes=[b_bank[6]])
                        P.op(act, lambda e: e.activation(out=vtoks, in_=psTvs[0:32, :], func=AF.Copy), reads=[b_bank[6]], writes=[bt["kts"]])
                        P.op(act, lambda e: e.activation(out=ktoks, in_=psTks[0:32, :], func=AF.Copy), reads=[b_bank[6]], writes=[bt["kts"]])
                        for q in range(2):
                            P.op(dve, lambda e, q=q: e.tensor_scalar(out=kt01[:, q, :], in0=ktoks, scalar1=sm01[:, q:q + 1], scalar2=None, op0=ALU.mult),
                                 reads=[bt["kts"]], writes=[bt["kts"]])
                    chk("h_tr")
                    for j in range(2):
                        P.op(dve, lambda e, j=j: e.tensor_scalar(out=ktokm[:, :, j, :], in0=ktok, scalar1=pm64[:, j:j + 1], scalar2=None, op0=ALU.mult),
                             reads=[bt["ktok"]], writes=[bt["ktokm"]])
                    iU = []
                    for half in range(0, NCH, 4):
                        i = acc_get()
                        iU.append(i)
                        P.op(pe, [lambda e, c=c, i=i, half=half: e.matmul(banks[i][:, (c - half) * 128:(c - half + 1) * 128],
                                                                           ktokm[:, c // 2, c % 2, :], vtok[:, c // 2, :], start=True, stop=True)
                                  for c in range(half, min(NCH, half + 4))], reads=[bt["ktokm"], bt["vtok"]], writes=[b_bank[i]])
                    if last_round:
                        iUs = acc_get()
                        P.op(pe, [lambda e, q=q: e.matmul(banks[iUs][:, q * 128:(q + 1) * 128], kt01[:, q, :], vtoks, start=True, stop=True) for q in range(2)],
                             reads=[bt["kts"]], writes=[b_bank[iUs]])
                    chk("h_U")
                    Ubank = lambda c: banks[iU[c // 4]][:, (c % 4) * 128:(c % 4 + 1) * 128]
                    bU = lambda c: b_bank[iU[c // 4]]
                    cur = 0
                    for c in range(NCH):
                        fin = (c == NCH - 1)
                        dst = LD[:, hg, 0:128] if fin else Mf[1 - cur]
                        dbuf = bLD if fin else bt[f"M{1 - cur}"]
                        if c == 0:
                            P.op(dve, lambda e, dst=dst: e.tensor_scalar(out=dst, in0=Ubank(0), scalar1=td[:, 0:1], scalar2=None, op0=ALU.mult),
                                 reads=[bU(0), bt["td"]], writes=[dbuf])
                        else:
                            P.op(dve, lambda e, c=c, cur=cur: e.tensor_scalar(out=Mtmp, in0=Mf[cur], scalar1=td[:, c:c + 1], scalar2=None, op0=ALU.mult),
                                 reads=[bt[f"M{cur}"], bt["td"]], writes=[bt["Mtmp"]])
                            P.op(dve, lambda e, c=c, dst=dst: e.scalar_tensor_tensor(out=dst, in0=Ubank(c), scalar=td[:, c:c + 1], in1=Mtmp,
                                                                                   op0=ALU.mult, op1=ALU.add),
                                 reads=[bU(c), bt["td"], bt["Mtmp"]], writes=[dbuf])
                        if not fin:
                            P.op(act, lambda e, c=c, cur=cur: e.activation(out=Mb[:, c + 1, :], in_=Mf[1 - cur], func=AF.Copy),
                                 reads=[bt[f"M{1 - cur}"]], writes=[bt["Mb"]])
                        cur = 1 - cur
                    P.op(act, lambda e: e.activation(out=LD[:, hg, 128:129], in_=tE[:, NCH - 1:NCH], func=AF.Copy), reads=[bt["tE"]], writes=[bLD])
                    if last_round:
                        for q in range(2):
                            P.op(act, lambda e, q=q: e.activation(out=Mbs[:, q, :], in_=S0s[:, q, hg, :], func=AF.Copy), reads=[bt["S0s"]], writes=[bt["Mbs"]])
                            P.op(dve, lambda e, q=q: e.tensor_scalar(out=Mtmp, in0=S0s[:, q, hg, :], scalar1=td[:, NCH + q:NCH + q + 1], scalar2=None, op0=ALU.mult),
                                 reads=[bt["S0s"], bt["td"]], writes=[bt["Mtmp"]])
                            P.op(dve, lambda e, q=q: e.scalar_tensor_tensor(out=NSs[:, q, hg, :], in0=banks[iUs][:, q * 128:(q + 1) * 128],
                                                                          scalar=td[:, NCH + q:NCH + q + 1], in1=Mtmp, op0=ALU.mult, op1=ALU.add),
                                 reads=[b_bank[iUs], bt["td"], bt["Mtmp"]], writes=[bt["NSs"]])
                    chk("h_chain")
                    iA = acc_get()
                    P.op(pe, [lambda e, pb=pb: e.matmul(banks[iA][:, pb * 128:(pb + 1) * 128], tkt[:, pb * 128:(pb + 1) * 128],
                                                        tqt[:, pb * 128:(pb + 1) * 128], start=True, stop=True) for pb in range(NPB)]
                         + ([lambda e: e.matmul(psATs[0:32, :], tkt[:, T:NX], tqt[:, T:NX], start=True, stop=True)] if last_round else []),
                         reads=[bt["tkt"], bt["tqt"]], writes=[b_bank[iA]] + ([b_bank[7]] if last_round else []))
                    P.op(dve, lambda e: e.tensor_tensor(out=atm, in0=banks[iA][:, 0:NPB * 128].rearrange("p (a b) -> p a b", b=128),
                                                        in1=maskA.unsqueeze(1).to_broadcast([128, NPB, 128]), op=ALU.mult),
                         reads=[b_bank[iA]], writes=[bt["atm"]])
                    if last_round:
                        P.op(dve, lambda e: e.tensor_tensor(out=atms, in0=psATs[0:32, :], in1=maskS, op=ALU.mult),
                             reads=[b_bank[7]], writes=[bt["atm"]])
                    chk("h_A")
                    iO = acc_get()
                    fl = []
                    for c in range(NCH):
                        pb, j = c // 2, c % 2
                        fl.append(lambda e, c=c, pb=pb, j=j: e.matmul(banks[iO][:, c * 64:(c + 1) * 64], vtok[:, pb, :], atm[:, pb, j * 64:(j + 1) * 64],
                                                                     start=True, stop=False))
                        fl.append(lambda e, c=c: e.matmul(banks[iO][:, c * 64:(c + 1) * 64], (zerob if c == 0 else Mb[:, c, :]), tqt[:, c * 64:(c + 1) * 64],
                                                         start=False, stop=True))
                    if last_round:
                        for q in range(2):
                            fl.append(lambda e, q=q: e.matmul(acc_seg(iO, T + 16 * q, 16), vtoks, atms[:, q * 16:(q + 1) * 16], start=True, stop=False))
                            fl.append(lambda e, q=q: e.matmul(acc_seg(iO, T + 16 * q, 16), Mbs[:, q, :], tqt[:, T + 16 * q:T + 16 * q + 16], start=False, stop=True))
                    P.op(pe, fl, reads=[bt["vtok"], bt["atm"], bt["Mb"], bt["tqt"], bt["kts"], bt["Mbs"]], writes=acc_bufs(iO, N))
                    for (c0, n) in segs(N):
                        P.op(act, lambda e, c0=c0, n=n: e.activation(out=gs_ol[:, hg, c0:c0 + n], in_=acc_seg(iO, c0, n), func=AF.Copy),
                             reads=acc_bufs(iO, N), writes=[b_gs[hg]])

                hstate = {}
                emit_proj(0)
                emit_chain(0)
                for hg in range(G):
                    if hg + 1 < G:
                        emit_proj(hg + 1)
                    emit_rec(hg)
                    if hg + 1 < G:
                        emit_chain(hg + 1)
                chk("p1h")
                if last_round:
                    for q in range(2):
                        P.dma(sp, hs[q, grp * G:(grp + 1) * G, :, :].rearrange("h k v -> k h v"), NSs[:, q, :, :], bt["NSs"], reads=[bt["NSs"]], writes=[bt["NSs"]])
                    out_bufs.append(bt["NSs"])

            def grp_start_ex(grp):
                gs_ol = gs_ol2[grp % 2]; gs_qc = gs_qc2[grp % 2]; gs_sg = gs_sg2[grp % 2]; b_gs = b_gs2[grp % 2]
                LD = LD2[grp % 2]; bLD = bt[f"LD{grp % 2}"]
                P.dma(sp, agh_in.rearrange("(g k) c -> k g c", k=128), LD, b_agh_in, reads=[bLD], writes=[b_agh_in])
                allgather(agh_in, agh_mid, agh, [b_agh_in], b_agh, b_agh_mid)

            def grp_finish(grp):
                gs_ol = gs_ol2[grp % 2]; gs_qc = gs_qc2[grp % 2]; gs_sg = gs_sg2[grp % 2]; b_gs = b_gs2[grp % 2]
                LD = LD2[grp % 2]; bLD = bt[f"LD{grp % 2}"]
                if r == 0:
                    P.op(dve, lambda e: e.memset(S_t, 0.0), writes=[bt["S"]])
                else:
                    P.dma(sp, S_t, sround[:, grp * G:(grp + 1) * G, :], bt["S"], reads=[b_sr[grp]], writes=[bt["S"]])
                P.op(dve, lambda e: e.memset(Sin, 0.0), writes=[bt["Sin"]])
                for j in range(NCORES):
                    jj = j % 2
                    P.dma(sp, LDj[jj], agh[j * G * 128:(j + 1) * G * 128, :].rearrange("(g k) c -> k g c", k=128), b_ld[jj], reads=[b_agh], writes=[b_ld[jj]])
                    P.op(dve, lambda e, j=j: e.scalar_tensor_tensor(out=Sin, in0=S_t, scalar=rm_t[:, j:j + 1], in1=Sin, op0=ALU.mult, op1=ALU.add),
                         reads=[bt["S"], bt["Sin"]], writes=[bt["Sin"]])
                    P.op(dve, lambda e, jj=jj: e.tensor_tensor(out=Stmp, in0=S_t, in1=LDj[jj][:, :, 128:129].to_broadcast([128, G, 128]), op=ALU.mult),
                         reads=[bt["S"], b_ld[jj]], writes=[bt["Stmp"]])
                    P.op(dve, lambda e, jj=jj: e.tensor_tensor(out=S_t, in0=Stmp, in1=LDj[jj][:, :, 0:128], op=ALU.add),
                         reads=[bt["Stmp"], b_ld[jj]], writes=[bt["S"]])
                if last_round:
                    P.dma(sp, hp[grp * G:(grp + 1) * G, :, :].rearrange("h k v -> k h v"), S_t, b_sr[grp], reads=[bt["S"]], writes=[b_sr[grp]])
                    out_bufs.append(b_sr[grp])
                else:
                    P.dma(sp, sround[:, grp * G:(grp + 1) * G, :], S_t, b_sr[grp], reads=[bt["S"]], writes=[b_sr[grp]])
                P.op(act, lambda e: e.activation(out=M0b, in_=Sin, func=AF.Copy), reads=[bt["Sin"]], writes=[bt["M0b"]])
                for hg in range(G):
                    hd = grp * G + hg
                    iC = acc_get()
                    P.op(pe, lambda e: e.matmul(banks[iC][:, 0:T], M0b[:, hg, :], gs_qc[:, hg, 0:T], start=True, stop=True),
                         reads=[bt["M0b"], b_gs[hg]], writes=[b_bank[iC]])
                    P.op(dve, lambda e: e.tensor_tensor(out=to32[:, 0:T], in0=banks[iC][:, 0:T], in1=gs_ol[:, hg, 0:T], op=ALU.add),
                         reads=[b_bank[iC], b_gs[hg]], writes=[bt["to32"]])
                    if last_round:
                        P.op(dve, lambda e: e.tensor_copy(out=to32[:, T:NX], in_=gs_ol[:, hg, T:NX]), reads=[b_gs[hg]], writes=[bt["to32"]])
                    P.op(act, lambda e: e.activation(out=osq[:, 0:N], in_=to32[:, 0:N], func=AF.Square), reads=[bt["to32"]], writes=[bt["osq"]])
                    iS = acc_get()
                    mm_acc(iS, N, [(onesb, lambda c0, n: osq[:, c0:c0 + n])], [bt["osq"]], first=True, last=True)
                    for (c0, n) in segs(N):
                        P.op(dve, lambda e, c0=c0, n=n: e.tensor_scalar(out=rs2[:, c0:c0 + n], in0=acc_seg(iS, c0, n), scalar1=1.0 / 128, scalar2=EPS,
                                                                         op0=ALU.mult, op1=ALU.add), reads=acc_bufs(iS, N), writes=[bt["rs2"]])
                    rsqrt_inplace(rs2[:, 0:N], bt["rs2"])
                    P.op(dve, lambda e: e.scalar_tensor_tensor(out=t1[:, 0:N], in0=to32[:, 0:N], scalar=og_t[:, 0:1], in1=rs2[:, 0:N], op0=ALU.mult, op1=ALU.mult),
                         reads=[bt["to32"], bt["rs2"]], writes=[bt["to32"]])
                    P.op(dve, lambda e: e.tensor_tensor(out=aux[:, hd, 0:N], in0=t1[:, 0:N], in1=gs_sg[:, hg, 0:N], op=ALU.mult),
                         reads=[bt["to32"], b_gs[hg]], writes=[b_aux[hd]])


            grp_local(0)
            grp_start_ex(0)
            for grp in range(1, cfg.NG):
                grp_local(grp)
                grp_finish(grp - 1)
                grp_start_ex(grp)
            grp_finish(cfg.NG - 1)
            chk("p1")
            P.barrier()
            for fc in range(FC):
                j = fc % 2
                xr = ostage[j][:, 0:(NPB + 1) * 128].rearrange("p (a b) -> p a b", b=128)
                for tbk in range(NPB):
                    P.dma(sp, xr[:, tbk, :], xp[r * T + tbk * 128:r * T + tbk * 128 + 128, fc * 128:(fc + 1) * 128], b_ost[j], writes=[b_ost[j]])
                if last_round:
                    P.dma(sp, xr[0:EX, NPB, :], xs[:, fc * 128:(fc + 1) * 128], b_ost[j], writes=[b_ost[j]])
                i = acc_get()
                for half in range(2):
                    wt, wb = wload("w_hout", fc * 2 + half, WC)
                    w3 = wt[:, 0:WC].rearrange("p (k c) -> p k c", c=128)
                    fl = []
                    for (c0, n) in segs(N):
                        for k in range(KH):
                            kc = half * KH + k
                            fl.append(lambda e, c0=c0, n=n, k=k, kc=kc, w3=w3, half=half: e.matmul(acc_seg(i, c0, n), w3[:, k, :], aux[:, kc, c0:c0 + n],
                                                                                                    start=(half == 0 and k == 0), stop=False))
                    rd = [wb] + b_aux[0:FC]
                    if half == 1:
                        for tbk in range(NPB):
                            fl.append(lambda e, tbk=tbk, xr=xr: e.matmul(banks[i][:, tbk * 128:(tbk + 1) * 128], xr[:, tbk, :], identf, start=False, stop=True))
                        if last_round:
                            fl.append(lambda e, xr=xr: e.matmul(acc_seg(i, T, EX), xr[0:EX, NPB, :], identf[0:EX, 0:EX], start=False, stop=True))
                        rd = rd + [b_ost[j]]
                    P.op(pe, fl, reads=rd, writes=acc_bufs(i, N))
                for (c0, n) in segs(N):
                    P.op(act, lambda e, c0=c0, n=n, fc=fc, i=i: e.activation(out=x_sb[:, fc, c0:c0 + n], in_=acc_seg(i, c0, n), func=AF.Copy),
                         reads=acc_bufs(i, N), writes=[b_x[fc]])
            chk("p2")
            ffn(0, N)
            chk("ffn0")
            fm_norm(N)
            fm_norm_apply(N, 2)
            cuh = aux
            offs = [(2, 0, T)] + ([(T + 4, T, 16), (T + 22, T + 16, 16)] if last_round else [])
            cst = ostage[0][:, 0:4 * FC].rearrange("p (q t f) -> p q t f", q=2, t=2)
            for fc in range(FC):
                igc = acc_get(); proj("w_cin", (fc * 2 + 0) * 2, N, b_h, hrd, igc)
                iu = acc_get(); proj("w_cin", (fc * 2 + 1) * 2, N, b_h, hrd, iu)
                j = fc % 2
                for (c0, n) in segs(N):
                    P.op(act, lambda e, c0=c0, n=n, j=j, igc=igc: e.activation(out=gtmp[j][:, c0:c0 + n], in_=acc_seg(igc, c0, n), func=AF.Copy),
                         reads=acc_bufs(igc, N), writes=[b_gtmp[j]])
                    P.op(dve, lambda e, c0=c0, n=n, j=j, iu=iu: e.tensor_tensor(out=gtmp[j][:, c0:c0 + n], in0=gtmp[j][:, c0:c0 + n], in1=acc_seg(iu, c0, n), op=ALU.mult),
                         reads=acc_bufs(iu, N) + [b_gtmp[j]], writes=[b_gtmp[j]])
                for (co, ct, n) in offs:
                    P.op(act, lambda e, co=co, ct=ct, n=n, fc=fc, j=j: e.activation(out=cuh[:, fc, co:co + n], in_=gtmp[j][:, ct:ct + n], func=AF.Copy),
                         reads=[b_gtmp[j]], writes=[b_aux[fc]])
                P.op(dve, lambda e, fc=fc, j=j: e.tensor_copy(out=hal_t[:, :, fc], in_=gtmp[j][:, T - 2:T]), reads=[b_gtmp[j]], writes=[b_hal])
                if last_round:
                    for q in range(2):
                        P.op(dve, lambda e, fc=fc, j=j, q=q: e.tensor_copy(out=cst[:, q, :, fc], in_=gtmp[j][:, T + 16 * q + 14:T + 16 * q + 16]),
                             reads=[b_gtmp[j]], writes=[b_ost[0]])
            hal2 = hal_t.rearrange("p a b -> p (a b)")
            P.dma(sp, agc_in, hal2, b_agc_in, reads=[b_hal], writes=[b_agc_in])
            allgather(agc_in, agc_mid, agc, [b_agc_in], b_agc, b_agc_mid)
            P.dma(sp, halg, agc.rearrange("(r p) c -> p r c", p=128), b_halg, reads=[b_agc], writes=[b_halg])
            halo2 = halo.rearrange("p a b -> p (a b)")
            P.op(dve, lambda e: e.tensor_scalar(out=halo2, in0=prev7, scalar1=rm_t[:, 16:17], scalar2=None, op0=ALU.mult),
                 reads=[b_prev7, b_halg], writes=[b_halo])
            for j in range(NCORES):
                P.op(dve, lambda e, j=j: e.scalar_tensor_tensor(out=halo2, in0=halg[:, j, :], scalar=rm_t[:, 8 + j:9 + j], in1=halo2, op0=ALU.mult, op1=ALU.add),
                     reads=[b_halg, b_halo], writes=[b_halo])
            P.op(dve, lambda e: e.tensor_copy(out=prev7, in_=halg[:, NCORES - 1, :]), reads=[b_halg, b_halo], writes=[b_prev7])
            iT = acc_get()
            P.op(pe, lambda e: e.transpose(banks[iT][0:FC * 2, 0:128], hal2, identf), reads=[b_hal], writes=[b_bank[iT]])
            P.op(act, lambda e: e.activation(out=ostage[1][0:FC * 2, 0:128], in_=banks[iT][0:FC * 2, 0:128], func=AF.Copy), reads=[b_bank[iT]], writes=[b_ost[1]])
            P.dma(sp, cp[r, :, :].rearrange("t (f p) -> (t f) p", p=128), ostage[1][0:FC * 2, 0:128], b_ost[1], reads=[b_ost[1]], writes=[b_ost[1]])
            if last_round:
                for q in range(2):
                    iT = acc_get()
                    P.op(pe, lambda e, q=q, iT=iT: e.transpose(banks[iT][0:FC * 2, 0:128], ostage[0][:, q * FC * 2:(q + 1) * FC * 2], identf),
                         reads=[b_ost[0]], writes=[b_bank[iT]])
                    P.op(act, lambda e, q=q, iT=iT: e.activation(out=ostage[1][0:FC * 2, 128 * (q + 1):128 * (q + 2)], in_=banks[iT][0:FC * 2, 0:128], func=AF.Copy),
                         reads=[b_bank[iT]], writes=[b_ost[1]])
                    P.dma(sp, cs[q, :, :].rearrange("t (f p) -> (t f) p", p=128), ostage[1][0:FC * 2, 128 * (q + 1):128 * (q + 2)], b_ost[1],
                          reads=[b_ost[1]], writes=[b_ost[1]])
            out_bufs.append(b_ost[1])
            for fc in range(FC):
                P.op(act, lambda e, fc=fc: e.activation(out=cuh[:, fc, 0:2], in_=halo[:, :, fc], func=AF.Copy), reads=[b_halo], writes=[b_aux[fc]])
            if last_round:
                P.dma(sp, ostage[0][0:4 * FC, 0:128], sc.rearrange("q t (f p) -> (q t f) p", p=128), b_ost[0], writes=[b_ost[0]])
                iT = acc_get()
                P.op(pe, lambda e: e.transpose(banks[iT][:, 0:4 * FC], ostage[0][0:4 * FC, 0:128], identf[0:4 * FC, 0:4 * FC]), reads=[b_ost[0]], writes=[b_bank[iT]])
                scv = banks[iT][:, 0:4 * FC].rearrange("p (q t f) -> p q t f", q=2, t=2)
                for q in range(2):
                    for t_ in range(2):
                        co = (T + 2 if q == 0 else T + 20) + t_
                        P.op(act, lambda e, q=q, t_=t_, co=co: e.activation(out=cuh[:, :, co], in_=scv[:, q, t_, :], func=AF.Copy),
                             reads=[b_bank[iT]], writes=b_aux[0:FC])
            for fc in range(FC):
                igb = acc_get(); proj("w_cin", FC * 4 + fc * 2, N, b_h, hrd, igb)
                j = fc % 2
                for (co, ct, n) in offs:
                    P.op(dve, lambda e, co=co, ct=ct, n=n, fc=fc, j=j: e.tensor_scalar(out=gtmp[j][:, ct:ct + n], in0=cuh[:, fc, co:co + n],
                                                                                   scalar1=cw_t[:, 2, fc:fc + 1], scalar2=None, op0=ALU.mult),
                         reads=[b_aux[fc]], writes=[b_gtmp[j]])
                    P.op(dve, lambda e, co=co, ct=ct, n=n, fc=fc, j=j: e.scalar_tensor_tensor(out=gtmp[j][:, ct:ct + n], in0=cuh[:, fc, co - 1:co - 1 + n],
                                                                                          scalar=cw_t[:, 1, fc:fc + 1], in1=gtmp[j][:, ct:ct + n], op0=ALU.mult, op1=ALU.add),
                         reads=[b_aux[fc], b_gtmp[j]], writes=[b_gtmp[j]])
                    P.op(dve, lambda e, co=co, ct=ct, n=n, fc=fc, j=j: e.scalar_tensor_tensor(out=gtmp[j][:, ct:ct + n], in0=cuh[:, fc, co - 2:co - 2 + n],
                                                                                          scalar=cw_t[:, 0, fc:fc + 1], in1=gtmp[j][:, ct:ct + n], op0=ALU.mult, op1=ALU.add),
                         reads=[b_aux[fc], b_gtmp[j]], writes=[b_gtmp[j]])
                for (c0, n) in segs(N):
                    P.op(dve, lambda e, c0=c0, n=n, fc=fc, j=j, igb=igb: e.tensor_tensor(out=cuh[:, fc, c0:c0 + n], in0=gtmp[j][:, c0:c0 + n], in1=acc_seg(igb, c0, n), op=ALU.mult),
                         reads=acc_bufs(igb, N) + [b_gtmp[j], b_aux[fc]], writes=[b_aux[fc]])
            for fc in range(FC):
                i = acc_get()
                proj("w_cout", fc * 2, N, b_aux[0:FC], lambda kc, c0, n: cuh[:, kc, c0:c0 + n], i)
                for (c0, n) in segs(N):
                    P.op(dve, lambda e, c0=c0, n=n, fc=fc, i=i: e.tensor_tensor(out=x_sb[:, fc, c0:c0 + n], in0=x_sb[:, fc, c0:c0 + n], in1=acc_seg(i, c0, n), op=ALU.add),
                         reads=acc_bufs(i, N) + [b_x[fc]], writes=[b_x[fc]])
            chk("conv")
            ffn(1, N)
            chk("ffn1")
            fm_norm(N)
            out_blocks = [(yp[r * T + tbk * 128:r * T + tbk * 128 + 128, :], 128, tbk * 128) for tbk in range(NPB)]
            if last_round:
                out_blocks.append((ys[0:EX, :], EX, T))
            FB = 1024 // 128
            oc_ = 0
            for (dst, nt, c0) in out_blocks:
                for f8 in range(0, FC, FB):
                    nf8 = min(FB, FC - f8)
                    so = oc_ % 2
                    oc_ += 1
                    for f4 in range(f8, f8 + nf8, 4):
                        nf = min(4, f8 + nf8 - f4)
                        i = acc_get()
                        for f in range(f4, f4 + nf):
                            j = f % 2
                            P.op(dve, lambda e, f=f, j=j, c0=c0, nt=nt: e.scalar_tensor_tensor(out=gtmp[j][:, 0:nt], in0=x_sb[:, f, c0:c0 + nt], scalar=norms_t[:, 4, f:f + 1],
                                                                                             in1=rstd_t[:, c0:c0 + nt], op0=ALU.mult, op1=ALU.mult),
                                 reads=[b_x[f], b_rstd], writes=[b_gtmp[j]])
                            P.op(pe, lambda e, f=f, f4=f4, j=j, nt=nt, i=i: e.transpose(banks[i][0:nt, (f - f4) * 128:(f - f4 + 1) * 128], gtmp[j][:, 0:nt], identf),
                                 reads=[b_gtmp[j]], writes=[b_bank[i]])
                        P.op(act, lambda e, f4=f4, f8=f8, nf=nf, nt=nt, i=i, so=so: e.activation(out=ostage[so][0:nt, (f4 - f8) * 128:(f4 - f8 + nf) * 128],
                                                                                                in_=banks[i][0:nt, 0:nf * 128], func=AF.Copy),
                             reads=[b_bank[i]], writes=[b_ost[so]])
                    P.dma(sp, dst[:, f8 * 128:(f8 + nf8) * 128], ostage[so][0:nt, 0:nf8 * 128], b_ost[so], reads=[b_ost[so]], writes=[b_ost[so]])
            out_bufs.extend(b_ost)

        try:
            if not _go:
                raise _Stop()
            chk("wag")
            for r in range(R):
                round_body(r)
        except _Stop:
            pass
        P.barrier()
        seen = set()
        for bf in out_bufs:
            sm = bf.dsem
            if sm is not None and id(sm) not in seen and sm.v:
                seen.add(id(sm))
                nc.sync.wait_ge(sm.h, sm.v)
        print("instructions:", P.nins, "sems:", P.nsem)
    return nc


def make_in_maps(cfg, inp):
    D, NH, FC, T, R = cfg.D, cfg.NH, cfg.FC, cfg.T, cfg.ROUNDS
    W = prep_weights(cfg, inp)
    lbl = np.ascontiguousarray(inp["hgrn_lb_logits"].reshape(3, NH, 128).transpose(2, 0, 1)).astype(np.float32)
    nv = np.stack([inp["norm_mix"][0], inp["norm_ffn"][0], inp["norm_mix"][1], inp["norm_ffn"][1], inp["norm_final"]])
    norms = np.ascontiguousarray(nv.reshape(5, FC, 128).transpose(2, 0, 1)).astype(np.float32)
    cw = np.ascontiguousarray(inp["conv_w"][0].reshape(3, FC, 128).transpose(2, 0, 1)).astype(np.float32)
    ogain = np.ascontiguousarray(inp["hgrn_out_gain"][0].reshape(128, 1)).astype(np.float32)
    xp_full = inp["x_prompt"][0]
    consts = make_consts(cfg)
    maps = []
    for c in range(NCORES):
        m = {}
        m["xp"] = np.ascontiguousarray(np.concatenate([xp_full[(r * NCORES + c) * T:(r * NCORES + c + 1) * T] for r in range(R)], axis=0))
        m["xs"] = np.ascontiguousarray(inp["x_sample"][cfg.SPC * c:cfg.SPC * (c + 1)].reshape(cfg.EX, D))
        m["sh"] = np.ascontiguousarray(inp["state_hgrn"][0, cfg.SPC * c:cfg.SPC * (c + 1)])
        m["sc"] = np.ascontiguousarray(inp["state_conv"][0, cfg.SPC * c:cfg.SPC * (c + 1)])
        m["lbl"] = lbl; m["ogain"] = ogain; m["norms"] = norms; m["cw"] = cw
        m["consts"] = consts
        rm = np.zeros((128, 17), np.float32)
        rm[:, c] = 1.0
        if c >= 1:
            rm[:, 8 + c - 1] = 1.0
        else:
            rm[:, 16] = 1.0
        m["rmask"] = rm
        for n in WNAMES:
            PT = PT_OF(n)
            RPR = PT * 16
            for pi, (t0, nt) in enumerate(wpieces(cfg, n)):
                ch = W[n][t0:t0 + nt].reshape(nt // PT, PT * 128, -1)
                blk = WINV[c]
                m[f"{n}_{pi}"] = np.ascontiguousarray(ch[:, blk * RPR:(blk + 1) * RPR, :].reshape(nt // PT * RPR, -1))
        maps.append(m)
    return maps


def assemble(cfg, res):
    D, NH, T, R = cfg.D, cfg.NH, cfg.T, cfg.ROUNDS
    yp = np.zeros((1, cfg.SEQ, D), np.float32)
    for c in range(NCORES):
        for r in range(R):
            j = r * NCORES + c
            yp[0, j * T:(j + 1) * T] = res[c]["yp"][r * T:(r + 1) * T]
    ys = np.concatenate([res[c]["ys"].reshape(cfg.SPC, cfg.DS, D) for c in range(NCORES)], axis=0)
    hp = res[0]["hp"].reshape(1, 1, NH, 128, 128)
    hs = np.concatenate([res[c]["hs"] for c in range(NCORES)], axis=0).reshape(1, cfg.DB, NH, 128, 128)
    cp = res[NCORES - 1]["cp"][R - 1].reshape(1, 1, 2, D)
    cs = np.concatenate([res[c]["cs"] for c in range(NCORES)], axis=0).reshape(1, cfg.DB, 2, D)
    return (yp, ys, hp, hs, cp, cs)


_NC_CACHE = {}


def run(cfg, inp):
    key = (cfg.D, cfg.NH, cfg.DFF, cfg.SEQ, cfg.T)
    if key not in _NC_CACHE:
        _NC_CACHE[key] = build(cfg)
    nc = _NC_CACHE[key]
    maps = make_in_maps(cfg, inp)
    res = run_bass_kernel_spmd(nc, maps, core_ids=list(range(NCORES)))
    return assemble(cfg, res.results)


def kernel(**inputs):
    inp = {k: np.asarray(v) for k, v in inputs.items()}
    return run(FULL, inp)
```
